# Optimizing a Trainium2 kernel written in Bass

```python
import math
import jax
import jax.numpy as jnp
from jax import lax
import numpy as np

D_MODEL = 1024
BATCH = 16
SEQ = 256
DEPTH = 4
DEC_BATCH = 4
DEC_SEQ = 4096
PAST_LEN = 512

GRID_W = 64
EPS = 1e-6
A_WIDTH = 512
A_CONV = 3
SSM_INNER = 1024
SSM_HEAD_DIM = 64
SSM_HEADS = SSM_INNER // SSM_HEAD_DIM
SSM_GROUPS = 2
SSM_STATE = 128
SSM_CONV = 3
SSM_CONV_DIM = SSM_INNER + 2 * SSM_GROUPS * SSM_STATE
CHUNK = 128
MLA_HEADS = 8
Q_LORA = 256
KV_LORA = 256
NOPE_DIM = 64
ROPE_DIM = 32
V_DIM = 64
QK_DIM = NOPE_DIM + ROPE_DIM
ROPE_BASE = 10000.0
Q_BLOCK = 128
FF_DIM = -(-8 * D_MODEL // (3 * 256)) * 256
N_BRANCH = 3
IN_SIZES = (A_WIDTH, A_WIDTH, A_WIDTH,
            SSM_INNER, SSM_CONV_DIM, SSM_HEADS,
            Q_LORA, KV_LORA, ROPE_DIM,
            N_BRANCH * D_MODEL)
IN_COLS = sum(IN_SIZES)

kernel_name = 'hybrid_diffusion_parallel_conv_ssd_mla_step'


def rms_norm(x, w):
    xf = x.astype(jnp.float32)
    y = xf * lax.rsqrt(jnp.mean(xf * xf, axis=-1, keepdims=True) + EPS)
    return (y * w.astype(jnp.float32)).astype(x.dtype)


def split_columns(proj):
    outs, start = [], 0
    for size in IN_SIZES:
        outs.append(proj[..., start:start + size])
        start += size
    return outs


def conv3_centred(x, w):
    xp = jnp.pad(x, ((0, 0), (1, 1), (0, 0)))
    return xp[:, :-2] * w[0] + xp[:, 1:-1] * w[1] + xp[:, 2:] * w[2]


def axial_rope_tables(n_tokens):
    n_rows = n_tokens // GRID_W
    row = jnp.repeat(jnp.arange(n_rows, dtype=jnp.float32), GRID_W)
    col = jnp.tile(jnp.arange(GRID_W, dtype=jnp.float32), n_rows)
    pairs_per_axis = ROPE_DIM // 4
    inv = ROPE_BASE ** (-jnp.arange(pairs_per_axis, dtype=jnp.float32) / pairs_per_axis)
    ang = jnp.concatenate([row[:, None] * inv, col[:, None] * inv], axis=-1)
    return jnp.cos(ang), jnp.sin(ang)


def apply_rope(x, cos, sin):
    half = ROPE_DIM // 2
    x1, x2 = x[..., :half], x[..., half:]
    cos, sin = cos.astype(x.dtype), sin.astype(x.dtype)
    return jnp.concatenate([x1 * cos - x2 * sin, x1 * sin + x2 * cos], axis=-1)


def short_conv_branch(a_x, a_b, a_c, conv_w, w_out):
    return (a_b * conv3_centred(a_c * a_x, conv_w)) @ w_out


def ssd_chunked(x, dt, a, bmat, cmat, h0):
    bsz, seq = x.shape[:2]
    nc = seq // CHUNK
    hg = SSM_HEADS // SSM_GROUPS
    xg = x.reshape(bsz, nc, CHUNK, SSM_GROUPS, hg, SSM_HEAD_DIM)
    dtg = dt.reshape(bsz, nc, CHUNK, SSM_GROUPS, hg)
    bg = bmat.reshape(bsz, nc, CHUNK, SSM_GROUPS, SSM_STATE)
    cg = cmat.reshape(bsz, nc, CHUNK, SSM_GROUPS, SSM_STATE)
    cum = jnp.cumsum(dtg * a.reshape(SSM_GROUPS, hg), axis=2)
    seg = cum[:, :, :, None] - cum[:, :, None, :]
    lower = jnp.tril(jnp.ones((CHUNK, CHUNK), dtype=bool))[:, :, None, None]
    decay = jnp.exp(jnp.where(lower, seg, -jnp.inf))
    cb = jnp.einsum('bcign,bcjgn->bcijg', cg, bg)
    w_intra = decay * cb[..., None] * dtg[:, :, None]
    y_diag = jnp.einsum('bcijgh,bcjghp->bcighp', w_intra, xg)
    to_end = jnp.exp(cum[:, :, -1:] - cum) * dtg
    chunk_states = jnp.einsum('bcjgh,bcjgn,bcjghp->bcghpn', to_end, bg, xg)
    chunk_decay = jnp.exp(cum[:, :, -1])

    def step(h, inp):
        st, dec = inp
        return h * dec[..., None, None] + st, h

    h_init = h0.reshape(bsz, SSM_GROUPS, hg, SSM_HEAD_DIM, SSM_STATE)
    h_final, h_enter = lax.scan(step, h_init,
                                (jnp.moveaxis(chunk_states, 1, 0), jnp.moveaxis(chunk_decay, 1, 0)))
    h_enter = jnp.moveaxis(h_enter, 0, 1)
    y_off = jnp.einsum('bcign,bcghpn->bcighp', cg, h_enter) * jnp.exp(cum)[..., None]
    y = (y_diag + y_off).reshape(bsz, seq, SSM_HEADS, SSM_HEAD_DIM)
    return y, h_final.reshape(bsz, SSM_HEADS, SSM_HEAD_DIM, SSM_STATE)


def ssm_branch(s_z, s_xbc, s_dt, conv_w, conv_b, a_log, dt_bias, d_skip, norm_w, w_out, h0_fwd, h0_bwd):
    bsz, seq = s_z.shape[:2]
    xbc = jax.nn.silu(conv3_centred(s_xbc, conv_w) + conv_b)
    gn = SSM_GROUPS * SSM_STATE
    x = xbc[..., :SSM_INNER].reshape(bsz, seq, SSM_HEADS, SSM_HEAD_DIM).astype(jnp.float32)
    bm = xbc[..., SSM_INNER:SSM_INNER + gn].reshape(bsz, seq, SSM_GROUPS, SSM_STATE).astype(jnp.float32)
    cm = xbc[..., SSM_INNER + gn:].reshape(bsz, seq, SSM_GROUPS, SSM_STATE).astype(jnp.float32)
    dt_raw = s_dt.astype(jnp.float32)
    ys, finals = [], []
    for d, h0 in enumerate((h0_fwd, h0_bwd)):
        dt = jax.nn.softplus(dt_raw + dt_bias[d].astype(jnp.float32))
        a = -jnp.exp(a_log[d].astype(jnp.float32))
        if d == 0:
            y_d, h_d = ssd_chunked(x, dt, a, bm, cm, h0.astype(jnp.float32))
        else:
            y_d, h_d = ssd_chunked(jnp.flip(x, 1), jnp.flip(dt, 1), a, jnp.flip(bm, 1),
                                   jnp.flip(cm, 1), h0.astype(jnp.float32))
            y_d = jnp.flip(y_d, 1)
        ys.append(y_d + d_skip[d].astype(jnp.float32)[:, None] * x)
        finals.append(h_d.astype(s_z.dtype))
    y = (ys[0] + ys[1]).reshape(bsz, seq, SSM_INNER)
    y = rms_norm(y * jax.nn.silu(s_z.astype(jnp.float32)), norm_w).astype(s_z.dtype)
    return y @ w_out, finals[0], finals[1]


def block_attention(q_nope, q_rope, k_nope, k_rope, v):
    bsz, lq = q_nope.shape[:2]
    nb = lq // Q_BLOCK
    scale = 1.0 / math.sqrt(QK_DIM)

    def to_blocks(t):
        return jnp.moveaxis(t.reshape(bsz, nb, Q_BLOCK, *t.shape[2:]), 1, 0)

    def one_block(qb):
        qn, qr = qb
        s = jnp.einsum('bqhd,bkhd->bhqk', qn, k_nope) + jnp.einsum('bqhd,bkd->bhqk', qr, k_rope)
        p = jax.nn.softmax(s.astype(jnp.float32) * scale, axis=-1).astype(v.dtype)
        return jnp.einsum('bhqk,bkhd->bqhd', p, v)

    o = lax.map(one_block, (to_blocks(q_nope), to_blocks(q_rope)))
    return jnp.moveaxis(o, 0, 1).reshape(bsz, lq, MLA_HEADS * V_DIM)


def mla_branch(m_cq, m_ckv, m_kr, q_norm_w, w_uq, kv_norm_w, w_ukv, w_out, ctx_kv, rope_tabs):
    bsz, seq = m_cq.shape[:2]
    q = (rms_norm(m_cq, q_norm_w) @ w_uq).reshape(bsz, seq, MLA_HEADS, QK_DIM)
    q_nope, q_rope = q[..., :NOPE_DIM], q[..., NOPE_DIM:]
    ckv = rms_norm(m_ckv, kv_norm_w)
    if rope_tabs is None:
        ckv_all, kr_all = ckv, m_kr
    else:
        cos, sin = rope_tabs
        q_rope = apply_rope(q_rope, cos[:, None], sin[:, None])
        ckv_all = jnp.concatenate([ctx_kv[0], ckv], axis=1)
        kr_all = jnp.concatenate([ctx_kv[1], apply_rope(m_kr, cos, sin)], axis=1)
    kv = (ckv_all @ w_ukv).reshape(bsz, ckv_all.shape[1], MLA_HEADS, NOPE_DIM + V_DIM)
    out = block_attention(q_nope, q_rope, kv[..., :NOPE_DIM], kr_all, kv[..., NOPE_DIM:])
    return out @ w_out, ckv, m_kr


def trunk_layer(x, cond, p, ctx_kv, h0_fwd, h0_bwd, rope_tabs):
    mod = jax.nn.silu(cond) @ p['w_ada'] + p['b_ada']
    sh1, sc1, g1, sh2, sc2, g2 = [m[:, None, :] for m in jnp.split(mod, 6, axis=-1)]
    h = rms_norm(x, p['norm1_w']) * (1.0 + sc1) + sh1
    a_x, a_b, a_c, s_z, s_xbc, s_dt, m_cq, m_ckv, m_kr, gate_logits = split_columns(h @ p['w_in'])
    y_a = short_conv_branch(a_x, a_b, a_c, p['a_conv_w'], p['w_a_out'])
    y_b, hf, hb = ssm_branch(s_z, s_xbc, s_dt, p['ssm_conv_w'], p['ssm_conv_b'], p['ssm_a_log'],
                             p['ssm_dt_bias'], p['ssm_d'], p['ssm_norm_w'], p['w_b_out'], h0_fwd, h0_bwd)
    y_c, ckv, kr = mla_branch(m_cq, m_ckv, m_kr, p['q_norm_w'], p['w_uq'], p['kv_norm_w'],
                              p['w_ukv'], p['w_c_out'], ctx_kv, rope_tabs)
    gates = jax.nn.sigmoid(gate_logits.astype(jnp.float32)).astype(x.dtype)
    gates = gates.reshape(x.shape[0], x.shape[1], N_BRANCH, D_MODEL)
    merged = gates[..., 0, :] * y_a + gates[..., 1, :] * y_b + gates[..., 2, :] * y_c
    x = x + g1 * (merged @ p['w_o'])
    h2 = rms_norm(x, p['norm2_w']) * (1.0 + sc2) + sh2
    ff = (jax.nn.silu(h2 @ p['w_ff1']) * (h2 @ p['w_ff3'])) @ p['w_ff2']
    x = x + g2 * ff
    return x, ckv, kr, hf, hb


def setup_inputs(seed: int = 0) -> dict:
    key = jax.random.key(seed)
    ks = jax.random.split(key, 32)
    f32 = jnp.float32

    def nrm(k, shape, scale=1.0):
        return jax.random.normal(k, shape, f32) * scale

    def gain(k, shape):
        return 1.0 + 0.1 * jax.random.normal(k, shape, f32)

    dt0 = jnp.exp(jax.random.uniform(ks[13], (DEPTH, 2, SSM_HEADS), f32, math.log(1e-3), math.log(1e-1)))
    st_shape = (DEC_BATCH, DEPTH, SSM_HEADS, SSM_HEAD_DIM, SSM_STATE)
    return {
        'x_prompt': nrm(ks[0], (BATCH, SEQ, D_MODEL)),
        'x_sample': nrm(ks[1], (DEC_BATCH, DEC_SEQ, D_MODEL)),
        'c': nrm(ks[2], (DEC_BATCH, D_MODEL)),
        'cache_ckv': nrm(ks[3], (DEC_BATCH, DEPTH, PAST_LEN, KV_LORA)),
        'cache_krope': nrm(ks[4], (DEC_BATCH, DEPTH, PAST_LEN, ROPE_DIM)),
        'state_ssm_fwd': nrm(ks[5], st_shape, 0.5),
        'state_ssm_bwd': nrm(ks[6], st_shape, 0.5),
        'c_ctx': nrm(ks[7], (D_MODEL,)),
        'w_in': nrm(ks[8], (DEPTH, D_MODEL, IN_COLS), D_MODEL ** -0.5),
        'a_conv_w': nrm(ks[9], (DEPTH, A_CONV, A_WIDTH), A_CONV ** -0.5),
        'w_a_out': nrm(ks[10], (DEPTH, A_WIDTH, D_MODEL), A_WIDTH ** -0.5),
        'ssm_conv_w': nrm(ks[11], (DEPTH, SSM_CONV, SSM_CONV_DIM), SSM_CONV ** -0.5),
        'ssm_conv_b': nrm(ks[12], (DEPTH, SSM_CONV_DIM), 0.02),
        'ssm_a_log': jnp.log(jax.random.uniform(ks[14], (DEPTH, 2, SSM_HEADS), f32, 1.0, 16.0)),
        'ssm_dt_bias': dt0 + jnp.log(-jnp.expm1(-dt0)),
        'ssm_d': gain(ks[15], (DEPTH, 2, SSM_HEADS)),
        'ssm_norm_w': gain(ks[16], (DEPTH, SSM_INNER)),
        'w_b_out': nrm(ks[17], (DEPTH, SSM_INNER, D_MODEL), SSM_INNER ** -0.5),
        'q_norm_w': gain(ks[18], (DEPTH, Q_LORA)),
        'w_uq': nrm(ks[19], (DEPTH, Q_LORA, MLA_HEADS * QK_DIM), Q_LORA ** -0.5),
        'kv_norm_w': gain(ks[20], (DEPTH, KV_LORA)),
        'w_ukv': nrm(ks[21], (DEPTH, KV_LORA, MLA_HEADS * (NOPE_DIM + V_DIM)), KV_LORA ** -0.5),
        'w_c_out': nrm(ks[22], (DEPTH, MLA_HEADS * V_DIM, D_MODEL), (MLA_HEADS * V_DIM) ** -0.5),
        'w_o': nrm(ks[23], (DEPTH, D_MODEL, D_MODEL), D_MODEL ** -0.5),
        'w_ada': nrm(ks[24], (DEPTH, D_MODEL, 6 * D_MODEL), 0.5 * D_MODEL ** -0.5),
        'b_ada': nrm(ks[25], (DEPTH, 6 * D_MODEL), 0.02),
        'norm1_w': gain(ks[26], (DEPTH, D_MODEL)),
        'norm2_w': gain(ks[27], (DEPTH, D_MODEL)),
        'w_ff1': nrm(ks[28], (DEPTH, D_MODEL, FF_DIM), D_MODEL ** -0.5),
        'w_ff3': nrm(ks[29], (DEPTH, D_MODEL, FF_DIM), D_MODEL ** -0.5),
        'w_ff2': nrm(ks[30], (DEPTH, FF_DIM, D_MODEL), FF_DIM ** -0.5),
        'final_norm_w': gain(ks[31], (D_MODEL,)),
    }


def reference(x_prompt, x_sample, c, cache_ckv, cache_krope, state_ssm_fwd, state_ssm_bwd, c_ctx,
              w_in, a_conv_w, w_a_out, ssm_conv_w, ssm_conv_b, ssm_a_log, ssm_dt_bias, ssm_d,
              ssm_norm_w, w_b_out, q_norm_w, w_uq, kv_norm_w, w_ukv, w_c_out, w_o, w_ada, b_ada,
              norm1_w, norm2_w, w_ff1, w_ff3, w_ff2, final_norm_w):
    def layer_params(l):
        return {
            'w_in': w_in[l], 'a_conv_w': a_conv_w[l], 'w_a_out': w_a_out[l],
            'ssm_conv_w': ssm_conv_w[l], 'ssm_conv_b': ssm_conv_b[l], 'ssm_a_log': ssm_a_log[l],
            'ssm_dt_bias': ssm_dt_bias[l], 'ssm_d': ssm_d[l], 'ssm_norm_w': ssm_norm_w[l],
            'w_b_out': w_b_out[l], 'q_norm_w': q_norm_w[l], 'w_uq': w_uq[l], 'kv_norm_w': kv_norm_w[l],
            'w_ukv': w_ukv[l], 'w_c_out': w_c_out[l], 'w_o': w_o[l], 'w_ada': w_ada[l],
            'b_ada': b_ada[l], 'norm1_w': norm1_w[l], 'norm2_w': norm2_w[l],
            'w_ff1': w_ff1[l], 'w_ff3': w_ff3[l], 'w_ff2': w_ff2[l],
        }

    h = x_prompt
    zero_state = jnp.zeros((x_prompt.shape[0], SSM_HEADS, SSM_HEAD_DIM, SSM_STATE), x_prompt.dtype)
    cond_ctx = c_ctx[None, :]
    ckv_list, kr_list, hf_list, hb_list = [], [], [], []
    for l in range(DEPTH):
        h, ckv_l, kr_l, hf_l, hb_l = trunk_layer(h, cond_ctx, layer_params(l), None,
                                                 zero_state, zero_state, None)
        ckv_list.append(ckv_l)
        kr_list.append(kr_l)
        hf_list.append(hf_l)
        hb_list.append(hb_l)
    y_prompt = rms_norm(h, final_norm_w)
    new_ckv = jnp.stack(ckv_list, axis=1)
    new_krope = jnp.stack(kr_list, axis=1)
    new_ssm_fwd = jnp.stack(hf_list, axis=1)
    new_ssm_bwd = jnp.stack(hb_list, axis=1)

    rope_tabs = axial_rope_tables(x_sample.shape[1])
    h = x_sample
    for l in range(DEPTH):
        h, _, _, _, _ = trunk_layer(h, c, layer_params(l), (cache_ckv[:, l], cache_krope[:, l]),
                                    state_ssm_fwd[:, l], state_ssm_bwd[:, l], rope_tabs)
    y_sample = rms_norm(h, final_norm_w)
    return (y_prompt, y_sample, new_ckv, new_krope, new_ssm_fwd, new_ssm_bwd)
```

```python
import contextlib
import math
import os

import numpy as np
import concourse.bass as bass
import concourse.mybir as mybir
from concourse.bass_utils import run_bass_kernel_spmd

F32 = mybir.dt.float32
BF16 = mybir.dt.bfloat16
AF = mybir.ActivationFunctionType
ALU = mybir.AluOpType

D = 1024
DEPTH = 4
TS = 4096
TPR = 512
T = TS + TPR
NB = T // 512
NCH = T // 128
EPS = 1e-6
IN_COLS = 7728
FF = 2816
C_AX, C_AB, C_AC, C_Z, C_XBC, C_DT, C_CQ, C_CKV, C_KR, C_G = 0, 512, 1024, 1536, 2560, 4096, 4112, 4368, 4624, 4656
SCALE = 1.0 / math.sqrt(96.0)
SEQS = [(0, 4096, True), (4096, 256, False), (4352, 256, False)]

ENGS = ("pe", "act", "dve", "pool", "sp")


class Sched:
    def __init__(self, nc, st, n_dma_sems=48):
        self.nc = nc
        self.n_dma_sems = n_dma_sems
        self.esem = {e: st.enter_context(nc.semaphore("s_" + e)) for e in ENGS if e != "sp"}
        self.dsem = [st.enter_context(nc.semaphore("d%d" % i)) for i in range(n_dma_sems)]
        self.dummy = st.enter_context(nc.sbuf_tensor("bar_dummy", [128, 2], F32))
        self.ecount = {e: 0 for e in ENGS}
        self.dma_val = [0] * n_dma_sems
        self.dma_last = [None] * n_dma_sems
        self.dma_rr = 0
        self.waited = {e: {} for e in ENGS}
        self.barrier_tok = None
        self.need_barrier = {e: False for e in ENGS}
        self._reset()

    def _reset(self):
        self.ops = {e: [] for e in ENGS}
        self.last_write = {}
        self.readers = {}
        self.dma_toks = []

    def _record(self, eng, fn, reads, writes, dma, extra_deps=()):
        deps = set(extra_deps)
        for r in reads:
            w = self.last_write.get(r)
            if w is not None:
                deps.add(w)
        for r in writes:
            w = self.last_write.get(r)
            if w is not None:
                deps.add(w)
            for rd in self.readers.get(r, ()):
                deps.add(rd)
        idx = len(self.ops[eng])
        if dma:
            si = self.dma_rr
            self.dma_rr = (self.dma_rr + 1) % self.n_dma_sems
            prev = self.dma_last[si]
            if prev is not None:
                deps.add(prev)
            self.dma_val[si] += 16
            tok = ("dma", si, self.dma_val[si])
            self.dma_last[si] = tok
            self.dma_toks.append(tok)
        else:
            tok = ("eng", eng, idx)
        deps = {d for d in deps if not (d[0] == "eng" and d[1] == "pe" and eng == "pe")}
        deps.discard(tok)
        if self.need_barrier[eng] and self.barrier_tok is not None:
            deps.add(self.barrier_tok)
            self.need_barrier[eng] = False
        self.ops[eng].append(dict(fn=fn, deps=deps, flag=False, tok=tok))
        for r in reads:
            lst = self.readers.setdefault(r, [])
            if tok[0] == "eng":
                lst[:] = [t for t in lst if not (t[0] == "eng" and t[1] == eng)]
            lst.append(tok)
        for r in writes:
            self.last_write[r] = tok
            self.readers[r] = []
        return tok

    def op(self, eng, fn, reads=(), writes=(), extra_deps=()):
        return self._record(eng, fn, reads, writes, False, extra_deps)

    def dma(self, eng, fn, reads=(), writes=(), extra_deps=()):
        return self._record(eng, fn, reads, writes, True, extra_deps)

    def flush(self, final=False):
        nc = self.nc
        deps = set()
        for e in ENGS:
            if e == "sp":
                continue
            for i in range(len(self.ops[e]) - 1, -1, -1):
                if self.ops[e][i]["tok"][0] == "eng":
                    deps.add(self.ops[e][i]["tok"])
                    break
        latest = {}
        for t in self.dma_toks:
            if t[1] not in latest or latest[t[1]][2] < t[2]:
                latest[t[1]] = t
        deps.update(latest.values())
        dummy = self.dummy
        coll = self._record("dve", lambda e: e.memset(dummy[:, 0:1], 0.0), (), (), False, deps)
        for e in ENGS:
            for o in self.ops[e]:
                for d in o["deps"]:
                    if d[0] == "eng":
                        self.ops[d[1]][d[2]]["flag"] = True
        self.ops["dve"][coll[2]]["flag"] = True
        cnt = {}
        for e in ENGS:
            c = self.ecount[e]
            arr = []
            for o in self.ops[e]:
                if o["flag"]:
                    c += 1
                arr.append(c)
            cnt[e] = arr
        esem, dsem = self.esem, self.dsem

        def resolve(d):
            if d[0] == "eng":
                return esem[d[1]], cnt[d[1]][d[2]]
            if d[0] == "abs":
                return d[1], d[2]
            return dsem[d[1]], d[2]

        coll_abs = ("abs", esem["dve"], cnt["dve"][coll[2]])

        def run(ename, eng):
            waited = self.waited[ename]
            for o in self.ops[ename]:
                need = {}
                for d in o["deps"]:
                    s, v = resolve(d)
                    k = id(s)
                    if waited.get(k, 0) >= v:
                        continue
                    if k not in need or need[k][1] < v:
                        need[k] = (s, v)
                for k, (s, v) in need.items():
                    eng.wait_ge(s, v)
                    waited[k] = v
                ins = o["fn"](eng)
                if o["tok"][0] == "dma":
                    ins.then_inc(dsem[o["tok"][1]], 16)
                elif o["flag"]:
                    ins.then_inc(esem[ename], 1)
            if final and ename == "sp":
                s, v = resolve(coll_abs)
                eng.wait_ge(s, v)

        if os.environ.get("KDBG_SIM"):
            self._simulate(resolve, coll_abs, final)

        with nc.Block() as block:
            @block.sync
            def _(sync):
                run("sp", sync)

            @block.tensor
            def _(tensor):
                run("pe", tensor)

            @block.scalar
            def _(scalar):
                run("act", scalar)

            @block.vector
            def _(vector):
                run("dve", vector)

            @block.gpsimd
            def _(gpsimd):
                run("pool", gpsimd)

        for e in ENGS:
            if cnt[e]:
                self.ecount[e] = cnt[e][-1]
        self.barrier_tok = coll_abs
        self.need_barrier = {e: True for e in ENGS}
        self._reset()


def _sched_simulate(self, resolve, coll_abs, final):
    if not hasattr(self, "sim_sem"):
        self.sim_sem = {}
    sem = self.sim_sem
    pos = {e: 0 for e in ENGS}
    progress = True
    while progress:
        progress = False
        for e in ENGS:
            while pos[e] < len(self.ops[e]):
                o = self.ops[e][pos[e]]
                ok = True
                for d in o["deps"]:
                    s_, v = resolve(d)
                    if sem.get(id(s_), 0) < v:
                        ok = False
                        break
                if not ok:
                    break
                if o["tok"][0] == "dma":
                    k = id(self.dsem[o["tok"][1]])
                    sem[k] = sem.get(k, 0) + 16
                elif o["flag"]:
                    k = id(self.esem[e])
                    sem[k] = sem.get(k, 0) + 1
                pos[e] += 1
                progress = True
    stuck = {e: (pos[e], len(self.ops[e])) for e in ENGS if pos[e] < len(self.ops[e])}
    if stuck:
        print("SCHED DEADLOCK:", stuck)
        for e in stuck:
            o = self.ops[e][pos[e]]
            print("  ", e, "op", pos[e], "tok", o["tok"], "deps", [(d, resolve(d)[1], sem.get(id(resolve(d)[0]), 0)) for d in o["deps"]])
        raise RuntimeError("sched deadlock")
    else:
        print("sched sim ok:", {e: len(self.ops[e]) for e in ENGS})


Sched._simulate = _sched_simulate


class Ring:
    def __init__(self, tiles, name):
        self.tiles = tiles
        self.name = name
        self.i = 0

    def next(self):
        i = self.i
        self.i = (self.i + 1) % len(self.tiles)
        return self.tiles[i], (self.name, i)


PRM_LAYOUT = [("n1w", 4 * 8), ("n2w", 4 * 8), ("fnw", 8), ("snw", 4 * 8), ("qnw", 4 * 2), ("kvnw", 4 * 2),
              ("aconv", 4 * 3 * 4), ("sconvw", 4 * 3 * 12), ("sconvb", 4 * 12), ("dexp", 4 * 2 * 8),
              ("alog", 4 * 2 * 16), ("dtb", 4 * 2 * 16), ("bada", 4 * 48)]
PRM_OFF = {}
_o = 0
for _n, _s in PRM_LAYOUT:
    PRM_OFF[_n] = (_o, _s)
    _o += _s
NPRM = _o


def _pc(v):
    v = np.asarray(v, np.float32)
    lead = v.shape[:-1]
    c = v.shape[-1] // 128
    v = v.reshape(lead + (c, 128))
    v = np.moveaxis(v, -1, 0)
    return np.ascontiguousarray(v).reshape(128, -1)


def pack_params(inp):
    parts = {
        "n1w": _pc(inp["norm1_w"]), "n2w": _pc(inp["norm2_w"]), "fnw": _pc(inp["final_norm_w"]),
        "snw": _pc(inp["ssm_norm_w"]), "qnw": _pc(inp["q_norm_w"]), "kvnw": _pc(inp["kv_norm_w"]),
        "aconv": _pc(inp["a_conv_w"]), "sconvw": _pc(inp["ssm_conv_w"]), "sconvb": _pc(inp["ssm_conv_b"]),
        "dexp": _pc(np.repeat(np.asarray(inp["ssm_d"], np.float32), 64, axis=-1)),
        "alog": np.broadcast_to(np.asarray(inp["ssm_a_log"], np.float32).reshape(1, -1), (128, 128)),
        "dtb": np.broadcast_to(np.asarray(inp["ssm_dt_bias"], np.float32).reshape(1, -1), (128, 128)),
        "bada": _pc(inp["b_ada"]),
    }
    out = np.zeros((128, NPRM), np.float32)
    for n, (o, s) in PRM_OFF.items():
        assert parts[n].shape == (128, s), (n, parts[n].shape, s)
        out[:, o:o + s] = parts[n]
    return out


def build_program(stop=None, dump=(), nl=DEPTH):
    nc = bass.Bass("TRN2", target_bir_lowering=False)

    def din(name, shape, dt=F32):
        return nc.dram_tensor(name, list(shape), dt, kind="ExternalInput").ap()

    def dout(name, shape, dt=F32):
        return nc.dram_tensor(name, list(shape), dt, kind="ExternalOutput").ap()

    def dscr(name, shape, dt):
        kind = "ExternalOutput" if name in dump else "Internal"
        return nc.dram_tensor(name, list(shape), dt, kind=kind).ap()

    xT0 = din("xT0", [D, T])
    cond = din("cond", [128, 8, 2])
    cckvT = din("cckvT", [DEPTH, 256, 512])
    ckrT = din("ckrT", [DEPTH, 32, 512])
    h0 = din("h0", [DEPTH, 2, 128, 1024])
    ropeC = din("ropeC", [32, TS])
    ropeS = din("ropeS", [32, TS])
    cst = din("cst", [128, 512])
    prm_d = din("prm", [128, NPRM])
    w_in = din("w_in", [nl, D, IN_COLS])
    w_kr2 = din("w_kr2", [nl, D, 2, 96])
    w_uq2 = din("w_uq2", [nl, 256, 2, 8, 96])
    w_ukv = din("w_ukv", [nl, 256, 1024])
    w_a_out = din("w_a_out", [nl, 512, D])
    w_b_out = din("w_b_out", [nl, D, D])
    w_c_out = din("w_c_out", [nl, 512, D])
    w_o = din("w_o", [nl, D, D])
    w_ada = din("w_ada", [nl, D, 6 * D])
    w_ff1 = din("w_ff1", [nl, D, FF])
    w_ff3 = din("w_ff3", [nl, D, FF])
    w_ff2 = din("w_ff2", [nl, FF, D])

    yT = dout("yT", [D, T])
    nckvT = dout("nckvT", [DEPTH, 256, 512])
    nkrT = dout("nkrT", [DEPTH, 32, 512])
    nssm = dout("nssm", [DEPTH, 2, 2, 128, 1024])

    XT = dscr("XT", [D, T], F32)
    HT = dscr("HT", [D, T], BF16)
    UT = dscr("UT", [512, T], BF16)
    ABT = dscr("ABT", [512, T], BF16)
    ZT = dscr("ZT", [D, T], BF16)
    XBCT = dscr("XBCT", [1536, T], BF16)
    DTT = dscr("DTT", [T, 16], F32)
    QT = dscr("QT", [8, 96, T], BF16)
    CKVT = dscr("CKVT", [256, T], BF16)
    KRT = dscr("KRT", [32, T], BF16)
    VAT = dscr("VAT", [512, T], BF16)
    XSC = dscr("XSC", [1536, T], BF16)
    XTOK = dscr("XTOK", [T, 1024], BF16)
    BTOK = dscr("BTOK", [T, 256], BF16)
    HENT = dscr("HENT", [2, NCH, 128, 1024], BF16)
    YBT = dscr("YBT", [D, T], BF16)
    ATT = dscr("ATT", [512, T], BF16)

    with contextlib.ExitStack() as gst:
        S = Sched(nc, gst)

        _uid = [0]

        def sb(st, name, shape, dt):
            _uid[0] += 1
            return st.enter_context(nc.sbuf_tensor("sb%d_%s" % (_uid[0], name), list(shape), dt))

        prm = sb(gst, "prm", [128, NPRM], F32)
        cstt = sb(gst, "cstt", [128, 512], F32)
        identb = sb(gst, "identb", [128, 128], BF16)
        MOD = sb(gst, "MOD", [128, DEPTH, 48, 2], F32)
        A1 = sb(gst, "A1", [128, DEPTH, 8, 2], F32)
        A2 = sb(gst, "A2", [128, DEPTH, 8, 2], F32)
        dsum = sb(gst, "dsum", [128, DEPTH, 8], F32)
        wring = Ring([sb(gst, "wr%d" % i, [128, 6144], BF16) for i in range(4)], "wr")
        uqw = sb(gst, "uqw", [128, 2, 2 * 8 * 96], BF16)
        krw = sb(gst, "krw", [128, 8, 2 * 96], BF16)
        dtw = sb(gst, "dtw", [128, 8, 16], BF16)
        ukvw = sb(gst, "ukvw", [128, 2, 1024], BF16)
        psum = [gst.enter_context(nc.psum_tensor("ps%d" % i, [128, 512], F32)) for i in range(7)]
        psbT = gst.enter_context(nc.psum_tensor("psbT", [128, 1024], BF16))
        pring = Ring(psum[0:5], "ps")
        plong = Ring(psum[5:7], "pl")
        triF = cstt[:, 0:128]
        triB = cstt[:, 128:256]
        ones = cstt[:, 256:384]

        def P(name, l=None):
            o, s = PRM_OFF[name]
            v = prm[:, o:o + s]
            return v

        def pv(name, pattern, **kw):
            o, s = PRM_OFF[name]
            return prm[:, o:o + s].rearrange(pattern, **kw)

        n1w = pv("n1w", "p (l c) -> p l c", l=4)
        n2w = pv("n2w", "p (l c) -> p l c", l=4)
        fnw = P("fnw")
        snw = pv("snw", "p (l c) -> p l c", l=4)
        qnw = pv("qnw", "p (l c) -> p l c", l=4)
        kvnw = pv("kvnw", "p (l c) -> p l c", l=4)
        aconv = pv("aconv", "p (l k c) -> p l k c", l=4, k=3)
        sconvw = pv("sconvw", "p (l k c) -> p l k c", l=4, k=3)
        sconvb = pv("sconvb", "p (l c) -> p l c", l=4)
        dexp = pv("dexp", "p (l d c) -> p l d c", l=4, d=2)
        alog = pv("alog", "p (l d h) -> p l d h", l=4, d=2)
        dtb = pv("dtb", "p (l d h) -> p l d h", l=4, d=2)
        bada = pv("bada", "p (l c) -> p l c", l=4)

        def mm(out, lhsT, rhs, start, stop, rd, wr):
            S.op("pe", lambda e: e.matmul(out, lhsT=lhsT, rhs=rhs, start=start, stop=stop), reads=rd, writes=wr)

        def act(out, in_, func, rd, wr, **kw):
            S.op("act", lambda e: e.activation(out=out, in_=in_, func=func, **kw), reads=rd, writes=wr)

        def tt(eng, out, in0, in1, op, rd, wr):
            S.op(eng, lambda e: e.tensor_tensor(out=out, in0=in0, in1=in1, op=op), reads=rd, writes=wr)

        def ts(eng, out, in0, s1, s2, op0, op1, rd, wr):
            if op1 is None:
                S.op(eng, lambda e: e.tensor_scalar(out=out, in0=in0, scalar1=s1, scalar2=None, op0=op0), reads=rd, writes=wr)
            else:
                S.op(eng, lambda e: e.tensor_scalar(out=out, in0=in0, scalar1=s1, scalar2=s2, op0=op0, op1=op1), reads=rd, writes=wr)

        def stt(eng, out, in0, scalar, in1, op0, op1, rd, wr):
            S.op(eng, lambda e: e.scalar_tensor_tensor(out=out, in0=in0, scalar=scalar, in1=in1, op0=op0, op1=op1), reads=rd, writes=wr)

        def cp(eng, out, in_, rd, wr):
            if eng == "act":
                act(out, in_, AF.Copy, rd, wr)
            else:
                S.op(eng, lambda e: e.tensor_copy(out=out, in_=in_), reads=rd, writes=wr)

        def load(out, in_, rd, wr, eng="sp"):
            return S.dma(eng, lambda e: e.dma_start(out=out, in_=in_), reads=rd, writes=wr)

        def store(out, in_, rd, wr, eng="act"):
            return S.dma(eng, lambda e: e.dma_start(out=out, in_=in_), reads=rd, writes=wr)

        def wload(src2d, n_kc, ncols):
            slot, key = wring.next()
            view = slot[:, 0:n_kc * ncols].rearrange("p (k n) -> p k n", k=n_kc)
            S.dma("pool", lambda e: e.dma_start(out=view, in_=src2d.rearrange("(k p) n -> p k n", p=128)), reads=[], writes=[key])
            return view, key

        def rstd_from_ssq(ps_ap, n_feat, out_ap, tmp_ap, rd, wr, tmpkey):
            act(tmp_ap, ps_ap, AF.Ln, rd, [tmpkey], scale=1.0 / n_feat, bias=EPS)
            act(out_ap, tmp_ap, AF.Exp, [tmpkey], wr, scale=-0.5)

        with contextlib.ExitStack() as st:
            condt = sb(st, "condt", [128, 8, 2], F32)
            scb = sb(st, "scb", [128, 8, 16], BF16)
            load(prm[:], prm_d, [], ["prm"])
            load(cstt[:], cst, [], ["cst"])
            load(condt[:], cond, [], ["condt"])
            cp("dve", identb[:], cstt[:, 384:512], ["cst"], ["identb"])
            S.op("pool", lambda e: e.memset(scb[:], 0.0), writes=["scb"])
            act(scb[:, :, 0:2], condt[:], AF.Silu, ["condt", "scb"], ["scb"])
            _ncg = int(os.environ.get("KDBG_NCG", "12"))
            for l in range(nl):
                for cg in range(_ncg):
                    wv, wk = wload(w_ada[l, :, cg * 512:(cg + 1) * 512], 8, 512)
                    for oc in range(4):
                        ps, pk = pring.next()
                        for kc in range(8):
                            mm(ps[:, 0:16], wv[:, kc, oc * 128:(oc + 1) * 128], scb[:, kc, :], kc == 0, kc == 7, [wk, "scb"], [pk])
                        ci = cg * 4 + oc
                        act(MOD[:, l, ci, :], ps[:, 0:2], AF.Identity, [pk, "prm"], ["MOD"], bias=bada[:, l, ci:ci + 1])
            for l in range(nl):
                for (Ax, k0, nw) in ((A1, 8, n1w), (A2, 32, n2w)):
                    ts("dve", Ax[:, l, :, :], MOD[:, l, k0:k0 + 8, :], 1.0, None, ALU.add, None, ["MOD"], ["A"])
                    tt("dve", Ax[:, l, :, :], Ax[:, l, :, :], nw[:, l, :].unsqueeze(2).to_broadcast([128, 8, 2]), ALU.mult, ["A", "prm"], ["A"])
                tt("dve", dsum[:, l, :], dexp[:, l, 0, :], dexp[:, l, 1, :], ALU.add, ["prm"], ["dsum"])
            S.flush()
        if stop == "p0":
            dbgo = dout("dbg_mod", [128, DEPTH * 48 * 2])
            load(dbgo, MOD[:].rearrange("p l c r -> p (l c r)"), ["MOD"], ["dbgo"])
            S.flush(final=True)
            return nc

        def modcol(l, kind, kc, r):
            return MOD[:, l, kind * 8 + kc, r:r + 1]

        def norm_mod(st_tiles, xt, ht, Acol, Bcol, xkey, hkey):
            sqt, rst, lnt, tmpf = st_tiles
            ps, pk = pring.next()
            for kc in range(8):
                act(sqt[:, kc % 2, :], xt[:, kc, :], AF.Square, [xkey], [("sq", kc % 2)])
                mm(ps[:], ones, sqt[:, kc % 2, :], kc == 0, kc == 7, ["cst", ("sq", kc % 2)], [pk])
            rstd_from_ssq(ps[:], 1024.0, rst[:], lnt[:], [pk], ["rst"], "lnt")
            for kc in range(8):
                if Bcol is None:
                    stt("dve", ht[:, kc, :], xt[:, kc, :], Acol(kc), rst[:], ALU.mult, ALU.mult, [xkey, "rst", "A", "prm"], [hkey])
                else:
                    stt("dve", tmpf[:, kc % 2, :], xt[:, kc, :], Acol(kc), rst[:], ALU.mult, ALU.mult, [xkey, "rst", "A", "prm"], [("tmpf", kc % 2)])
                    ts("pool", ht[:, kc, :], tmpf[:, kc % 2, :], Bcol(kc), None, ALU.add, None, [("tmpf", kc % 2), "MOD"], [hkey])

        for l in range(nl):
            xsrc = xT0 if l == 0 else XT
            with contextlib.ExitStack() as st:
                xt = sb(st, "xt", [128, 8, 512], F32)
                ht = sb(st, "ht", [128, 8, 512], BF16)
                sqt = sb(st, "sqt", [128, 2, 512], F32)
                rst = sb(st, "rst", [128, 512], F32)
                lnt = sb(st, "lnt", [128, 512], F32)
                tmpf = sb(st, "tmpf", [128, 2, 512], F32)
                stg = Ring([sb(st, "stg%d" % i, [128, 4, 512], BF16) for i in range(3)], "stg")
                axt = sb(st, "axt", [128, 4, 512], BF16)
                cqf = sb(st, "cqf", [128, 4, 512], F32)
                nrf = sb(st, "nrf", [128, 2, 512], F32)
                cqn = sb(st, "cqn", [128, 2, 512], BF16)
                ckvn = sb(st, "ckvn", [128, 2, 512], BF16)
                qst = sb(st, "qst", [128, 8, 512], BF16)
                rC = sb(st, "rC", [128, 512], F32)
                rS = sb(st, "rS", [128, 512], F32)
                t1 = sb(st, "t1", [128, 512], F32)
                t2 = sb(st, "t2", [128, 512], F32)
                krt = sb(st, "krt", [128, 512], BF16)
                krf = sb(st, "krf", [128, 512], F32)
                dts = sb(st, "dts", [128, 4, 16], F32)
                S.dma("pool", lambda e, l=l: e.dma_start(out=uqw[:], in_=w_uq2[l].rearrange("(k p) v h r -> p k (v h r)", p=128)), writes=["uqw"])
                S.dma("pool", lambda e, l=l: e.dma_start(out=krw[:], in_=w_kr2[l].rearrange("(k p) v r -> p k (v r)", p=128)), writes=["krw"])
                S.dma("pool", lambda e, l=l: e.dma_start(out=dtw[:], in_=w_in[l, :, C_DT:C_DT + 16].rearrange("(k p) n -> p k n", p=128)), writes=["dtw"])
                S.dma("pool", lambda e, l=l: e.dma_start(out=ukvw[:], in_=w_ukv[l].rearrange("(k p) n -> p k n", p=128)), writes=["ukvw"])
                uq5 = uqw[:].rearrange("p k (v h r) -> p k v h r", v=2, h=8)
                kr4 = krw[:].rearrange("p k (v r) -> p k v r", v=2)
                _steps = os.environ.get("KDBG_P1", "groups,mla,q,kr,dt").split(",")
                _blks = [int(v) for v in os.environ.get("KDBG_BLKS", ",".join(str(i) for i in range(NB))).split(",")]
                for b in _blks:
                    r = 0 if b < 8 else 1
                    smp = b < 8
                    t0 = b * 512
                    load(xt[:], xsrc[:, t0:t0 + 512].rearrange("(k p) t -> p k t", p=128), [("XT", b)], ["xt"])
                    if smp:
                        load(rC[64:96, :], ropeC[:, t0:t0 + 512], [], ["rC"])
                        load(rS[64:96, :], ropeS[:, t0:t0 + 512], [], ["rS"])
                    norm_mod((sqt, rst, lnt, tmpf), xt, ht,
                             lambda kc: A1[:, l, kc, r:r + 1], lambda kc: modcol(l, 0, kc, r), "xt", "ht")
                    store(HT[:, t0:t0 + 512].rearrange("(k p) t -> p k t", p=128), ht[:], ["ht"], [("HT", b)])
                    groups = [("ax", C_AX), ("ab", C_AB), ("ac", C_AC), ("z", C_Z), ("z", C_Z + 512),
                              ("xbc", C_XBC), ("xbc", C_XBC + 512), ("xbc", C_XBC + 1024), ("cqkv", C_CQ)]
                    if "groups" not in _steps:
                        groups = []
                    for gi, (kind, c0) in enumerate(groups):
                        wv, wk = wload(w_in[l, :, c0:c0 + 512], 8, 512)
                        if kind in ("ab", "ac", "z", "xbc"):
                            sg, sk = stg.next()
                        for oc in range(4):
                            ps, pk = pring.next()
                            for kc in range(8):
                                mm(ps[:], wv[:, kc, oc * 128:(oc + 1) * 128], ht[:, kc, :], kc == 0, kc == 7, [wk, "ht"], [pk])
                            if kind == "ax":
                                cp("act", axt[:, oc, :], ps[:], [pk], ["axt"])
                            elif kind == "ab":
                                cp("act", sg[:, oc, :], ps[:], [pk], [sk])
                            elif kind == "ac":
                                tt("dve", sg[:, oc, :], ps[:], axt[:, oc, :], ALU.mult, [pk, "axt"], [sk])
                            elif kind == "z":
                                act(sg[:, oc, :], ps[:], AF.Silu, [pk], [sk])
                            elif kind == "xbc":
                                cp("act" if oc % 2 == 0 else "dve", sg[:, oc, :], ps[:], [pk], [sk])
                            else:
                                cp("act" if oc % 2 == 0 else "dve", cqf[:, oc, :], ps[:], [pk], [("cqf", oc // 2)])
                        if kind in ("ab", "ac", "z", "xbc"):
                            dst = {"ab": ABT, "ac": UT, "z": ZT, "xbc": XBCT}[kind]
                            r0 = c0 - {"ab": C_AB, "ac": C_AC, "z": C_Z, "xbc": C_XBC}[kind]
                            store(dst[r0:r0 + 512, t0:t0 + 512].rearrange("(k p) t -> p k t", p=128), sg[:], [sk], [(kind + "T", b, r0)])
                    for half, nw, dstb in ((0, qnw, cqn), (1, kvnw, ckvn)) if "mla" in _steps else ():
                        ps, pk = pring.next()
                        for j in range(2):
                            act(sqt[:, j, :], cqf[:, half * 2 + j, :], AF.Square, [("cqf", half)], [("sq", j)])
                            mm(ps[:], ones, sqt[:, j, :], j == 0, j == 1, ["cst", ("sq", j)], [pk])
                        rstd_from_ssq(ps[:], 256.0, rst[:], lnt[:], [pk], ["rst"], "lnt")
                        for j in range(2):
                            stt("dve", nrf[:, j, :], cqf[:, half * 2 + j, :], nw[:, l, j:j + 1], rst[:], ALU.mult, ALU.mult,
                                [("cqf", half), "rst", "prm"], [("nrf", j)])
                            cp("pool", dstb[:, j, :], nrf[:, j, :], [("nrf", j)], [("lat", half)])
                        if half == 1:
                            store(CKVT[:, t0:t0 + 512].rearrange("(k p) t -> p k t", p=128), ckvn[:], [("lat", 1)], [("CKVT", b)])
                            if not smp:
                                store(nckvT[l].rearrange("(k p) t -> p k t", p=128), nrf[:], [("nrf", 0), ("nrf", 1)], [("nckv", l)])
                    for h in range(8) if "q" in _steps else ():
                        psn, pkn = pring.next()
                        for kc in range(2):
                            mm(psn[0:96, :], uq5[:, kc, 0, h, :], cqn[:, kc, :], kc == 0, kc == 1, ["uqw", ("lat", 0)], [pkn])
                        if smp:
                            pss, pks = pring.next()
                            for kc in range(2):
                                mm(pss[0:96, :], uq5[:, kc, 1, h, :], cqn[:, kc, :], kc == 0, kc == 1, ["uqw", ("lat", 0)], [pks])
                            cp("act", qst[0:64, h, :], psn[0:64, :], [pkn], [("qst", h)])
                            tt("dve", t1[64:96, :], psn[64:96, :], rC[64:96, :], ALU.mult, [pkn, "rC"], ["t1"])
                            tt("dve", t2[64:96, :], pss[64:96, :], rS[64:96, :], ALU.mult, [pks, "rS"], ["t2"])
                            tt("pool", qst[64:96, h, :], t1[64:96, :], t2[64:96, :], ALU.add, ["t1", "t2"], [("qst", h)])
                        else:
                            cp("act", qst[0:96, h, :], psn[0:96, :], [pkn], [("qst", h)])
                    if "q" in _steps:
                        store(QT[:, :, t0:t0 + 512].rearrange("h r t -> r h t"), qst[0:96, :, :], [("qst", h) for h in range(8)], [("QT", b)])
                    if "kr" not in _steps:
                        continue
                    psn, pkn = pring.next()
                    for kc in range(8):
                        mm(psn[0:96, :], kr4[:, kc, 0, :], ht[:, kc, :], kc == 0, kc == 7, ["krw", "ht"], [pkn])
                    if smp:
                        pss, pks = pring.next()
                        for kc in range(8):
                            mm(pss[0:96, :], kr4[:, kc, 1, :], ht[:, kc, :], kc == 0, kc == 7, ["krw", "ht"], [pks])
                        tt("dve", t1[64:96, :], psn[64:96, :], rC[64:96, :], ALU.mult, [pkn, "rC"], ["t1"])
                        tt("dve", t2[64:96, :], pss[64:96, :], rS[64:96, :], ALU.mult, [pks, "rS"], ["t2"])
                        tt("pool", krt[64:96, :], t1[64:96, :], t2[64:96, :], ALU.add, ["t1", "t2"], ["krt"])
                    else:
                        cp("dve", krf[64:96, :], psn[64:96, :], [pkn], ["krf"])
                        cp("pool", krt[64:96, :], krf[64:96, :], ["krf"], ["krt"])
                        store(nkrT[l], krf[64:96, :], ["krf"], [("nkr", l)])
                    store(KRT[:, t0:t0 + 512], krt[64:96, :], ["krt"], [("KRT", b)])
                    if "dt" not in _steps:
                        continue
                    ps, pk = pring.next()
                    for tl in range(4):
                        for kc in range(8):
                            mm(ps[:, tl * 16:(tl + 1) * 16], ht[:, kc, tl * 128:(tl + 1) * 128], dtw[:, kc, :], kc == 0, kc == 7, ["dtw", "ht"], [pk])
                    cp("dve", dts[:].rearrange("p a h -> p (a h)"), ps[:, 0:64], [pk], ["dts"])
                    store(DTT[t0:t0 + 512, :].rearrange("(a p) h -> p a h", p=128), dts[:], ["dts"], [("DTT", b)])
                S.flush()
            if stop == "p1":
                break
            with contextlib.ExitStack() as st:
                ub = sb(st, "ub", [128, 4, 514], BF16)
                abt = sb(st, "abt", [128, 4, 512], BF16)
                acc = sb(st, "acc", [128, 2, 512], F32)
                vat = sb(st, "vat", [128, 4, 512], BF16)
                xb = sb(st, "xb", [128, 12, 514], BF16)
                xsc = sb(st, "xsc", [128, 12, 512], BF16)
                xtk = sb(st, "xtk", [128, 1024], BF16)
                btk = sb(st, "btk", [128, 256], BF16)
                segs = [(b * 512, 512, 0, TS) for b in range(8)] + [(TS, 256, TS, TS + 256), (TS + 256, 256, TS + 256, T)]
                for (t0, n, s0, s1) in segs:
                    lo, hi = max(t0 - 1, s0), min(t0 + n + 1, s1)
                    off = lo - (t0 - 1)
                    if lo == t0:
                        S.op("pool", lambda e: e.memset(ub[:, :, 0:1], 0.0), writes=["ub"])
                        S.op("pool", lambda e: e.memset(xb[:, :, 0:1], 0.0), writes=["xb"])
                    if hi == t0 + n:
                        S.op("pool", lambda e, n=n: e.memset(ub[:, :, n + 1:n + 2], 0.0), writes=["ub"])
                        S.op("pool", lambda e, n=n: e.memset(xb[:, :, n + 1:n + 2], 0.0), writes=["xb"])
                    load(ub[:, :, off:off + hi - lo], UT[:, lo:hi].rearrange("(k p) t -> p k t", p=128), [], ["ub"])
                    load(abt[:, :, 0:n], ABT[:, t0:t0 + n].rearrange("(k p) t -> p k t", p=128), [], ["abt"])
                    load(xb[:, :, off:off + hi - lo], XBCT[:, lo:hi].rearrange("(k p) t -> p k t", p=128), [], ["xb"])
                    for c in range(16):
                        src, cw, ci, skey = (ub, aconv, c, "ub") if c < 4 else (xb, sconvw, c - 4, "xb")
                        a = acc[:, c % 2, 0:n]
                        ak = ("acc", c % 2)
                        ts("dve", a, src[:, ci, 1:n + 1], cw[:, l, 1, ci:ci + 1], None, ALU.mult, None, [skey, "prm"], [ak])
                        stt("dve", a, src[:, ci, 0:n], cw[:, l, 0, ci:ci + 1], a, ALU.mult, ALU.add, [skey, "prm", ak], [ak])
                        stt("dve", a, src[:, ci, 2:n + 2], cw[:, l, 2, ci:ci + 1], a, ALU.mult, ALU.add, [skey, "prm", ak], [ak])
                        if c < 4:
                            tt("pool", vat[:, ci, 0:n], a, abt[:, ci, 0:n], ALU.mult, [ak, "abt"], ["vat"])
                        else:
                            act(xsc[:, ci, 0:n], a, AF.Silu, [ak, "prm"], ["xsc"], bias=sconvb[:, l, ci:ci + 1])
                    store(VAT[:, t0:t0 + n].rearrange("(k p) t -> p k t", p=128), vat[:, :, 0:n], ["vat"], [])
                    store(XSC[:, t0:t0 + n].rearrange("(k p) t -> p k t", p=128), xsc[:, :, 0:n], ["xsc"], [])
                    for tl in range(n // 128):
                        for c in range(8):
                            S.op("pe", lambda e, c=c, tl=tl: e.transpose(psbT[:, c * 128:(c + 1) * 128], xsc[:, c, tl * 128:(tl + 1) * 128], identb[:]),
                                 reads=["xsc", "identb"], writes=["psbT"])
                        cp("act", xtk[:], psbT[:, 0:1024], ["psbT"], ["xtk"])
                        store(XTOK[t0 + tl * 128:t0 + (tl + 1) * 128, :], xtk[:], ["xtk"], [])
                        for c in range(2):
                            S.op("pe", lambda e, c=c, tl=tl: e.transpose(psbT[:, c * 128:(c + 1) * 128], xsc[:, 8 + c, tl * 128:(tl + 1) * 128], identb[:]),
                                 reads=["xsc", "identb"], writes=["psbT"])
                        cp("dve", btk[:], psbT[:, 0:256], ["psbT"], ["btk"])
                        store(BTOK[t0 + tl * 128:t0 + (tl + 1) * 128, :], btk[:], ["btk"], [])
                S.flush()
            if stop == "conv":
                break
            with contextlib.ExitStack() as st:
                dtr = sb(st, "dtr", [128, NCH, 16], F32)
                ea = sb(st, "ea", [128, 2, 16], F32)
                dtd = [sb(st, "dtd%d" % d, [128, NCH, 16], F32) for d in range(2)]
                dta = [sb(st, "dta%d" % d, [128, NCH, 16], F32) for d in range(2)]
                ctk = [sb(st, "ctk%d" % d, [128, NCH, 16], F32) for d in range(2)]
                tot = [sb(st, "tot%d" % d, [128, NCH, 16], F32) for d in range(2)]
                ted = [sb(st, "ted%d" % d, [128, NCH, 16], F32) for d in range(2)]
                cdc = [sb(st, "cdc%d" % d, [128, NCH, 16], F32) for d in range(2)]
                tri = [triF, triB]
                load(dtr[:], DTT.rearrange("(c p) h -> p c h", p=128), [], ["dtr"])
                act(ea[:], alog[:, l, :, :], AF.Exp, ["prm"], ["ea"])
                for d in range(2):
                    tt("dve", dtd[d][:], dtr[:], dtb[:, l, d, :].unsqueeze(1).to_broadcast([128, NCH, 16]), ALU.add, ["dtr", "prm"], [("dtd", d)])
                    act(dtd[d][:], dtd[d][:], AF.Exp, [("dtd", d)], [("dtd", d)])
                    act(dtd[d][:], dtd[d][:], AF.Ln, [("dtd", d)], [("dtd", d)], bias=1.0)
                    stt("dve", dta[d][:], dtd[d][:], -1.0, ea[:, d, :].unsqueeze(1).to_broadcast([128, NCH, 16]), ALU.mult, ALU.mult,
                        [("dtd", d), "ea"], [("dta", d)])
                    flat = dta[d][:].rearrange("p c h -> p (c h)")
                    for (lhs, dstt, dk) in ((tri[d], ctk[d], "ctk"), (ones, tot[d], "tot")):
                        dflat = dstt[:].rearrange("p c h -> p (c h)")
                        for (a0, a1) in ((0, 512), (512, NCH * 16)):
                            ps, pk = pring.next()
                            mm(ps[:, 0:a1 - a0], lhs, flat[:, a0:a1], True, True, ["cst", ("dta", d)], [pk])
                            cp("dve", dflat[:, a0:a1], ps[:, 0:a1 - a0], [pk], [(dk, d)])
                    tt("dve", ted[d][:], tot[d][:], ctk[d][:], ALU.subtract, [("tot", d), ("ctk", d)], [("ted", d)])
                    act(ted[d][:], ted[d][:], AF.Exp, [("ted", d)], [("ted", d)])
                    tt("dve", ted[d][:], ted[d][:], dtd[d][:], ALU.mult, [("ted", d), ("dtd", d)], [("ted", d)])
                    act(cdc[d][:], tot[d][:], AF.Exp, [("tot", d)], [("cdc", d)])
                with contextlib.ExitStack() as st2:
                    St = sb(st2, "St", [128, 2, 512], F32)
                    hbr = Ring([sb(st2, "hb%d" % i, [128, 1024], BF16) for i in range(2)], "hb")
                    xkr = Ring([sb(st2, "xk%d" % i, [128, 1024], BF16) for i in range(2)], "xk")
                    bkr = Ring([sb(st2, "bk%d" % i, [128, 256], BF16) for i in range(2)], "bk")
                    xwr = Ring([sb(st2, "xw%d" % i, [128, 1024], BF16) for i in range(2)], "xw")
                    for d in range(2):
                        for si, (s0, slen, smp) in enumerate(SEQS):
                            nchk, c0 = slen // 128, s0 // 128
                            if smp:
                                load(St[:], h0[l, d].rearrange("n (g f) -> n g f", g=2), [], ["St"])
                            else:
                                S.op("pool", lambda e: e.memset(St[:], 0.0), writes=["St"])
                            order = range(nchk) if d == 0 else range(nchk - 1, -1, -1)
                            for ci in order:
                                c = c0 + ci
                                hb, hk = hbr.next()
                                cp("act", hb[:], St[:].rearrange("p g f -> p (g f)"), ["St"], [hk])
                                store(HENT[d, c], hb[:], [hk], [])
                                xk, xkk = xkr.next()
                                bk, bkk = bkr.next()
                                xw, xwk = xwr.next()
                                load(xk[:], XTOK[c * 128:(c + 1) * 128, :], [], [xkk])
                                load(bk[:], BTOK[c * 128:(c + 1) * 128, :], [], [bkk])
                                tt("pool", xw[:].rearrange("p (h q) -> p h q", h=16), xk[:].rearrange("p (h q) -> p h q", h=16),
                                   ted[d][:, c, :].unsqueeze(2).to_broadcast([128, 16, 64]), ALU.mult, [xkk, ("ted", d)], [xwk])
                                for g in range(2):
                                    ps, pk = pring.next()
                                    mm(ps[:], bk[:, g * 128:(g + 1) * 128], xw[:, g * 512:(g + 1) * 512], True, True, [bkk, xwk], [pk])
                                    sg3 = St[:, g, :].rearrange("p (h q) -> p h q", h=8)
                                    tt("dve", sg3, sg3, cdc[d][:, c, g * 8:(g + 1) * 8].unsqueeze(2).to_broadcast([128, 8, 64]), ALU.mult,
                                       ["St", ("cdc", d)], ["St"])
                                    tt("dve", St[:, g, :], St[:, g, :], ps[:], ALU.add, ["St", pk], ["St"])
                            if not smp:
                                store(nssm[l, d, si - 1], St[:].rearrange("p g f -> p (g f)"), ["St"], [])
                    S.flush()
                if stop == "ssd1":
                    break
                with contextlib.ExitStack() as st2:
                    xs3r = Ring([sb(st2, "xs3%d" % i, [128, 12, 128], BF16) for i in range(2)], "xs3")
                    xk2r = Ring([sb(st2, "xk2%d" % i, [128, 1024], BF16) for i in range(2)], "xk2")
                    her = [Ring([sb(st2, "he%d%d" % (d, i), [128, 1024], BF16) for i in range(2)], "he%d" % d) for d in range(2)]
                    zsr = Ring([sb(st2, "zs%d" % i, [128, 8, 128], BF16) for i in range(2)], "zs")
                    cbm = [sb(st2, "cbm%d" % d, [128, 2, 128], F32) for d in range(2)]
                    Rt_ = sb(st2, "Rt_", [128, 16, 128], F32)
                    cbdt = sb(st2, "cbdt", [128, 16, 128], F32)
                    crs = Ring([sb(st2, "crs%d" % i, [128, 512], F32) for i in range(2)], "crs")
                    arg = Ring([sb(st2, "arg%d" % i, [128, 512], F32) for i in range(2)], "arg")
                    Et = Ring([sb(st2, "Et%d" % i, [128, 512], F32) for i in range(2)], "Et")
                    ECt = Ring([sb(st2, "ECt%d" % i, [128, 512], F32) for i in range(2)], "ECt")
                    Wt = [sb(st2, "Wt%d" % d, [128, 16, 128], BF16) for d in range(2)]
                    Csc = [sb(st2, "Csc%d" % d, [128, 16, 128], BF16) for d in range(2)]
                    yg = sb(st2, "yg", [128, 8, 128], F32)
                    sqy = sb(st2, "sqy", [128, 2, 128], F32)
                    rsy = sb(st2, "rsy", [128, 128], F32)
                    lny = sb(st2, "lny", [128, 128], F32)
                    ynr = Ring([sb(st2, "yn%d" % i, [128, 8, 128], BF16) for i in range(2)], "yn")
                    for c in range(NCH):
                        tk0 = c * 128
                        xs3, xs3k = xs3r.next()
                        xk2, xk2k = xk2r.next()
                        zs, zsk = zsr.next()
                        load(xs3[:], XSC[:, tk0:tk0 + 128].rearrange("(k p) t -> p k t", p=128), [], [xs3k])
                        load(xk2[:], XTOK[tk0:tk0 + 128, :], [], [xk2k])
                        load(zs[:], ZT[:, tk0:tk0 + 128].rearrange("(k p) t -> p k t", p=128), [], [zsk])
                        he = []
                        for d in range(2):
                            t_, k_ = her[d].next()
                            load(t_[:], HENT[d, c], [], [k_])
                            he.append((t_, k_))
                        psA, pkA = pring.next()
                        for g in range(2):
                            mm(psA[:, g * 128:(g + 1) * 128], xs3[:, 8 + g, :], xs3[:, 10 + g, :], True, True, [xs3k], [pkA])
                        for d in range(2):
                            tt("dve", cbm[d][:], psA[:, 0:256].rearrange("p (g i) -> p g i", g=2), tri[d].unsqueeze(1).to_broadcast([128, 2, 128]),
                               ALU.mult, [pkA, "cst"], [("cbm", d)])
                        psY = []
                        for i in range(2):
                            psY.append(plong.next())
                        for d in range(2):
                            tt("pool", Rt_[:], tri[d].unsqueeze(1).to_broadcast([128, 16, 128]),
                               dta[d][:, c, :].unsqueeze(2).to_broadcast([128, 16, 128]), ALU.mult, ["cst", ("dta", d)], ["Rt_"])
                            for g in range(2):
                                tt("pool", cbdt[:, g * 8:(g + 1) * 8, :], cbm[d][:, g, :].unsqueeze(1).to_broadcast([128, 8, 128]),
                                   dtd[d][:, c, g * 8:(g + 1) * 8].unsqueeze(2).to_broadcast([128, 8, 128]), ALU.mult, [("cbm", d), ("dtd", d)], ["cbdt"])
                            for q in range(4):
                                g = q // 2
                                psc, pkc = pring.next()
                                mm(psc[:], ones, Rt_[:, 4 * q:4 * q + 4, :].rearrange("p h i -> p (h i)"), True, True, ["cst", "Rt_"], [pkc])
                                cr, crk = crs.next()
                                ar, ark = arg.next()
                                et, etk = Et.next()
                                ec, eck = ECt.next()
                                cp("dve", cr[:], psc[:], [pkc], [crk])
                                tt("dve", ar[:].rearrange("p (h i) -> p h i", h=4), psc[:].rearrange("p (h i) -> p h i", h=4),
                                   ctk[d][:, c, 4 * q:4 * q + 4].unsqueeze(2).to_broadcast([128, 4, 128]), ALU.subtract, [pkc, ("ctk", d)], [ark])
                                ts("pool", ar[:], ar[:], 0.0, None, ALU.min, None, [ark], [ark])
                                act(et[:], ar[:], AF.Exp, [ark], [etk])
                                tt("dve", Wt[d][:, 4 * q:4 * q + 4, :].rearrange("p h i -> p (h i)"), et[:],
                                   cbdt[:, 4 * q:4 * q + 4, :].rearrange("p h i -> p (h i)"), ALU.mult, [etk, "cbdt"], [("Wt", d)])
                                act(ec[:], cr[:], AF.Exp, [crk], [eck])
                                tt("pool", Csc[d][:, 4 * q:4 * q + 4, :], ec[:].rearrange("p (h i) -> p h i", h=4),
                                   xs3[:, 10 + g, :].unsqueeze(1).to_broadcast([128, 4, 128]), ALU.mult, [eck, xs3k], [("Csc", d)])
                        for h in range(16):
                            kc, half = h // 2, h % 2
                            pY, pYk = psY[kc // 4]
                            out = pY[half * 64:(half + 1) * 64, (kc % 4) * 128:(kc % 4 + 1) * 128]
                            for d in range(2):
                                mm(out, xk2[:, h * 64:(h + 1) * 64], Wt[d][:, h, :], d == 0, False, [xk2k, ("Wt", d)], [pYk])
                                mm(out, he[d][0][:, h * 64:(h + 1) * 64], Csc[d][:, h, :], False, d == 1, [he[d][1], ("Csc", d)], [pYk])
                        for kc in range(8):
                            pY, pYk = psY[kc // 4]
                            stt("dve", yg[:, kc, :], xs3[:, kc, :], dsum[:, l, kc:kc + 1], pY[:, (kc % 4) * 128:(kc % 4 + 1) * 128], ALU.mult, ALU.add,
                                [xs3k, "dsum", pYk], ["yg"])
                        tt("pool", yg[:], yg[:], zs[:], ALU.mult, ["yg", zsk], ["yg"])
                        ps, pk = pring.next()
                        for kc in range(8):
                            act(sqy[:, kc % 2, :], yg[:, kc, :], AF.Square, ["yg"], [("sqy", kc % 2)])
                            mm(ps[:, 0:128], ones, sqy[:, kc % 2, :], kc == 0, kc == 7, ["cst", ("sqy", kc % 2)], [pk])
                        rstd_from_ssq(ps[:, 0:128], 1024.0, rsy[:], lny[:], [pk], ["rsy"], "lny")
                        yn, ynk = ynr.next()
                        for kc in range(8):
                            stt("dve", yn[:, kc, :], yg[:, kc, :], snw[:, l, kc:kc + 1], rsy[:], ALU.mult, ALU.mult, ["yg", "rsy", "prm"], [ynk])
                        store(YBT[:, tk0:tk0 + 128].rearrange("(k p) t -> p k t", p=128), yn[:], [ynk], [])
                    S.flush()
            if stop == "ssd":
                break
            with contextlib.ExitStack() as st:
                ckvall = sb(st, "ckvall", [128, 2, 4608], BF16)
                KTr = [sb(st, "KT%d" % i, [128, 4608], BF16) for i in range(2)]
                VA = [sb(st, "VA%d" % i, [128, 36, 128], BF16) for i in range(2)]
                qhr = Ring([sb(st, "qh%d" % i, [128, 4096], BF16) for i in range(2)], "qh")
                PT = Ring([sb(st, "PT%d" % i, [128, 512], BF16) for i in range(3)], "PT")
                Lt = sb(st, "Lt", [128, 512], F32)
                Rt = sb(st, "Rt", [128, 512], F32)
                ATr = Ring([sb(st, "AT%d" % i, [128, 512], BF16) for i in range(2)], "AT")
                S.op("pool", lambda e: e.memset(VA[0][:, :, 64:128], 1.0), writes=[("VA", 0)])
                S.op("pool", lambda e: e.memset(VA[1][:, :, 0:64], 1.0), writes=[("VA", 1)])
                for (nctx, k0, nlat, q0, nq, qblk) in ((512, 0, TS, 0, TS, 512), (0, TS, 256, TS, 256, 256), (0, TS + 256, 256, TS + 256, 256, 256)):
                    nk = nctx + nlat
                    ntile = nk // 128
                    if nctx:
                        S.dma("pool", lambda e: e.dma_start(out=ckvall[:, :, 0:512], in_=cckvT[l].rearrange("(k p) t -> p k t", p=128)), writes=["ckvall"])
                        for i in range(2):
                            S.dma("pool", lambda e, i=i: e.dma_start(out=KTr[i][64:96, 0:512], in_=ckrT[l]), writes=[("KT", i)])
                    load(ckvall[:, :, nctx:nk], CKVT[:, k0:k0 + nlat].rearrange("(k p) t -> p k t", p=128), [], ["ckvall"])
                    for i in range(2):
                        load(KTr[i][64:96, nctx:nk], KRT[:, k0:k0 + nlat], [], [("KT", i)])
                    for h in range(8):
                        par = h % 2
                        va, vak = VA[par], ("VA", par)
                        KT, ktk = KTr[par], ("KT", par)
                        voff = par * 64
                        for kb in range((nk + 511) // 512):
                            w = min(512, nk - kb * 512)
                            ps, pk = pring.next()
                            for kc in range(2):
                                mm(ps[0:64, 0:w], ukvw[:, kc, h * 128:h * 128 + 64], ckvall[:, kc, kb * 512:kb * 512 + w], kc == 0, kc == 1, ["ukvw", "ckvall"], [pk])
                            cp("act" if kb % 2 == 0 else "dve", KT[0:64, kb * 512:kb * 512 + w], ps[0:64, 0:w], [pk], [ktk])
                        for tg in range(0, ntile, 8):
                            nt = min(8, ntile - tg)
                            ps, pk = pring.next()
                            for j in range(nt):
                                for kc in range(2):
                                    mm(ps[:, j * 64:(j + 1) * 64], ckvall[:, kc, (tg + j) * 128:(tg + j + 1) * 128], ukvw[:, kc, h * 128 + 64:h * 128 + 128],
                                       kc == 0, kc == 1, ["ukvw", "ckvall"], [pk])
                            cp("dve", va[:, tg:tg + nt, voff:voff + 64], ps[:, 0:nt * 64].rearrange("p (t d) -> p t d", d=64), [pk], [vak])
                        qh, qhk = qhr.next()
                        load(qh[0:96, 0:nq], QT[h, :, q0:q0 + nq], [], [qhk])
                        for qb in range(nq // qblk):
                            psO, pkO = plong.next()
                            for t in range(ntile):
                                psS, pkS = pring.next()
                                mm(psS[:, 0:qblk], KT[0:96, t * 128:(t + 1) * 128], qh[0:96, qb * qblk:(qb + 1) * qblk], True, True, [ktk, qhk], [pkS])
                                pt, ptk = PT.next()
                                act(pt[:, 0:qblk], psS[:, 0:qblk], AF.Exp, [pkS], [ptk], scale=SCALE)
                                mm(psO[:, 0:qblk], va[:, t, :], pt[:, 0:qblk], t == 0, t == ntile - 1, [vak, ptk], [pkO])
                            orow, drow = voff, 64 - voff
                            act(Lt[orow:orow + 64, 0:qblk], psO[drow:drow + 64, 0:qblk], AF.Ln, [pkO], ["Lt"])
                            act(Rt[orow:orow + 64, 0:qblk], Lt[orow:orow + 64, 0:qblk], AF.Exp, ["Lt"], ["Rt"], scale=-1.0)
                            at, atk = ATr.next()
                            tt("dve", at[orow:orow + 64, 0:qblk], psO[orow:orow + 64, 0:qblk], Rt[orow:orow + 64, 0:qblk], ALU.mult, [pkO, "Rt"], [atk])
                            r0 = (h // 2) * 128 + orow
                            store(ATT[r0:r0 + 64, q0 + qb * qblk:q0 + (qb + 1) * qblk], at[orow:orow + 64, 0:qblk], [atk], [])
                S.flush()
            if stop == "attn":
                break
            with contextlib.ExitStack() as st:
                xt = sb(st, "xt3", [128, 8, 512], F32)
                ht = sb(st, "ht3", [128, 8, 512], BF16)
                va_ = sb(st, "va3", [128, 4, 512], BF16)
                yb_ = sb(st, "yb3", [128, 8, 512], BF16)
                at_ = sb(st, "at3", [128, 4, 512], BF16)
                macc = sb(st, "macc", [128, 4, 512], F32)
                sigr = Ring([sb(st, "sig%d" % i, [128, 512], F32) for i in range(2)], "sig")
                tmr = Ring([sb(st, "tm%d" % i, [128, 512], F32) for i in range(2)], "tm")
                merged = sb(st, "merged", [128, 8, 512], BF16)
                h2 = sb(st, "h2", [128, 8, 512], BF16)
                gt = sb(st, "gt", [128, 22, 512], BF16)
                sar = Ring([sb(st, "sa%d" % i, [128, 512], F32) for i in range(2)], "sa")
                sqt = sb(st, "sqt3", [128, 2, 512], F32)
                rst = sb(st, "rst3", [128, 512], F32)
                lnt = sb(st, "lnt3", [128, 512], F32)
                tmpf = sb(st, "tmpf3", [128, 2, 512], F32)
                last = (l == nl - 1)
                yo = sb(st, "yo", [128, 8, 512], F32) if last else None
                for b in range(NB):
                    r = 0 if b < 8 else 1
                    t0 = b * 512
                    fm = lambda dr: dr[:, t0:t0 + 512].rearrange("(k p) t -> p k t", p=128)
                    load(xt[:], fm(xsrc), [], ["xt"])
                    load(ht[:], fm(HT), [], ["ht"])
                    load(va_[:], fm(VAT), [], ["va_"])
                    load(yb_[:], fm(YBT), [], ["yb_"])
                    load(at_[:], fm(ATT), [], ["at_"])
                    for cg in range(2):
                        for br, (wsrc, nkc, rt_, rk) in enumerate(((w_a_out, 4, va_, "va_"), (w_b_out, 8, yb_, "yb_"), (w_c_out, 4, at_, "at_"))):
                            wy, wyk = wload(wsrc[l, :, cg * 512:(cg + 1) * 512], nkc, 512)
                            c0 = C_G + br * 1024 + cg * 512
                            wg, wgk = wload(w_in[l, :, c0:c0 + 512], 8, 512)
                            for oc in range(4):
                                psYy, pky = pring.next()
                                for kc in range(nkc):
                                    mm(psYy[:], wy[:, kc, oc * 128:(oc + 1) * 128], rt_[:, kc, :], kc == 0, kc == nkc - 1, [wyk, rk], [pky])
                                psG, pkg = pring.next()
                                for kc in range(8):
                                    mm(psG[:], wg[:, kc, oc * 128:(oc + 1) * 128], ht[:, kc, :], kc == 0, kc == 7, [wgk, "ht"], [pkg])
                                sg, sgk = sigr.next()
                                act(sg[:], psG[:], AF.Sigmoid, [pkg], [sgk])
                                if br == 0:
                                    tt("dve", macc[:, oc, :], psYy[:], sg[:], ALU.mult, [pky, sgk], [("macc", oc)])
                                else:
                                    tm, tmk = tmr.next()
                                    tt("dve", tm[:], psYy[:], sg[:], ALU.mult, [pky, sgk], [tmk])
                                    if br == 1:
                                        tt("pool", macc[:, oc, :], macc[:, oc, :], tm[:], ALU.add, [("macc", oc), tmk], [("macc", oc)])
                                    else:
                                        tt("pool", merged[:, cg * 4 + oc, :], macc[:, oc, :], tm[:], ALU.add, [("macc", oc), tmk], ["merged"])
                    for cg in range(2):
                        wo, wok = wload(w_o[l, :, cg * 512:(cg + 1) * 512], 8, 512)
                        for oc in range(4):
                            ps, pk = pring.next()
                            for kc in range(8):
                                mm(ps[:], wo[:, kc, oc * 128:(oc + 1) * 128], merged[:, kc, :], kc == 0, kc == 7, [wok, "merged"], [pk])
                            o8 = cg * 4 + oc
                            stt("dve", xt[:, o8, :], ps[:], modcol(l, 2, o8, r), xt[:, o8, :], ALU.mult, ALU.add, [pk, "MOD", "xt"], ["xt"])
                    norm_mod((sqt, rst, lnt, tmpf), xt, h2, lambda kc: A2[:, l, kc, r:r + 1], lambda kc: modcol(l, 3, kc, r), "xt", "h2")
                    for fg in range(6):
                        ncol = 512 if fg < 5 else 256
                        w1, w1k = wload(w_ff1[l, :, fg * 512:fg * 512 + ncol], 8, ncol)
                        w3, w3k = wload(w_ff3[l, :, fg * 512:fg * 512 + ncol], 8, ncol)
                        for oc in range(ncol // 128):
                            j = fg * 4 + oc
                            psA, pka = pring.next()
                            for kc in range(8):
                                mm(psA[:], w1[:, kc, oc * 128:(oc + 1) * 128], h2[:, kc, :], kc == 0, kc == 7, [w1k, "h2"], [pka])
                            psB, pkb = pring.next()
                            for kc in range(8):
                                mm(psB[:], w3[:, kc, oc * 128:(oc + 1) * 128], h2[:, kc, :], kc == 0, kc == 7, [w3k, "h2"], [pkb])
                            sa, sak = sar.next()
                            act(sa[:], psA[:], AF.Silu, [pka], [sak])
                            tt("dve", gt[:, j, :], sa[:], psB[:], ALU.mult, [sak, pkb], [("gt", j)])
                    for cg in range(4):
                        w2, w2k = wload(w_ff2[l, :, cg * 256:(cg + 1) * 256], 22, 256)
                        for oc in range(2):
                            ps, pk = pring.next()
                            for j in range(22):
                                mm(ps[:], w2[:, j, oc * 128:(oc + 1) * 128], gt[:, j, :], j == 0, j == 21, [w2k, ("gt", j)], [pk])
                            o8 = cg * 2 + oc
                            stt("dve", xt[:, o8, :], ps[:], modcol(l, 5, o8, r), xt[:, o8, :], ALU.mult, ALU.add, [pk, "MOD", "xt"], ["xt"])
                    if not last:
                        store(fm(XT), xt[:], ["xt"], [])
                    else:
                        norm_mod((sqt, rst, lnt, tmpf), xt, yo, lambda kc: fnw[:, kc:kc + 1], None, "xt", "yo")
                        store(fm(yT), yo[:], ["yo"], [])
                    if stop == "p3" and "XT" in dump:
                        store(fm(XT), xt[:], ["xt"], [])
                S.flush()
        S.flush(final=True)
    return nc


def rope_tables():
    n_rows = TS // 64
    row = np.repeat(np.arange(n_rows, dtype=np.float32), 64)
    col = np.tile(np.arange(64, dtype=np.float32), n_rows)
    inv = (np.float32(10000.0) ** (-np.arange(8, dtype=np.float32) / np.float32(8))).astype(np.float32)
    ang = np.concatenate([row[:, None] * inv, col[:, None] * inv], axis=-1).astype(np.float32)
    cos, sin = np.cos(ang).astype(np.float32), np.sin(ang).astype(np.float32)
    C = np.concatenate([cos, cos], axis=1).T
    Sg = np.concatenate([-sin, sin], axis=1).T
    return np.ascontiguousarray(C), np.ascontiguousarray(Sg)


def make_in_maps(inp, nl=DEPTH):
    f = lambda k: np.asarray(inp[k], np.float32)
    ropeC, ropeS = rope_tables()
    k = np.arange(128)
    triF = (k[:, None] <= k[None, :]).astype(np.float32)
    triB = (k[:, None] >= k[None, :]).astype(np.float32)
    cst = np.concatenate([triF, triB, np.ones((128, 128), np.float32), np.eye(128, dtype=np.float32)], axis=1)
    prm = pack_params(inp)
    w_in = f("w_in")
    w_kr2 = np.zeros((DEPTH, D, 2, 96), np.float32)
    krc = w_in[:, :, C_KR:C_KR + 32]
    w_kr2[:, :, 0, 64:96] = krc
    w_kr2[:, :, 1, 64:80] = krc[:, :, 16:32]
    w_kr2[:, :, 1, 80:96] = krc[:, :, 0:16]
    wuq = f("w_uq").reshape(DEPTH, 256, 8, 96)
    w_uq2 = np.zeros((DEPTH, 256, 2, 8, 96), np.float32)
    w_uq2[:, :, 0] = wuq
    w_uq2[:, :, 1, :, 0:64] = wuq[..., 0:64]
    w_uq2[:, :, 1, :, 64:80] = wuq[..., 80:96]
    w_uq2[:, :, 1, :, 80:96] = wuq[..., 64:80]
    shared = {
        "ropeC": ropeC, "ropeS": ropeS, "cst": cst, "prm": prm, "w_in": w_in, "w_kr2": w_kr2, "w_uq2": w_uq2,
        "w_ukv": f("w_ukv"), "w_a_out": f("w_a_out"), "w_b_out": f("w_b_out"), "w_c_out": f("w_c_out"),
        "w_o": f("w_o"), "w_ada": f("w_ada"), "w_ff1": f("w_ff1"), "w_ff3": f("w_ff3"), "w_ff2": f("w_ff2"),
    }
    for k in ("w_in", "w_kr2", "w_uq2", "w_ukv", "w_a_out", "w_b_out", "w_c_out", "w_o", "w_ada", "w_ff1", "w_ff3", "w_ff2"):
        shared[k] = np.ascontiguousarray(shared[k][:nl])
    xs, xp, c, cctx = f("x_sample"), f("x_prompt"), f("c"), f("c_ctx")
    cckv, ckr = f("cache_ckv"), f("cache_krope")
    sf, sbw = f("state_ssm_fwd"), f("state_ssm_bwd")
    maps = []
    for r in range(8):
        bs = r % 4
        xT0 = np.concatenate([xs[bs].T, xp[2 * r].T, xp[2 * r + 1].T], axis=1)
        cd = np.stack([c[bs], cctx], axis=1)
        cd = cd.reshape(8, 128, 2).transpose(1, 0, 2)
        h0 = np.stack([sf[bs], sbw[bs]], axis=1)
        h0 = h0.transpose(0, 1, 4, 2, 3).reshape(DEPTH, 2, 128, 1024)
        m = dict(shared)
        m.update({
            "xT0": np.ascontiguousarray(xT0), "cond": np.ascontiguousarray(cd),
            "cckvT": np.ascontiguousarray(cckv[bs].transpose(0, 2, 1)),
            "ckrT": np.ascontiguousarray(ckr[bs].transpose(0, 2, 1)),
            "h0": np.ascontiguousarray(h0),
        })
        maps.append(m)
    return maps


_NC_CACHE = {}


def kernel(**inputs):
    maps = make_in_maps(inputs)
    if "nc" not in _NC_CACHE:
        _NC_CACHE["nc"] = build_program()
    nc = _NC_CACHE["nc"]
    res = run_bass_kernel_spmd(nc, maps, core_ids=list(range(8)))
    R = res.results
    y_prompt = np.zeros((16, 256, D), np.float32)
    y_sample = np.zeros((4, TS, D), np.float32)
    new_ckv = np.zeros((16, DEPTH, 256, 256), np.float32)
    new_kr = np.zeros((16, DEPTH, 256, 32), np.float32)
    new_f = np.zeros((16, DEPTH, 16, 64, 128), np.float32)
    new_b = np.zeros((16, DEPTH, 16, 64, 128), np.float32)
    for r in range(8):
        yT = R[r]["yT"]
        if r < 4:
            y_sample[r] = yT[:, :TS].T
        for s in range(2):
            q = 2 * r + s
            y_prompt[q] = yT[:, TS + s * 256:TS + (s + 1) * 256].T
            new_ckv[q] = R[r]["nckvT"][:, :, s * 256:(s + 1) * 256].transpose(0, 2, 1)
            new_kr[q] = R[r]["nkrT"][:, :, s * 256:(s + 1) * 256].transpose(0, 2, 1)
            st = R[r]["nssm"][:, :, s].reshape(DEPTH, 2, 128, 16, 64).transpose(0, 1, 3, 4, 2)
            new_f[q] = st[:, 0]
            new_b[q] = st[:, 1]
    return (y_prompt, y_sample, new_ckv, new_kr, new_f, new_b)
```

```python
import contextlib
import math
import os

import numpy as np
import concourse.bass as bass
import concourse.mybir as mybir
from concourse.bass_utils import run_bass_kernel_spmd

F32 = mybir.dt.float32
BF16 = mybir.dt.bfloat16
AF = mybir.ActivationFunctionType
ALU = mybir.AluOpType

D = 1024
DEPTH = 4
TS = 4096
TPR = 512
T = TS + TPR
NB = T // 512
NCH = T // 128
EPS = 1e-6
IN_COLS = 7728
FF = 2816
C_AX, C_AB, C_AC, C_Z, C_XBC, C_DT, C_CQ, C_CKV, C_KR, C_G = 0, 512, 1024, 1536, 2560, 4096, 4112, 4368, 4624, 4656
SCALE = 1.0 / math.sqrt(96.0)
SEQS = [(0, 4096, True), (4096, 256, False), (4352, 256, False)]

ENGS = ("pe", "act", "dve", "pool", "sp")


class Sched:
    def __init__(self, nc, st, n_dma_sems=48):
        self.nc = nc
        self.n_dma_sems = n_dma_sems
        self.esem = {e: st.enter_context(nc.semaphore("s_" + e)) for e in ENGS if e != "sp"}
        self.dsem = [st.enter_context(nc.semaphore("d%d" % i)) for i in range(n_dma_sems)]
        self.dummy = st.enter_context(nc.sbuf_tensor("bar_dummy", [128, 2], F32))
        self.ecount = {e: 0 for e in ENGS}
        self.dma_val = [0] * n_dma_sems
        self.dma_last = [None] * n_dma_sems
        self.dma_rr = 0
        self.waited = {e: {} for e in ENGS}
        self.barrier_tok = None
        self.need_barrier = {e: False for e in ENGS}
        self._reset()

    def _reset(self):
        self.ops = {e: [] for e in ENGS}
        self.last_write = {}
        self.readers = {}
        self.dma_toks = []

    def _record(self, eng, fn, reads, writes, dma, extra_deps=()):
        deps = set(extra_deps)
        for r in reads:
            w = self.last_write.get(r)
            if w is not None:
                deps.add(w)
        for r in writes:
            w = self.last_write.get(r)
            if w is not None:
                deps.add(w)
            for rd in self.readers.get(r, ()):
                deps.add(rd)
        idx = len(self.ops[eng])
        if dma:
            si = self.dma_rr
            self.dma_rr = (self.dma_rr + 1) % self.n_dma_sems
            prev = self.dma_last[si]
            if prev is not None:
                deps.add(prev)
            self.dma_val[si] += 16
            tok = ("dma", si, self.dma_val[si])
            self.dma_last[si] = tok
            self.dma_toks.append(tok)
        else:
            tok = ("eng", eng, idx)
        deps = {d for d in deps if not (d[0] == "eng" and d[1] == "pe" and eng == "pe")}
        deps.discard(tok)
        if self.need_barrier[eng] and self.barrier_tok is not None:
            deps.add(self.barrier_tok)
            self.need_barrier[eng] = False
        self.ops[eng].append(dict(fn=fn, deps=deps, flag=False, tok=tok))
        for r in reads:
            lst = self.readers.setdefault(r, [])
            if tok[0] == "eng":
                lst[:] = [t for t in lst if not (t[0] == "eng" and t[1] == eng)]
            lst.append(tok)
        for r in writes:
            self.last_write[r] = tok
            self.readers[r] = []
        return tok

    def op(self, eng, fn, reads=(), writes=(), extra_deps=()):
        return self._record(eng, fn, reads, writes, False, extra_deps)

    def dma(self, eng, fn, reads=(), writes=(), extra_deps=()):
        return self._record(eng, fn, reads, writes, True, extra_deps)

    def flush(self, final=False):
        nc = self.nc
        deps = set()
        for e in ENGS:
            if e == "sp":
                continue
            for i in range(len(self.ops[e]) - 1, -1, -1):
                if self.ops[e][i]["tok"][0] == "eng":
                    deps.add(self.ops[e][i]["tok"])
                    break
        latest = {}
        for t in self.dma_toks:
            if t[1] not in latest or latest[t[1]][2] < t[2]:
                latest[t[1]] = t
        deps.update(latest.values())
        dummy = self.dummy
        coll = self._record("dve", lambda e: e.memset(dummy[:, 0:1], 0.0), (), (), False, deps)
        for e in ENGS:
            for o in self.ops[e]:
                for d in o["deps"]:
                    if d[0] == "eng":
                        self.ops[d[1]][d[2]]["flag"] = True
        self.ops["dve"][coll[2]]["flag"] = True
        cnt = {}
        for e in ENGS:
            c = self.ecount[e]
            arr = []
            for o in self.ops[e]:
                if o["flag"]:
                    c += 1
                arr.append(c)
            cnt[e] = arr
        esem, dsem = self.esem, self.dsem

        def resolve(d):
            if d[0] == "eng":
                return esem[d[1]], cnt[d[1]][d[2]]
            if d[0] == "abs":
                return d[1], d[2]
            return dsem[d[1]], d[2]

        coll_abs = ("abs", esem["dve"], cnt["dve"][coll[2]])

        def run(ename, eng):
            waited = self.waited[ename]
            for o in self.ops[ename]:
                need = {}
                for d in o["deps"]:
                    s, v = resolve(d)
                    k = id(s)
                    if waited.get(k, 0) >= v:
                        continue
                    if k not in need or need[k][1] < v:
                        need[k] = (s, v)
                for k, (s, v) in need.items():
                    eng.wait_ge(s, v)
                    waited[k] = v
                ins = o["fn"](eng)
                if o["tok"][0] == "dma":
                    ins.then_inc(dsem[o["tok"][1]], 16)
                elif o["flag"]:
                    ins.then_inc(esem[ename], 1)
            if final and ename == "sp":
                s, v = resolve(coll_abs)
                eng.wait_ge(s, v)

        if os.environ.get("KDBG_SIM"):
            self._simulate(resolve, coll_abs, final)

        with nc.Block() as block:
            @block.sync
            def _(sync):
                run("sp", sync)

            @block.tensor
            def _(tensor):
                run("pe", tensor)

            @block.scalar
            def _(scalar):
                run("act", scalar)

            @block.vector
            def _(vector):
                run("dve", vector)

            @block.gpsimd
            def _(gpsimd):
                run("pool", gpsimd)

        for e in ENGS:
            if cnt[e]:
                self.ecount[e] = cnt[e][-1]
        self.barrier_tok = coll_abs
        self.need_barrier = {e: True for e in ENGS}
        self._reset()


def _sched_simulate(self, resolve, coll_abs, final):
    if not hasattr(self, "sim_sem"):
        self.sim_sem = {}
    sem = self.sim_sem
    pos = {e: 0 for e in ENGS}
    progress = True
    while progress:
        progress = False
        for e in ENGS:
            while pos[e] < len(self.ops[e]):
                o = self.ops[e][pos[e]]
                ok = True
                for d in o["deps"]:
                    s_, v = resolve(d)
                    if sem.get(id(s_), 0) < v:
                        ok = False
                        break
                if not ok:
                    break
                if o["tok"][0] == "dma":
                    k = id(self.dsem[o["tok"][1]])
                    sem[k] = sem.get(k, 0) + 16
                elif o["flag"]:
                    k = id(self.esem[e])
                    sem[k] = sem.get(k, 0) + 1
                pos[e] += 1
                progress = True
    stuck = {e: (pos[e], len(self.ops[e])) for e in ENGS if pos[e] < len(self.ops[e])}
    if stuck:
        print("SCHED DEADLOCK:", stuck)
        for e in stuck:
            o = self.ops[e][pos[e]]
            print("  ", e, "op", pos[e], "tok", o["tok"], "deps", [(d, resolve(d)[1], sem.get(id(resolve(d)[0]), 0)) for d in o["deps"]])
        raise RuntimeError("sched deadlock")
    else:
        print("sched sim ok:", {e: len(self.ops[e]) for e in ENGS})


Sched._simulate = _sched_simulate


class Ring:
    def __init__(self, tiles, name):
        self.tiles = tiles
        self.name = name
        self.i = 0

    def next(self):
        i = self.i
        self.i = (self.i + 1) % len(self.tiles)
        return self.tiles[i], (self.name, i)


PRM_LAYOUT = [("n1w", 4 * 8), ("n2w", 4 * 8), ("fnw", 8), ("snw", 4 * 8), ("qnw", 4 * 2), ("kvnw", 4 * 2),
              ("aconv", 4 * 3 * 4), ("sconvw", 4 * 3 * 12), ("sconvb", 4 * 12), ("dexp", 4 * 2 * 8),
              ("alog", 4 * 2 * 16), ("dtb", 4 * 2 * 16), ("bada", 4 * 48)]
PRM_OFF = {}
_o = 0
for _n, _s in PRM_LAYOUT:
    PRM_OFF[_n] = (_o, _s)
    _o += _s
NPRM = _o


def _pc(v):
    v = np.asarray(v, np.float32)
    lead = v.shape[:-1]
    c = v.shape[-1] // 128
    v = v.reshape(lead + (c, 128))
    v = np.moveaxis(v, -1, 0)
    return np.ascontiguousarray(v).reshape(128, -1)


def pack_params(inp):
    parts = {
        "n1w": _pc(inp["norm1_w"]), "n2w": _pc(inp["norm2_w"]), "fnw": _pc(inp["final_norm_w"]),
        "snw": _pc(inp["ssm_norm_w"]), "qnw": _pc(inp["q_norm_w"]), "kvnw": _pc(inp["kv_norm_w"]),
        "aconv": _pc(inp["a_conv_w"]), "sconvw": _pc(inp["ssm_conv_w"]), "sconvb": _pc(inp["ssm_conv_b"]),
        "dexp": _pc(np.repeat(np.asarray(inp["ssm_d"], np.float32), 64, axis=-1)),
        "alog": np.broadcast_to(np.asarray(inp["ssm_a_log"], np.float32).reshape(1, -1), (128, 128)),
        "dtb": np.broadcast_to(np.asarray(inp["ssm_dt_bias"], np.float32).reshape(1, -1), (128, 128)),
        "bada": _pc(inp["b_ada"]),
    }
    out = np.zeros((128, NPRM), np.float32)
    for n, (o, s) in PRM_OFF.items():
        assert parts[n].shape == (128, s), (n, parts[n].shape, s)
        out[:, o:o + s] = parts[n]
    return out


def build_program(stop=None, dump=(), nl=DEPTH):
    nc = bass.Bass("TRN2", target_bir_lowering=False)

    def din(name, shape, dt=F32):
        return nc.dram_tensor(name, list(shape), dt, kind="ExternalInput").ap()

    def dout(name, shape, dt=F32):
        return nc.dram_tensor(name, list(shape), dt, kind="ExternalOutput").ap()

    def dscr(name, shape, dt):
        kind = "ExternalOutput" if name in dump else "Internal"
        return nc.dram_tensor(name, list(shape), dt, kind=kind).ap()

    xT0 = din("xT0", [D, T])
    cond = din("cond", [128, 8, 2])
    cckvT = din("cckvT", [DEPTH, 256, 512])
    ckrT = din("ckrT", [DEPTH, 32, 512])
    h0 = din("h0", [DEPTH, 2, 128, 1024])
    ropeC = din("ropeC", [32, TS])
    ropeS = din("ropeS", [32, TS])
    cst = din("cst", [128, 512])
    prm_d = din("prm", [128, NPRM])
    w_in = din("w_in", [nl, D, IN_COLS])
    w_kr2 = din("w_kr2", [nl, D, 2, 96])
    w_uq2 = din("w_uq2", [nl, 256, 2, 8, 96])
    w_ukv = din("w_ukv", [nl, 256, 1024])
    w_a_out = din("w_a_out", [nl, 512, D])
    w_b_out = din("w_b_out", [nl, D, D])
    w_c_out = din("w_c_out", [nl, 512, D])
    w_o = din("w_o", [nl, D, D])
    w_ada = din("w_ada", [nl, D, 6 * D])
    w_ff1 = din("w_ff1", [nl, D, FF])
    w_ff3 = din("w_ff3", [nl, D, FF])
    w_ff2 = din("w_ff2", [nl, FF, D])

    yT = dout("yT", [D, T])
    nckvT = dout("nckvT", [DEPTH, 256, 512])
    nkrT = dout("nkrT", [DEPTH, 32, 512])
    nssm = dout("nssm", [DEPTH, 2, 2, 128, 1024])

    XT = dscr("XT", [D, T], F32)
    HT = dscr("HT", [D, T], BF16)
    UT = dscr("UT", [512, T], BF16)
    ABT = dscr("ABT", [512, T], BF16)
    ZT = dscr("ZT", [D, T], BF16)
    XBCT = dscr("XBCT", [1536, T], BF16)
    DTT = dscr("DTT", [T, 16], F32)
    QT = dscr("QT", [8, 96, T], BF16)
    CKVT = dscr("CKVT", [256, T], BF16)
    KRT = dscr("KRT", [32, T], BF16)
    VAT = dscr("VAT", [512, T], BF16)
    XSC = dscr("XSC", [1536, T], BF16)
    XTOK = dscr("XTOK", [T, 1024], BF16)
    BTOK = dscr("BTOK", [T, 256], BF16)
    HENT = dscr("HENT", [2, NCH, 128, 1024], BF16)
    YBT = dscr("YBT", [D, T], BF16)
    ATT = dscr("ATT", [512, T], BF16)

    with contextlib.ExitStack() as gst:
        S = Sched(nc, gst)

        _uid = [0]

        def sb(st, name, shape, dt):
            _uid[0] += 1
            return st.enter_context(nc.sbuf_tensor("sb%d_%s" % (_uid[0], name), list(shape), dt))

        prm = sb(gst, "prm", [128, NPRM], F32)
        cstt = sb(gst, "cstt", [128, 512], F32)
        identb = sb(gst, "identb", [128, 128], BF16)
        MOD = sb(gst, "MOD", [128, DEPTH, 48, 2], F32)
        A1 = sb(gst, "A1", [128, DEPTH, 8, 2], F32)
        A2 = sb(gst, "A2", [128, DEPTH, 8, 2], F32)
        dsum = sb(gst, "dsum", [128, DEPTH, 8], F32)
        wring = Ring([sb(gst, "wr%d" % i, [128, 6144], BF16) for i in range(4)], "wr")
        uqw = sb(gst, "uqw", [128, 2, 2 * 8 * 96], BF16)
        krw = sb(gst, "krw", [128, 8, 2 * 96], BF16)
        dtw = sb(gst, "dtw", [128, 8, 16], BF16)
        ukvw = sb(gst, "ukvw", [128, 2, 1024], BF16)
        psum = [gst.enter_context(nc.psum_tensor("ps%d" % i, [128, 512], F32)) for i in range(7)]
        psbT = gst.enter_context(nc.psum_tensor("psbT", [128, 1024], BF16))
        pring = Ring(psum[0:5], "ps")
        plong = Ring(psum[5:7], "pl")
        triF = cstt[:, 0:128]
        triB = cstt[:, 128:256]
        ones = cstt[:, 256:384]

        def P(name, l=None):
            o, s = PRM_OFF[name]
            v = prm[:, o:o + s]
            return v

        def pv(name, pattern, **kw):
            o, s = PRM_OFF[name]
            return prm[:, o:o + s].rearrange(pattern, **kw)

        n1w = pv("n1w", "p (l c) -> p l c", l=4)
        n2w = pv("n2w", "p (l c) -> p l c", l=4)
        fnw = P("fnw")
        snw = pv("snw", "p (l c) -> p l c", l=4)
        qnw = pv("qnw", "p (l c) -> p l c", l=4)
        kvnw = pv("kvnw", "p (l c) -> p l c", l=4)
        aconv = pv("aconv", "p (l k c) -> p l k c", l=4, k=3)
        sconvw = pv("sconvw", "p (l k c) -> p l k c", l=4, k=3)
        sconvb = pv("sconvb", "p (l c) -> p l c", l=4)
        dexp = pv("dexp", "p (l d c) -> p l d c", l=4, d=2)
        alog = pv("alog", "p (l d h) -> p l d h", l=4, d=2)
        dtb = pv("dtb", "p (l d h) -> p l d h", l=4, d=2)
        bada = pv("bada", "p (l c) -> p l c", l=4)

        def mm(out, lhsT, rhs, start, stop, rd, wr):
            S.op("pe", lambda e: e.matmul(out, lhsT=lhsT, rhs=rhs, start=start, stop=stop), reads=rd, writes=wr)

        def act(out, in_, func, rd, wr, **kw):
            S.op("act", lambda e: e.activation(out=out, in_=in_, func=func, **kw), reads=rd, writes=wr)

        def tt(eng, out, in0, in1, op, rd, wr):
            S.op(eng, lambda e: e.tensor_tensor(out=out, in0=in0, in1=in1, op=op), reads=rd, writes=wr)

        def ts(eng, out, in0, s1, s2, op0, op1, rd, wr):
            if op1 is None:
                S.op(eng, lambda e: e.tensor_scalar(out=out, in0=in0, scalar1=s1, scalar2=None, op0=op0), reads=rd, writes=wr)
            else:
                S.op(eng, lambda e: e.tensor_scalar(out=out, in0=in0, scalar1=s1, scalar2=s2, op0=op0, op1=op1), reads=rd, writes=wr)

        def stt(eng, out, in0, scalar, in1, op0, op1, rd, wr):
            S.op(eng, lambda e: e.scalar_tensor_tensor(out=out, in0=in0, scalar=scalar, in1=in1, op0=op0, op1=op1), reads=rd, writes=wr)

        def cp(eng, out, in_, rd, wr):
            if eng == "act":
                act(out, in_, AF.Copy, rd, wr)
            else:
                S.op(eng, lambda e: e.tensor_copy(out=out, in_=in_), reads=rd, writes=wr)

        def load(out, in_, rd, wr, eng="sp"):
            return S.dma(eng, lambda e: e.dma_start(out=out, in_=in_), reads=rd, writes=wr)

        def store(out, in_, rd, wr, eng="act"):
            return S.dma(eng, lambda e: e.dma_start(out=out, in_=in_), reads=rd, writes=wr)

        def wload(src2d, n_kc, ncols):
            slot, key = wring.next()
            view = slot[:, 0:n_kc * ncols].rearrange("p (k n) -> p k n", k=n_kc)
            S.dma("pool", lambda e: e.dma_start(out=view, in_=src2d.rearrange("(k p) n -> p k n", p=128)), reads=[], writes=[key])
            return view, key

        def rstd_from_ssq(ps_ap, n_feat, out_ap, tmp_ap, rd, wr, tmpkey):
            act(tmp_ap, ps_ap, AF.Ln, rd, [tmpkey], scale=1.0 / n_feat, bias=EPS)
            act(out_ap, tmp_ap, AF.Exp, [tmpkey], wr, scale=-0.5)

        with contextlib.ExitStack() as st:
            condt = sb(st, "condt", [128, 8, 2], F32)
            scb = sb(st, "scb", [128, 8, 16], BF16)
            load(prm[:], prm_d, [], ["prm"])
            load(cstt[:], cst, [], ["cst"])
            load(condt[:], cond, [], ["condt"])
            cp("dve", identb[:], cstt[:, 384:512], ["cst"], ["identb"])
            S.op("pool", lambda e: e.memset(scb[:], 0.0), writes=["scb"])
            act(scb[:, :, 0:2], condt[:], AF.Silu, ["condt", "scb"], ["scb"])
            _ncg = int(os.environ.get("KDBG_NCG", "12"))
            for l in range(nl):
                for cg in range(_ncg):
                    wv, wk = wload(w_ada[l, :, cg * 512:(cg + 1) * 512], 8, 512)
                    for oc in range(4):
                        ps, pk = pring.next()
                        for kc in range(8):
                            mm(ps[:, 0:16], wv[:, kc, oc * 128:(oc + 1) * 128], scb[:, kc, :], kc == 0, kc == 7, [wk, "scb"], [pk])
                        ci = cg * 4 + oc
                        act(MOD[:, l, ci, :], ps[:, 0:2], AF.Identity, [pk, "prm"], ["MOD"], bias=bada[:, l, ci:ci + 1])
            for l in range(nl):
                for (Ax, k0, nw) in ((A1, 8, n1w), (A2, 32, n2w)):
                    ts("dve", Ax[:, l, :, :], MOD[:, l, k0:k0 + 8, :], 1.0, None, ALU.add, None, ["MOD"], ["A"])
                    tt("dve", Ax[:, l, :, :], Ax[:, l, :, :], nw[:, l, :].unsqueeze(2).to_broadcast([128, 8, 2]), ALU.mult, ["A", "prm"], ["A"])
                tt("dve", dsum[:, l, :], dexp[:, l, 0, :], dexp[:, l, 1, :], ALU.add, ["prm"], ["dsum"])
            S.flush()
        if stop == "p0":
            dbgo = dout("dbg_mod", [128, DEPTH * 48 * 2])
            load(dbgo, MOD[:].rearrange("p l c r -> p (l c r)"), ["MOD"], ["dbgo"])
            S.flush(final=True)
            return nc

        def modcol(l, kind, kc, r):
            return MOD[:, l, kind * 8 + kc, r:r + 1]

        def norm_mod(st_tiles, xt, ht, Acol, Bcol, xkey, hkey):
            sqt, rst, lnt, tmpf = st_tiles
            ps, pk = pring.next()
            for kc in range(8):
                act(sqt[:, kc % 2, :], xt[:, kc, :], AF.Square, [xkey], [("sq", kc % 2)])
                mm(ps[:], ones, sqt[:, kc % 2, :], kc == 0, kc == 7, ["cst", ("sq", kc % 2)], [pk])
            rstd_from_ssq(ps[:], 1024.0, rst[:], lnt[:], [pk], ["rst"], "lnt")
            for kc in range(8):
                if Bcol is None:
                    stt("dve", ht[:, kc, :], xt[:, kc, :], Acol(kc), rst[:], ALU.mult, ALU.mult, [xkey, "rst", "A", "prm"], [hkey])
                else:
                    stt("dve", tmpf[:, kc % 2, :], xt[:, kc, :], Acol(kc), rst[:], ALU.mult, ALU.mult, [xkey, "rst", "A", "prm"], [("tmpf", kc % 2)])
                    ts("pool", ht[:, kc, :], tmpf[:, kc % 2, :], Bcol(kc), None, ALU.add, None, [("tmpf", kc % 2), "MOD"], [hkey])

        for l in range(nl):
            xsrc = xT0 if l == 0 else XT
            with contextlib.ExitStack() as st:
                xt = sb(st, "xt", [128, 8, 512], F32)
                ht = sb(st, "ht", [128, 8, 512], BF16)
                sqt = sb(st, "sqt", [128, 2, 512], F32)
                rst = sb(st, "rst", [128, 512], F32)
                lnt = sb(st, "lnt", [128, 512], F32)
                tmpf = sb(st, "tmpf", [128, 2, 512], F32)
                stg = Ring([sb(st, "stg%d" % i, [128, 4, 512], BF16) for i in range(3)], "stg")
                axt = sb(st, "axt", [128, 4, 512], BF16)
                cqf = sb(st, "cqf", [128, 4, 512], F32)
                nrf = sb(st, "nrf", [128, 2, 512], F32)
                cqn = sb(st, "cqn", [128, 2, 512], BF16)
                ckvn = sb(st, "ckvn", [128, 2, 512], BF16)
                qst = sb(st, "qst", [128, 8, 512], BF16)
                rC = sb(st, "rC", [128, 512], F32)
                rS = sb(st, "rS", [128, 512], F32)
                t1 = sb(st, "t1", [128, 512], F32)
                t2 = sb(st, "t2", [128, 512], F32)
                krt = sb(st, "krt", [128, 512], BF16)
                krf = sb(st, "krf", [128, 512], F32)
                dts = sb(st, "dts", [128, 4, 16], F32)
                S.dma("pool", lambda e, l=l: e.dma_start(out=uqw[:], in_=w_uq2[l].rearrange("(k p) v h r -> p k (v h r)", p=128)), writes=["uqw"])
                S.dma("pool", lambda e, l=l: e.dma_start(out=krw[:], in_=w_kr2[l].rearrange("(k p) v r -> p k (v r)", p=128)), writes=["krw"])
                S.dma("pool", lambda e, l=l: e.dma_start(out=dtw[:], in_=w_in[l, :, C_DT:C_DT + 16].rearrange("(k p) n -> p k n", p=128)), writes=["dtw"])
                S.dma("pool", lambda e, l=l: e.dma_start(out=ukvw[:], in_=w_ukv[l].rearrange("(k p) n -> p k n", p=128)), writes=["ukvw"])
                uq5 = uqw[:].rearrange("p k (v h r) -> p k v h r", v=2, h=8)
                kr4 = krw[:].rearrange("p k (v r) -> p k v r", v=2)
                _steps = os.environ.get("KDBG_P1", "groups,mla,q,kr,dt").split(",")
                _blks = [int(v) for v in os.environ.get("KDBG_BLKS", ",".join(str(i) for i in range(NB))).split(",")]
                for b in _blks:
                    r = 0 if b < 8 else 1
                    smp = b < 8
                    t0 = b * 512
                    load(xt[:], xsrc[:, t0:t0 + 512].rearrange("(k p) t -> p k t", p=128), [("XT", b)], ["xt"])
                    if smp:
                        load(rC[64:96, :], ropeC[:, t0:t0 + 512], [], ["rC"])
                        load(rS[64:96, :], ropeS[:, t0:t0 + 512], [], ["rS"])
                    norm_mod((sqt, rst, lnt, tmpf), xt, ht,
                             lambda kc: A1[:, l, kc, r:r + 1], lambda kc: modcol(l, 0, kc, r), "xt", "ht")
                    store(HT[:, t0:t0 + 512].rearrange("(k p) t -> p k t", p=128), ht[:], ["ht"], [("HT", b)])
                    groups = [("ax", C_AX), ("ab", C_AB), ("ac", C_AC), ("z", C_Z), ("z", C_Z + 512),
                              ("xbc", C_XBC), ("xbc", C_XBC + 512), ("xbc", C_XBC + 1024), ("cqkv", C_CQ)]
                    if "groups" not in _steps:
                        groups = []
                    for gi, (kind, c0) in enumerate(groups):
                        wv, wk = wload(w_in[l, :, c0:c0 + 512], 8, 512)
                        if kind in ("ab", "ac", "z", "xbc"):
                            sg, sk = stg.next()
                        for oc in range(4):
                            ps, pk = pring.next()
                            for kc in range(8):
                                mm(ps[:], wv[:, kc, oc * 128:(oc + 1) * 128], ht[:, kc, :], kc == 0, kc == 7, [wk, "ht"], [pk])
                            if kind == "ax":
                                cp("act", axt[:, oc, :], ps[:], [pk], ["axt"])
                            elif kind == "ab":
                                cp("act", sg[:, oc, :], ps[:], [pk], [sk])
                            elif kind == "ac":
                                tt("dve", sg[:, oc, :], ps[:], axt[:, oc, :], ALU.mult, [pk, "axt"], [sk])
                            elif kind == "z":
                                act(sg[:, oc, :], ps[:], AF.Silu, [pk], [sk])
                            elif kind == "xbc":
                                cp("act" if oc % 2 == 0 else "dve", sg[:, oc, :], ps[:], [pk], [sk])
                            else:
                                cp("act" if oc % 2 == 0 else "dve", cqf[:, oc, :], ps[:], [pk], [("cqf", oc // 2)])
                        if kind in ("ab", "ac", "z", "xbc"):
                            dst = {"ab": ABT, "ac": UT, "z": ZT, "xbc": XBCT}[kind]
                            r0 = c0 - {"ab": C_AB, "ac": C_AC, "z": C_Z, "xbc": C_XBC}[kind]
                            store(dst[r0:r0 + 512, t0:t0 + 512].rearrange("(k p) t -> p k t", p=128), sg[:], [sk], [(kind + "T", b, r0)])
                    for half, nw, dstb in ((0, qnw, cqn), (1, kvnw, ckvn)) if "mla" in _steps else ():
                        ps, pk = pring.next()
                        for j in range(2):
                            act(sqt[:, j, :], cqf[:, half * 2 + j, :], AF.Square, [("cqf", half)], [("sq", j)])
                            mm(ps[:], ones, sqt[:, j, :], j == 0, j == 1, ["cst", ("sq", j)], [pk])
                        rstd_from_ssq(ps[:], 256.0, rst[:], lnt[:], [pk], ["rst"], "lnt")
                        for j in range(2):
                            stt("dve", nrf[:, j, :], cqf[:, half * 2 + j, :], nw[:, l, j:j + 1], rst[:], ALU.mult, ALU.mult,
                                [("cqf", half), "rst", "prm"], [("nrf", j)])
                            cp("pool", dstb[:, j, :], nrf[:, j, :], [("nrf", j)], [("lat", half)])
                        if half == 1:
                            store(CKVT[:, t0:t0 + 512].rearrange("(k p) t -> p k t", p=128), ckvn[:], [("lat", 1)], [("CKVT", b)])
                            if not smp:
                                store(nckvT[l].rearrange("(k p) t -> p k t", p=128), nrf[:], [("nrf", 0), ("nrf", 1)], [("nckv", l)])
                    for h in range(8) if "q" in _steps else ():
                        psn, pkn = pring.next()
                        for kc in range(2):
                            mm(psn[0:96, :], uq5[:, kc, 0, h, :], cqn[:, kc, :], kc == 0, kc == 1, ["uqw", ("lat", 0)], [pkn])
                        if smp:
                            pss, pks = pring.next()
                            for kc in range(2):
                                mm(pss[0:96, :], uq5[:, kc, 1, h, :], cqn[:, kc, :], kc == 0, kc == 1, ["uqw", ("lat", 0)], [pks])
                            cp("act", qst[0:64, h, :], psn[0:64, :], [pkn], [("qst", h)])
                            tt("dve", t1[64:96, :], psn[64:96, :], rC[64:96, :], ALU.mult, [pkn, "rC"], ["t1"])
                            tt("dve", t2[64:96, :], pss[64:96, :], rS[64:96, :], ALU.mult, [pks, "rS"], ["t2"])
                            tt("pool", qst[64:96, h, :], t1[64:96, :], t2[64:96, :], ALU.add, ["t1", "t2"], [("qst", h)])
                        else:
                            cp("act", qst[0:96, h, :], psn[0:96, :], [pkn], [("qst", h)])
                    if "q" in _steps:
                        store(QT[:, :, t0:t0 + 512].rearrange("h r t -> r h t"), qst[0:96, :, :], [("qst", h) for h in range(8)], [("QT", b)])
                    if "kr" not in _steps:
                        continue
                    psn, pkn = pring.next()
                    for kc in range(8):
                        mm(psn[0:96, :], kr4[:, kc, 0, :], ht[:, kc, :], kc == 0, kc == 7, ["krw", "ht"], [pkn])
                    if smp:
                        pss, pks = pring.next()
                        for kc in range(8):
                            mm(pss[0:96, :], kr4[:, kc, 1, :], ht[:, kc, :], kc == 0, kc == 7, ["krw", "ht"], [pks])
                        tt("dve", t1[64:96, :], psn[64:96, :], rC[64:96, :], ALU.mult, [pkn, "rC"], ["t1"])
                        tt("dve", t2[64:96, :], pss[64:96, :], rS[64:96, :], ALU.mult, [pks, "rS"], ["t2"])
                        tt("pool", krt[64:96, :], t1[64:96, :], t2[64:96, :], ALU.add, ["t1", "t2"], ["krt"])
                    else:
                        cp("dve", krf[64:96, :], psn[64:96, :], [pkn], ["krf"])
                        cp("pool", krt[64:96, :], krf[64:96, :], ["krf"], ["krt"])
                        store(nkrT[l], krf[64:96, :], ["krf"], [("nkr", l)])
                    store(KRT[:, t0:t0 + 512], krt[64:96, :], ["krt"], [("KRT", b)])
                    if "dt" not in _steps:
                        continue
                    ps, pk = pring.next()
                    for tl in range(4):
                        for kc in range(8):
                            mm(ps[:, tl * 16:(tl + 1) * 16], ht[:, kc, tl * 128:(tl + 1) * 128], dtw[:, kc, :], kc == 0, kc == 7, ["dtw", "ht"], [pk])
                    cp("dve", dts[:].rearrange("p a h -> p (a h)"), ps[:, 0:64], [pk], ["dts"])
                    store(DTT[t0:t0 + 512, :].rearrange("(a p) h -> p a h", p=128), dts[:], ["dts"], [("DTT", b)])
                S.flush()
            if stop == "p1":
                break
            with contextlib.ExitStack() as st:
                ub = sb(st, "ub", [128, 4, 514], BF16)
                abt = sb(st, "abt", [128, 4, 512], BF16)
                acc = sb(st, "acc", [128, 2, 512], F32)
                vat = sb(st, "vat", [128, 4, 512], BF16)
                xb = sb(st, "xb", [128, 12, 514], BF16)
                xsc = sb(st, "xsc", [128, 12, 512], BF16)
                xtk = sb(st, "xtk", [128, 1024], BF16)
                btk = sb(st, "btk", [128, 256], BF16)
                segs = [(b * 512, 512, 0, TS) for b in range(8)] + [(TS, 256, TS, TS + 256), (TS + 256, 256, TS + 256, T)]
                for (t0, n, s0, s1) in segs:
                    lo, hi = max(t0 - 1, s0), min(t0 + n + 1, s1)
                    off = lo - (t0 - 1)
                    if lo == t0:
                        S.op("pool", lambda e: e.memset(ub[:, :, 0:1], 0.0), writes=["ub"])
                        S.op("pool", lambda e: e.memset(xb[:, :, 0:1], 0.0), writes=["xb"])
                    if hi == t0 + n:
                        S.op("pool", lambda e, n=n: e.memset(ub[:, :, n + 1:n + 2], 0.0), writes=["ub"])
                        S.op("pool", lambda e, n=n: e.memset(xb[:, :, n + 1:n + 2], 0.0), writes=["xb"])
                    load(ub[:, :, off:off + hi - lo], UT[:, lo:hi].rearrange("(k p) t -> p k t", p=128), [], ["ub"])
                    load(abt[:, :, 0:n], ABT[:, t0:t0 + n].rearrange("(k p) t -> p k t", p=128), [], ["abt"])
                    load(xb[:, :, off:off + hi - lo], XBCT[:, lo:hi].rearrange("(k p) t -> p k t", p=128), [], ["xb"])
                    for c in range(16):
                        src, cw, ci, skey = (ub, aconv, c, "ub") if c < 4 else (xb, sconvw, c - 4, "xb")
                        a = acc[:, c % 2, 0:n]
                        ak = ("acc", c % 2)
                        ts("dve", a, src[:, ci, 1:n + 1], cw[:, l, 1, ci:ci + 1], None, ALU.mult, None, [skey, "prm"], [ak])
                        stt("dve", a, src[:, ci, 0:n], cw[:, l, 0, ci:ci + 1], a, ALU.mult, ALU.add, [skey, "prm", ak], [ak])
                        stt("dve", a, src[:, ci, 2:n + 2], cw[:, l, 2, ci:ci + 1], a, ALU.mult, ALU.add, [skey, "prm", ak], [ak])
                        if c < 4:
                            tt("pool", vat[:, ci, 0:n], a, abt[:, ci, 0:n], ALU.mult, [ak, "abt"], ["vat"])
                        else:
                            act(xsc[:, ci, 0:n], a, AF.Silu, [ak, "prm"], ["xsc"], bias=sconvb[:, l, ci:ci + 1])
                    store(VAT[:, t0:t0 + n].rearrange("(k p) t -> p k t", p=128), vat[:, :, 0:n], ["vat"], [])
                    store(XSC[:, t0:t0 + n].rearrange("(k p) t -> p k t", p=128), xsc[:, :, 0:n], ["xsc"], [])
                    for tl in range(n // 128):
                        for c in range(8):
                            S.op("pe", lambda e, c=c, tl=tl: e.transpose(psbT[:, c * 128:(c + 1) * 128], xsc[:, c, tl * 128:(tl + 1) * 128], identb[:]),
                                 reads=["xsc", "identb"], writes=["psbT"])
                        cp("act", xtk[:], psbT[:, 0:1024], ["psbT"], ["xtk"])
                        store(XTOK[t0 + tl * 128:t0 + (tl + 1) * 128, :], xtk[:], ["xtk"], [])
                        for c in range(2):
                            S.op("pe", lambda e, c=c, tl=tl: e.transpose(psbT[:, c * 128:(c + 1) * 128], xsc[:, 8 + c, tl * 128:(tl + 1) * 128], identb[:]),
                                 reads=["xsc", "identb"], writes=["psbT"])
                        cp("dve", btk[:], psbT[:, 0:256], ["psbT"], ["btk"])
                        store(BTOK[t0 + tl * 128:t0 + (tl + 1) * 128, :], btk[:], ["btk"], [])
                S.flush()
            if stop == "conv":
                break
            with contextlib.ExitStack() as st:
                dtr = sb(st, "dtr", [128, NCH, 16], F32)
                ea = sb(st, "ea", [128, 2, 16], F32)
                dtd = [sb(st, "dtd%d" % d, [128, NCH, 16], F32) for d in range(2)]
                dta = [sb(st, "dta%d" % d, [128, NCH, 16], F32) for d in range(2)]
                ctk = [sb(st, "ctk%d" % d, [128, NCH, 16], F32) for d in range(2)]
                tot = [sb(st, "tot%d" % d, [128, NCH, 16], F32) for d in range(2)]
                ted = [sb(st, "ted%d" % d, [128, NCH, 16], F32) for d in range(2)]
                cdc = [sb(st, "cdc%d" % d, [128, NCH, 16], F32) for d in range(2)]
                tri = [triF, triB]
                load(dtr[:], DTT.rearrange("(c p) h -> p c h", p=128), [], ["dtr"])
                act(ea[:], alog[:, l, :, :], AF.Exp, ["prm"], ["ea"])
                for d in range(2):
                    tt("dve", dtd[d][:], dtr[:], dtb[:, l, d, :].unsqueeze(1).to_broadcast([128, NCH, 16]), ALU.add, ["dtr", "prm"], [("dtd", d)])
                    act(dtd[d][:], dtd[d][:], AF.Exp, [("dtd", d)], [("dtd", d)])
                    act(dtd[d][:], dtd[d][:], AF.Ln, [("dtd", d)], [("dtd", d)], bias=1.0)
                    stt("dve", dta[d][:], dtd[d][:], -1.0, ea[:, d, :].unsqueeze(1).to_broadcast([128, NCH, 16]), ALU.mult, ALU.mult,
                        [("dtd", d), "ea"], [("dta", d)])
                    flat = dta[d][:].rearrange("p c h -> p (c h)")
                    for (lhs, dstt, dk) in ((tri[d], ctk[d], "ctk"), (ones, tot[d], "tot")):
                        dflat = dstt[:].rearrange("p c h -> p (c h)")
                        for (a0, a1) in ((0, 512), (512, NCH * 16)):
                            ps, pk = pring.next()
                            mm(ps[:, 0:a1 - a0], lhs, flat[:, a0:a1], True, True, ["cst", ("dta", d)], [pk])
                            cp("dve", dflat[:, a0:a1], ps[:, 0:a1 - a0], [pk], [(dk, d)])
                    tt("dve", ted[d][:], tot[d][:], ctk[d][:], ALU.subtract, [("tot", d), ("ctk", d)], [("ted", d)])
                    act(ted[d][:], ted[d][:], AF.Exp, [("ted", d)], [("ted", d)])
                    tt("dve", ted[d][:], ted[d][:], dtd[d][:], ALU.mult, [("ted", d), ("dtd", d)], [("ted", d)])
                    act(cdc[d][:], tot[d][:], AF.Exp, [("tot", d)], [("cdc", d)])
                with contextlib.ExitStack() as st2:
                    St = sb(st2, "St", [128, 2, 512], F32)
                    hbr = Ring([sb(st2, "hb%d" % i, [128, 1024], BF16) for i in range(2)], "hb")
                    xkr = Ring([sb(st2, "xk%d" % i, [128, 1024], BF16) for i in range(2)], "xk")
                    bkr = Ring([sb(st2, "bk%d" % i, [128, 256], BF16) for i in range(2)], "bk")
                    xwr = Ring([sb(st2, "xw%d" % i, [128, 1024], BF16) for i in range(2)], "xw")
                    for d in range(2):
                        for si, (s0, slen, smp) in enumerate(SEQS):
                            nchk, c0 = slen // 128, s0 // 128
                            if smp:
                                load(St[:], h0[l, d].rearrange("n (g f) -> n g f", g=2), [], ["St"])
                            else:
                                S.op("pool", lambda e: e.memset(St[:], 0.0), writes=["St"])
                            order = range(nchk) if d == 0 else range(nchk - 1, -1, -1)
                            for ci in order:
                                c = c0 + ci
                                hb, hk = hbr.next()
                                cp("act", hb[:], St[:].rearrange("p g f -> p (g f)"), ["St"], [hk])
                                store(HENT[d, c], hb[:], [hk], [])
                                xk, xkk = xkr.next()
                                bk, bkk = bkr.next()
                                xw, xwk = xwr.next()
                                load(xk[:], XTOK[c * 128:(c + 1) * 128, :], [], [xkk])
                                load(bk[:], BTOK[c * 128:(c + 1) * 128, :], [], [bkk])
                                tt("pool", xw[:].rearrange("p (h q) -> p h q", h=16), xk[:].rearrange("p (h q) -> p h q", h=16),
                                   ted[d][:, c, :].unsqueeze(2).to_broadcast([128, 16, 64]), ALU.mult, [xkk, ("ted", d)], [xwk])
                                for g in range(2):
                                    ps, pk = pring.next()
                                    mm(ps[:], bk[:, g * 128:(g + 1) * 128], xw[:, g * 512:(g + 1) * 512], True, True, [bkk, xwk], [pk])
                                    sg3 = St[:, g, :].rearrange("p (h q) -> p h q", h=8)
                                    tt("dve", sg3, sg3, cdc[d][:, c, g * 8:(g + 1) * 8].unsqueeze(2).to_broadcast([128, 8, 64]), ALU.mult,
                                       ["St", ("cdc", d)], ["St"])
                                    tt("dve", St[:, g, :], St[:, g, :], ps[:], ALU.add, ["St", pk], ["St"])
                            if not smp:
                                store(nssm[l, d, si - 1], St[:].rearrange("p g f -> p (g f)"), ["St"], [])
                    S.flush()
                if stop == "ssd1":
                    break
                with contextlib.ExitStack() as st2:
                    xs3r = Ring([sb(st2, "xs3%d" % i, [128, 12, 128], BF16) for i in range(2)], "xs3")
                    xk2r = Ring([sb(st2, "xk2%d" % i, [128, 1024], BF16) for i in range(2)], "xk2")
                    her = [Ring([sb(st2, "he%d%d" % (d, i), [128, 1024], BF16) for i in range(2)], "he%d" % d) for d in range(2)]
                    zsr = Ring([sb(st2, "zs%d" % i, [128, 8, 128], BF16) for i in range(2)], "zs")
                    cbm = [sb(st2, "cbm%d" % d, [128, 2, 128], F32) for d in range(2)]
                    Rt_ = sb(st2, "Rt_", [128, 16, 128], F32)
                    cbdt = sb(st2, "cbdt", [128, 16, 128], F32)
                    crs = Ring([sb(st2, "crs%d" % i, [128, 512], F32) for i in range(2)], "crs")
                    arg = Ring([sb(st2, "arg%d" % i, [128, 512], F32) for i in range(2)], "arg")
                    Et = Ring([sb(st2, "Et%d" % i, [128, 512], F32) for i in range(2)], "Et")
                    ECt = Ring([sb(st2, "ECt%d" % i, [128, 512], F32) for i in range(2)], "ECt")
                    Wt = [sb(st2, "Wt%d" % d, [128, 16, 128], BF16) for d in range(2)]
                    Csc = [sb(st2, "Csc%d" % d, [128, 16, 128], BF16) for d in range(2)]
                    yg = sb(st2, "yg", [128, 8, 128], F32)
                    sqy = sb(st2, "sqy", [128, 2, 128], F32)
                    rsy = sb(st2, "rsy", [128, 128], F32)
                    lny = sb(st2, "lny", [128, 128], F32)
                    ynr = Ring([sb(st2, "yn%d" % i, [128, 8, 128], BF16) for i in range(2)], "yn")
                    for c in range(NCH):
                        tk0 = c * 128
                        xs3, xs3k = xs3r.next()
                        xk2, xk2k = xk2r.next()
                        zs, zsk = zsr.next()
                        load(xs3[:], XSC[:, tk0:tk0 + 128].rearrange("(k p) t -> p k t", p=128), [], [xs3k])
                        load(xk2[:], XTOK[tk0:tk0 + 128, :], [], [xk2k])
                        load(zs[:], ZT[:, tk0:tk0 + 128].rearrange("(k p) t -> p k t", p=128), [], [zsk])
                        he = []
                        for d in range(2):
                            t_, k_ = her[d].next()
                            load(t_[:], HENT[d, c], [], [k_])
                            he.append((t_, k_))
                        psA, pkA = pring.next()
                        for g in range(2):
                            mm(psA[:, g * 128:(g + 1) * 128], xs3[:, 8 + g, :], xs3[:, 10 + g, :], True, True, [xs3k], [pkA])
                        for d in range(2):
                            tt("dve", cbm[d][:], psA[:, 0:256].rearrange("p (g i) -> p g i", g=2), tri[d].unsqueeze(1).to_broadcast([128, 2, 128]),
                               ALU.mult, [pkA, "cst"], [("cbm", d)])
                        psY = []
                        for i in range(2):
                            psY.append(plong.next())
                        for d in range(2):
                            tt("pool", Rt_[:], tri[d].unsqueeze(1).to_broadcast([128, 16, 128]),
                               dta[d][:, c, :].unsqueeze(2).to_broadcast([128, 16, 128]), ALU.mult, ["cst", ("dta", d)], ["Rt_"])
                            for g in range(2):
                                tt("pool", cbdt[:, g * 8:(g + 1) * 8, :], cbm[d][:, g, :].unsqueeze(1).to_broadcast([128, 8, 128]),
                                   dtd[d][:, c, g * 8:(g + 1) * 8].unsqueeze(2).to_broadcast([128, 8, 128]), ALU.mult, [("cbm", d), ("dtd", d)], ["cbdt"])
                            for q in range(4):
                                g = q // 2
                                psc, pkc = pring.next()
                                mm(psc[:], ones, Rt_[:, 4 * q:4 * q + 4, :].rearrange("p h i -> p (h i)"), True, True, ["cst", "Rt_"], [pkc])
                                cr, crk = crs.next()
                                ar, ark = arg.next()
                                et, etk = Et.next()
                                ec, eck = ECt.next()
                                cp("dve", cr[:], psc[:], [pkc], [crk])
                                tt("dve", ar[:].rearrange("p (h i) -> p h i", h=4), psc[:].rearrange("p (h i) -> p h i", h=4),
                                   ctk[d][:, c, 4 * q:4 * q + 4].unsqueeze(2).to_broadcast([128, 4, 128]), ALU.subtract, [pkc, ("ctk", d)], [ark])
                                ts("pool", ar[:], ar[:], 0.0, None, ALU.min, None, [ark], [ark])
                                act(et[:], ar[:], AF.Exp, [ark], [etk])
                                tt("dve", Wt[d][:, 4 * q:4 * q + 4, :].rearrange("p h i -> p (h i)"), et[:],
                                   cbdt[:, 4 * q:4 * q + 4, :].rearrange("p h i -> p (h i)"), ALU.mult, [etk, "cbdt"], [("Wt", d)])
                                act(ec[:], cr[:], AF.Exp, [crk], [eck])
                                tt("pool", Csc[d][:, 4 * q:4 * q + 4, :], ec[:].rearrange("p (h i) -> p h i", h=4),
                                   xs3[:, 10 + g, :].unsqueeze(1).to_broadcast([128, 4, 128]), ALU.mult, [eck, xs3k], [("Csc", d)])
                        for h in range(16):
                            kc, half = h // 2, h % 2
                            pY, pYk = psY[kc // 4]
                            out = pY[half * 64:(half + 1) * 64, (kc % 4) * 128:(kc % 4 + 1) * 128]
                            for d in range(2):
                                mm(out, xk2[:, h * 64:(h + 1) * 64], Wt[d][:, h, :], d == 0, False, [xk2k, ("Wt", d)], [pYk])
                                mm(out, he[d][0][:, h * 64:(h + 1) * 64], Csc[d][:, h, :], False, d == 1, [he[d][1], ("Csc", d)], [pYk])
                        for kc in range(8):
                            pY, pYk = psY[kc // 4]
                            stt("dve", yg[:, kc, :], xs3[:, kc, :], dsum[:, l, kc:kc + 1], pY[:, (kc % 4) * 128:(kc % 4 + 1) * 128], ALU.mult, ALU.add,
                                [xs3k, "dsum", pYk], ["yg"])
                        tt("pool", yg[:], yg[:], zs[:], ALU.mult, ["yg", zsk], ["yg"])
                        ps, pk = pring.next()
                        for kc in range(8):
                            act(sqy[:, kc % 2, :], yg[:, kc, :], AF.Square, ["yg"], [("sqy", kc % 2)])
                            mm(ps[:, 0:128], ones, sqy[:, kc % 2, :], kc == 0, kc == 7, ["cst", ("sqy", kc % 2)], [pk])
                        rstd_from_ssq(ps[:, 0:128], 1024.0, rsy[:], lny[:], [pk], ["rsy"], "lny")
                        yn, ynk = ynr.next()
                        for kc in range(8):
                            stt("dve", yn[:, kc, :], yg[:, kc, :], snw[:, l, kc:kc + 1], rsy[:], ALU.mult, ALU.mult, ["yg", "rsy", "prm"], [ynk])
                        store(YBT[:, tk0:tk0 + 128].rearrange("(k p) t -> p k t", p=128), yn[:], [ynk], [])
                    S.flush()
            if stop == "ssd":
                break
            with contextlib.ExitStack() as st:
                ckvall = sb(st, "ckvall", [128, 2, 4608], BF16)
                KTr = [sb(st, "KT%d" % i, [128, 4608], BF16) for i in range(2)]
                VA = [sb(st, "VA%d" % i, [128, 36, 128], BF16) for i in range(2)]
                qhr = Ring([sb(st, "qh%d" % i, [128, 4096], BF16) for i in range(2)], "qh")
                PT = Ring([sb(st, "PT%d" % i, [128, 512], BF16) for i in range(3)], "PT")
                Lt = sb(st, "Lt", [128, 512], F32)
                Rt = sb(st, "Rt", [128, 512], F32)
                ATr = Ring([sb(st, "AT%d" % i, [128, 512], BF16) for i in range(2)], "AT")
                S.op("pool", lambda e: e.memset(VA[0][:, :, 64:128], 1.0), writes=[("VA", 0)])
                S.op("pool", lambda e: e.memset(VA[1][:, :, 0:64], 1.0), writes=[("VA", 1)])
                for (nctx, k0, nlat, q0, nq, qblk) in ((512, 0, TS, 0, TS, 512), (0, TS, 256, TS, 256, 256), (0, TS + 256, 256, TS + 256, 256, 256)):
                    nk = nctx + nlat
                    ntile = nk // 128
                    if nctx:
                        S.dma("pool", lambda e: e.dma_start(out=ckvall[:, :, 0:512], in_=cckvT[l].rearrange("(k p) t -> p k t", p=128)), writes=["ckvall"])
                        for i in range(2):
                            S.dma("pool", lambda e, i=i: e.dma_start(out=KTr[i][64:96, 0:512], in_=ckrT[l]), writes=[("KT", i)])
                    load(ckvall[:, :, nctx:nk], CKVT[:, k0:k0 + nlat].rearrange("(k p) t -> p k t", p=128), [], ["ckvall"])
                    for i in range(2):
                        load(KTr[i][64:96, nctx:nk], KRT[:, k0:k0 + nlat], [], [("KT", i)])
                    for h in range(8):
                        par = h % 2
                        va, vak = VA[par], ("VA", par)
                        KT, ktk = KTr[par], ("KT", par)
                        voff = par * 64
                        for kb in range((nk + 511) // 512):
                            w = min(512, nk - kb * 512)
                            ps, pk = pring.next()
                            for kc in range(2):
                                mm(ps[0:64, 0:w], ukvw[:, kc, h * 128:h * 128 + 64], ckvall[:, kc, kb * 512:kb * 512 + w], kc == 0, kc == 1, ["ukvw", "ckvall"], [pk])
                            cp("act" if kb % 2 == 0 else "dve", KT[0:64, kb * 512:kb * 512 + w], ps[0:64, 0:w], [pk], [ktk])
                        for tg in range(0, ntile, 8):
                            nt = min(8, ntile - tg)
                            ps, pk = pring.next()
                            for j in range(nt):
                                for kc in range(2):
                                    mm(ps[:, j * 64:(j + 1) * 64], ckvall[:, kc, (tg + j) * 128:(tg + j + 1) * 128], ukvw[:, kc, h * 128 + 64:h * 128 + 128],
                                       kc == 0, kc == 1, ["ukvw", "ckvall"], [pk])
                            cp("dve", va[:, tg:tg + nt, voff:voff + 64], ps[:, 0:nt * 64].rearrange("p (t d) -> p t d", d=64), [pk], [vak])
                        qh, qhk = qhr.next()
                        load(qh[0:96, 0:nq], QT[h, :, q0:q0 + nq], [], [qhk])
                        items = [(qb, t) for qb in range(nq // qblk) for t in range(ntile)]
                        SKEW = 2
                        pend = {}
                        cur = {}
                        for i in range(len(items) + SKEW):
                            if i < len(items):
                                qb, t = items[i]
                                psS, pkS = pring.next()
                                mm(psS[:, 0:qblk], KT[0:96, t * 128:(t + 1) * 128], qh[0:96, qb * qblk:(qb + 1) * qblk], True, True, [ktk, qhk], [pkS])
                                pend[i] = (psS, pkS)
                            if i >= SKEW:
                                qb, t = items[i - SKEW]
                                psS, pkS = pend.pop(i - SKEW)
                                if t == 0:
                                    cur[qb] = plong.next()
                                psO, pkO = cur[qb]
                                pt, ptk = PT.next()
                                act(pt[:, 0:qblk], psS[:, 0:qblk], AF.Exp, [pkS], [ptk], scale=SCALE)
                                mm(psO[:, 0:qblk], va[:, t, :], pt[:, 0:qblk], t == 0, t == ntile - 1, [vak, ptk], [pkO])
                                if t == ntile - 1:
                                    orow, drow = voff, 64 - voff
                                    act(Lt[orow:orow + 64, 0:qblk], psO[drow:drow + 64, 0:qblk], AF.Ln, [pkO], ["Lt"])
                                    act(Rt[orow:orow + 64, 0:qblk], Lt[orow:orow + 64, 0:qblk], AF.Exp, ["Lt"], ["Rt"], scale=-1.0)
                                    at, atk = ATr.next()
                                    tt("dve", at[orow:orow + 64, 0:qblk], psO[orow:orow + 64, 0:qblk], Rt[orow:orow + 64, 0:qblk], ALU.mult, [pkO, "Rt"], [atk])
                                    r0 = (h // 2) * 128 + orow
                                    store(ATT[r0:r0 + 64, q0 + qb * qblk:q0 + (qb + 1) * qblk], at[orow:orow + 64, 0:qblk], [atk], [])
                S.flush()
            if stop == "attn":
                break
            with contextlib.ExitStack() as st:
                xt = sb(st, "xt3", [128, 8, 512], F32)
                ht = sb(st, "ht3", [128, 8, 512], BF16)
                va_ = sb(st, "va3", [128, 4, 512], BF16)
                yb_ = sb(st, "yb3", [128, 8, 512], BF16)
                at_ = sb(st, "at3", [128, 4, 512], BF16)
                macc = sb(st, "macc", [128, 4, 512], F32)
                sigr = Ring([sb(st, "sig%d" % i, [128, 512], F32) for i in range(2)], "sig")
                tmr = Ring([sb(st, "tm%d" % i, [128, 512], F32) for i in range(2)], "tm")
                merged = sb(st, "merged", [128, 8, 512], BF16)
                h2 = sb(st, "h2", [128, 8, 512], BF16)
                gt = sb(st, "gt", [128, 22, 512], BF16)
                sar = Ring([sb(st, "sa%d" % i, [128, 512], F32) for i in range(2)], "sa")
                sqt = sb(st, "sqt3", [128, 2, 512], F32)
                rst = sb(st, "rst3", [128, 512], F32)
                lnt = sb(st, "lnt3", [128, 512], F32)
                tmpf = sb(st, "tmpf3", [128, 2, 512], F32)
                last = (l == nl - 1)
                yo = sb(st, "yo", [128, 8, 512], F32) if last else None
                for b in range(NB):
                    r = 0 if b < 8 else 1
                    t0 = b * 512
                    fm = lambda dr: dr[:, t0:t0 + 512].rearrange("(k p) t -> p k t", p=128)
                    load(xt[:], fm(xsrc), [], ["xt"])
                    load(ht[:], fm(HT), [], ["ht"])
                    load(va_[:], fm(VAT), [], ["va_"])
                    load(yb_[:], fm(YBT), [], ["yb_"])
                    load(at_[:], fm(ATT), [], ["at_"])
                    for cg in range(2):
                        for br, (wsrc, nkc, rt_, rk) in enumerate(((w_a_out, 4, va_, "va_"), (w_b_out, 8, yb_, "yb_"), (w_c_out, 4, at_, "at_"))):
                            wy, wyk = wload(wsrc[l, :, cg * 512:(cg + 1) * 512], nkc, 512)
                            c0 = C_G + br * 1024 + cg * 512
                            wg, wgk = wload(w_in[l, :, c0:c0 + 512], 8, 512)
                            for oc in range(4):
                                psYy, pky = pring.next()
                                for kc in range(nkc):
                                    mm(psYy[:], wy[:, kc, oc * 128:(oc + 1) * 128], rt_[:, kc, :], kc == 0, kc == nkc - 1, [wyk, rk], [pky])
                                psG, pkg = pring.next()
                                for kc in range(8):
                                    mm(psG[:], wg[:, kc, oc * 128:(oc + 1) * 128], ht[:, kc, :], kc == 0, kc == 7, [wgk, "ht"], [pkg])
                                sg, sgk = sigr.next()
                                act(sg[:], psG[:], AF.Sigmoid, [pkg], [sgk])
                                if br == 0:
                                    tt("dve", macc[:, oc, :], psYy[:], sg[:], ALU.mult, [pky, sgk], [("macc", oc)])
                                else:
                                    tm, tmk = tmr.next()
                                    tt("dve", tm[:], psYy[:], sg[:], ALU.mult, [pky, sgk], [tmk])
                                    if br == 1:
                                        tt("pool", macc[:, oc, :], macc[:, oc, :], tm[:], ALU.add, [("macc", oc), tmk], [("macc", oc)])
                                    else:
                                        tt("pool", merged[:, cg * 4 + oc, :], macc[:, oc, :], tm[:], ALU.add, [("macc", oc), tmk], ["merged"])
                    for cg in range(2):
                        wo, wok = wload(w_o[l, :, cg * 512:(cg + 1) * 512], 8, 512)
                        for oc in range(4):
                            ps, pk = pring.next()
                            for kc in range(8):
                                mm(ps[:], wo[:, kc, oc * 128:(oc + 1) * 128], merged[:, kc, :], kc == 0, kc == 7, [wok, "merged"], [pk])
                            o8 = cg * 4 + oc
                            stt("dve", xt[:, o8, :], ps[:], modcol(l, 2, o8, r), xt[:, o8, :], ALU.mult, ALU.add, [pk, "MOD", "xt"], ["xt"])
                    norm_mod((sqt, rst, lnt, tmpf), xt, h2, lambda kc: A2[:, l, kc, r:r + 1], lambda kc: modcol(l, 3, kc, r), "xt", "h2")
                    for fg in range(6):
                        ncol = 512 if fg < 5 else 256
                        w1, w1k = wload(w_ff1[l, :, fg * 512:fg * 512 + ncol], 8, ncol)
                        w3, w3k = wload(w_ff3[l, :, fg * 512:fg * 512 + ncol], 8, ncol)
                        for oc in range(ncol // 128):
                            j = fg * 4 + oc
                            psA, pka = pring.next()
                            for kc in range(8):
                                mm(psA[:], w1[:, kc, oc * 128:(oc + 1) * 128], h2[:, kc, :], kc == 0, kc == 7, [w1k, "h2"], [pka])
                            psB, pkb = pring.next()
                            for kc in range(8):
                                mm(psB[:], w3[:, kc, oc * 128:(oc + 1) * 128], h2[:, kc, :], kc == 0, kc == 7, [w3k, "h2"], [pkb])
                            sa, sak = sar.next()
                            act(sa[:], psA[:], AF.Silu, [pka], [sak])
                            tt("dve", gt[:, j, :], sa[:], psB[:], ALU.mult, [sak, pkb], [("gt", j)])
                    for cg in range(4):
                        w2, w2k = wload(w_ff2[l, :, cg * 256:(cg + 1) * 256], 22, 256)
                        for oc in range(2):
                            ps, pk = pring.next()
                            for j in range(22):
                                mm(ps[:], w2[:, j, oc * 128:(oc + 1) * 128], gt[:, j, :], j == 0, j == 21, [w2k, ("gt", j)], [pk])
                            o8 = cg * 2 + oc
                            stt("dve", xt[:, o8, :], ps[:], modcol(l, 5, o8, r), xt[:, o8, :], ALU.mult, ALU.add, [pk, "MOD", "xt"], ["xt"])
                    if not last:
                        store(fm(XT), xt[:], ["xt"], [])
                    else:
                        norm_mod((sqt, rst, lnt, tmpf), xt, yo, lambda kc: fnw[:, kc:kc + 1], None, "xt", "yo")
                        store(fm(yT), yo[:], ["yo"], [])
                    if stop == "p3" and "XT" in dump:
                        store(fm(XT), xt[:], ["xt"], [])
                S.flush()
        S.flush(final=True)
    return nc


def rope_tables():
    n_rows = TS // 64
    row = np.repeat(np.arange(n_rows, dtype=np.float32), 64)
    col = np.tile(np.arange(64, dtype=np.float32), n_rows)
    inv = (np.float32(10000.0) ** (-np.arange(8, dtype=np.float32) / np.float32(8))).astype(np.float32)
    ang = np.concatenate([row[:, None] * inv, col[:, None] * inv], axis=-1).astype(np.float32)
    cos, sin = np.cos(ang).astype(np.float32), np.sin(ang).astype(np.float32)
    C = np.concatenate([cos, cos], axis=1).T
    Sg = np.concatenate([-sin, sin], axis=1).T
    return np.ascontiguousarray(C), np.ascontiguousarray(Sg)


def make_in_maps(inp, nl=DEPTH):
    f = lambda k: np.asarray(inp[k], np.float32)
    ropeC, ropeS = rope_tables()
    k = np.arange(128)
    triF = (k[:, None] <= k[None, :]).astype(np.float32)
    triB = (k[:, None] >= k[None, :]).astype(np.float32)
    cst = np.concatenate([triF, triB, np.ones((128, 128), np.float32), np.eye(128, dtype=np.float32)], axis=1)
    prm = pack_params(inp)
    w_in = f("w_in")
    w_kr2 = np.zeros((DEPTH, D, 2, 96), np.float32)
    krc = w_in[:, :, C_KR:C_KR + 32]
    w_kr2[:, :, 0, 64:96] = krc
    w_kr2[:, :, 1, 64:80] = krc[:, :, 16:32]
    w_kr2[:, :, 1, 80:96] = krc[:, :, 0:16]
    wuq = f("w_uq").reshape(DEPTH, 256, 8, 96)
    w_uq2 = np.zeros((DEPTH, 256, 2, 8, 96), np.float32)
    w_uq2[:, :, 0] = wuq
    w_uq2[:, :, 1, :, 0:64] = wuq[..., 0:64]
    w_uq2[:, :, 1, :, 64:80] = wuq[..., 80:96]
    w_uq2[:, :, 1, :, 80:96] = wuq[..., 64:80]
    shared = {
        "ropeC": ropeC, "ropeS": ropeS, "cst": cst, "prm": prm, "w_in": w_in, "w_kr2": w_kr2, "w_uq2": w_uq2,
        "w_ukv": f("w_ukv"), "w_a_out": f("w_a_out"), "w_b_out": f("w_b_out"), "w_c_out": f("w_c_out"),
        "w_o": f("w_o"), "w_ada": f("w_ada"), "w_ff1": f("w_ff1"), "w_ff3": f("w_ff3"), "w_ff2": f("w_ff2"),
    }
    for k in ("w_in", "w_kr2", "w_uq2", "w_ukv", "w_a_out", "w_b_out", "w_c_out", "w_o", "w_ada", "w_ff1", "w_ff3", "w_ff2"):
        shared[k] = np.ascontiguousarray(shared[k][:nl])
    xs, xp, c, cctx = f("x_sample"), f("x_prompt"), f("c"), f("c_ctx")
    cckv, ckr = f("cache_ckv"), f("cache_krope")
    sf, sbw = f("state_ssm_fwd"), f("state_ssm_bwd")
    maps = []
    for r in range(8):
        bs = r % 4
        xT0 = np.concatenate([xs[bs].T, xp[2 * r].T, xp[2 * r + 1].T], axis=1)
        cd = np.stack([c[bs], cctx], axis=1)
        cd = cd.reshape(8, 128, 2).transpose(1, 0, 2)
        h0 = np.stack([sf[bs], sbw[bs]], axis=1)
        h0 = h0.transpose(0, 1, 4, 2, 3).reshape(DEPTH, 2, 128, 1024)
        m = dict(shared)
        m.update({
            "xT0": np.ascontiguousarray(xT0), "cond": np.ascontiguousarray(cd),
            "cckvT": np.ascontiguousarray(cckv[bs].transpose(0, 2, 1)),
            "ckrT": np.ascontiguousarray(ckr[bs].transpose(0, 2, 1)),
            "h0": np.ascontiguousarray(h0),
        })
        maps.append(m)
    return maps


_NC_CACHE = {}


def kernel(**inputs):
    maps = make_in_maps(inputs)
    if "nc" not in _NC_CACHE:
        _NC_CACHE["nc"] = build_program()
    nc = _NC_CACHE["nc"]
    res = run_bass_kernel_spmd(nc, maps, core_ids=list(range(8)))
    R = res.results
    y_prompt = np.zeros((16, 256, D), np.float32)
    y_sample = np.zeros((4, TS, D), np.float32)
    new_ckv = np.zeros((16, DEPTH, 256, 256), np.float32)
    new_kr = np.zeros((16, DEPTH, 256, 32), np.float32)
    new_f = np.zeros((16, DEPTH, 16, 64, 128), np.float32)
    new_b = np.zeros((16, DEPTH, 16, 64, 128), np.float32)
    for r in range(8):
        yT = R[r]["yT"]
        if r < 4:
            y_sample[r] = yT[:, :TS].T
        for s in range(2):
            q = 2 * r + s
            y_prompt[q] = yT[:, TS + s * 256:TS + (s + 1) * 256].T
            new_ckv[q] = R[r]["nckvT"][:, :, s * 256:(s + 1) * 256].transpose(0, 2, 1)
            new_kr[q] = R[r]["nkrT"][:, :, s * 256:(s + 1) * 256].transpose(0, 2, 1)
            st = R[r]["nssm"][:, :, s].reshape(DEPTH, 2, 128, 16, 64).transpose(0, 1, 3, 4, 2)
            new_f[q] = st[:, 0]
            new_b[q] = st[:, 1]
    return (y_prompt, y_sample, new_ckv, new_kr, new_f, new_b)
```

```python
import contextlib
import math
import os

import numpy as np
import concourse.bass as bass
import concourse.mybir as mybir
from concourse.bass_utils import run_bass_kernel_spmd

F32 = mybir.dt.float32
BF16 = mybir.dt.bfloat16
AF = mybir.ActivationFunctionType
ALU = mybir.AluOpType

D = 1024
DEPTH = 4
TS = 4096
TPR = 512
T = TS + TPR
NB = T // 512
NCH = T // 128
EPS = 1e-6
IN_COLS = 7728
FF = 2816
C_AX, C_AB, C_AC, C_Z, C_XBC, C_DT, C_CQ, C_CKV, C_KR, C_G = 0, 512, 1024, 1536, 2560, 4096, 4112, 4368, 4624, 4656
SCALE = 1.0 / math.sqrt(96.0)
SEQS = [(0, 4096, True), (4096, 256, False), (4352, 256, False)]

ENGS = ("pe", "act", "dve", "pool", "sp")


class Sched:
    def __init__(self, nc, st, n_dma_sems=48):
        self.nc = nc
        self.n_dma_sems = n_dma_sems
        self.esem = {e: st.enter_context(nc.semaphore("s_" + e)) for e in ENGS if e != "sp"}
        self.dsem = [st.enter_context(nc.semaphore("d%d" % i)) for i in range(n_dma_sems)]
        self.dummy = st.enter_context(nc.sbuf_tensor("bar_dummy", [128, 2], F32))
        self.ecount = {e: 0 for e in ENGS}
        self.dma_val = [0] * n_dma_sems
        self.dma_last = [None] * n_dma_sems
        self.dma_rr = 0
        self.waited = {e: {} for e in ENGS}
        self.barrier_tok = None
        self.need_barrier = {e: False for e in ENGS}
        self._reset()

    def _reset(self):
        self.ops = {e: [] for e in ENGS}
        self.last_write = {}
        self.readers = {}
        self.dma_toks = []

    def _record(self, eng, fn, reads, writes, dma, extra_deps=()):
        deps = set(extra_deps)
        for r in reads:
            w = self.last_write.get(r)
            if w is not None:
                deps.add(w)
        for r in writes:
            w = self.last_write.get(r)
            if w is not None:
                deps.add(w)
            for rd in self.readers.get(r, ()):
                deps.add(rd)
        idx = len(self.ops[eng])
        if dma:
            si = self.dma_rr
            self.dma_rr = (self.dma_rr + 1) % self.n_dma_sems
            prev = self.dma_last[si]
            if prev is not None:
                deps.add(prev)
            self.dma_val[si] += 16
            tok = ("dma", si, self.dma_val[si])
            self.dma_last[si] = tok
            self.dma_toks.append(tok)
        else:
            tok = ("eng", eng, idx)
        deps = {d for d in deps if not (d[0] == "eng" and d[1] == "pe" and eng == "pe")}
        deps.discard(tok)
        if self.need_barrier[eng] and self.barrier_tok is not None:
            deps.add(self.barrier_tok)
            self.need_barrier[eng] = False
        self.ops[eng].append(dict(fn=fn, deps=deps, flag=False, tok=tok))
        for r in reads:
            lst = self.readers.setdefault(r, [])
            if tok[0] == "eng":
                lst[:] = [t for t in lst if not (t[0] == "eng" and t[1] == eng)]
            lst.append(tok)
        for r in writes:
            self.last_write[r] = tok
            self.readers[r] = []
        return tok

    def op(self, eng, fn, reads=(), writes=(), extra_deps=()):
        return self._record(eng, fn, reads, writes, False, extra_deps)

    def dma(self, eng, fn, reads=(), writes=(), extra_deps=()):
        return self._record(eng, fn, reads, writes, True, extra_deps)

    def flush(self, final=False):
        nc = self.nc
        deps = set()
        for e in ENGS:
            if e == "sp":
                continue
            for i in range(len(self.ops[e]) - 1, -1, -1):
                if self.ops[e][i]["tok"][0] == "eng":
                    deps.add(self.ops[e][i]["tok"])
                    break
        latest = {}
        for t in self.dma_toks:
            if t[1] not in latest or latest[t[1]][2] < t[2]:
                latest[t[1]] = t
        deps.update(latest.values())
        dummy = self.dummy
        coll = self._record("dve", lambda e: e.memset(dummy[:, 0:1], 0.0), (), (), False, deps)
        for e in ENGS:
            for o in self.ops[e]:
                for d in o["deps"]:
                    if d[0] == "eng":
                        self.ops[d[1]][d[2]]["flag"] = True
        self.ops["dve"][coll[2]]["flag"] = True
        cnt = {}
        for e in ENGS:
            c = self.ecount[e]
            arr = []
            for o in self.ops[e]:
                if o["flag"]:
                    c += 1
                arr.append(c)
            cnt[e] = arr
        esem, dsem = self.esem, self.dsem

        def resolve(d):
            if d[0] == "eng":
                return esem[d[1]], cnt[d[1]][d[2]]
            if d[0] == "abs":
                return d[1], d[2]
            return dsem[d[1]], d[2]

        coll_abs = ("abs", esem["dve"], cnt["dve"][coll[2]])

        def run(ename, eng):
            waited = self.waited[ename]
            for o in self.ops[ename]:
                need = {}
                for d in o["deps"]:
                    s, v = resolve(d)
                    k = id(s)
                    if waited.get(k, 0) >= v:
                        continue
                    if k not in need or need[k][1] < v:
                        need[k] = (s, v)
                for k, (s, v) in need.items():
                    eng.wait_ge(s, v)
                    waited[k] = v
                ins = o["fn"](eng)
                if o["tok"][0] == "dma":
                    ins.then_inc(dsem[o["tok"][1]], 16)
                elif o["flag"]:
                    ins.then_inc(esem[ename], 1)
            if final and ename == "sp":
                s, v = resolve(coll_abs)
                eng.wait_ge(s, v)

        if os.environ.get("KDBG_SIM"):
            self._simulate(resolve, coll_abs, final)

        with nc.Block() as block:
            @block.sync
            def _(sync):
                run("sp", sync)

            @block.tensor
            def _(tensor):
                run("pe", tensor)

            @block.scalar
            def _(scalar):
                run("act", scalar)

            @block.vector
            def _(vector):
                run("dve", vector)

            @block.gpsimd
            def _(gpsimd):
                run("pool", gpsimd)

        for e in ENGS:
            if cnt[e]:
                self.ecount[e] = cnt[e][-1]
        self.barrier_tok = coll_abs
        self.need_barrier = {e: True for e in ENGS}
        self._reset()


def _sched_simulate(self, resolve, coll_abs, final):
    if not hasattr(self, "sim_sem"):
        self.sim_sem = {}
    sem = self.sim_sem
    pos = {e: 0 for e in ENGS}
    progress = True
    while progress:
        progress = False
        for e in ENGS:
            while pos[e] < len(self.ops[e]):
                o = self.ops[e][pos[e]]
                ok = True
                for d in o["deps"]:
                    s_, v = resolve(d)
                    if sem.get(id(s_), 0) < v:
                        ok = False
                        break
                if not ok:
                    break
                if o["tok"][0] == "dma":
                    k = id(self.dsem[o["tok"][1]])
                    sem[k] = sem.get(k, 0) + 16
                elif o["flag"]:
                    k = id(self.esem[e])
                    sem[k] = sem.get(k, 0) + 1
                pos[e] += 1
                progress = True
    stuck = {e: (pos[e], len(self.ops[e])) for e in ENGS if pos[e] < len(self.ops[e])}
    if stuck:
        print("SCHED DEADLOCK:", stuck)
        for e in stuck:
            o = self.ops[e][pos[e]]
            print("  ", e, "op", pos[e], "tok", o["tok"], "deps", [(d, resolve(d)[1], sem.get(id(resolve(d)[0]), 0)) for d in o["deps"]])
        raise RuntimeError("sched deadlock")
    else:
        print("sched sim ok:", {e: len(self.ops[e]) for e in ENGS})


Sched._simulate = _sched_simulate


class Ring:
    def __init__(self, tiles, name):
        self.tiles = tiles
        self.name = name
        self.i = 0

    def next(self):
        i = self.i
        self.i = (self.i + 1) % len(self.tiles)
        return self.tiles[i], (self.name, i)


PRM_LAYOUT = [("n1w", 4 * 8), ("n2w", 4 * 8), ("fnw", 8), ("snw", 4 * 8), ("qnw", 4 * 2), ("kvnw", 4 * 2),
              ("aconv", 4 * 3 * 4), ("sconvw", 4 * 3 * 12), ("sconvb", 4 * 12), ("dexp", 4 * 2 * 8),
              ("alog", 4 * 2 * 16), ("dtb", 4 * 2 * 16), ("bada", 4 * 48)]
PRM_OFF = {}
_o = 0
for _n, _s in PRM_LAYOUT:
    PRM_OFF[_n] = (_o, _s)
    _o += _s
NPRM = _o


def _pc(v):
    v = np.asarray(v, np.float32)
    lead = v.shape[:-1]
    c = v.shape[-1] // 128
    v = v.reshape(lead + (c, 128))
    v = np.moveaxis(v, -1, 0)
    return np.ascontiguousarray(v).reshape(128, -1)


def pack_params(inp):
    parts = {
        "n1w": _pc(inp["norm1_w"]), "n2w": _pc(inp["norm2_w"]), "fnw": _pc(inp["final_norm_w"]),
        "snw": _pc(inp["ssm_norm_w"]), "qnw": _pc(inp["q_norm_w"]), "kvnw": _pc(inp["kv_norm_w"]),
        "aconv": _pc(inp["a_conv_w"]), "sconvw": _pc(inp["ssm_conv_w"]), "sconvb": _pc(inp["ssm_conv_b"]),
        "dexp": _pc(np.repeat(np.asarray(inp["ssm_d"], np.float32), 64, axis=-1)),
        "alog": np.broadcast_to(np.asarray(inp["ssm_a_log"], np.float32).reshape(1, -1), (128, 128)),
        "dtb": np.broadcast_to(np.asarray(inp["ssm_dt_bias"], np.float32).reshape(1, -1), (128, 128)),
        "bada": _pc(inp["b_ada"]),
    }
    out = np.zeros((128, NPRM), np.float32)
    for n, (o, s) in PRM_OFF.items():
        assert parts[n].shape == (128, s), (n, parts[n].shape, s)
        out[:, o:o + s] = parts[n]
    return out


def build_program(stop=None, dump=(), nl=DEPTH):
    nc = bass.Bass("TRN2", target_bir_lowering=False)

    def din(name, shape, dt=F32):
        return nc.dram_tensor(name, list(shape), dt, kind="ExternalInput").ap()

    def dout(name, shape, dt=F32):
        return nc.dram_tensor(name, list(shape), dt, kind="ExternalOutput").ap()

    def dscr(name, shape, dt):
        kind = "ExternalOutput" if name in dump else "Internal"
        return nc.dram_tensor(name, list(shape), dt, kind=kind).ap()

    xT0 = din("xT0", [D, T])
    cond = din("cond", [128, 8, 2])
    cckvT = din("cckvT", [DEPTH, 256, 512])
    ckrT = din("ckrT", [DEPTH, 32, 512])
    h0 = din("h0", [DEPTH, 2, 128, 1024])
    ropeC = din("ropeC", [32, TS])
    ropeS = din("ropeS", [32, TS])
    cst = din("cst", [128, 512])
    prm_d = din("prm", [128, NPRM])
    w_in = din("w_in", [nl, D, IN_COLS])
    w_kr2 = din("w_kr2", [nl, D, 2, 96])
    w_uq2 = din("w_uq2", [nl, 256, 2, 8, 96])
    w_ukv = din("w_ukv", [nl, 256, 1024])
    w_a_out = din("w_a_out", [nl, 512, D])
    w_b_out = din("w_b_out", [nl, D, D])
    w_c_out = din("w_c_out", [nl, 512, D])
    w_o = din("w_o", [nl, D, D])
    w_ada = din("w_ada", [nl, D, 6 * D])
    w_ff1 = din("w_ff1", [nl, D, FF])
    w_ff3 = din("w_ff3", [nl, D, FF])
    w_ff2 = din("w_ff2", [nl, FF, D])

    yT = dout("yT", [D, T])
    nckvT = dout("nckvT", [DEPTH, 256, 512])
    nkrT = dout("nkrT", [DEPTH, 32, 512])
    nssm = dout("nssm", [DEPTH, 2, 2, 128, 1024])

    XT = dscr("XT", [D, T], F32)
    HT = dscr("HT", [D, T], BF16)
    UT = dscr("UT", [512, T], BF16)
    ABT = dscr("ABT", [512, T], BF16)
    ZT = dscr("ZT", [D, T], BF16)
    XBCT = dscr("XBCT", [1536, T], BF16)
    DTT = dscr("DTT", [T, 16], F32)
    QT = dscr("QT", [8, 96, T], BF16)
    CKVT = dscr("CKVT", [256, T], BF16)
    KRT = dscr("KRT", [32, T], BF16)
    VAT = dscr("VAT", [512, T], BF16)
    XSC = dscr("XSC", [1536, T], BF16)
    XTOK = dscr("XTOK", [T, 1024], BF16)
    BTOK = dscr("BTOK", [T, 256], BF16)
    HENT = dscr("HENT", [2, NCH, 128, 1024], BF16)
    YBT = dscr("YBT", [D, T], BF16)
    ATT = dscr("ATT", [512, T], BF16)

    WB = []
    for si in range(2):
        WB.append(dict(
            win=dscr("WBwin%d" % si, [D, IN_COLS], BF16), wa=dscr("WBwa%d" % si, [512, D], BF16),
            wb=dscr("WBwb%d" % si, [D, D], BF16), wc=dscr("WBwc%d" % si, [512, D], BF16),
            wo=dscr("WBwo%d" % si, [D, D], BF16), wf1=dscr("WBwf1%d" % si, [D, FF], BF16),
            wf3=dscr("WBwf3%d" % si, [D, FF], BF16), wf2=dscr("WBwf2%d" % si, [FF, D], BF16)))

    with contextlib.ExitStack() as gst:
        S = Sched(nc, gst)

        _uid = [0]

        def sb(st, name, shape, dt):
            _uid[0] += 1
            return st.enter_context(nc.sbuf_tensor("sb%d_%s" % (_uid[0], name), list(shape), dt))

        prm = sb(gst, "prm", [128, NPRM], F32)
        cstt = sb(gst, "cstt", [128, 512], F32)
        identb = sb(gst, "identb", [128, 128], BF16)
        MOD = sb(gst, "MOD", [128, DEPTH, 48, 2], F32)
        A1 = sb(gst, "A1", [128, DEPTH, 8, 2], F32)
        A2 = sb(gst, "A2", [128, DEPTH, 8, 2], F32)
        dsum = sb(gst, "dsum", [128, DEPTH, 8], F32)
        wring = Ring([sb(gst, "wr%d" % i, [128, 6144], BF16) for i in range(4)], "wr")
        uqw = sb(gst, "uqw", [128, 2, 2 * 8 * 96], BF16)
        krw = sb(gst, "krw", [128, 8, 2 * 96], BF16)
        dtw = sb(gst, "dtw", [128, 8, 16], BF16)
        ukvw = sb(gst, "ukvw", [128, 2, 1024], BF16)
        psum = [gst.enter_context(nc.psum_tensor("ps%d" % i, [128, 512], F32)) for i in range(7)]
        psbT = gst.enter_context(nc.psum_tensor("psbT", [128, 1024], BF16))
        pring = Ring(psum[0:5], "ps")
        plong = Ring(psum[5:7], "pl")
        triF = cstt[:, 0:128]
        triB = cstt[:, 128:256]
        ones = cstt[:, 256:384]

        def P(name, l=None):
            o, s = PRM_OFF[name]
            v = prm[:, o:o + s]
            return v

        def pv(name, pattern, **kw):
            o, s = PRM_OFF[name]
            return prm[:, o:o + s].rearrange(pattern, **kw)

        n1w = pv("n1w", "p (l c) -> p l c", l=4)
        n2w = pv("n2w", "p (l c) -> p l c", l=4)
        fnw = P("fnw")
        snw = pv("snw", "p (l c) -> p l c", l=4)
        qnw = pv("qnw", "p (l c) -> p l c", l=4)
        kvnw = pv("kvnw", "p (l c) -> p l c", l=4)
        aconv = pv("aconv", "p (l k c) -> p l k c", l=4, k=3)
        sconvw = pv("sconvw", "p (l k c) -> p l k c", l=4, k=3)
        sconvb = pv("sconvb", "p (l c) -> p l c", l=4)
        dexp = pv("dexp", "p (l d c) -> p l d c", l=4, d=2)
        alog = pv("alog", "p (l d h) -> p l d h", l=4, d=2)
        dtb = pv("dtb", "p (l d h) -> p l d h", l=4, d=2)
        bada = pv("bada", "p (l c) -> p l c", l=4)

        def mm(out, lhsT, rhs, start, stop, rd, wr):
            S.op("pe", lambda e: e.matmul(out, lhsT=lhsT, rhs=rhs, start=start, stop=stop), reads=rd, writes=wr)

        def act(out, in_, func, rd, wr, **kw):
            S.op("act", lambda e: e.activation(out=out, in_=in_, func=func, **kw), reads=rd, writes=wr)

        def tt(eng, out, in0, in1, op, rd, wr):
            S.op(eng, lambda e: e.tensor_tensor(out=out, in0=in0, in1=in1, op=op), reads=rd, writes=wr)

        def ts(eng, out, in0, s1, s2, op0, op1, rd, wr):
            if op1 is None:
                S.op(eng, lambda e: e.tensor_scalar(out=out, in0=in0, scalar1=s1, scalar2=None, op0=op0), reads=rd, writes=wr)
            else:
                S.op(eng, lambda e: e.tensor_scalar(out=out, in0=in0, scalar1=s1, scalar2=s2, op0=op0, op1=op1), reads=rd, writes=wr)

        def stt(eng, out, in0, scalar, in1, op0, op1, rd, wr):
            S.op(eng, lambda e: e.scalar_tensor_tensor(out=out, in0=in0, scalar=scalar, in1=in1, op0=op0, op1=op1), reads=rd, writes=wr)

        def cp(eng, out, in_, rd, wr):
            if eng == "act":
                act(out, in_, AF.Copy, rd, wr)
            else:
                S.op(eng, lambda e: e.tensor_copy(out=out, in_=in_), reads=rd, writes=wr)

        def load(out, in_, rd, wr, eng="sp"):
            return S.dma(eng, lambda e: e.dma_start(out=out, in_=in_), reads=rd, writes=wr)

        def store(out, in_, rd, wr, eng="act"):
            return S.dma(eng, lambda e: e.dma_start(out=out, in_=in_), reads=rd, writes=wr)

        def wload(src2d, n_kc, ncols):
            slot, key = wring.next()
            view = slot[:, 0:n_kc * ncols].rearrange("p (k n) -> p k n", k=n_kc)
            S.dma("pool", lambda e: e.dma_start(out=view, in_=src2d.rearrange("(k p) n -> p k n", p=128)), reads=[], writes=[key])
            return view, key

        def wloadb(src2d, n_kc, ncols):
            slot, key = wring.next()
            view = slot[:, 0:n_kc * ncols].rearrange("p (k n) -> p k n", k=n_kc)
            S.dma("pool", lambda e: e.dma_start(out=view, in_=src2d.rearrange("(k p) n -> p k n", p=128)), reads=[], writes=[key])
            return view, key

        def convert_weights(l, stage_ring):
            wb = WB[l % 2]
            jobs = []
            for c0 in list(range(0, 4096, 512)) + [C_CQ] + list(range(C_G, IN_COLS, 512)):
                jobs.append((w_in[l, :, c0:c0 + 512], wb["win"][:, c0:c0 + 512], 8, 512))
            for c0 in (0, 512):
                jobs.append((w_a_out[l, :, c0:c0 + 512], wb["wa"][:, c0:c0 + 512], 4, 512))
                jobs.append((w_b_out[l, :, c0:c0 + 512], wb["wb"][:, c0:c0 + 512], 8, 512))
                jobs.append((w_c_out[l, :, c0:c0 + 512], wb["wc"][:, c0:c0 + 512], 4, 512))
                jobs.append((w_o[l, :, c0:c0 + 512], wb["wo"][:, c0:c0 + 512], 8, 512))
            for fg in range(6):
                ncol = 512 if fg < 5 else 256
                jobs.append((w_ff1[l, :, fg * 512:fg * 512 + ncol], wb["wf1"][:, fg * 512:fg * 512 + ncol], 8, ncol))
                jobs.append((w_ff3[l, :, fg * 512:fg * 512 + ncol], wb["wf3"][:, fg * 512:fg * 512 + ncol], 8, ncol))
            for cg in range(4):
                jobs.append((w_ff2[l, :, cg * 256:(cg + 1) * 256], wb["wf2"][:, cg * 256:(cg + 1) * 256], 22, 256))
            pend = []
            nst = len(stage_ring.tiles)

            def issue_store(item):
                view, key, dst, n_kc = item
                S.dma("pool", lambda e: e.dma_start(out=dst.rearrange("(k p) n -> p k n", p=128), in_=view), reads=[key], writes=[])

            for (src, dst, n_kc, ncols) in jobs:
                if len(pend) == nst:
                    issue_store(pend.pop(0))
                slot, key = stage_ring.next()
                view = slot[:, 0:n_kc * ncols].rearrange("p (k n) -> p k n", k=n_kc)
                S.dma("pool", lambda e, view=view, src=src: e.dma_start(out=view, in_=src.rearrange("(k p) n -> p k n", p=128)), reads=[], writes=[key])
                pend.append((view, key, dst, n_kc))
            while pend:
                issue_store(pend.pop(0))

        def rstd_from_ssq(ps_ap, n_feat, out_ap, tmp_ap, rd, wr, tmpkey):
            act(tmp_ap, ps_ap, AF.Ln, rd, [tmpkey], scale=1.0 / n_feat, bias=EPS)
            act(out_ap, tmp_ap, AF.Exp, [tmpkey], wr, scale=-0.5)

        with contextlib.ExitStack() as st:
            condt = sb(st, "condt", [128, 8, 2], F32)
            scb = sb(st, "scb", [128, 8, 16], BF16)
            stg0 = Ring([sb(st, "cvs%d" % i, [128, 6144], BF16) for i in range(3)], "cvs")
            convert_weights(0, stg0)
            load(prm[:], prm_d, [], ["prm"])
            load(cstt[:], cst, [], ["cst"])
            load(condt[:], cond, [], ["condt"])
            cp("dve", identb[:], cstt[:, 384:512], ["cst"], ["identb"])
            S.op("pool", lambda e: e.memset(scb[:], 0.0), writes=["scb"])
            act(scb[:, :, 0:2], condt[:], AF.Silu, ["condt", "scb"], ["scb"])
            _ncg = int(os.environ.get("KDBG_NCG", "12"))
            for l in range(nl):
                for cg in range(_ncg):
                    wv, wk = wload(w_ada[l, :, cg * 512:(cg + 1) * 512], 8, 512)
                    for oc in range(4):
                        ps, pk = pring.next()
                        for kc in range(8):
                            mm(ps[:, 0:16], wv[:, kc, oc * 128:(oc + 1) * 128], scb[:, kc, :], kc == 0, kc == 7, [wk, "scb"], [pk])
                        ci = cg * 4 + oc
                        act(MOD[:, l, ci, :], ps[:, 0:2], AF.Identity, [pk, "prm"], ["MOD"], bias=bada[:, l, ci:ci + 1])
            for l in range(nl):
                for (Ax, k0, nw) in ((A1, 8, n1w), (A2, 32, n2w)):
                    ts("dve", Ax[:, l, :, :], MOD[:, l, k0:k0 + 8, :], 1.0, None, ALU.add, None, ["MOD"], ["A"])
                    tt("dve", Ax[:, l, :, :], Ax[:, l, :, :], nw[:, l, :].unsqueeze(2).to_broadcast([128, 8, 2]), ALU.mult, ["A", "prm"], ["A"])
                tt("dve", dsum[:, l, :], dexp[:, l, 0, :], dexp[:, l, 1, :], ALU.add, ["prm"], ["dsum"])
            S.flush()
        if stop == "p0":
            dbgo = dout("dbg_mod", [128, DEPTH * 48 * 2])
            load(dbgo, MOD[:].rearrange("p l c r -> p (l c r)"), ["MOD"], ["dbgo"])
            S.flush(final=True)
            return nc

        def modcol(l, kind, kc, r):
            return MOD[:, l, kind * 8 + kc, r:r + 1]

        def norm_mod(st_tiles, xt, ht, Acol, Bcol, xkey, hkey):
            sqt, rst, lnt, tmpf = st_tiles
            ps, pk = pring.next()
            for kc in range(8):
                act(sqt[:, kc % 2, :], xt[:, kc, :], AF.Square, [xkey], [("sq", kc % 2)])
                mm(ps[:], ones, sqt[:, kc % 2, :], kc == 0, kc == 7, ["cst", ("sq", kc % 2)], [pk])
            rstd_from_ssq(ps[:], 1024.0, rst[:], lnt[:], [pk], ["rst"], "lnt")
            for kc in range(8):
                if Bcol is None:
                    stt("dve", ht[:, kc, :], xt[:, kc, :], Acol(kc), rst[:], ALU.mult, ALU.mult, [xkey, "rst", "A", "prm"], [hkey])
                else:
                    stt("dve", tmpf[:, kc % 2, :], xt[:, kc, :], Acol(kc), rst[:], ALU.mult, ALU.mult, [xkey, "rst", "A", "prm"], [("tmpf", kc % 2)])
                    act(ht[:, kc, :], tmpf[:, kc % 2, :], AF.Identity, [("tmpf", kc % 2), "MOD"], [hkey], bias=Bcol(kc))

        for l in range(nl):
            xsrc = xT0 if l == 0 else XT
            with contextlib.ExitStack() as st:
                xt = sb(st, "xt", [128, 8, 512], F32)
                ht = sb(st, "ht", [128, 8, 512], BF16)
                sqt = sb(st, "sqt", [128, 2, 512], F32)
                rst = sb(st, "rst", [128, 512], F32)
                lnt = sb(st, "lnt", [128, 512], F32)
                tmpf = sb(st, "tmpf", [128, 2, 512], F32)
                stg = Ring([sb(st, "stg%d" % i, [128, 4, 512], BF16) for i in range(3)], "stg")
                axt = sb(st, "axt", [128, 4, 512], BF16)
                cqf = sb(st, "cqf", [128, 4, 512], F32)
                nrf = sb(st, "nrf", [128, 2, 512], F32)
                cqn = sb(st, "cqn", [128, 2, 512], BF16)
                ckvn = sb(st, "ckvn", [128, 2, 512], BF16)
                qst = sb(st, "qst", [128, 8, 512], BF16)
                rC = sb(st, "rC", [128, 512], F32)
                rS = sb(st, "rS", [128, 512], F32)
                t1 = sb(st, "t1", [128, 512], F32)
                t2 = sb(st, "t2", [128, 512], F32)
                krt = sb(st, "krt", [128, 512], BF16)
                krf = sb(st, "krf", [128, 512], F32)
                dts = sb(st, "dts", [128, 4, 16], F32)
                S.dma("pool", lambda e, l=l: e.dma_start(out=uqw[:], in_=w_uq2[l].rearrange("(k p) v h r -> p k (v h r)", p=128)), writes=["uqw"])
                S.dma("pool", lambda e, l=l: e.dma_start(out=krw[:], in_=w_kr2[l].rearrange("(k p) v r -> p k (v r)", p=128)), writes=["krw"])
                S.dma("pool", lambda e, l=l: e.dma_start(out=dtw[:], in_=w_in[l, :, C_DT:C_DT + 16].rearrange("(k p) n -> p k n", p=128)), writes=["dtw"])
                S.dma("pool", lambda e, l=l: e.dma_start(out=ukvw[:], in_=w_ukv[l].rearrange("(k p) n -> p k n", p=128)), writes=["ukvw"])
                uq5 = uqw[:].rearrange("p k (v h r) -> p k v h r", v=2, h=8)
                kr4 = krw[:].rearrange("p k (v r) -> p k v r", v=2)
                _steps = os.environ.get("KDBG_P1", "groups,mla,q,kr,dt").split(",")
                _blks = [int(v) for v in os.environ.get("KDBG_BLKS", ",".join(str(i) for i in range(NB))).split(",")]
                for b in _blks:
                    r = 0 if b < 8 else 1
                    smp = b < 8
                    t0 = b * 512
                    load(xt[:], xsrc[:, t0:t0 + 512].rearrange("(k p) t -> p k t", p=128), [("XT", b)], ["xt"])
                    if smp:
                        load(rC[64:96, :], ropeC[:, t0:t0 + 512], [], ["rC"])
                        load(rS[64:96, :], ropeS[:, t0:t0 + 512], [], ["rS"])
                    norm_mod((sqt, rst, lnt, tmpf), xt, ht,
                             lambda kc: A1[:, l, kc, r:r + 1], lambda kc: modcol(l, 0, kc, r), "xt", "ht")
                    store(HT[:, t0:t0 + 512].rearrange("(k p) t -> p k t", p=128), ht[:], ["ht"], [("HT", b)])
                    groups = [("ax", C_AX), ("ab", C_AB), ("ac", C_AC), ("z", C_Z), ("z", C_Z + 512),
                              ("xbc", C_XBC), ("xbc", C_XBC + 512), ("xbc", C_XBC + 1024), ("cqkv", C_CQ)]
                    if "groups" not in _steps:
                        groups = []
                    for gi, (kind, c0) in enumerate(groups):
                        wv, wk = wloadb(WB[l % 2]["win"][:, c0:c0 + 512], 8, 512)
                        if kind in ("ab", "ac", "z", "xbc"):
                            sg, sk = stg.next()
                        for oc in range(4):
                            ps, pk = pring.next()
                            for kc in range(8):
                                mm(ps[:], wv[:, kc, oc * 128:(oc + 1) * 128], ht[:, kc, :], kc == 0, kc == 7, [wk, "ht"], [pk])
                            if kind == "ax":
                                cp("act", axt[:, oc, :], ps[:], [pk], ["axt"])
                            elif kind == "ab":
                                cp("act", sg[:, oc, :], ps[:], [pk], [sk])
                            elif kind == "ac":
                                tt("dve", sg[:, oc, :], ps[:], axt[:, oc, :], ALU.mult, [pk, "axt"], [sk])
                            elif kind == "z":
                                act(sg[:, oc, :], ps[:], AF.Silu, [pk], [sk])
                            elif kind == "xbc":
                                cp("act" if oc % 2 == 0 else "dve", sg[:, oc, :], ps[:], [pk], [sk])
                            else:
                                cp("act" if oc % 2 == 0 else "dve", cqf[:, oc, :], ps[:], [pk], [("cqf", oc // 2)])
                        if kind in ("ab", "ac", "z", "xbc"):
                            dst = {"ab": ABT, "ac": UT, "z": ZT, "xbc": XBCT}[kind]
                            r0 = c0 - {"ab": C_AB, "ac": C_AC, "z": C_Z, "xbc": C_XBC}[kind]
                            store(dst[r0:r0 + 512, t0:t0 + 512].rearrange("(k p) t -> p k t", p=128), sg[:], [sk], [(kind + "T", b, r0)])
                    for half, nw, dstb in ((0, qnw, cqn), (1, kvnw, ckvn)) if "mla" in _steps else ():
                        ps, pk = pring.next()
                        for j in range(2):
                            act(sqt[:, j, :], cqf[:, half * 2 + j, :], AF.Square, [("cqf", half)], [("sq", j)])
                            mm(ps[:], ones, sqt[:, j, :], j == 0, j == 1, ["cst", ("sq", j)], [pk])
                        rstd_from_ssq(ps[:], 256.0, rst[:], lnt[:], [pk], ["rst"], "lnt")
                        for j in range(2):
                            stt("dve", nrf[:, j, :], cqf[:, half * 2 + j, :], nw[:, l, j:j + 1], rst[:], ALU.mult, ALU.mult,
                                [("cqf", half), "rst", "prm"], [("nrf", j)])
                            cp("act", dstb[:, j, :], nrf[:, j, :], [("nrf", j)], [("lat", half)])
                        if half == 1:
                            store(CKVT[:, t0:t0 + 512].rearrange("(k p) t -> p k t", p=128), ckvn[:], [("lat", 1)], [("CKVT", b)])
                            if not smp:
                                store(nckvT[l].rearrange("(k p) t -> p k t", p=128), nrf[:], [("nrf", 0), ("nrf", 1)], [("nckv", l)])
                    for h in range(8) if "q" in _steps else ():
                        psn, pkn = pring.next()
                        for kc in range(2):
                            mm(psn[0:96, :], uq5[:, kc, 0, h, :], cqn[:, kc, :], kc == 0, kc == 1, ["uqw", ("lat", 0)], [pkn])
                        if smp:
                            pss, pks = pring.next()
                            for kc in range(2):
                                mm(pss[0:96, :], uq5[:, kc, 1, h, :], cqn[:, kc, :], kc == 0, kc == 1, ["uqw", ("lat", 0)], [pks])
                            cp("act", qst[0:64, h, :], psn[0:64, :], [pkn], [("qst", h)])
                            tt("dve", t1[64:96, :], psn[64:96, :], rC[64:96, :], ALU.mult, [pkn, "rC"], ["t1"])
                            tt("dve", t2[64:96, :], pss[64:96, :], rS[64:96, :], ALU.mult, [pks, "rS"], ["t2"])
                            tt("dve", qst[64:96, h, :], t1[64:96, :], t2[64:96, :], ALU.add, ["t1", "t2"], [("qst", h)])
                        else:
                            cp("act", qst[0:96, h, :], psn[0:96, :], [pkn], [("qst", h)])
                    if "q" in _steps:
                        store(QT[:, :, t0:t0 + 512].rearrange("h r t -> r h t"), qst[0:96, :, :], [("qst", h) for h in range(8)], [("QT", b)])
                    if "kr" not in _steps:
                        continue
                    psn, pkn = pring.next()
                    for kc in range(8):
                        mm(psn[0:96, :], kr4[:, kc, 0, :], ht[:, kc, :], kc == 0, kc == 7, ["krw", "ht"], [pkn])
                    if smp:
                        pss, pks = pring.next()
                        for kc in range(8):
                            mm(pss[0:96, :], kr4[:, kc, 1, :], ht[:, kc, :], kc == 0, kc == 7, ["krw", "ht"], [pks])
                        tt("dve", t1[64:96, :], psn[64:96, :], rC[64:96, :], ALU.mult, [pkn, "rC"], ["t1"])
                        tt("dve", t2[64:96, :], pss[64:96, :], rS[64:96, :], ALU.mult, [pks, "rS"], ["t2"])
                        tt("dve", krt[64:96, :], t1[64:96, :], t2[64:96, :], ALU.add, ["t1", "t2"], ["krt"])
                    else:
                        cp("dve", krf[64:96, :], psn[64:96, :], [pkn], ["krf"])
                        cp("act", krt[64:96, :], krf[64:96, :], ["krf"], ["krt"])
                        store(nkrT[l], krf[64:96, :], ["krf"], [("nkr", l)])
                    store(KRT[:, t0:t0 + 512], krt[64:96, :], ["krt"], [("KRT", b)])
                    if "dt" not in _steps:
                        continue
                    ps, pk = pring.next()
                    for tl in range(4):
                        for kc in range(8):
                            mm(ps[:, tl * 16:(tl + 1) * 16], ht[:, kc, tl * 128:(tl + 1) * 128], dtw[:, kc, :], kc == 0, kc == 7, ["dtw", "ht"], [pk])
                    cp("dve", dts[:].rearrange("p a h -> p (a h)"), ps[:, 0:64], [pk], ["dts"])
                    store(DTT[t0:t0 + 512, :].rearrange("(a p) h -> p a h", p=128), dts[:], ["dts"], [("DTT", b)])
                S.flush()
            if stop == "p1":
                break
            with contextlib.ExitStack() as st:
                ub = sb(st, "ub", [128, 4, 514], BF16)
                abt = sb(st, "abt", [128, 4, 512], BF16)
                acc = sb(st, "acc", [128, 2, 512], F32)
                vat = sb(st, "vat", [128, 4, 512], BF16)
                xb = sb(st, "xb", [128, 12, 514], BF16)
                xsc = sb(st, "xsc", [128, 12, 512], BF16)
                xtk = sb(st, "xtk", [128, 1024], BF16)
                btk = sb(st, "btk", [128, 256], BF16)
                segs = [(b * 512, 512, 0, TS) for b in range(8)] + [(TS, 256, TS, TS + 256), (TS + 256, 256, TS + 256, T)]
                for (t0, n, s0, s1) in segs:
                    lo, hi = max(t0 - 1, s0), min(t0 + n + 1, s1)
                    off = lo - (t0 - 1)
                    if lo == t0:
                        S.op("pool", lambda e: e.memset(ub[:, :, 0:1], 0.0), writes=["ub"])
                        S.op("pool", lambda e: e.memset(xb[:, :, 0:1], 0.0), writes=["xb"])
                    if hi == t0 + n:
                        S.op("pool", lambda e, n=n: e.memset(ub[:, :, n + 1:n + 2], 0.0), writes=["ub"])
                        S.op("pool", lambda e, n=n: e.memset(xb[:, :, n + 1:n + 2], 0.0), writes=["xb"])
                    load(ub[:, :, off:off + hi - lo], UT[:, lo:hi].rearrange("(k p) t -> p k t", p=128), [], ["ub"])
                    load(abt[:, :, 0:n], ABT[:, t0:t0 + n].rearrange("(k p) t -> p k t", p=128), [], ["abt"])
                    load(xb[:, :, off:off + hi - lo], XBCT[:, lo:hi].rearrange("(k p) t -> p k t", p=128), [], ["xb"])
                    for c in range(16):
                        src, cw, ci, skey = (ub, aconv, c, "ub") if c < 4 else (xb, sconvw, c - 4, "xb")
                        a = acc[:, c % 2, 0:n]
                        ak = ("acc", c % 2)
                        ts("dve", a, src[:, ci, 1:n + 1], cw[:, l, 1, ci:ci + 1], None, ALU.mult, None, [skey, "prm"], [ak])
                        stt("dve", a, src[:, ci, 0:n], cw[:, l, 0, ci:ci + 1], a, ALU.mult, ALU.add, [skey, "prm", ak], [ak])
                        stt("dve", a, src[:, ci, 2:n + 2], cw[:, l, 2, ci:ci + 1], a, ALU.mult, ALU.add, [skey, "prm", ak], [ak])
                        if c < 4:
                            tt("dve", vat[:, ci, 0:n], a, abt[:, ci, 0:n], ALU.mult, [ak, "abt"], ["vat"])
                        else:
                            act(xsc[:, ci, 0:n], a, AF.Silu, [ak, "prm"], ["xsc"], bias=sconvb[:, l, ci:ci + 1])
                    store(VAT[:, t0:t0 + n].rearrange("(k p) t -> p k t", p=128), vat[:, :, 0:n], ["vat"], [])
                    store(XSC[:, t0:t0 + n].rearrange("(k p) t -> p k t", p=128), xsc[:, :, 0:n], ["xsc"], [])
                    for tl in range(n // 128):
                        for c in range(8):
                            S.op("pe", lambda e, c=c, tl=tl: e.transpose(psbT[:, c * 128:(c + 1) * 128], xsc[:, c, tl * 128:(tl + 1) * 128], identb[:]),
                                 reads=["xsc", "identb"], writes=["psbT"])
                        cp("act", xtk[:], psbT[:, 0:1024], ["psbT"], ["xtk"])
                        store(XTOK[t0 + tl * 128:t0 + (tl + 1) * 128, :], xtk[:], ["xtk"], [])
                        for c in range(2):
                            S.op("pe", lambda e, c=c, tl=tl: e.transpose(psbT[:, c * 128:(c + 1) * 128], xsc[:, 8 + c, tl * 128:(tl + 1) * 128], identb[:]),
                                 reads=["xsc", "identb"], writes=["psbT"])
                        cp("dve", btk[:], psbT[:, 0:256], ["psbT"], ["btk"])
                        store(BTOK[t0 + tl * 128:t0 + (tl + 1) * 128, :], btk[:], ["btk"], [])
                S.flush()
            if stop == "conv":
                break
            with contextlib.ExitStack() as st:
                dtr = sb(st, "dtr", [128, NCH, 16], F32)
                ea = sb(st, "ea", [128, 2, 16], F32)
                dtd = [sb(st, "dtd%d" % d, [128, NCH, 16], F32) for d in range(2)]
                dta = [sb(st, "dta%d" % d, [128, NCH, 16], F32) for d in range(2)]
                ctk = [sb(st, "ctk%d" % d, [128, NCH, 16], F32) for d in range(2)]
                tot = [sb(st, "tot%d" % d, [128, NCH, 16], F32) for d in range(2)]
                ted = [sb(st, "ted%d" % d, [128, NCH, 16], F32) for d in range(2)]
                cdc = [sb(st, "cdc%d" % d, [128, NCH, 16], F32) for d in range(2)]
                tri = [triF, triB]
                load(dtr[:], DTT.rearrange("(c p) h -> p c h", p=128), [], ["dtr"])
                act(ea[:], alog[:, l, :, :], AF.Exp, ["prm"], ["ea"])
                for d in range(2):
                    tt("dve", dtd[d][:], dtr[:], dtb[:, l, d, :].unsqueeze(1).to_broadcast([128, NCH, 16]), ALU.add, ["dtr", "prm"], [("dtd", d)])
                    act(dtd[d][:], dtd[d][:], AF.Exp, [("dtd", d)], [("dtd", d)])
                    act(dtd[d][:], dtd[d][:], AF.Ln, [("dtd", d)], [("dtd", d)], bias=1.0)
                    stt("dve", dta[d][:], dtd[d][:], -1.0, ea[:, d, :].unsqueeze(1).to_broadcast([128, NCH, 16]), ALU.mult, ALU.mult,
                        [("dtd", d), "ea"], [("dta", d)])
                    flat = dta[d][:].rearrange("p c h -> p (c h)")
                    for (lhs, dstt, dk) in ((tri[d], ctk[d], "ctk"), (ones, tot[d], "tot")):
                        dflat = dstt[:].rearrange("p c h -> p (c h)")
                        for (a0, a1) in ((0, 512), (512, NCH * 16)):
                            ps, pk = pring.next()
                            mm(ps[:, 0:a1 - a0], lhs, flat[:, a0:a1], True, True, ["cst", ("dta", d)], [pk])
                            cp("dve", dflat[:, a0:a1], ps[:, 0:a1 - a0], [pk], [(dk, d)])
                    tt("dve", ted[d][:], tot[d][:], ctk[d][:], ALU.subtract, [("tot", d), ("ctk", d)], [("ted", d)])
                    act(ted[d][:], ted[d][:], AF.Exp, [("ted", d)], [("ted", d)])
                    tt("dve", ted[d][:], ted[d][:], dtd[d][:], ALU.mult, [("ted", d), ("dtd", d)], [("ted", d)])
                    act(cdc[d][:], tot[d][:], AF.Exp, [("tot", d)], [("cdc", d)])
                with contextlib.ExitStack() as st2:
                    St = sb(st2, "St", [128, 2, 512], F32)
                    hbr = Ring([sb(st2, "hb%d" % i, [128, 1024], BF16) for i in range(2)], "hb")
                    xkr = Ring([sb(st2, "xk%d" % i, [128, 1024], BF16) for i in range(2)], "xk")
                    bkr = Ring([sb(st2, "bk%d" % i, [128, 256], BF16) for i in range(2)], "bk")
                    xwr = Ring([sb(st2, "xw%d" % i, [128, 1024], BF16) for i in range(2)], "xw")
                    for d in range(2):
                        for si, (s0, slen, smp) in enumerate(SEQS):
                            nchk, c0 = slen // 128, s0 // 128
                            if smp:
                                load(St[:], h0[l, d].rearrange("n (g f) -> n g f", g=2), [], ["St"])
                            else:
                                S.op("pool", lambda e: e.memset(St[:], 0.0), writes=["St"])
                            order = range(nchk) if d == 0 else range(nchk - 1, -1, -1)
                            for ci in order:
                                c = c0 + ci
                                hb, hk = hbr.next()
                                cp("act", hb[:], St[:].rearrange("p g f -> p (g f)"), ["St"], [hk])
                                store(HENT[d, c], hb[:], [hk], [])
                                xk, xkk = xkr.next()
                                bk, bkk = bkr.next()
                                xw, xwk = xwr.next()
                                load(xk[:], XTOK[c * 128:(c + 1) * 128, :], [], [xkk])
                                load(bk[:], BTOK[c * 128:(c + 1) * 128, :], [], [bkk])
                                tt("dve", xw[:].rearrange("p (h q) -> p h q", h=16), xk[:].rearrange("p (h q) -> p h q", h=16),
                                   ted[d][:, c, :].unsqueeze(2).to_broadcast([128, 16, 64]), ALU.mult, [xkk, ("ted", d)], [xwk])
                                for g in range(2):
                                    ps, pk = pring.next()
                                    mm(ps[:], bk[:, g * 128:(g + 1) * 128], xw[:, g * 512:(g + 1) * 512], True, True, [bkk, xwk], [pk])
                                    sg3 = St[:, g, :].rearrange("p (h q) -> p h q", h=8)
                                    tt("dve", sg3, sg3, cdc[d][:, c, g * 8:(g + 1) * 8].unsqueeze(2).to_broadcast([128, 8, 64]), ALU.mult,
                                       ["St", ("cdc", d)], ["St"])
                                    tt("dve", St[:, g, :], St[:, g, :], ps[:], ALU.add, ["St", pk], ["St"])
                            if not smp:
                                store(nssm[l, d, si - 1], St[:].rearrange("p g f -> p (g f)"), ["St"], [])
                    S.flush()
                if stop == "ssd1":
                    break
                with contextlib.ExitStack() as st2:
                    xs3r = Ring([sb(st2, "xs3%d" % i, [128, 12, 128], BF16) for i in range(2)], "xs3")
                    xk2r = Ring([sb(st2, "xk2%d" % i, [128, 1024], BF16) for i in range(2)], "xk2")
                    her = [Ring([sb(st2, "he%d%d" % (d, i), [128, 1024], BF16) for i in range(2)], "he%d" % d) for d in range(2)]
                    zsr = Ring([sb(st2, "zs%d" % i, [128, 8, 128], BF16) for i in range(2)], "zs")
                    cbm = [sb(st2, "cbm%d" % d, [128, 2, 128], F32) for d in range(2)]
                    Rt_ = sb(st2, "Rt_", [128, 16, 128], F32)
                    cbdt = sb(st2, "cbdt", [128, 16, 128], F32)
                    crs = Ring([sb(st2, "crs%d" % i, [128, 512], F32) for i in range(2)], "crs")
                    arg = Ring([sb(st2, "arg%d" % i, [128, 512], F32) for i in range(2)], "arg")
                    Et = Ring([sb(st2, "Et%d" % i, [128, 512], F32) for i in range(2)], "Et")
                    ECt = Ring([sb(st2, "ECt%d" % i, [128, 512], F32) for i in range(2)], "ECt")
                    Wt = [sb(st2, "Wt%d" % d, [128, 16, 128], BF16) for d in range(2)]
                    Csc = [sb(st2, "Csc%d" % d, [128, 16, 128], BF16) for d in range(2)]
                    yg = sb(st2, "yg", [128, 8, 128], F32)
                    sqy = sb(st2, "sqy", [128, 2, 128], F32)
                    rsy = sb(st2, "rsy", [128, 128], F32)
                    lny = sb(st2, "lny", [128, 128], F32)
                    ynr = Ring([sb(st2, "yn%d" % i, [128, 8, 128], BF16) for i in range(2)], "yn")
                    Wtr = [Ring([Wt[d], sb(st2, "Wtb%d" % d, [128, 16, 128], BF16)], "Wt%d" % d) for d in range(2)]
                    Cscr = [Ring([Csc[d], sb(st2, "Cscb%d" % d, [128, 16, 128], BF16)], "Csc%d" % d) for d in range(2)]

                    def stageA(c):
                        tk0 = c * 128
                        xs3, xs3k = xs3r.next()
                        xk2, xk2k = xk2r.next()
                        zs, zsk = zsr.next()
                        load(xs3[:], XSC[:, tk0:tk0 + 128].rearrange("(k p) t -> p k t", p=128), [], [xs3k])
                        load(xk2[:], XTOK[tk0:tk0 + 128, :], [], [xk2k])
                        load(zs[:], ZT[:, tk0:tk0 + 128].rearrange("(k p) t -> p k t", p=128), [], [zsk])
                        he = []
                        for d in range(2):
                            t_, k_ = her[d].next()
                            load(t_[:], HENT[d, c], [], [k_])
                            he.append((t_, k_))
                        psA, pkA = pring.next()
                        for g in range(2):
                            mm(psA[:, g * 128:(g + 1) * 128], xs3[:, 8 + g, :], xs3[:, 10 + g, :], True, True, [xs3k], [pkA])
                        for d in range(2):
                            tt("dve", cbm[d][:], psA[:, 0:256].rearrange("p (g i) -> p g i", g=2), tri[d].unsqueeze(1).to_broadcast([128, 2, 128]),
                               ALU.mult, [pkA, "cst"], [("cbm", d)])
                        WC = []
                        for d in range(2):
                            wt_, wtk = Wtr[d].next()
                            cs_, csk = Cscr[d].next()
                            WC.append((wt_, wtk, cs_, csk))
                            tt("pool", Rt_[:], tri[d].unsqueeze(1).to_broadcast([128, 16, 128]),
                               dta[d][:, c, :].unsqueeze(2).to_broadcast([128, 16, 128]), ALU.mult, ["cst", ("dta", d)], ["Rt_"])
                            for g in range(2):
                                tt("pool", cbdt[:, g * 8:(g + 1) * 8, :], cbm[d][:, g, :].unsqueeze(1).to_broadcast([128, 8, 128]),
                                   dtd[d][:, c, g * 8:(g + 1) * 8].unsqueeze(2).to_broadcast([128, 8, 128]), ALU.mult, [("cbm", d), ("dtd", d)], ["cbdt"])
                            for q in range(4):
                                g = q // 2
                                psc, pkc = pring.next()
                                mm(psc[:], ones, Rt_[:, 4 * q:4 * q + 4, :].rearrange("p h i -> p (h i)"), True, True, ["cst", "Rt_"], [pkc])
                                cr, crk = crs.next()
                                ar, ark = arg.next()
                                et, etk = Et.next()
                                ec, eck = ECt.next()
                                cp("dve", cr[:], psc[:], [pkc], [crk])
                                for hh in range(4):
                                    ts("dve", ar[:, hh * 128:(hh + 1) * 128], psc[:, hh * 128:(hh + 1) * 128], ctk[d][:, c, 4 * q + hh:4 * q + hh + 1], 0.0,
                                       ALU.subtract, ALU.min, [pkc, ("ctk", d)], [ark])
                                act(et[:], ar[:], AF.Exp, [ark], [etk])
                                tt("dve", wt_[:, 4 * q:4 * q + 4, :].rearrange("p h i -> p (h i)"), et[:],
                                   cbdt[:, 4 * q:4 * q + 4, :].rearrange("p h i -> p (h i)"), ALU.mult, [etk, "cbdt"], [wtk])
                                act(ec[:], cr[:], AF.Exp, [crk], [eck])
                                tt("pool", cs_[:, 4 * q:4 * q + 4, :], ec[:].rearrange("p (h i) -> p h i", h=4),
                                   xs3[:, 10 + g, :].unsqueeze(1).to_broadcast([128, 4, 128]), ALU.mult, [eck, xs3k], [csk])
                        return dict(c=c, xs3=(xs3, xs3k), xk2=(xk2, xk2k), zs=(zs, zsk), he=he, WC=WC)

                    def stageB(cx):
                        c = cx["c"]
                        tk0 = c * 128
                        xs3, xs3k = cx["xs3"]
                        xk2, xk2k = cx["xk2"]
                        zs, zsk = cx["zs"]
                        he, WC = cx["he"], cx["WC"]
                        psY = [plong.next(), plong.next()]
                        for h in range(16):
                            kc, half = h // 2, h % 2
                            pY, pYk = psY[kc // 4]
                            out = pY[half * 64:(half + 1) * 64, (kc % 4) * 128:(kc % 4 + 1) * 128]
                            for d in range(2):
                                wt_, wtk, cs_, csk = WC[d]
                                mm(out, xk2[:, h * 64:(h + 1) * 64], wt_[:, h, :], d == 0, False, [xk2k, wtk], [pYk])
                                mm(out, he[d][0][:, h * 64:(h + 1) * 64], cs_[:, h, :], False, d == 1, [he[d][1], csk], [pYk])
                        for kc in range(8):
                            pY, pYk = psY[kc // 4]
                            stt("dve", yg[:, kc, :], xs3[:, kc, :], dsum[:, l, kc:kc + 1], pY[:, (kc % 4) * 128:(kc % 4 + 1) * 128], ALU.mult, ALU.add,
                                [xs3k, "dsum", pYk], ["yg"])
                        tt("pool", yg[:], yg[:], zs[:], ALU.mult, ["yg", zsk], ["yg"])
                        ps, pk = pring.next()
                        for kc in range(8):
                            act(sqy[:, kc % 2, :], yg[:, kc, :], AF.Square, ["yg"], [("sqy", kc % 2)])
                            mm(ps[:, 0:128], ones, sqy[:, kc % 2, :], kc == 0, kc == 7, ["cst", ("sqy", kc % 2)], [pk])
                        rstd_from_ssq(ps[:, 0:128], 1024.0, rsy[:], lny[:], [pk], ["rsy"], "lny")
                        yn, ynk = ynr.next()
                        for kc in range(8):
                            stt("dve", yn[:, kc, :], yg[:, kc, :], snw[:, l, kc:kc + 1], rsy[:], ALU.mult, ALU.mult, ["yg", "rsy", "prm"], [ynk])
                        store(YBT[:, tk0:tk0 + 128].rearrange("(k p) t -> p k t", p=128), yn[:], [ynk], [])

                    ctxs = {0: stageA(0)}
                    for c in range(NCH):
                        if c + 1 < NCH:
                            ctxs[c + 1] = stageA(c + 1)
                        stageB(ctxs.pop(c))
                    S.flush()
            if stop == "ssd":
                break
            with contextlib.ExitStack() as st:
                ckvall = sb(st, "ckvall", [128, 2, 4608], BF16)
                KTr = [sb(st, "KT%d" % i, [128, 4608], BF16) for i in range(2)]
                VA = [sb(st, "VA%d" % i, [128, 36, 128], BF16) for i in range(2)]
                qhr = Ring([sb(st, "qh%d" % i, [128, 4096], BF16) for i in range(2)], "qh")
                PT = Ring([sb(st, "PT%d" % i, [128, 512], BF16) for i in range(3)], "PT")
                Lt = sb(st, "Lt", [128, 512], F32)
                Rt = sb(st, "Rt", [128, 512], F32)
                ATr = Ring([sb(st, "AT%d" % i, [128, 512], BF16) for i in range(2)], "AT")
                S.op("pool", lambda e: e.memset(VA[0][:, :, 64:128], 1.0), writes=[("VA", 0)])
                S.op("pool", lambda e: e.memset(VA[1][:, :, 0:64], 1.0), writes=[("VA", 1)])
                S.dma("pool", lambda e: e.dma_start(out=ckvall[:, :, 0:512], in_=cckvT[l].rearrange("(k p) t -> p k t", p=128)), writes=["ckvall"])
                for i in range(2):
                    S.dma("pool", lambda e, i=i: e.dma_start(out=KTr[i][64:96, 0:512], in_=ckrT[l]), writes=[("KT", i)])
                if l + 1 < nl:
                    stgA = Ring([sb(st, "cva%d" % i, [128, 6144], BF16) for i in range(3)], "cva")
                    convert_weights(l + 1, stgA)
                for (nctx, k0, nlat, q0, nq, qblk) in ((512, 0, TS, 0, TS, 512), (0, TS, 256, TS, 256, 256), (0, TS + 256, 256, TS + 256, 256, 256)):
                    nk = nctx + nlat
                    ntile = nk // 128
                    load(ckvall[:, :, nctx:nk], CKVT[:, k0:k0 + nlat].rearrange("(k p) t -> p k t", p=128), [], ["ckvall"])
                    for i in range(2):
                        load(KTr[i][64:96, nctx:nk], KRT[:, k0:k0 + nlat], [], [("KT", i)])
                    for h in range(8):
                        par = h % 2
                        va, vak = VA[par], ("VA", par)
                        KT, ktk = KTr[par], ("KT", par)
                        voff = par * 64
                        for kb in range((nk + 511) // 512):
                            w = min(512, nk - kb * 512)
                            ps, pk = pring.next()
                            for kc in range(2):
                                mm(ps[0:64, 0:w], ukvw[:, kc, h * 128:h * 128 + 64], ckvall[:, kc, kb * 512:kb * 512 + w], kc == 0, kc == 1, ["ukvw", "ckvall"], [pk])
                            cp("act" if kb % 2 == 0 else "dve", KT[0:64, kb * 512:kb * 512 + w], ps[0:64, 0:w], [pk], [ktk])
                        for tg in range(0, ntile, 8):
                            nt = min(8, ntile - tg)
                            ps, pk = pring.next()
                            for j in range(nt):
                                for kc in range(2):
                                    mm(ps[:, j * 64:(j + 1) * 64], ckvall[:, kc, (tg + j) * 128:(tg + j + 1) * 128], ukvw[:, kc, h * 128 + 64:h * 128 + 128],
                                       kc == 0, kc == 1, ["ukvw", "ckvall"], [pk])
                            cp("dve", va[:, tg:tg + nt, voff:voff + 64], ps[:, 0:nt * 64].rearrange("p (t d) -> p t d", d=64), [pk], [vak])
                        qh, qhk = qhr.next()
                        load(qh[0:96, 0:nq], QT[h, :, q0:q0 + nq], [], [qhk])
                        items = [(qb, t) for qb in range(nq // qblk) for t in range(ntile)]
                        SKEW = 2
                        pend = {}
                        cur = {}
                        for i in range(len(items) + SKEW):
                            if i < len(items):
                                qb, t = items[i]
                                psS, pkS = pring.next()
                                mm(psS[:, 0:qblk], KT[0:96, t * 128:(t + 1) * 128], qh[0:96, qb * qblk:(qb + 1) * qblk], True, True, [ktk, qhk], [pkS])
                                pend[i] = (psS, pkS)
                            if i >= SKEW:
                                qb, t = items[i - SKEW]
                                psS, pkS = pend.pop(i - SKEW)
                                if t == 0:
                                    cur[qb] = plong.next()
                                psO, pkO = cur[qb]
                                pt, ptk = PT.next()
                                act(pt[:, 0:qblk], psS[:, 0:qblk], AF.Exp, [pkS], [ptk], scale=SCALE)
                                mm(psO[:, 0:qblk], va[:, t, :], pt[:, 0:qblk], t == 0, t == ntile - 1, [vak, ptk], [pkO])
                                if t == ntile - 1:
                                    orow, drow = voff, 64 - voff
                                    act(Lt[orow:orow + 64, 0:qblk], psO[drow:drow + 64, 0:qblk], AF.Ln, [pkO], ["Lt"])
                                    act(Rt[orow:orow + 64, 0:qblk], Lt[orow:orow + 64, 0:qblk], AF.Exp, ["Lt"], ["Rt"], scale=-1.0)
                                    at, atk = ATr.next()
                                    tt("dve", at[orow:orow + 64, 0:qblk], psO[orow:orow + 64, 0:qblk], Rt[orow:orow + 64, 0:qblk], ALU.mult, [pkO, "Rt"], [atk])
                                    r0 = (h // 2) * 128 + orow
                                    store(ATT[r0:r0 + 64, q0 + qb * qblk:q0 + (qb + 1) * qblk], at[orow:orow + 64, 0:qblk], [atk], [])
                S.flush()
            if stop == "attn":
                break
            with contextlib.ExitStack() as st:
                xt = sb(st, "xt3", [128, 8, 512], F32)
                ht = sb(st, "ht3", [128, 8, 512], BF16)
                va_ = sb(st, "va3", [128, 4, 512], BF16)
                yb_ = sb(st, "yb3", [128, 8, 512], BF16)
                at_ = sb(st, "at3", [128, 4, 512], BF16)
                macc = sb(st, "macc", [128, 4, 512], F32)
                sigr = Ring([sb(st, "sig%d" % i, [128, 512], F32) for i in range(2)], "sig")
                tmr = Ring([sb(st, "tm%d" % i, [128, 512], F32) for i in range(2)], "tm")
                merged = sb(st, "merged", [128, 8, 512], BF16)
                h2 = sb(st, "h2", [128, 8, 512], BF16)
                gt = sb(st, "gt", [128, 22, 512], BF16)
                sar = Ring([sb(st, "sa%d" % i, [128, 512], F32) for i in range(2)], "sa")
                sqt = sb(st, "sqt3", [128, 2, 512], F32)
                rst = sb(st, "rst3", [128, 512], F32)
                lnt = sb(st, "lnt3", [128, 512], F32)
                tmpf = sb(st, "tmpf3", [128, 2, 512], F32)
                last = (l == nl - 1)
                yo = sb(st, "yo", [128, 8, 512], F32) if last else None
                for b in range(NB):
                    r = 0 if b < 8 else 1
                    t0 = b * 512
                    fm = lambda dr: dr[:, t0:t0 + 512].rearrange("(k p) t -> p k t", p=128)
                    load(xt[:], fm(xsrc), [], ["xt"])
                    load(ht[:], fm(HT), [], ["ht"])
                    load(va_[:], fm(VAT), [], ["va_"])
                    load(yb_[:], fm(YBT), [], ["yb_"])
                    load(at_[:], fm(ATT), [], ["at_"])
                    for cg in range(2):
                        for br, (wsrc, nkc, rt_, rk) in enumerate((("wa", 4, va_, "va_"), ("wb", 8, yb_, "yb_"), ("wc", 4, at_, "at_"))):
                            wy, wyk = wloadb(WB[l % 2][wsrc][:, cg * 512:(cg + 1) * 512], nkc, 512)
                            c0 = C_G + br * 1024 + cg * 512
                            wg, wgk = wloadb(WB[l % 2]["win"][:, c0:c0 + 512], 8, 512)
                            for oc in range(4):
                                psYy, pky = pring.next()
                                for kc in range(nkc):
                                    mm(psYy[:], wy[:, kc, oc * 128:(oc + 1) * 128], rt_[:, kc, :], kc == 0, kc == nkc - 1, [wyk, rk], [pky])
                                psG, pkg = pring.next()
                                for kc in range(8):
                                    mm(psG[:], wg[:, kc, oc * 128:(oc + 1) * 128], ht[:, kc, :], kc == 0, kc == 7, [wgk, "ht"], [pkg])
                                sg, sgk = sigr.next()
                                act(sg[:], psG[:], AF.Sigmoid, [pkg], [sgk])
                                if br == 0:
                                    tt("dve", macc[:, oc, :], psYy[:], sg[:], ALU.mult, [pky, sgk], [("macc", oc)])
                                else:
                                    tm, tmk = tmr.next()
                                    tt("dve", tm[:], psYy[:], sg[:], ALU.mult, [pky, sgk], [tmk])
                                    if br == 1:
                                        tt("dve", macc[:, oc, :], macc[:, oc, :], tm[:], ALU.add, [("macc", oc), tmk], [("macc", oc)])
                                    else:
                                        tt("dve", merged[:, cg * 4 + oc, :], macc[:, oc, :], tm[:], ALU.add, [("macc", oc), tmk], ["merged"])
                    for cg in range(2):
                        wo, wok = wloadb(WB[l % 2]["wo"][:, cg * 512:(cg + 1) * 512], 8, 512)
                        for oc in range(4):
                            ps, pk = pring.next()
                            for kc in range(8):
                                mm(ps[:], wo[:, kc, oc * 128:(oc + 1) * 128], merged[:, kc, :], kc == 0, kc == 7, [wok, "merged"], [pk])
                            o8 = cg * 4 + oc
                            stt("dve", xt[:, o8, :], ps[:], modcol(l, 2, o8, r), xt[:, o8, :], ALU.mult, ALU.add, [pk, "MOD", "xt"], ["xt"])
                    norm_mod((sqt, rst, lnt, tmpf), xt, h2, lambda kc: A2[:, l, kc, r:r + 1], lambda kc: modcol(l, 3, kc, r), "xt", "h2")
                    for fg in range(6):
                        ncol = 512 if fg < 5 else 256
                        w1, w1k = wloadb(WB[l % 2]["wf1"][:, fg * 512:fg * 512 + ncol], 8, ncol)
                        w3, w3k = wloadb(WB[l % 2]["wf3"][:, fg * 512:fg * 512 + ncol], 8, ncol)
                        for oc in range(ncol // 128):
                            j = fg * 4 + oc
                            psA, pka = pring.next()
                            for kc in range(8):
                                mm(psA[:], w1[:, kc, oc * 128:(oc + 1) * 128], h2[:, kc, :], kc == 0, kc == 7, [w1k, "h2"], [pka])
                            psB, pkb = pring.next()
                            for kc in range(8):
                                mm(psB[:], w3[:, kc, oc * 128:(oc + 1) * 128], h2[:, kc, :], kc == 0, kc == 7, [w3k, "h2"], [pkb])
                            sa, sak = sar.next()
                            act(sa[:], psA[:], AF.Silu, [pka], [sak])
                            tt("dve", gt[:, j, :], sa[:], psB[:], ALU.mult, [sak, pkb], [("gt", j)])
                    for cg in range(4):
                        w2, w2k = wloadb(WB[l % 2]["wf2"][:, cg * 256:(cg + 1) * 256], 22, 256)
                        for oc in range(2):
                            ps, pk = pring.next()
                            for j in range(22):
                                mm(ps[:], w2[:, j, oc * 128:(oc + 1) * 128], gt[:, j, :], j == 0, j == 21, [w2k, ("gt", j)], [pk])
                            o8 = cg * 2 + oc
                            stt("dve", xt[:, o8, :], ps[:], modcol(l, 5, o8, r), xt[:, o8, :], ALU.mult, ALU.add, [pk, "MOD", "xt"], ["xt"])
                    if not last:
                        store(fm(XT), xt[:], ["xt"], [])
                    else:
                        norm_mod((sqt, rst, lnt, tmpf), xt, yo, lambda kc: fnw[:, kc:kc + 1], None, "xt", "yo")
                        store(fm(yT), yo[:], ["yo"], [])
                    if stop == "p3" and "XT" in dump:
                        store(fm(XT), xt[:], ["xt"], [])
                S.flush()
        S.flush(final=True)
    return nc


def rope_tables():
    n_rows = TS // 64
    row = np.repeat(np.arange(n_rows, dtype=np.float32), 64)
    col = np.tile(np.arange(64, dtype=np.float32), n_rows)
    inv = (np.float32(10000.0) ** (-np.arange(8, dtype=np.float32) / np.float32(8))).astype(np.float32)
    ang = np.concatenate([row[:, None] * inv, col[:, None] * inv], axis=-1).astype(np.float32)
    cos, sin = np.cos(ang).astype(np.float32), np.sin(ang).astype(np.float32)
    C = np.concatenate([cos, cos], axis=1).T
    Sg = np.concatenate([-sin, sin], axis=1).T
    return np.ascontiguousarray(C), np.ascontiguousarray(Sg)


def make_in_maps(inp, nl=DEPTH):
    f = lambda k: np.asarray(inp[k], np.float32)
    ropeC, ropeS = rope_tables()
    k = np.arange(128)
    triF = (k[:, None] <= k[None, :]).astype(np.float32)
    triB = (k[:, None] >= k[None, :]).astype(np.float32)
    cst = np.concatenate([triF, triB, np.ones((128, 128), np.float32), np.eye(128, dtype=np.float32)], axis=1)
    prm = pack_params(inp)
    w_in = f("w_in")
    w_kr2 = np.zeros((DEPTH, D, 2, 96), np.float32)
    krc = w_in[:, :, C_KR:C_KR + 32]
    w_kr2[:, :, 0, 64:96] = krc
    w_kr2[:, :, 1, 64:80] = krc[:, :, 16:32]
    w_kr2[:, :, 1, 80:96] = krc[:, :, 0:16]
    wuq = f("w_uq").reshape(DEPTH, 256, 8, 96)
    w_uq2 = np.zeros((DEPTH, 256, 2, 8, 96), np.float32)
    w_uq2[:, :, 0] = wuq
    w_uq2[:, :, 1, :, 0:64] = wuq[..., 0:64]
    w_uq2[:, :, 1, :, 64:80] = wuq[..., 80:96]
    w_uq2[:, :, 1, :, 80:96] = wuq[..., 64:80]
    shared = {
        "ropeC": ropeC, "ropeS": ropeS, "cst": cst, "prm": prm, "w_in": w_in, "w_kr2": w_kr2, "w_uq2": w_uq2,
        "w_ukv": f("w_ukv"), "w_a_out": f("w_a_out"), "w_b_out": f("w_b_out"), "w_c_out": f("w_c_out"),
        "w_o": f("w_o"), "w_ada": f("w_ada"), "w_ff1": f("w_ff1"), "w_ff3": f("w_ff3"), "w_ff2": f("w_ff2"),
    }
    for k in ("w_in", "w_kr2", "w_uq2", "w_ukv", "w_a_out", "w_b_out", "w_c_out", "w_o", "w_ada", "w_ff1", "w_ff3", "w_ff2"):
        shared[k] = np.ascontiguousarray(shared[k][:nl])
    xs, xp, c, cctx = f("x_sample"), f("x_prompt"), f("c"), f("c_ctx")
    cckv, ckr = f("cache_ckv"), f("cache_krope")
    sf, sbw = f("state_ssm_fwd"), f("state_ssm_bwd")
    maps = []
    for r in range(8):
        bs = r % 4
        xT0 = np.concatenate([xs[bs].T, xp[2 * r].T, xp[2 * r + 1].T], axis=1)
        cd = np.stack([c[bs], cctx], axis=1)
        cd = cd.reshape(8, 128, 2).transpose(1, 0, 2)
        h0 = np.stack([sf[bs], sbw[bs]], axis=1)
        h0 = h0.transpose(0, 1, 4, 2, 3).reshape(DEPTH, 2, 128, 1024)
        m = dict(shared)
        m.update({
            "xT0": np.ascontiguousarray(xT0), "cond": np.ascontiguousarray(cd),
            "cckvT": np.ascontiguousarray(cckv[bs].transpose(0, 2, 1)),
            "ckrT": np.ascontiguousarray(ckr[bs].transpose(0, 2, 1)),
            "h0": np.ascontiguousarray(h0),
        })
        maps.append(m)
    return maps


_NC_CACHE = {}


def kernel(**inputs):
    maps = make_in_maps(inputs)
    if "nc" not in _NC_CACHE:
        _NC_CACHE["nc"] = build_program()
    nc = _NC_CACHE["nc"]
    res = run_bass_kernel_spmd(nc, maps, core_ids=list(range(8)))
    R = res.results
    y_prompt = np.zeros((16, 256, D), np.float32)
    y_sample = np.zeros((4, TS, D), np.float32)
    new_ckv = np.zeros((16, DEPTH, 256, 256), np.float32)
    new_kr = np.zeros((16, DEPTH, 256, 32), np.float32)
    new_f = np.zeros((16, DEPTH, 16, 64, 128), np.float32)
    new_b = np.zeros((16, DEPTH, 16, 64, 128), np.float32)
    for r in range(8):
        yT = R[r]["yT"]
        if r < 4:
            y_sample[r] = yT[:, :TS].T
        for s in range(2):
            q = 2 * r + s
            y_prompt[q] = yT[:, TS + s * 256:TS + (s + 1) * 256].T
            new_ckv[q] = R[r]["nckvT"][:, :, s * 256:(s + 1) * 256].transpose(0, 2, 1)
            new_kr[q] = R[r]["nkrT"][:, :, s * 256:(s + 1) * 256].transpose(0, 2, 1)
            st = R[r]["nssm"][:, :, s].reshape(DEPTH, 2, 128, 16, 64).transpose(0, 1, 3, 4, 2)
            new_f[q] = st[:, 0]
            new_b[q] = st[:, 1]
    return (y_prompt, y_sample, new_ckv, new_kr, new_f, new_b)
```

```python
import contextlib
import math
import os

import numpy as np
import concourse.bass as bass
import concourse.mybir as mybir
from concourse.bass_utils import run_bass_kernel_spmd

F32 = mybir.dt.float32
BF16 = mybir.dt.bfloat16
AF = mybir.ActivationFunctionType
ALU = mybir.AluOpType

D = 1024
DEPTH = 4
TS = 4096
TPR = 512
T = TS + TPR
NB = T // 512
NCH = T // 128
EPS = 1e-6
IN_COLS = 7728
FF = 2816
C_AX, C_AB, C_AC, C_Z, C_XBC, C_DT, C_CQ, C_CKV, C_KR, C_G = 0, 512, 1024, 1536, 2560, 4096, 4112, 4368, 4624, 4656
SCALE = 1.0 / math.sqrt(96.0)
SEQS = [(0, 4096, True), (4096, 256, False), (4352, 256, False)]

ENGS = ("pe", "act", "dve", "pool", "sp")


class Sched:
    def __init__(self, nc, st, n_dma_sems=48):
        self.nc = nc
        self.n_dma_sems = n_dma_sems
        self.esem = {e: st.enter_context(nc.semaphore("s_" + e)) for e in ENGS if e != "sp"}
        self.dsem = [st.enter_context(nc.semaphore("d%d" % i)) for i in range(n_dma_sems)]
        self.dummy = st.enter_context(nc.sbuf_tensor("bar_dummy", [128, 2], F32))
        self.ecount = {e: 0 for e in ENGS}
        self.dma_val = [0] * n_dma_sems
        self.dma_last = [None] * n_dma_sems
        self.dma_rr = {"sw": 0, "hw": 0}
        self.n_sw = 16
        self.waited = {e: {} for e in ENGS}
        self.barrier_tok = None
        self.need_barrier = {e: False for e in ENGS}
        self._reset()

    def _reset(self):
        self.ops = {e: [] for e in ENGS}
        self.last_write = {}
        self.readers = {}
        self.dma_toks = []

    def _record(self, eng, fn, reads, writes, dma, extra_deps=()):
        deps = set(extra_deps)
        for r in reads:
            w = self.last_write.get(r)
            if w is not None:
                deps.add(w)
        for r in writes:
            w = self.last_write.get(r)
            if w is not None:
                deps.add(w)
            for rd in self.readers.get(r, ()):
                deps.add(rd)
        idx = len(self.ops[eng])
        if dma:
            if eng == "pool":
                si = self.dma_rr["sw"]
                self.dma_rr["sw"] = (si + 1) % self.n_sw
            else:
                si = self.n_sw + self.dma_rr["hw"]
                self.dma_rr["hw"] = (self.dma_rr["hw"] + 1) % (self.n_dma_sems - self.n_sw)
            prev = self.dma_last[si]
            if prev is not None:
                deps.add(prev)
            self.dma_val[si] += 16
            tok = ("dma", si, self.dma_val[si])
            self.dma_last[si] = tok
            self.dma_toks.append(tok)
        else:
            tok = ("eng", eng, idx)
        deps = {d for d in deps if not (d[0] == "eng" and d[1] == "pe" and eng == "pe")}
        deps.discard(tok)
        if self.need_barrier[eng] and self.barrier_tok is not None:
            deps.add(self.barrier_tok)
            self.need_barrier[eng] = False
        self.ops[eng].append(dict(fn=fn, deps=deps, flag=False, tok=tok))
        for r in reads:
            lst = self.readers.setdefault(r, [])
            if tok[0] == "eng":
                lst[:] = [t for t in lst if not (t[0] == "eng" and t[1] == eng)]
            lst.append(tok)
        for r in writes:
            self.last_write[r] = tok
            self.readers[r] = []
        return tok

    def op(self, eng, fn, reads=(), writes=(), extra_deps=()):
        return self._record(eng, fn, reads, writes, False, extra_deps)

    def dma(self, eng, fn, reads=(), writes=(), extra_deps=()):
        return self._record(eng, fn, reads, writes, True, extra_deps)

    def flush(self, final=False):
        nc = self.nc
        deps = set()
        for e in ENGS:
            if e == "sp":
                continue
            for i in range(len(self.ops[e]) - 1, -1, -1):
                if self.ops[e][i]["tok"][0] == "eng":
                    deps.add(self.ops[e][i]["tok"])
                    break
        latest = {}
        for t in self.dma_toks:
            if t[1] not in latest or latest[t[1]][2] < t[2]:
                latest[t[1]] = t
        deps.update(latest.values())
        dummy = self.dummy
        coll = self._record("dve", lambda e: e.memset(dummy[:, 0:1], 0.0), (), (), False, deps)
        for e in ENGS:
            for o in self.ops[e]:
                for d in o["deps"]:
                    if d[0] == "eng":
                        self.ops[d[1]][d[2]]["flag"] = True
        self.ops["dve"][coll[2]]["flag"] = True
        cnt = {}
        for e in ENGS:
            c = self.ecount[e]
            arr = []
            for o in self.ops[e]:
                if o["flag"]:
                    c += 1
                arr.append(c)
            cnt[e] = arr
        esem, dsem = self.esem, self.dsem

        def resolve(d):
            if d[0] == "eng":
                return esem[d[1]], cnt[d[1]][d[2]]
            if d[0] == "abs":
                return d[1], d[2]
            return dsem[d[1]], d[2]

        coll_abs = ("abs", esem["dve"], cnt["dve"][coll[2]])

        def run(ename, eng):
            waited = self.waited[ename]
            for o in self.ops[ename]:
                need = {}
                for d in o["deps"]:
                    s, v = resolve(d)
                    k = id(s)
                    if waited.get(k, 0) >= v:
                        continue
                    if k not in need or need[k][1] < v:
                        need[k] = (s, v)
                for k, (s, v) in need.items():
                    eng.wait_ge(s, v)
                    waited[k] = v
                ins = o["fn"](eng)
                if o["tok"][0] == "dma":
                    ins.then_inc(dsem[o["tok"][1]], 16)
                elif o["flag"]:
                    ins.then_inc(esem[ename], 1)
            if final and ename == "sp":
                s, v = resolve(coll_abs)
                eng.wait_ge(s, v)

        if os.environ.get("KDBG_SIM"):
            self._simulate(resolve, coll_abs, final)

        with nc.Block() as block:
            @block.sync
            def _(sync):
                run("sp", sync)

            @block.tensor
            def _(tensor):
                run("pe", tensor)

            @block.scalar
            def _(scalar):
                run("act", scalar)

            @block.vector
            def _(vector):
                run("dve", vector)

            @block.gpsimd
            def _(gpsimd):
                run("pool", gpsimd)

        for e in ENGS:
            if cnt[e]:
                self.ecount[e] = cnt[e][-1]
        self.barrier_tok = coll_abs
        self.need_barrier = {e: True for e in ENGS}
        self._reset()


def _sched_simulate(self, resolve, coll_abs, final):
    if not hasattr(self, "sim_sem"):
        self.sim_sem = {}
    sem = self.sim_sem
    pos = {e: 0 for e in ENGS}
    progress = True
    while progress:
        progress = False
        for e in ENGS:
            while pos[e] < len(self.ops[e]):
                o = self.ops[e][pos[e]]
                ok = True
                for d in o["deps"]:
                    s_, v = resolve(d)
                    if sem.get(id(s_), 0) < v:
                        ok = False
                        break
                if not ok:
                    break
                if o["tok"][0] == "dma":
                    k = id(self.dsem[o["tok"][1]])
                    sem[k] = sem.get(k, 0) + 16
                elif o["flag"]:
                    k = id(self.esem[e])
                    sem[k] = sem.get(k, 0) + 1
                pos[e] += 1
                progress = True
    stuck = {e: (pos[e], len(self.ops[e])) for e in ENGS if pos[e] < len(self.ops[e])}
    if stuck:
        print("SCHED DEADLOCK:", stuck)
        for e in stuck:
            o = self.ops[e][pos[e]]
            print("  ", e, "op", pos[e], "tok", o["tok"], "deps", [(d, resolve(d)[1], sem.get(id(resolve(d)[0]), 0)) for d in o["deps"]])
        raise RuntimeError("sched deadlock")
    else:
        print("sched sim ok:", {e: len(self.ops[e]) for e in ENGS})


Sched._simulate = _sched_simulate


class Ring:
    def __init__(self, tiles, name):
        self.tiles = tiles
        self.name = name
        self.i = 0

    def next(self):
        i = self.i
        self.i = (self.i + 1) % len(self.tiles)
        return self.tiles[i], (self.name, i)


PRM_LAYOUT = [("n1w", 4 * 8), ("n2w", 4 * 8), ("fnw", 8), ("snw", 4 * 8), ("qnw", 4 * 2), ("kvnw", 4 * 2),
              ("aconv", 4 * 3 * 4), ("sconvw", 4 * 3 * 12), ("sconvb", 4 * 12), ("dexp", 4 * 2 * 8),
              ("alog", 4 * 2 * 16), ("dtb", 4 * 2 * 16), ("bada", 4 * 48)]
PRM_OFF = {}
_o = 0
for _n, _s in PRM_LAYOUT:
    PRM_OFF[_n] = (_o, _s)
    _o += _s
NPRM = _o


def _pc(v):
    v = np.asarray(v, np.float32)
    lead = v.shape[:-1]
    c = v.shape[-1] // 128
    v = v.reshape(lead + (c, 128))
    v = np.moveaxis(v, -1, 0)
    return np.ascontiguousarray(v).reshape(128, -1)


def pack_params(inp):
    parts = {
        "n1w": _pc(inp["norm1_w"]), "n2w": _pc(inp["norm2_w"]), "fnw": _pc(inp["final_norm_w"]),
        "snw": _pc(inp["ssm_norm_w"]), "qnw": _pc(inp["q_norm_w"]), "kvnw": _pc(inp["kv_norm_w"]),
        "aconv": _pc(inp["a_conv_w"]), "sconvw": _pc(inp["ssm_conv_w"]), "sconvb": _pc(inp["ssm_conv_b"]),
        "dexp": _pc(np.repeat(np.asarray(inp["ssm_d"], np.float32), 64, axis=-1)),
        "alog": np.broadcast_to(np.asarray(inp["ssm_a_log"], np.float32).reshape(1, -1), (128, 128)),
        "dtb": np.broadcast_to(np.asarray(inp["ssm_dt_bias"], np.float32).reshape(1, -1), (128, 128)),
        "bada": _pc(inp["b_ada"]),
    }
    out = np.zeros((128, NPRM), np.float32)
    for n, (o, s) in PRM_OFF.items():
        assert parts[n].shape == (128, s), (n, parts[n].shape, s)
        out[:, o:o + s] = parts[n]
    return out


def build_program(stop=None, dump=(), nl=DEPTH):
    nc = bass.Bass("TRN2", target_bir_lowering=False)

    def din(name, shape, dt=F32):
        return nc.dram_tensor(name, list(shape), dt, kind="ExternalInput").ap()

    def dout(name, shape, dt=F32):
        return nc.dram_tensor(name, list(shape), dt, kind="ExternalOutput").ap()

    def dscr(name, shape, dt):
        kind = "ExternalOutput" if name in dump else "Internal"
        return nc.dram_tensor(name, list(shape), dt, kind=kind).ap()

    xT0 = din("xT0", [D, T])
    cond = din("cond", [128, 8, 2])
    cckvT = din("cckvT", [DEPTH, 256, 512])
    ckrT = din("ckrT", [DEPTH, 32, 512])
    h0 = din("h0", [DEPTH, 2, 128, 1024])
    ropeC = din("ropeC", [32, TS])
    ropeS = din("ropeS", [32, TS])
    cst = din("cst", [128, 2560])
    prm_d = din("prm", [128, NPRM])
    w_in = din("w_in", [nl, D, IN_COLS])
    w_kr2 = din("w_kr2", [nl, D, 2, 96])
    w_uq2 = din("w_uq2", [nl, 256, 2, 8, 96])
    w_ukv = din("w_ukv", [nl, 256, 1024])
    w_a_out = din("w_a_out", [nl, 512, D])
    w_b_out = din("w_b_out", [nl, D, D])
    w_c_out = din("w_c_out", [nl, 512, D])
    w_o = din("w_o", [nl, D, D])
    w_ada = din("w_ada", [nl, D, 6 * D])
    w_ff1 = din("w_ff1", [nl, D, FF])
    w_ff3 = din("w_ff3", [nl, D, FF])
    w_ff2 = din("w_ff2", [nl, FF, D])

    yT = dout("yT", [D, T])
    nckvT = dout("nckvT", [DEPTH, 256, 512])
    nkrT = dout("nkrT", [DEPTH, 32, 512])
    nssm = dout("nssm", [DEPTH, 2, 2, 128, 1024])

    XT = dscr("XT", [D, T], F32)
    HT = dscr("HT", [D, T], BF16)
    UT = dscr("UT", [512, T], BF16)
    ABT = dscr("ABT", [512, T], BF16)
    ZT = dscr("ZT", [D, T], BF16)
    XBCT = dscr("XBCT", [1536, T], BF16)
    DTT = dscr("DTT", [T, 16], F32)
    QT = dscr("QT", [8, 96, T], BF16)
    CKVT = dscr("CKVT", [256, T], BF16)
    KRT = dscr("KRT", [32, T], BF16)
    VAT = dscr("VAT", [512, T], BF16)
    XSC = dscr("XSC", [1536, T], BF16)
    XTOK = dscr("XTOK", [T, 1024], BF16)
    BTOK = dscr("BTOK", [T, 256], BF16)
    HENT = dscr("HENT", [2, NCH, 128, 1024], BF16)
    YBT = dscr("YBT", [D, T], BF16)
    ATT = dscr("ATT", [512, T], BF16)

    WB = []
    for si in range(2):
        WB.append(dict(
            win=dscr("WBwin%d" % si, [D, IN_COLS], BF16), wa=dscr("WBwa%d" % si, [512, D], BF16),
            wb=dscr("WBwb%d" % si, [D, D], BF16), wc=dscr("WBwc%d" % si, [512, D], BF16),
            wo=dscr("WBwo%d" % si, [D, D], BF16), wf1=dscr("WBwf1%d" % si, [D, FF], BF16),
            wf3=dscr("WBwf3%d" % si, [D, FF], BF16), wf2=dscr("WBwf2%d" % si, [FF, D], BF16)))

    with contextlib.ExitStack() as gst:
        S = Sched(nc, gst)

        _uid = [0]

        def sb(st, name, shape, dt):
            _uid[0] += 1
            return st.enter_context(nc.sbuf_tensor("sb%d_%s" % (_uid[0], name), list(shape), dt))

        prm = sb(gst, "prm", [128, NPRM], F32)
        cstt = sb(gst, "cstt", [128, 2560], F32)
        identb = sb(gst, "identb", [128, 128], BF16)
        MOD = sb(gst, "MOD", [128, DEPTH, 48, 2], F32)
        A1 = sb(gst, "A1", [128, DEPTH, 8, 2], F32)
        A2 = sb(gst, "A2", [128, DEPTH, 8, 2], F32)
        dsum = sb(gst, "dsum", [128, DEPTH, 8], F32)
        wring = Ring([sb(gst, "wr%d" % i, [128, 6144], BF16) for i in range(4)], "wr")
        uqw = sb(gst, "uqw", [128, 2, 2 * 8 * 96], BF16)
        krw = sb(gst, "krw", [128, 8, 2 * 96], BF16)
        dtw = sb(gst, "dtw", [128, 8, 16], BF16)
        ukvw = sb(gst, "ukvw", [128, 2, 1024], BF16)
        psum = [gst.enter_context(nc.psum_tensor("ps%d" % i, [128, 512], F32)) for i in range(7)]
        psbT = gst.enter_context(nc.psum_tensor("psbT", [128, 1024], BF16))
        pring = Ring(psum[0:5], "ps")
        plong = Ring(psum[5:7], "pl")
        triF = cstt[:, 0:128]
        triB = cstt[:, 128:256]
        ones = cstt[:, 256:384]
        identF = cstt[:, 384:512]
        sel = cstt[0:16, 512:2560].rearrange("p (h j) -> p h j", h=16)

        def P(name, l=None):
            o, s = PRM_OFF[name]
            v = prm[:, o:o + s]
            return v

        def pv(name, pattern, **kw):
            o, s = PRM_OFF[name]
            return prm[:, o:o + s].rearrange(pattern, **kw)

        n1w = pv("n1w", "p (l c) -> p l c", l=4)
        n2w = pv("n2w", "p (l c) -> p l c", l=4)
        fnw = P("fnw")
        snw = pv("snw", "p (l c) -> p l c", l=4)
        qnw = pv("qnw", "p (l c) -> p l c", l=4)
        kvnw = pv("kvnw", "p (l c) -> p l c", l=4)
        aconv = pv("aconv", "p (l k c) -> p l k c", l=4, k=3)
        sconvw = pv("sconvw", "p (l k c) -> p l k c", l=4, k=3)
        sconvb = pv("sconvb", "p (l c) -> p l c", l=4)
        dexp = pv("dexp", "p (l d c) -> p l d c", l=4, d=2)
        alog = pv("alog", "p (l d h) -> p l d h", l=4, d=2)
        dtb = pv("dtb", "p (l d h) -> p l d h", l=4, d=2)
        bada = pv("bada", "p (l c) -> p l c", l=4)

        def mm(out, lhsT, rhs, start, stop, rd, wr):
            S.op("pe", lambda e: e.matmul(out, lhsT=lhsT, rhs=rhs, start=start, stop=stop), reads=rd, writes=wr)

        def act(out, in_, func, rd, wr, **kw):
            S.op("act", lambda e: e.activation(out=out, in_=in_, func=func, **kw), reads=rd, writes=wr)

        def tt(eng, out, in0, in1, op, rd, wr):
            S.op(eng, lambda e: e.tensor_tensor(out=out, in0=in0, in1=in1, op=op), reads=rd, writes=wr)

        def ts(eng, out, in0, s1, s2, op0, op1, rd, wr):
            if op1 is None:
                S.op(eng, lambda e: e.tensor_scalar(out=out, in0=in0, scalar1=s1, scalar2=None, op0=op0), reads=rd, writes=wr)
            else:
                S.op(eng, lambda e: e.tensor_scalar(out=out, in0=in0, scalar1=s1, scalar2=s2, op0=op0, op1=op1), reads=rd, writes=wr)

        def stt(eng, out, in0, scalar, in1, op0, op1, rd, wr):
            S.op(eng, lambda e: e.scalar_tensor_tensor(out=out, in0=in0, scalar=scalar, in1=in1, op0=op0, op1=op1), reads=rd, writes=wr)

        def cp(eng, out, in_, rd, wr):
            if eng == "act":
                act(out, in_, AF.Copy, rd, wr)
            else:
                S.op(eng, lambda e: e.tensor_copy(out=out, in_=in_), reads=rd, writes=wr)

        def load(out, in_, rd, wr, eng="sp"):
            return S.dma(eng, lambda e: e.dma_start(out=out, in_=in_), reads=rd, writes=wr)

        def store(out, in_, rd, wr, eng="act"):
            return S.dma(eng, lambda e: e.dma_start(out=out, in_=in_), reads=rd, writes=wr)

        def wload(src2d, n_kc, ncols):
            slot, key = wring.next()
            view = slot[:, 0:n_kc * ncols].rearrange("p (k n) -> p k n", k=n_kc)
            S.dma("pool", lambda e: e.dma_start(out=view, in_=src2d.rearrange("(k p) n -> p k n", p=128)), reads=[], writes=[key])
            return view, key

        def wloadb(src2d, n_kc, ncols):
            slot, key = wring.next()
            view = slot[:, 0:n_kc * ncols].rearrange("p (k n) -> p k n", k=n_kc)
            S.dma("pool", lambda e: e.dma_start(out=view, in_=src2d.rearrange("(k p) n -> p k n", p=128)), reads=[], writes=[key])
            return view, key

        def convert_weights(l, stage_ring):
            wb = WB[l % 2]
            jobs = []
            for c0 in list(range(0, 4096, 512)) + [C_CQ] + list(range(C_G, IN_COLS, 512)):
                jobs.append((w_in[l, :, c0:c0 + 512], wb["win"][:, c0:c0 + 512], 8, 512))
            for c0 in (0, 512):
                jobs.append((w_a_out[l, :, c0:c0 + 512], wb["wa"][:, c0:c0 + 512], 4, 512))
                jobs.append((w_b_out[l, :, c0:c0 + 512], wb["wb"][:, c0:c0 + 512], 8, 512))
                jobs.append((w_c_out[l, :, c0:c0 + 512], wb["wc"][:, c0:c0 + 512], 4, 512))
                jobs.append((w_o[l, :, c0:c0 + 512], wb["wo"][:, c0:c0 + 512], 8, 512))
            for fg in range(6):
                ncol = 512 if fg < 5 else 256
                jobs.append((w_ff1[l, :, fg * 512:fg * 512 + ncol], wb["wf1"][:, fg * 512:fg * 512 + ncol], 8, ncol))
                jobs.append((w_ff3[l, :, fg * 512:fg * 512 + ncol], wb["wf3"][:, fg * 512:fg * 512 + ncol], 8, ncol))
            for cg in range(4):
                jobs.append((w_ff2[l, :, cg * 256:(cg + 1) * 256], wb["wf2"][:, cg * 256:(cg + 1) * 256], 22, 256))
            pend = []
            nst = len(stage_ring.tiles)

            def issue_store(item):
                view, key, dst, n_kc = item
                S.dma("pool", lambda e: e.dma_start(out=dst.rearrange("(k p) n -> p k n", p=128), in_=view), reads=[key], writes=[])

            for (src, dst, n_kc, ncols) in jobs:
                if len(pend) == nst:
                    issue_store(pend.pop(0))
                slot, key = stage_ring.next()
                view = slot[:, 0:n_kc * ncols].rearrange("p (k n) -> p k n", k=n_kc)
                S.dma("pool", lambda e, view=view, src=src: e.dma_start(out=view, in_=src.rearrange("(k p) n -> p k n", p=128)), reads=[], writes=[key])
                pend.append((view, key, dst, n_kc))
            while pend:
                issue_store(pend.pop(0))

        def rstd_from_ssq(ps_ap, n_feat, out_ap, tmp_ap, rd, wr, tmpkey):
            act(tmp_ap, ps_ap, AF.Ln, rd, [tmpkey], scale=1.0 / n_feat, bias=EPS)
            act(out_ap, tmp_ap, AF.Exp, [tmpkey], wr, scale=-0.5)

        with contextlib.ExitStack() as st:
            condt = sb(st, "condt", [128, 8, 2], F32)
            scb = sb(st, "scb", [128, 8, 16], BF16)
            stg0 = Ring([sb(st, "cvs%d" % i, [128, 6144], BF16) for i in range(3)], "cvs")
            convert_weights(0, stg0)
            load(prm[:], prm_d, [], ["prm"])
            load(cstt[:], cst, [], ["cst"])
            load(condt[:], cond, [], ["condt"])
            cp("dve", identb[:], cstt[:, 384:512], ["cst"], ["identb"])
            S.op("pool", lambda e: e.memset(scb[:], 0.0), writes=["scb"])
            act(scb[:, :, 0:2], condt[:], AF.Silu, ["condt", "scb"], ["scb"])
            _ncg = int(os.environ.get("KDBG_NCG", "12"))
            for l in range(nl):
                for cg in range(_ncg):
                    wv, wk = wload(w_ada[l, :, cg * 512:(cg + 1) * 512], 8, 512)
                    for oc in range(4):
                        ps, pk = pring.next()
                        for kc in range(8):
                            mm(ps[:, 0:16], wv[:, kc, oc * 128:(oc + 1) * 128], scb[:, kc, :], kc == 0, kc == 7, [wk, "scb"], [pk])
                        ci = cg * 4 + oc
                        act(MOD[:, l, ci, :], ps[:, 0:2], AF.Identity, [pk, "prm"], ["MOD"], bias=bada[:, l, ci:ci + 1])
            for l in range(nl):
                for (Ax, k0, nw) in ((A1, 8, n1w), (A2, 32, n2w)):
                    ts("dve", Ax[:, l, :, :], MOD[:, l, k0:k0 + 8, :], 1.0, None, ALU.add, None, ["MOD"], ["A"])
                    tt("dve", Ax[:, l, :, :], Ax[:, l, :, :], nw[:, l, :].unsqueeze(2).to_broadcast([128, 8, 2]), ALU.mult, ["A", "prm"], ["A"])
                tt("dve", dsum[:, l, :], dexp[:, l, 0, :], dexp[:, l, 1, :], ALU.add, ["prm"], ["dsum"])
            S.flush()
        if stop == "p0":
            dbgo = dout("dbg_mod", [128, DEPTH * 48 * 2])
            load(dbgo, MOD[:].rearrange("p l c r -> p (l c r)"), ["MOD"], ["dbgo"])
            S.flush(final=True)
            return nc

        def modcol(l, kind, kc, r):
            return MOD[:, l, kind * 8 + kc, r:r + 1]

        def norm_mod(st_tiles, xt, ht, Acol, Bcol, xkey, hkey):
            sqt, rst, lnt, tmpf = st_tiles
            ps, pk = pring.next()
            for kc in range(8):
                act(sqt[:, kc % 2, :], xt[:, kc, :], AF.Square, [xkey], [("sq", kc % 2)])
                mm(ps[:], ones, sqt[:, kc % 2, :], kc == 0, kc == 7, ["cst", ("sq", kc % 2)], [pk])
            rstd_from_ssq(ps[:], 1024.0, rst[:], lnt[:], [pk], ["rst"], "lnt")
            for kc in range(8):
                if Bcol is None:
                    stt("dve", ht[:, kc, :], xt[:, kc, :], Acol(kc), rst[:], ALU.mult, ALU.mult, [xkey, "rst", "A", "prm"], [hkey])
                else:
                    stt("dve", tmpf[:, kc % 2, :], xt[:, kc, :], Acol(kc), rst[:], ALU.mult, ALU.mult, [xkey, "rst", "A", "prm"], [("tmpf", kc % 2)])
                    act(ht[:, kc, :], tmpf[:, kc % 2, :], AF.Identity, [("tmpf", kc % 2), "MOD"], [hkey], bias=Bcol(kc))

        for l in range(nl):
            xsrc = xT0 if l == 0 else XT
            with contextlib.ExitStack() as st:
                xt = sb(st, "xt", [128, 8, 512], F32)
                ht = sb(st, "ht", [128, 8, 512], BF16)
                sqt = sb(st, "sqt", [128, 2, 512], F32)
                rst = sb(st, "rst", [128, 512], F32)
                lnt = sb(st, "lnt", [128, 512], F32)
                tmpf = sb(st, "tmpf", [128, 2, 512], F32)
                stg = Ring([sb(st, "stg%d" % i, [128, 4, 512], BF16) for i in range(3)], "stg")
                axt = sb(st, "axt", [128, 4, 512], BF16)
                cqf = sb(st, "cqf", [128, 4, 512], F32)
                nrf = sb(st, "nrf", [128, 2, 512], F32)
                cqn = sb(st, "cqn", [128, 2, 512], BF16)
                ckvn = sb(st, "ckvn", [128, 2, 512], BF16)
                qst = sb(st, "qst", [128, 8, 512], BF16)
                rC = sb(st, "rC", [128, 512], F32)
                rS = sb(st, "rS", [128, 512], F32)
                t1 = sb(st, "t1", [128, 512], F32)
                t2 = sb(st, "t2", [128, 512], F32)
                krt = sb(st, "krt", [128, 512], BF16)
                krf = sb(st, "krf", [128, 512], F32)
                dts = sb(st, "dts", [128, 4, 16], F32)
                S.dma("pool", lambda e, l=l: e.dma_start(out=uqw[:], in_=w_uq2[l].rearrange("(k p) v h r -> p k (v h r)", p=128)), writes=["uqw"])
                S.dma("pool", lambda e, l=l: e.dma_start(out=krw[:], in_=w_kr2[l].rearrange("(k p) v r -> p k (v r)", p=128)), writes=["krw"])
                S.dma("pool", lambda e, l=l: e.dma_start(out=dtw[:], in_=w_in[l, :, C_DT:C_DT + 16].rearrange("(k p) n -> p k n", p=128)), writes=["dtw"])
                S.dma("pool", lambda e, l=l: e.dma_start(out=ukvw[:], in_=w_ukv[l].rearrange("(k p) n -> p k n", p=128)), writes=["ukvw"])
                uq5 = uqw[:].rearrange("p k (v h r) -> p k v h r", v=2, h=8)
                kr4 = krw[:].rearrange("p k (v r) -> p k v r", v=2)
                _steps = os.environ.get("KDBG_P1", "groups,mla,q,kr,dt").split(",")
                _blks = [int(v) for v in os.environ.get("KDBG_BLKS", ",".join(str(i) for i in range(NB))).split(",")]
                for b in _blks:
                    r = 0 if b < 8 else 1
                    smp = b < 8
                    t0 = b * 512
                    load(xt[:], xsrc[:, t0:t0 + 512].rearrange("(k p) t -> p k t", p=128), [("XT", b)], ["xt"])
                    if smp:
                        load(rC[64:96, :], ropeC[:, t0:t0 + 512], [], ["rC"])
                        load(rS[64:96, :], ropeS[:, t0:t0 + 512], [], ["rS"])
                    norm_mod((sqt, rst, lnt, tmpf), xt, ht,
                             lambda kc: A1[:, l, kc, r:r + 1], lambda kc: modcol(l, 0, kc, r), "xt", "ht")
                    store(HT[:, t0:t0 + 512].rearrange("(k p) t -> p k t", p=128), ht[:], ["ht"], [("HT", b)])
                    groups = [("ax", C_AX), ("ab", C_AB), ("ac", C_AC), ("z", C_Z), ("z", C_Z + 512),
                              ("xbc", C_XBC), ("xbc", C_XBC + 512), ("xbc", C_XBC + 1024), ("cqkv", C_CQ)]
                    if "groups" not in _steps:
                        groups = []
                    for gi, (kind, c0) in enumerate(groups):
                        wv, wk = wloadb(WB[l % 2]["win"][:, c0:c0 + 512], 8, 512)
                        if kind in ("ab", "ac", "z", "xbc"):
                            sg, sk = stg.next()
                        for oc in range(4):
                            ps, pk = pring.next()
                            for kc in range(8):
                                mm(ps[:], wv[:, kc, oc * 128:(oc + 1) * 128], ht[:, kc, :], kc == 0, kc == 7, [wk, "ht"], [pk])
                            if kind == "ax":
                                cp("act", axt[:, oc, :], ps[:], [pk], ["axt"])
                            elif kind == "ab":
                                cp("act", sg[:, oc, :], ps[:], [pk], [sk])
                            elif kind == "ac":
                                tt("dve", sg[:, oc, :], ps[:], axt[:, oc, :], ALU.mult, [pk, "axt"], [sk])
                            elif kind == "z":
                                act(sg[:, oc, :], ps[:], AF.Silu, [pk], [sk])
                            elif kind == "xbc":
                                cp("act" if oc % 2 == 0 else "dve", sg[:, oc, :], ps[:], [pk], [sk])
                            else:
                                cp("act" if oc % 2 == 0 else "dve", cqf[:, oc, :], ps[:], [pk], [("cqf", oc // 2)])
                        if kind in ("ab", "ac", "z", "xbc"):
                            dst = {"ab": ABT, "ac": UT, "z": ZT, "xbc": XBCT}[kind]
                            r0 = c0 - {"ab": C_AB, "ac": C_AC, "z": C_Z, "xbc": C_XBC}[kind]
                            store(dst[r0:r0 + 512, t0:t0 + 512].rearrange("(k p) t -> p k t", p=128), sg[:], [sk], [(kind + "T", b, r0)])
                    for half, nw, dstb in ((0, qnw, cqn), (1, kvnw, ckvn)) if "mla" in _steps else ():
                        ps, pk = pring.next()
                        for j in range(2):
                            act(sqt[:, j, :], cqf[:, half * 2 + j, :], AF.Square, [("cqf", half)], [("sq", j)])
                            mm(ps[:], ones, sqt[:, j, :], j == 0, j == 1, ["cst", ("sq", j)], [pk])
                        rstd_from_ssq(ps[:], 256.0, rst[:], lnt[:], [pk], ["rst"], "lnt")
                        for j in range(2):
                            stt("dve", nrf[:, j, :], cqf[:, half * 2 + j, :], nw[:, l, j:j + 1], rst[:], ALU.mult, ALU.mult,
                                [("cqf", half), "rst", "prm"], [("nrf", j)])
                            cp("act", dstb[:, j, :], nrf[:, j, :], [("nrf", j)], [("lat", half)])
                        if half == 1:
                            store(CKVT[:, t0:t0 + 512].rearrange("(k p) t -> p k t", p=128), ckvn[:], [("lat", 1)], [("CKVT", b)])
                            if not smp:
                                store(nckvT[l].rearrange("(k p) t -> p k t", p=128), nrf[:], [("nrf", 0), ("nrf", 1)], [("nckv", l)])
                    for h in range(8) if "q" in _steps else ():
                        psn, pkn = pring.next()
                        for kc in range(2):
                            mm(psn[0:96, :], uq5[:, kc, 0, h, :], cqn[:, kc, :], kc == 0, kc == 1, ["uqw", ("lat", 0)], [pkn])
                        if smp:
                            pss, pks = pring.next()
                            for kc in range(2):
                                mm(pss[0:96, :], uq5[:, kc, 1, h, :], cqn[:, kc, :], kc == 0, kc == 1, ["uqw", ("lat", 0)], [pks])
                            cp("act", qst[0:64, h, :], psn[0:64, :], [pkn], [("qst", h)])
                            tt("dve", t1[64:96, :], psn[64:96, :], rC[64:96, :], ALU.mult, [pkn, "rC"], ["t1"])
                            tt("dve", t2[64:96, :], pss[64:96, :], rS[64:96, :], ALU.mult, [pks, "rS"], ["t2"])
                            tt("dve", qst[64:96, h, :], t1[64:96, :], t2[64:96, :], ALU.add, ["t1", "t2"], [("qst", h)])
                        else:
                            cp("act", qst[0:96, h, :], psn[0:96, :], [pkn], [("qst", h)])
                    if "q" in _steps:
                        store(QT[:, :, t0:t0 + 512].rearrange("h r t -> r h t"), qst[0:96, :, :], [("qst", h) for h in range(8)], [("QT", b)])
                    if "kr" not in _steps:
                        continue
                    psn, pkn = pring.next()
                    for kc in range(8):
                        mm(psn[0:96, :], kr4[:, kc, 0, :], ht[:, kc, :], kc == 0, kc == 7, ["krw", "ht"], [pkn])
                    if smp:
                        pss, pks = pring.next()
                        for kc in range(8):
                            mm(pss[0:96, :], kr4[:, kc, 1, :], ht[:, kc, :], kc == 0, kc == 7, ["krw", "ht"], [pks])
                        tt("dve", t1[64:96, :], psn[64:96, :], rC[64:96, :], ALU.mult, [pkn, "rC"], ["t1"])
                        tt("dve", t2[64:96, :], pss[64:96, :], rS[64:96, :], ALU.mult, [pks, "rS"], ["t2"])
                        tt("dve", krt[64:96, :], t1[64:96, :], t2[64:96, :], ALU.add, ["t1", "t2"], ["krt"])
                    else:
                        cp("dve", krf[64:96, :], psn[64:96, :], [pkn], ["krf"])
                        cp("act", krt[64:96, :], krf[64:96, :], ["krf"], ["krt"])
                        store(nkrT[l], krf[64:96, :], ["krf"], [("nkr", l)])
                    store(KRT[:, t0:t0 + 512], krt[64:96, :], ["krt"], [("KRT", b)])
                    if "dt" not in _steps:
                        continue
                    ps, pk = pring.next()
                    for tl in range(4):
                        for kc in range(8):
                            mm(ps[:, tl * 16:(tl + 1) * 16], ht[:, kc, tl * 128:(tl + 1) * 128], dtw[:, kc, :], kc == 0, kc == 7, ["dtw", "ht"], [pk])
                    cp("dve", dts[:].rearrange("p a h -> p (a h)"), ps[:, 0:64], [pk], ["dts"])
                    store(DTT[t0:t0 + 512, :].rearrange("(a p) h -> p a h", p=128), dts[:], ["dts"], [("DTT", b)])
                S.flush()
            if stop == "p1":
                break
            with contextlib.ExitStack() as st:
                ub = sb(st, "ub", [128, 4, 514], BF16)
                abt = sb(st, "abt", [128, 4, 512], BF16)
                acc = sb(st, "acc", [128, 2, 512], F32)
                vat = sb(st, "vat", [128, 4, 512], BF16)
                xb = sb(st, "xb", [128, 12, 514], BF16)
                xsc = sb(st, "xsc", [128, 12, 512], BF16)
                xtk = sb(st, "xtk", [128, 1024], BF16)
                btk = sb(st, "btk", [128, 256], BF16)
                segs = [(b * 512, 512, 0, TS) for b in range(8)] + [(TS, 256, TS, TS + 256), (TS + 256, 256, TS + 256, T)]
                for (t0, n, s0, s1) in segs:
                    lo, hi = max(t0 - 1, s0), min(t0 + n + 1, s1)
                    off = lo - (t0 - 1)
                    if lo == t0:
                        S.op("pool", lambda e: e.memset(ub[:, :, 0:1], 0.0), writes=["ub"])
                        S.op("pool", lambda e: e.memset(xb[:, :, 0:1], 0.0), writes=["xb"])
                    if hi == t0 + n:
                        S.op("pool", lambda e, n=n: e.memset(ub[:, :, n + 1:n + 2], 0.0), writes=["ub"])
                        S.op("pool", lambda e, n=n: e.memset(xb[:, :, n + 1:n + 2], 0.0), writes=["xb"])
                    load(ub[:, :, off:off + hi - lo], UT[:, lo:hi].rearrange("(k p) t -> p k t", p=128), [], ["ub"])
                    load(abt[:, :, 0:n], ABT[:, t0:t0 + n].rearrange("(k p) t -> p k t", p=128), [], ["abt"])
                    load(xb[:, :, off:off + hi - lo], XBCT[:, lo:hi].rearrange("(k p) t -> p k t", p=128), [], ["xb"])
                    for c in range(16):
                        src, cw, ci, skey = (ub, aconv, c, "ub") if c < 4 else (xb, sconvw, c - 4, "xb")
                        a = acc[:, c % 2, 0:n]
                        ak = ("acc", c % 2)
                        ts("dve", a, src[:, ci, 1:n + 1], cw[:, l, 1, ci:ci + 1], None, ALU.mult, None, [skey, "prm"], [ak])
                        stt("dve", a, src[:, ci, 0:n], cw[:, l, 0, ci:ci + 1], a, ALU.mult, ALU.add, [skey, "prm", ak], [ak])
                        stt("dve", a, src[:, ci, 2:n + 2], cw[:, l, 2, ci:ci + 1], a, ALU.mult, ALU.add, [skey, "prm", ak], [ak])
                        if c < 4:
                            tt("dve", vat[:, ci, 0:n], a, abt[:, ci, 0:n], ALU.mult, [ak, "abt"], ["vat"])
                        else:
                            act(xsc[:, ci, 0:n], a, AF.Silu, [ak, "prm"], ["xsc"], bias=sconvb[:, l, ci:ci + 1])
                    store(VAT[:, t0:t0 + n].rearrange("(k p) t -> p k t", p=128), vat[:, :, 0:n], ["vat"], [])
                    store(XSC[:, t0:t0 + n].rearrange("(k p) t -> p k t", p=128), xsc[:, :, 0:n], ["xsc"], [])
                    for tl in range(n // 128):
                        for c in range(8):
                            S.op("pe", lambda e, c=c, tl=tl: e.transpose(psbT[:, c * 128:(c + 1) * 128], xsc[:, c, tl * 128:(tl + 1) * 128], identb[:]),
                                 reads=["xsc", "identb"], writes=["psbT"])
                        cp("act", xtk[:], psbT[:, 0:1024], ["psbT"], ["xtk"])
                        store(XTOK[t0 + tl * 128:t0 + (tl + 1) * 128, :], xtk[:], ["xtk"], [])
                        for c in range(2):
                            S.op("pe", lambda e, c=c, tl=tl: e.transpose(psbT[:, c * 128:(c + 1) * 128], xsc[:, 8 + c, tl * 128:(tl + 1) * 128], identb[:]),
                                 reads=["xsc", "identb"], writes=["psbT"])
                        cp("dve", btk[:], psbT[:, 0:256], ["psbT"], ["btk"])
                        store(BTOK[t0 + tl * 128:t0 + (tl + 1) * 128, :], btk[:], ["btk"], [])
                S.flush()
            if stop == "conv":
                break
            with contextlib.ExitStack() as st:
                dtr = sb(st, "dtr", [128, NCH, 16], F32)
                ea = sb(st, "ea", [128, 2, 16], F32)
                dtd = [sb(st, "dtd%d" % d, [128, NCH, 16], F32) for d in range(2)]
                dta = [sb(st, "dta%d" % d, [128, NCH, 16], F32) for d in range(2)]
                ctk = [sb(st, "ctk%d" % d, [128, NCH, 16], F32) for d in range(2)]
                tot = [sb(st, "tot%d" % d, [128, NCH, 16], F32) for d in range(2)]
                ted = [sb(st, "ted%d" % d, [128, NCH, 16], F32) for d in range(2)]
                cdc = [sb(st, "cdc%d" % d, [128, NCH, 16], F32) for d in range(2)]
                lndt = [sb(st, "lndt%d" % d, [128, NCH, 16], F32) for d in range(2)]
                tri = [triF, triB]
                load(dtr[:], DTT.rearrange("(c p) h -> p c h", p=128), [], ["dtr"])
                act(ea[:], alog[:, l, :, :], AF.Exp, ["prm"], ["ea"])
                for d in range(2):
                    tt("dve", dtd[d][:], dtr[:], dtb[:, l, d, :].unsqueeze(1).to_broadcast([128, NCH, 16]), ALU.add, ["dtr", "prm"], [("dtd", d)])
                    act(dtd[d][:], dtd[d][:], AF.Exp, [("dtd", d)], [("dtd", d)])
                    act(dtd[d][:], dtd[d][:], AF.Ln, [("dtd", d)], [("dtd", d)], bias=1.0)
                    act(lndt[d][:], dtd[d][:], AF.Ln, [("dtd", d)], [("lndt", d)])
                    stt("dve", dta[d][:], dtd[d][:], -1.0, ea[:, d, :].unsqueeze(1).to_broadcast([128, NCH, 16]), ALU.mult, ALU.mult,
                        [("dtd", d), "ea"], [("dta", d)])
                    flat = dta[d][:].rearrange("p c h -> p (c h)")
                    for (lhs, dstt, dk) in ((tri[d], ctk[d], "ctk"), (ones, tot[d], "tot")):
                        dflat = dstt[:].rearrange("p c h -> p (c h)")
                        for (a0, a1) in ((0, 512), (512, NCH * 16)):
                            ps, pk = pring.next()
                            mm(ps[:, 0:a1 - a0], lhs, flat[:, a0:a1], True, True, ["cst", ("dta", d)], [pk])
                            cp("dve", dflat[:, a0:a1], ps[:, 0:a1 - a0], [pk], [(dk, d)])
                    tt("dve", ted[d][:], tot[d][:], ctk[d][:], ALU.subtract, [("tot", d), ("ctk", d)], [("ted", d)])
                    act(ted[d][:], ted[d][:], AF.Exp, [("ted", d)], [("ted", d)])
                    tt("dve", ted[d][:], ted[d][:], dtd[d][:], ALU.mult, [("ted", d), ("dtd", d)], [("ted", d)])
                    act(cdc[d][:], tot[d][:], AF.Exp, [("tot", d)], [("cdc", d)])
                with contextlib.ExitStack() as st2:
                    St = sb(st2, "St", [128, 2, 512], F32)
                    hbr = Ring([sb(st2, "hb%d" % i, [128, 1024], BF16) for i in range(2)], "hb")
                    xkr = Ring([sb(st2, "xk%d" % i, [128, 1024], BF16) for i in range(2)], "xk")
                    bkr = Ring([sb(st2, "bk%d" % i, [128, 256], BF16) for i in range(2)], "bk")
                    xwr = Ring([sb(st2, "xw%d" % i, [128, 1024], BF16) for i in range(2)], "xw")
                    for d in range(2):
                        for si, (s0, slen, smp) in enumerate(SEQS):
                            nchk, c0 = slen // 128, s0 // 128
                            if smp:
                                load(St[:], h0[l, d].rearrange("n (g f) -> n g f", g=2), [], ["St"])
                            else:
                                S.op("pool", lambda e: e.memset(St[:], 0.0), writes=["St"])
                            order = range(nchk) if d == 0 else range(nchk - 1, -1, -1)
                            for ci in order:
                                c = c0 + ci
                                hb, hk = hbr.next()
                                cp("act", hb[:], St[:].rearrange("p g f -> p (g f)"), ["St"], [hk])
                                store(HENT[d, c], hb[:], [hk], [])
                                xk, xkk = xkr.next()
                                bk, bkk = bkr.next()
                                xw, xwk = xwr.next()
                                load(xk[:], XTOK[c * 128:(c + 1) * 128, :], [], [xkk])
                                load(bk[:], BTOK[c * 128:(c + 1) * 128, :], [], [bkk])
                                tt("dve", xw[:].rearrange("p (h q) -> p h q", h=16), xk[:].rearrange("p (h q) -> p h q", h=16),
                                   ted[d][:, c, :].unsqueeze(2).to_broadcast([128, 16, 64]), ALU.mult, [xkk, ("ted", d)], [xwk])
                                for g in range(2):
                                    ps, pk = pring.next()
                                    mm(ps[:], bk[:, g * 128:(g + 1) * 128], xw[:, g * 512:(g + 1) * 512], True, True, [bkk, xwk], [pk])
                                    sg3 = St[:, g, :].rearrange("p (h q) -> p h q", h=8)
                                    tt("dve", sg3, sg3, cdc[d][:, c, g * 8:(g + 1) * 8].unsqueeze(2).to_broadcast([128, 8, 64]), ALU.mult,
                                       ["St", ("cdc", d)], ["St"])
                                    tt("dve", St[:, g, :], St[:, g, :], ps[:], ALU.add, ["St", pk], ["St"])
                            if not smp:
                                store(nssm[l, d, si - 1], St[:].rearrange("p g f -> p (g f)"), ["St"], [])
                    S.flush()
                if stop == "ssd1":
                    break
                with contextlib.ExitStack() as st2:
                    xs3r = Ring([sb(st2, "xs3%d" % i, [128, 12, 128], BF16) for i in range(2)], "xs3")
                    xk2r = Ring([sb(st2, "xk2%d" % i, [128, 1024], BF16) for i in range(2)], "xk2")
                    her = [Ring([sb(st2, "he%d%d" % (d, i), [128, 1024], BF16) for i in range(2)], "he%d" % d) for d in range(2)]
                    zsr = Ring([sb(st2, "zs%d" % i, [128, 8, 128], BF16) for i in range(2)], "zs")
                    cbm = [sb(st2, "cbm%d" % d, [128, 2, 128], F32) for d in range(2)]
                    crs = Ring([sb(st2, "crs%d" % i, [128, 512], F32) for i in range(4)], "crs")
                    arg = Ring([sb(st2, "arg%d" % i, [128, 512], F32) for i in range(4)], "arg")
                    Wt = [sb(st2, "Wt%d" % d, [128, 16, 128], BF16) for d in range(2)]
                    Csc = [sb(st2, "Csc%d" % d, [128, 16, 128], BF16) for d in range(2)]
                    yg = sb(st2, "yg", [128, 8, 128], F32)
                    sqy = sb(st2, "sqy", [128, 2, 128], F32)
                    rsy = sb(st2, "rsy", [128, 128], F32)
                    lny = sb(st2, "lny", [128, 128], F32)
                    ynr = Ring([sb(st2, "yn%d" % i, [128, 8, 128], BF16) for i in range(2)], "yn")
                    cumTr = Ring([sb(st2, "cumT%d" % i, [128, 128], F32) for i in range(2)], "cumT")
                    Wtr = [Ring([Wt[d], sb(st2, "Wtb%d" % d, [128, 16, 128], BF16)], "Wt%d" % d) for d in range(2)]
                    Cscr = [Ring([Csc[d], sb(st2, "Cscb%d" % d, [128, 16, 128], BF16)], "Csc%d" % d) for d in range(2)]

                    def stageA(c):
                        tk0 = c * 128
                        xs3, xs3k = xs3r.next()
                        xk2, xk2k = xk2r.next()
                        zs, zsk = zsr.next()
                        load(xs3[:], XSC[:, tk0:tk0 + 128].rearrange("(k p) t -> p k t", p=128), [], [xs3k])
                        load(xk2[:], XTOK[tk0:tk0 + 128, :], [], [xk2k])
                        load(zs[:], ZT[:, tk0:tk0 + 128].rearrange("(k p) t -> p k t", p=128), [], [zsk])
                        he = []
                        for d in range(2):
                            t_, k_ = her[d].next()
                            load(t_[:], HENT[d, c], [], [k_])
                            he.append((t_, k_))
                        psA, pkA = pring.next()
                        for g in range(2):
                            mm(psA[:, g * 128:(g + 1) * 128], xs3[:, 8 + g, :], xs3[:, 10 + g, :], True, True, [xs3k], [pkA])
                        for d in range(2):
                            tt("dve", cbm[d][:], psA[:, 0:256].rearrange("p (g i) -> p g i", g=2), tri[d].unsqueeze(1).to_broadcast([128, 2, 128]),
                               ALU.mult, [pkA, "cst"], [("cbm", d)])
                        WC = []
                        for d in range(2):
                            wt_, wtk = Wtr[d].next()
                            cs_, csk = Cscr[d].next()
                            WC.append((wt_, wtk, cs_, csk))
                            psT, pkT = pring.next()
                            mm(psT[0:16, 0:128], ctk[d][:, c, :], identF, True, True, [("ctk", d), "cst"], [pkT])
                            cT, cTk = cumTr.next()
                            cp("act", cT[0:16, :], psT[0:16, 0:128], [pkT], [cTk])
                            qs = []
                            for q in range(4):
                                psc, pkc = pring.next()
                                for hh in range(4):
                                    mm(psc[:, hh * 128:(hh + 1) * 128], sel[:, 4 * q + hh, :], cT[0:16, :], True, True, ["cst", cTk], [pkc])
                                crn, arn = crs.next(), arg.next()
                                qs.append((psc, pkc, crn, arn, arn, crn))
                            for q in range(4):
                                psc, pkc, (cr, crk), (ar, ark), _, _ = qs[q]
                                cp("dve", cr[:], psc[:], [pkc], [crk])
                                for hh in range(4):
                                    ts("dve", ar[:, hh * 128:(hh + 1) * 128], psc[:, hh * 128:(hh + 1) * 128], ctk[d][:, c, 4 * q + hh:4 * q + hh + 1], 0.0,
                                       ALU.subtract, ALU.min, [pkc, ("ctk", d)], [ark])
                            for q in range(4):
                                _, _, (cr, crk), (ar, ark), (et, etk), (ec, eck) = qs[q]
                                for hh in range(4):
                                    act(et[:, hh * 128:(hh + 1) * 128], ar[:, hh * 128:(hh + 1) * 128], AF.Exp, [ark, ("lndt", d)], [etk],
                                        bias=lndt[d][:, c, 4 * q + hh:4 * q + hh + 1])
                                act(ec[:], cr[:], AF.Exp, [crk], [eck])
                            for q in range(4):
                                g = q // 2
                                _, _, _, _, (et, etk), (ec, eck) = qs[q]
                                tt("dve", wt_[:, 4 * q:4 * q + 4, :], et[:].rearrange("p (h i) -> p h i", h=4),
                                   cbm[d][:, g, :].unsqueeze(1).to_broadcast([128, 4, 128]), ALU.mult, [etk, ("cbm", d)], [wtk])
                                tt("dve", cs_[:, 4 * q:4 * q + 4, :], ec[:].rearrange("p (h i) -> p h i", h=4),
                                   xs3[:, 10 + g, :].unsqueeze(1).to_broadcast([128, 4, 128]), ALU.mult, [eck, xs3k], [csk])
                        return dict(c=c, xs3=(xs3, xs3k), xk2=(xk2, xk2k), zs=(zs, zsk), he=he, WC=WC)

                    def stageB(cx):
                        c = cx["c"]
                        tk0 = c * 128
                        xs3, xs3k = cx["xs3"]
                        xk2, xk2k = cx["xk2"]
                        zs, zsk = cx["zs"]
                        he, WC = cx["he"], cx["WC"]
                        psY = [plong.next(), plong.next()]
                        for h in range(16):
                            kc, half = h // 2, h % 2
                            pY, pYk = psY[kc // 4]
                            out = pY[half * 64:(half + 1) * 64, (kc % 4) * 128:(kc % 4 + 1) * 128]
                            for d in range(2):
                                wt_, wtk, cs_, csk = WC[d]
                                mm(out, xk2[:, h * 64:(h + 1) * 64], wt_[:, h, :], d == 0, False, [xk2k, wtk], [pYk])
                                mm(out, he[d][0][:, h * 64:(h + 1) * 64], cs_[:, h, :], False, d == 1, [he[d][1], csk], [pYk])
                        for kc in range(8):
                            pY, pYk = psY[kc // 4]
                            stt("dve", yg[:, kc, :], xs3[:, kc, :], dsum[:, l, kc:kc + 1], pY[:, (kc % 4) * 128:(kc % 4 + 1) * 128], ALU.mult, ALU.add,
                                [xs3k, "dsum", pYk], ["yg"])
                        tt("dve", yg[:], yg[:], zs[:], ALU.mult, ["yg", zsk], ["yg"])
                        ps, pk = pring.next()
                        for kc in range(8):
                            act(sqy[:, kc % 2, :], yg[:, kc, :], AF.Square, ["yg"], [("sqy", kc % 2)])
                            mm(ps[:, 0:128], ones, sqy[:, kc % 2, :], kc == 0, kc == 7, ["cst", ("sqy", kc % 2)], [pk])
                        rstd_from_ssq(ps[:, 0:128], 1024.0, rsy[:], lny[:], [pk], ["rsy"], "lny")
                        yn, ynk = ynr.next()
                        for kc in range(8):
                            stt("dve", yn[:, kc, :], yg[:, kc, :], snw[:, l, kc:kc + 1], rsy[:], ALU.mult, ALU.mult, ["yg", "rsy", "prm"], [ynk])
                        store(YBT[:, tk0:tk0 + 128].rearrange("(k p) t -> p k t", p=128), yn[:], [ynk], [])

                    ctxs = {0: stageA(0)}
                    for c in range(NCH):
                        if c + 1 < NCH:
                            ctxs[c + 1] = stageA(c + 1)
                        stageB(ctxs.pop(c))
                    S.flush()
            if stop == "ssd":
                break
            with contextlib.ExitStack() as st:
                ckvall = sb(st, "ckvall", [128, 2, 4608], BF16)
                KTr = [sb(st, "KT%d" % i, [128, 4608], BF16) for i in range(2)]
                VA = [sb(st, "VA%d" % i, [128, 36, 128], BF16) for i in range(2)]
                qhr = Ring([sb(st, "qh%d" % i, [128, 4096], BF16) for i in range(2)], "qh")
                PT = Ring([sb(st, "PT%d" % i, [128, 512], BF16) for i in range(3)], "PT")
                Lt = sb(st, "Lt", [128, 512], F32)
                Rt = sb(st, "Rt", [128, 512], F32)
                ATr = Ring([sb(st, "AT%d" % i, [128, 512], BF16) for i in range(2)], "AT")
                S.op("pool", lambda e: e.memset(VA[0][:, :, 64:128], 1.0), writes=[("VA", 0)])
                S.op("pool", lambda e: e.memset(VA[1][:, :, 0:64], 1.0), writes=[("VA", 1)])
                S.dma("pool", lambda e: e.dma_start(out=ckvall[:, :, 0:512], in_=cckvT[l].rearrange("(k p) t -> p k t", p=128)), writes=["ckvall"])
                for i in range(2):
                    S.dma("pool", lambda e, i=i: e.dma_start(out=KTr[i][64:96, 0:512], in_=ckrT[l]), writes=[("KT", i)])
                if l + 1 < nl:
                    stgA = Ring([sb(st, "cva%d" % i, [128, 6144], BF16) for i in range(3)], "cva")
                    convert_weights(l + 1, stgA)
                for (nctx, k0, nlat, q0, nq, qblk) in ((512, 0, TS, 0, TS, 512), (0, TS, 256, TS, 256, 256), (0, TS + 256, 256, TS + 256, 256, 256)):
                    nk = nctx + nlat
                    ntile = nk // 128
                    load(ckvall[:, :, nctx:nk], CKVT[:, k0:k0 + nlat].rearrange("(k p) t -> p k t", p=128), [], ["ckvall"])
                    for i in range(2):
                        load(KTr[i][64:96, nctx:nk], KRT[:, k0:k0 + nlat], [], [("KT", i)])
                    for h in range(8):
                        par = h % 2
                        va, vak = VA[par], ("VA", par)
                        KT, ktk = KTr[par], ("KT", par)
                        voff = par * 64
                        for kb in range((nk + 511) // 512):
                            w = min(512, nk - kb * 512)
                            ps, pk = pring.next()
                            for kc in range(2):
                                mm(ps[0:64, 0:w], ukvw[:, kc, h * 128:h * 128 + 64], ckvall[:, kc, kb * 512:kb * 512 + w], kc == 0, kc == 1, ["ukvw", "ckvall"], [pk])
                            cp("act" if kb % 2 == 0 else "dve", KT[0:64, kb * 512:kb * 512 + w], ps[0:64, 0:w], [pk], [ktk])
                        for tg in range(0, ntile, 8):
                            nt = min(8, ntile - tg)
                            ps, pk = pring.next()
                            for j in range(nt):
                                for kc in range(2):
                                    mm(ps[:, j * 64:(j + 1) * 64], ckvall[:, kc, (tg + j) * 128:(tg + j + 1) * 128], ukvw[:, kc, h * 128 + 64:h * 128 + 128],
                                       kc == 0, kc == 1, ["ukvw", "ckvall"], [pk])
                            cp("dve", va[:, tg:tg + nt, voff:voff + 64], ps[:, 0:nt * 64].rearrange("p (t d) -> p t d", d=64), [pk], [vak])
                        qh, qhk = qhr.next()
                        load(qh[0:96, 0:nq], QT[h, :, q0:q0 + nq], [], [qhk])
                        items = [(qb, t) for qb in range(nq // qblk) for t in range(ntile)]
                        SKEW = 2
                        pend = {}
                        cur = {}
                        for i in range(len(items) + SKEW):
                            if i < len(items):
                                qb, t = items[i]
                                psS, pkS = pring.next()
                                mm(psS[:, 0:qblk], KT[0:96, t * 128:(t + 1) * 128], qh[0:96, qb * qblk:(qb + 1) * qblk], True, True, [ktk, qhk], [pkS])
                                pend[i] = (psS, pkS)
                            if i >= SKEW:
                                qb, t = items[i - SKEW]
                                psS, pkS = pend.pop(i - SKEW)
                                if t == 0:
                                    cur[qb] = plong.next()
                                psO, pkO = cur[qb]
                                pt, ptk = PT.next()
                                act(pt[:, 0:qblk], psS[:, 0:qblk], AF.Exp, [pkS], [ptk], scale=SCALE)
                                mm(psO[:, 0:qblk], va[:, t, :], pt[:, 0:qblk], t == 0, t == ntile - 1, [vak, ptk], [pkO])
                                if t == ntile - 1:
                                    orow, drow = voff, 64 - voff
                                    act(Lt[orow:orow + 64, 0:qblk], psO[drow:drow + 64, 0:qblk], AF.Ln, [pkO], ["Lt"])
                                    act(Rt[orow:orow + 64, 0:qblk], Lt[orow:orow + 64, 0:qblk], AF.Exp, ["Lt"], ["Rt"], scale=-1.0)
                                    at, atk = ATr.next()
                                    tt("dve", at[orow:orow + 64, 0:qblk], psO[orow:orow + 64, 0:qblk], Rt[orow:orow + 64, 0:qblk], ALU.mult, [pkO, "Rt"], [atk])
                                    r0 = (h // 2) * 128 + orow
                                    store(ATT[r0:r0 + 64, q0 + qb * qblk:q0 + (qb + 1) * qblk], at[orow:orow + 64, 0:qblk], [atk], [])
                S.flush()
            if stop == "attn":
                break
            with contextlib.ExitStack() as st:
                xt = sb(st, "xt3", [128, 8, 512], F32)
                ht = sb(st, "ht3", [128, 8, 512], BF16)
                va_ = sb(st, "va3", [128, 4, 512], BF16)
                yb_ = sb(st, "yb3", [128, 8, 512], BF16)
                at_ = sb(st, "at3", [128, 4, 512], BF16)
                macc = sb(st, "macc", [128, 4, 512], F32)
                sigr = Ring([sb(st, "sig%d" % i, [128, 512], F32) for i in range(2)], "sig")
                tmr = Ring([sb(st, "tm%d" % i, [128, 512], F32) for i in range(2)], "tm")
                merged = sb(st, "merged", [128, 8, 512], BF16)
                h2 = sb(st, "h2", [128, 8, 512], BF16)
                gt = sb(st, "gt", [128, 22, 512], BF16)
                sar = Ring([sb(st, "sa%d" % i, [128, 512], F32) for i in range(2)], "sa")
                sqt = sb(st, "sqt3", [128, 2, 512], F32)
                rst = sb(st, "rst3", [128, 512], F32)
                lnt = sb(st, "lnt3", [128, 512], F32)
                tmpf = sb(st, "tmpf3", [128, 2, 512], F32)
                last = (l == nl - 1)
                yo = sb(st, "yo", [128, 8, 512], F32) if last else None
                for b in range(NB):
                    r = 0 if b < 8 else 1
                    t0 = b * 512
                    fm = lambda dr: dr[:, t0:t0 + 512].rearrange("(k p) t -> p k t", p=128)
                    load(xt[:], fm(xsrc), [], ["xt"])
                    load(ht[:], fm(HT), [], ["ht"])
                    load(va_[:], fm(VAT), [], ["va_"])
                    load(yb_[:], fm(YBT), [], ["yb_"])
                    load(at_[:], fm(ATT), [], ["at_"])
                    for cg in range(2):
                        for br, (wsrc, nkc, rt_, rk) in enumerate((("wa", 4, va_, "va_"), ("wb", 8, yb_, "yb_"), ("wc", 4, at_, "at_"))):
                            wy, wyk = wloadb(WB[l % 2][wsrc][:, cg * 512:(cg + 1) * 512], nkc, 512)
                            c0 = C_G + br * 1024 + cg * 512
                            wg, wgk = wloadb(WB[l % 2]["win"][:, c0:c0 + 512], 8, 512)
                            for oc in range(4):
                                psYy, pky = pring.next()
                                for kc in range(nkc):
                                    mm(psYy[:], wy[:, kc, oc * 128:(oc + 1) * 128], rt_[:, kc, :], kc == 0, kc == nkc - 1, [wyk, rk], [pky])
                                psG, pkg = pring.next()
                                for kc in range(8):
                                    mm(psG[:], wg[:, kc, oc * 128:(oc + 1) * 128], ht[:, kc, :], kc == 0, kc == 7, [wgk, "ht"], [pkg])
                                sg, sgk = sigr.next()
                                act(sg[:], psG[:], AF.Sigmoid, [pkg], [sgk])
                                if br == 0:
                                    tt("dve", macc[:, oc, :], psYy[:], sg[:], ALU.mult, [pky, sgk], [("macc", oc)])
                                else:
                                    tm, tmk = tmr.next()
                                    tt("dve", tm[:], psYy[:], sg[:], ALU.mult, [pky, sgk], [tmk])
                                    if br == 1:
                                        tt("dve", macc[:, oc, :], macc[:, oc, :], tm[:], ALU.add, [("macc", oc), tmk], [("macc", oc)])
                                    else:
                                        tt("dve", merged[:, cg * 4 + oc, :], macc[:, oc, :], tm[:], ALU.add, [("macc", oc), tmk], ["merged"])
                    for cg in range(2):
                        wo, wok = wloadb(WB[l % 2]["wo"][:, cg * 512:(cg + 1) * 512], 8, 512)
                        for oc in range(4):
                            ps, pk = pring.next()
                            for kc in range(8):
                                mm(ps[:], wo[:, kc, oc * 128:(oc + 1) * 128], merged[:, kc, :], kc == 0, kc == 7, [wok, "merged"], [pk])
                            o8 = cg * 4 + oc
                            stt("dve", xt[:, o8, :], ps[:], modcol(l, 2, o8, r), xt[:, o8, :], ALU.mult, ALU.add, [pk, "MOD", "xt"], ["xt"])
                    norm_mod((sqt, rst, lnt, tmpf), xt, h2, lambda kc: A2[:, l, kc, r:r + 1], lambda kc: modcol(l, 3, kc, r), "xt", "h2")
                    for fg in range(6):
                        ncol = 512 if fg < 5 else 256
                        w1, w1k = wloadb(WB[l % 2]["wf1"][:, fg * 512:fg * 512 + ncol], 8, ncol)
                        w3, w3k = wloadb(WB[l % 2]["wf3"][:, fg * 512:fg * 512 + ncol], 8, ncol)
                        for oc in range(ncol // 128):
                            j = fg * 4 + oc
                            psA, pka = pring.next()
                            for kc in range(8):
                                mm(psA[:], w1[:, kc, oc * 128:(oc + 1) * 128], h2[:, kc, :], kc == 0, kc == 7, [w1k, "h2"], [pka])
                            psB, pkb = pring.next()
                            for kc in range(8):
                                mm(psB[:], w3[:, kc, oc * 128:(oc + 1) * 128], h2[:, kc, :], kc == 0, kc == 7, [w3k, "h2"], [pkb])
                            sa, sak = sar.next()
                            act(sa[:], psA[:], AF.Silu, [pka], [sak])
                            tt("dve", gt[:, j, :], sa[:], psB[:], ALU.mult, [sak, pkb], [("gt", j)])
                    for cg in range(4):
                        w2, w2k = wloadb(WB[l % 2]["wf2"][:, cg * 256:(cg + 1) * 256], 22, 256)
                        for oc in range(2):
                            ps, pk = pring.next()
                            for j in range(22):
                                mm(ps[:], w2[:, j, oc * 128:(oc + 1) * 128], gt[:, j, :], j == 0, j == 21, [w2k, ("gt", j)], [pk])
                            o8 = cg * 2 + oc
                            stt("dve", xt[:, o8, :], ps[:], modcol(l, 5, o8, r), xt[:, o8, :], ALU.mult, ALU.add, [pk, "MOD", "xt"], ["xt"])
                    if not last:
                        store(fm(XT), xt[:], ["xt"], [])
                    else:
                        norm_mod((sqt, rst, lnt, tmpf), xt, yo, lambda kc: fnw[:, kc:kc + 1], None, "xt", "yo")
                        store(fm(yT), yo[:], ["yo"], [])
                    if stop == "p3" and "XT" in dump:
                        store(fm(XT), xt[:], ["xt"], [])
                S.flush()
        S.flush(final=True)
    return nc


def rope_tables():
    n_rows = TS // 64
    row = np.repeat(np.arange(n_rows, dtype=np.float32), 64)
    col = np.tile(np.arange(64, dtype=np.float32), n_rows)
    inv = (np.float32(10000.0) ** (-np.arange(8, dtype=np.float32) / np.float32(8))).astype(np.float32)
    ang = np.concatenate([row[:, None] * inv, col[:, None] * inv], axis=-1).astype(np.float32)
    cos, sin = np.cos(ang).astype(np.float32), np.sin(ang).astype(np.float32)
    C = np.concatenate([cos, cos], axis=1).T
    Sg = np.concatenate([-sin, sin], axis=1).T
    return np.ascontiguousarray(C), np.ascontiguousarray(Sg)


def make_in_maps(inp, nl=DEPTH):
    f = lambda k: np.asarray(inp[k], np.float32)
    ropeC, ropeS = rope_tables()
    k = np.arange(128)
    triF = (k[:, None] <= k[None, :]).astype(np.float32)
    triB = (k[:, None] >= k[None, :]).astype(np.float32)
    selc = np.zeros((128, 16, 128), np.float32)
    for hh in range(16):
        selc[hh, hh, :] = 1.0
    cst = np.concatenate([triF, triB, np.ones((128, 128), np.float32), np.eye(128, dtype=np.float32), selc.reshape(128, 2048)], axis=1)
    prm = pack_params(inp)
    w_in = f("w_in")
    w_kr2 = np.zeros((DEPTH, D, 2, 96), np.float32)
    krc = w_in[:, :, C_KR:C_KR + 32]
    w_kr2[:, :, 0, 64:96] = krc
    w_kr2[:, :, 1, 64:80] = krc[:, :, 16:32]
    w_kr2[:, :, 1, 80:96] = krc[:, :, 0:16]
    wuq = f("w_uq").reshape(DEPTH, 256, 8, 96)
    w_uq2 = np.zeros((DEPTH, 256, 2, 8, 96), np.float32)
    w_uq2[:, :, 0] = wuq
    w_uq2[:, :, 1, :, 0:64] = wuq[..., 0:64]
    w_uq2[:, :, 1, :, 64:80] = wuq[..., 80:96]
    w_uq2[:, :, 1, :, 80:96] = wuq[..., 64:80]
    shared = {
        "ropeC": ropeC, "ropeS": ropeS, "cst": cst, "prm": prm, "w_in": w_in, "w_kr2": w_kr2, "w_uq2": w_uq2,
        "w_ukv": f("w_ukv"), "w_a_out": f("w_a_out"), "w_b_out": f("w_b_out"), "w_c_out": f("w_c_out"),
        "w_o": f("w_o"), "w_ada": f("w_ada"), "w_ff1": f("w_ff1"), "w_ff3": f("w_ff3"), "w_ff2": f("w_ff2"),
    }
    for k in ("w_in", "w_kr2", "w_uq2", "w_ukv", "w_a_out", "w_b_out", "w_c_out", "w_o", "w_ada", "w_ff1", "w_ff3", "w_ff2"):
        shared[k] = np.ascontiguousarray(shared[k][:nl])
    xs, xp, c, cctx = f("x_sample"), f("x_prompt"), f("c"), f("c_ctx")
    cckv, ckr = f("cache_ckv"), f("cache_krope")
    sf, sbw = f("state_ssm_fwd"), f("state_ssm_bwd")
    maps = []
    for r in range(8):
        bs = r % 4
        xT0 = np.concatenate([xs[bs].T, xp[2 * r].T, xp[2 * r + 1].T], axis=1)
        cd = np.stack([c[bs], cctx], axis=1)
        cd = cd.reshape(8, 128, 2).transpose(1, 0, 2)
        h0 = np.stack([sf[bs], sbw[bs]], axis=1)
        h0 = h0.transpose(0, 1, 4, 2, 3).reshape(DEPTH, 2, 128, 1024)
        m = dict(shared)
        m.update({
            "xT0": np.ascontiguousarray(xT0), "cond": np.ascontiguousarray(cd),
            "cckvT": np.ascontiguousarray(cckv[bs].transpose(0, 2, 1)),
            "ckrT": np.ascontiguousarray(ckr[bs].transpose(0, 2, 1)),
            "h0": np.ascontiguousarray(h0),
        })
        maps.append(m)
    return maps


_NC_CACHE = {}


def kernel(**inputs):
    maps = make_in_maps(inputs)
    if "nc" not in _NC_CACHE:
        _NC_CACHE["nc"] = build_program()
    nc = _NC_CACHE["nc"]
    res = run_bass_kernel_spmd(nc, maps, core_ids=list(range(8)))
    R = res.results
    y_prompt = np.zeros((16, 256, D), np.float32)
    y_sample = np.zeros((4, TS, D), np.float32)
    new_ckv = np.zeros((16, DEPTH, 256, 256), np.float32)
    new_kr = np.zeros((16, DEPTH, 256, 32), np.float32)
    new_f = np.zeros((16, DEPTH, 16, 64, 128), np.float32)
    new_b = np.zeros((16, DEPTH, 16, 64, 128), np.float32)
    for r in range(8):
        yT = R[r]["yT"]
        if r < 4:
            y_sample[r] = yT[:, :TS].T
        for s in range(2):
            q = 2 * r + s
            y_prompt[q] = yT[:, TS + s * 256:TS + (s + 1) * 256].T
            new_ckv[q] = R[r]["nckvT"][:, :, s * 256:(s + 1) * 256].transpose(0, 2, 1)
            new_kr[q] = R[r]["nkrT"][:, :, s * 256:(s + 1) * 256].transpose(0, 2, 1)
            st = R[r]["nssm"][:, :, s].reshape(DEPTH, 2, 128, 16, 64).transpose(0, 1, 3, 4, 2)
            new_f[q] = st[:, 0]
            new_b[q] = st[:, 1]
    return (y_prompt, y_sample, new_ckv, new_kr, new_f, new_b)
```

```python
import contextlib
import math
import os

import numpy as np
import concourse.bass as bass
import concourse.mybir as mybir
from concourse.bass_utils import run_bass_kernel_spmd

F32 = mybir.dt.float32
BF16 = mybir.dt.bfloat16
AF = mybir.ActivationFunctionType
ALU = mybir.AluOpType

D = 1024
DEPTH = 4
TS = 4096
TPR = 512
T = TS + TPR
NB = T // 512
NCH = T // 128
EPS = 1e-6
IN_COLS = 7728
FF = 2816
C_AX, C_AB, C_AC, C_Z, C_XBC, C_DT, C_CQ, C_CKV, C_KR, C_G = 0, 512, 1024, 1536, 2560, 4096, 4112, 4368, 4624, 4656
SCALE = 1.0 / math.sqrt(96.0)
SEQS = [(0, 4096, True), (4096, 256, False), (4352, 256, False)]

ENGS = ("pe", "act", "dve", "pool", "sp")


class Sched:
    def __init__(self, nc, st, n_dma_sems=48):
        self.nc = nc
        self.n_dma_sems = n_dma_sems
        self.esem = {e: st.enter_context(nc.semaphore("s_" + e)) for e in ENGS if e != "sp"}
        self.dsem = [st.enter_context(nc.semaphore("d%d" % i)) for i in range(n_dma_sems)]
        self.dummy = st.enter_context(nc.sbuf_tensor("bar_dummy", [128, 2], F32))
        self.ecount = {e: 0 for e in ENGS}
        self.dma_val = [0] * n_dma_sems
        self.dma_last = [None] * n_dma_sems
        self.dma_rr = {"sw": 0, "hw": 0}
        self.n_sw = 16
        self.waited = {e: {} for e in ENGS}
        self.barrier_tok = None
        self.need_barrier = {e: False for e in ENGS}
        self._reset()

    def _reset(self):
        self.ops = {e: [] for e in ENGS}
        self.last_write = {}
        self.readers = {}
        self.dma_toks = []

    def _record(self, eng, fn, reads, writes, dma, extra_deps=()):
        deps = set(extra_deps)
        for r in reads:
            w = self.last_write.get(r)
            if w is not None:
                deps.add(w)
        for r in writes:
            w = self.last_write.get(r)
            if w is not None:
                deps.add(w)
            for rd in self.readers.get(r, ()):
                deps.add(rd)
        idx = len(self.ops[eng])
        if dma:
            if eng == "pool":
                si = self.dma_rr["sw"]
                self.dma_rr["sw"] = (si + 1) % self.n_sw
            else:
                si = self.n_sw + self.dma_rr["hw"]
                self.dma_rr["hw"] = (self.dma_rr["hw"] + 1) % (self.n_dma_sems - self.n_sw)
            prev = self.dma_last[si]
            if prev is not None:
                deps.add(prev)
            self.dma_val[si] += 16
            tok = ("dma", si, self.dma_val[si])
            self.dma_last[si] = tok
            self.dma_toks.append(tok)
        else:
            tok = ("eng", eng, idx)
        deps = {d for d in deps if not (d[0] == "eng" and d[1] == "pe" and eng == "pe")}
        deps.discard(tok)
        if self.need_barrier[eng] and self.barrier_tok is not None:
            deps.add(self.barrier_tok)
            self.need_barrier[eng] = False
        self.ops[eng].append(dict(fn=fn, deps=deps, flag=False, tok=tok))
        for r in reads:
            lst = self.readers.setdefault(r, [])
            if tok[0] == "eng":
                lst[:] = [t for t in lst if not (t[0] == "eng" and t[1] == eng)]
            lst.append(tok)
        for r in writes:
            self.last_write[r] = tok
            self.readers[r] = []
        return tok

    def op(self, eng, fn, reads=(), writes=(), extra_deps=()):
        return self._record(eng, fn, reads, writes, False, extra_deps)

    def dma(self, eng, fn, reads=(), writes=(), extra_deps=()):
        return self._record(eng, fn, reads, writes, True, extra_deps)

    def flush(self, final=False):
        nc = self.nc
        deps = set()
        for e in ENGS:
            if e == "sp":
                continue
            for i in range(len(self.ops[e]) - 1, -1, -1):
                if self.ops[e][i]["tok"][0] == "eng":
                    deps.add(self.ops[e][i]["tok"])
                    break
        latest = {}
        for t in self.dma_toks:
            if t[1] not in latest or latest[t[1]][2] < t[2]:
                latest[t[1]] = t
        deps.update(latest.values())
        dummy = self.dummy
        coll = self._record("dve", lambda e: e.memset(dummy[:, 0:1], 0.0), (), (), False, deps)
        for e in ENGS:
            for o in self.ops[e]:
                for d in o["deps"]:
                    if d[0] == "eng":
                        self.ops[d[1]][d[2]]["flag"] = True
        self.ops["dve"][coll[2]]["flag"] = True
        cnt = {}
        for e in ENGS:
            c = self.ecount[e]
            arr = []
            for o in self.ops[e]:
                if o["flag"]:
                    c += 1
                arr.append(c)
            cnt[e] = arr
        esem, dsem = self.esem, self.dsem

        def resolve(d):
            if d[0] == "eng":
                return esem[d[1]], cnt[d[1]][d[2]]
            if d[0] == "abs":
                return d[1], d[2]
            return dsem[d[1]], d[2]

        coll_abs = ("abs", esem["dve"], cnt["dve"][coll[2]])

        def run(ename, eng):
            waited = self.waited[ename]
            for o in self.ops[ename]:
                need = {}
                for d in o["deps"]:
                    s, v = resolve(d)
                    k = id(s)
                    if waited.get(k, 0) >= v:
                        continue
                    if k not in need or need[k][1] < v:
                        need[k] = (s, v)
                for k, (s, v) in need.items():
                    eng.wait_ge(s, v)
                    waited[k] = v
                ins = o["fn"](eng)
                if o["tok"][0] == "dma":
                    ins.then_inc(dsem[o["tok"][1]], 16)
                elif o["flag"]:
                    ins.then_inc(esem[ename], 1)
            if final and ename == "sp":
                s, v = resolve(coll_abs)
                eng.wait_ge(s, v)

        if os.environ.get("KDBG_SIM"):
            self._simulate(resolve, coll_abs, final)

        with nc.Block() as block:
            @block.sync
            def _(sync):
                run("sp", sync)

            @block.tensor
            def _(tensor):
                run("pe", tensor)

            @block.scalar
            def _(scalar):
                run("act", scalar)

            @block.vector
            def _(vector):
                run("dve", vector)

            @block.gpsimd
            def _(gpsimd):
                run("pool", gpsimd)

        for e in ENGS:
            if cnt[e]:
                self.ecount[e] = cnt[e][-1]
        self.barrier_tok = coll_abs
        self.need_barrier = {e: True for e in ENGS}
        self._reset()


def _sched_simulate(self, resolve, coll_abs, final):
    if not hasattr(self, "sim_sem"):
        self.sim_sem = {}
    sem = self.sim_sem
    pos = {e: 0 for e in ENGS}
    progress = True
    while progress:
        progress = False
        for e in ENGS:
            while pos[e] < len(self.ops[e]):
                o = self.ops[e][pos[e]]
                ok = True
                for d in o["deps"]:
                    s_, v = resolve(d)
                    if sem.get(id(s_), 0) < v:
                        ok = False
                        break
                if not ok:
                    break
                if o["tok"][0] == "dma":
                    k = id(self.dsem[o["tok"][1]])
                    sem[k] = sem.get(k, 0) + 16
                elif o["flag"]:
                    k = id(self.esem[e])
                    sem[k] = sem.get(k, 0) + 1
                pos[e] += 1
                progress = True
    stuck = {e: (pos[e], len(self.ops[e])) for e in ENGS if pos[e] < len(self.ops[e])}
    if stuck:
        print("SCHED DEADLOCK:", stuck)
        for e in stuck:
            o = self.ops[e][pos[e]]
            print("  ", e, "op", pos[e], "tok", o["tok"], "deps", [(d, resolve(d)[1], sem.get(id(resolve(d)[0]), 0)) for d in o["deps"]])
        raise RuntimeError("sched deadlock")
    else:
        print("sched sim ok:", {e: len(self.ops[e]) for e in ENGS})


Sched._simulate = _sched_simulate


class Ring:
    def __init__(self, tiles, name):
        self.tiles = tiles
        self.name = name
        self.i = 0

    def next(self):
        i = self.i
        self.i = (self.i + 1) % len(self.tiles)
        return self.tiles[i], (self.name, i)


PRM_LAYOUT = [("n1w", 4 * 8), ("n2w", 4 * 8), ("fnw", 8), ("snw", 4 * 8), ("qnw", 4 * 2), ("kvnw", 4 * 2),
              ("aconv", 4 * 3 * 4), ("sconvw", 4 * 3 * 12), ("sconvb", 4 * 12), ("dexp", 4 * 2 * 8),
              ("alog", 4 * 2 * 16), ("dtb", 4 * 2 * 16), ("bada", 4 * 48)]
PRM_OFF = {}
_o = 0
for _n, _s in PRM_LAYOUT:
    PRM_OFF[_n] = (_o, _s)
    _o += _s
NPRM = _o


def _pc(v):
    v = np.asarray(v, np.float32)
    lead = v.shape[:-1]
    c = v.shape[-1] // 128
    v = v.reshape(lead + (c, 128))
    v = np.moveaxis(v, -1, 0)
    return np.ascontiguousarray(v).reshape(128, -1)


def pack_params(inp):
    parts = {
        "n1w": _pc(inp["norm1_w"]), "n2w": _pc(inp["norm2_w"]), "fnw": _pc(inp["final_norm_w"]),
        "snw": _pc(inp["ssm_norm_w"]), "qnw": _pc(inp["q_norm_w"]), "kvnw": _pc(inp["kv_norm_w"]),
        "aconv": _pc(inp["a_conv_w"]), "sconvw": _pc(inp["ssm_conv_w"]), "sconvb": _pc(inp["ssm_conv_b"]),
        "dexp": _pc(np.repeat(np.asarray(inp["ssm_d"], np.float32), 64, axis=-1)),
        "alog": np.broadcast_to(np.asarray(inp["ssm_a_log"], np.float32).reshape(1, -1), (128, 128)),
        "dtb": np.broadcast_to(np.asarray(inp["ssm_dt_bias"], np.float32).reshape(1, -1), (128, 128)),
        "bada": _pc(inp["b_ada"]),
    }
    out = np.zeros((128, NPRM), np.float32)
    for n, (o, s) in PRM_OFF.items():
        assert parts[n].shape == (128, s), (n, parts[n].shape, s)
        out[:, o:o + s] = parts[n]
    return out


def build_program(stop=None, dump=(), nl=DEPTH):
    nc = bass.Bass("TRN2", target_bir_lowering=False)

    def din(name, shape, dt=F32):
        return nc.dram_tensor(name, list(shape), dt, kind="ExternalInput").ap()

    def dout(name, shape, dt=F32):
        return nc.dram_tensor(name, list(shape), dt, kind="ExternalOutput").ap()

    def dscr(name, shape, dt):
        kind = "ExternalOutput" if name in dump else "Internal"
        return nc.dram_tensor(name, list(shape), dt, kind=kind).ap()

    xT0 = din("xT0", [D, T])
    cond = din("cond", [128, 8, 2])
    cckvT = din("cckvT", [DEPTH, 256, 512])
    ckrT = din("ckrT", [DEPTH, 32, 512])
    h0 = din("h0", [DEPTH, 2, 128, 1024])
    ropeC = din("ropeC", [32, TS])
    ropeS = din("ropeS", [32, TS])
    cst = din("cst", [128, 2560])
    prm_d = din("prm", [128, NPRM])
    w_in = din("w_in", [nl, D, IN_COLS])
    w_kr2 = din("w_kr2", [nl, D, 2, 96])
    w_uq2 = din("w_uq2", [nl, 256, 2, 8, 96])
    w_ukv = din("w_ukv", [nl, 256, 1024])
    w_a_out = din("w_a_out", [nl, 512, D])
    w_b_out = din("w_b_out", [nl, D, D])
    w_c_out = din("w_c_out", [nl, 512, D])
    w_o = din("w_o", [nl, D, D])
    w_ada = din("w_ada", [nl, D, 6 * D])
    w_ff1 = din("w_ff1", [nl, D, FF])
    w_ff3 = din("w_ff3", [nl, D, FF])
    w_ff2 = din("w_ff2", [nl, FF, D])

    yT = dout("yT", [D, T])
    nckvT = dout("nckvT", [DEPTH, 256, 512])
    nkrT = dout("nkrT", [DEPTH, 32, 512])
    nssm = dout("nssm", [DEPTH, 2, 2, 128, 1024])

    XT = dscr("XT", [D, T], F32)
    HT = dscr("HT", [D, T], BF16)
    UT = dscr("UT", [512, T], BF16)
    ABT = dscr("ABT", [512, T], BF16)
    ZT = dscr("ZT", [D, T], BF16)
    XBCT = dscr("XBCT", [1536, T], BF16)
    DTT = dscr("DTT", [T, 16], F32)
    QT = dscr("QT", [8, 96, T], BF16)
    CKVT = dscr("CKVT", [256, T], BF16)
    KRT = dscr("KRT", [32, T], BF16)
    VAT = dscr("VAT", [512, T], BF16)
    XSC = dscr("XSC", [1536, T], BF16)
    XTOK = dscr("XTOK", [T, 1024], BF16)
    BTOK = dscr("BTOK", [T, 256], BF16)
    HENT = dscr("HENT", [2, NCH, 128, 1024], BF16)
    YBT = dscr("YBT", [D, T], BF16)
    ATT = dscr("ATT", [512, T], BF16)

    WB = []
    for si in range(2):
        WB.append(dict(
            win=dscr("WBwin%d" % si, [D, IN_COLS], BF16), wa=dscr("WBwa%d" % si, [512, D], BF16),
            wb=dscr("WBwb%d" % si, [D, D], BF16), wc=dscr("WBwc%d" % si, [512, D], BF16),
            wo=dscr("WBwo%d" % si, [D, D], BF16), wf1=dscr("WBwf1%d" % si, [D, FF], BF16),
            wf3=dscr("WBwf3%d" % si, [D, FF], BF16), wf2=dscr("WBwf2%d" % si, [FF, D], BF16)))

    with contextlib.ExitStack() as gst:
        S = Sched(nc, gst)

        _uid = [0]

        def sb(st, name, shape, dt):
            _uid[0] += 1
            return st.enter_context(nc.sbuf_tensor("sb%d_%s" % (_uid[0], name), list(shape), dt))

        prm = sb(gst, "prm", [128, NPRM], F32)
        cstt = sb(gst, "cstt", [128, 2560], F32)
        identb = sb(gst, "identb", [128, 128], BF16)
        MOD = sb(gst, "MOD", [128, DEPTH, 48, 2], F32)
        A1 = sb(gst, "A1", [128, DEPTH, 8, 2], F32)
        A2 = sb(gst, "A2", [128, DEPTH, 8, 2], F32)
        dsum = sb(gst, "dsum", [128, DEPTH, 8], F32)
        wring = Ring([sb(gst, "wr%d" % i, [128, 6144], BF16) for i in range(4)], "wr")
        uqw = sb(gst, "uqw", [128, 2, 2 * 8 * 96], BF16)
        krw = sb(gst, "krw", [128, 8, 2 * 96], BF16)
        dtw = sb(gst, "dtw", [128, 8, 16], BF16)
        ukvw = sb(gst, "ukvw", [128, 2, 1024], BF16)
        psum = [gst.enter_context(nc.psum_tensor("ps%d" % i, [128, 512], F32)) for i in range(7)]
        psbT = gst.enter_context(nc.psum_tensor("psbT", [128, 1024], BF16))
        pring = Ring(psum[0:5], "ps")
        plong = Ring(psum[5:7], "pl")
        triF = cstt[:, 0:128]
        triB = cstt[:, 128:256]
        ones = cstt[:, 256:384]
        identF = cstt[:, 384:512]
        sel = cstt[0:16, 512:2560].rearrange("p (h j) -> p h j", h=16)

        def P(name, l=None):
            o, s = PRM_OFF[name]
            v = prm[:, o:o + s]
            return v

        def pv(name, pattern, **kw):
            o, s = PRM_OFF[name]
            return prm[:, o:o + s].rearrange(pattern, **kw)

        n1w = pv("n1w", "p (l c) -> p l c", l=4)
        n2w = pv("n2w", "p (l c) -> p l c", l=4)
        fnw = P("fnw")
        snw = pv("snw", "p (l c) -> p l c", l=4)
        qnw = pv("qnw", "p (l c) -> p l c", l=4)
        kvnw = pv("kvnw", "p (l c) -> p l c", l=4)
        aconv = pv("aconv", "p (l k c) -> p l k c", l=4, k=3)
        sconvw = pv("sconvw", "p (l k c) -> p l k c", l=4, k=3)
        sconvb = pv("sconvb", "p (l c) -> p l c", l=4)
        dexp = pv("dexp", "p (l d c) -> p l d c", l=4, d=2)
        alog = pv("alog", "p (l d h) -> p l d h", l=4, d=2)
        dtb = pv("dtb", "p (l d h) -> p l d h", l=4, d=2)
        bada = pv("bada", "p (l c) -> p l c", l=4)

        def mm(out, lhsT, rhs, start, stop, rd, wr):
            S.op("pe", lambda e: e.matmul(out, lhsT=lhsT, rhs=rhs, start=start, stop=stop), reads=rd, writes=wr)

        def act(out, in_, func, rd, wr, **kw):
            S.op("act", lambda e: e.activation(out=out, in_=in_, func=func, **kw), reads=rd, writes=wr)

        def tt(eng, out, in0, in1, op, rd, wr):
            S.op(eng, lambda e: e.tensor_tensor(out=out, in0=in0, in1=in1, op=op), reads=rd, writes=wr)

        def ts(eng, out, in0, s1, s2, op0, op1, rd, wr):
            if op1 is None:
                S.op(eng, lambda e: e.tensor_scalar(out=out, in0=in0, scalar1=s1, scalar2=None, op0=op0), reads=rd, writes=wr)
            else:
                S.op(eng, lambda e: e.tensor_scalar(out=out, in0=in0, scalar1=s1, scalar2=s2, op0=op0, op1=op1), reads=rd, writes=wr)

        def stt(eng, out, in0, scalar, in1, op0, op1, rd, wr):
            S.op(eng, lambda e: e.scalar_tensor_tensor(out=out, in0=in0, scalar=scalar, in1=in1, op0=op0, op1=op1), reads=rd, writes=wr)

        def cp(eng, out, in_, rd, wr):
            if eng == "act":
                act(out, in_, AF.Copy, rd, wr)
            else:
                S.op(eng, lambda e: e.tensor_copy(out=out, in_=in_), reads=rd, writes=wr)

        def load(out, in_, rd, wr, eng="sp"):
            return S.dma(eng, lambda e: e.dma_start(out=out, in_=in_), reads=rd, writes=wr)

        def store(out, in_, rd, wr, eng="act"):
            return S.dma(eng, lambda e: e.dma_start(out=out, in_=in_), reads=rd, writes=wr)

        def wload(src2d, n_kc, ncols):
            slot, key = wring.next()
            view = slot[:, 0:n_kc * ncols].rearrange("p (k n) -> p k n", k=n_kc)
            S.dma("pool", lambda e: e.dma_start(out=view, in_=src2d.rearrange("(k p) n -> p k n", p=128)), reads=[], writes=[key])
            return view, key

        def wloadb(src2d, n_kc, ncols):
            slot, key = wring.next()
            view = slot[:, 0:n_kc * ncols].rearrange("p (k n) -> p k n", k=n_kc)
            S.dma("pool", lambda e: e.dma_start(out=view, in_=src2d.rearrange("(k p) n -> p k n", p=128)), reads=[], writes=[key])
            return view, key

        def convert_weights(l, stage_ring):
            wb = WB[l % 2]
            jobs = []
            for c0 in list(range(0, 4096, 512)) + [C_CQ] + list(range(C_G, IN_COLS, 512)):
                jobs.append((w_in[l, :, c0:c0 + 512], wb["win"][:, c0:c0 + 512], 8, 512))
            for c0 in (0, 512):
                jobs.append((w_a_out[l, :, c0:c0 + 512], wb["wa"][:, c0:c0 + 512], 4, 512))
                jobs.append((w_b_out[l, :, c0:c0 + 512], wb["wb"][:, c0:c0 + 512], 8, 512))
                jobs.append((w_c_out[l, :, c0:c0 + 512], wb["wc"][:, c0:c0 + 512], 4, 512))
                jobs.append((w_o[l, :, c0:c0 + 512], wb["wo"][:, c0:c0 + 512], 8, 512))
            for fg in range(6):
                ncol = 512 if fg < 5 else 256
                jobs.append((w_ff1[l, :, fg * 512:fg * 512 + ncol], wb["wf1"][:, fg * 512:fg * 512 + ncol], 8, ncol))
                jobs.append((w_ff3[l, :, fg * 512:fg * 512 + ncol], wb["wf3"][:, fg * 512:fg * 512 + ncol], 8, ncol))
            for cg in range(4):
                jobs.append((w_ff2[l, :, cg * 256:(cg + 1) * 256], wb["wf2"][:, cg * 256:(cg + 1) * 256], 22, 256))
            pend = []
            nst = len(stage_ring.tiles)

            def issue_store(item):
                view, key, dst, n_kc = item
                S.dma("pool", lambda e: e.dma_start(out=dst.rearrange("(k p) n -> p k n", p=128), in_=view), reads=[key], writes=[])

            for (src, dst, n_kc, ncols) in jobs:
                if len(pend) == nst:
                    issue_store(pend.pop(0))
                slot, key = stage_ring.next()
                view = slot[:, 0:n_kc * ncols].rearrange("p (k n) -> p k n", k=n_kc)
                S.dma("pool", lambda e, view=view, src=src: e.dma_start(out=view, in_=src.rearrange("(k p) n -> p k n", p=128)), reads=[], writes=[key])
                pend.append((view, key, dst, n_kc))
            while pend:
                issue_store(pend.pop(0))

        def rstd_from_ssq(ps_ap, n_feat, out_ap, tmp_ap, rd, wr, tmpkey):
            act(tmp_ap, ps_ap, AF.Ln, rd, [tmpkey], scale=1.0 / n_feat, bias=EPS)
            act(out_ap, tmp_ap, AF.Exp, [tmpkey], wr, scale=-0.5)

        with contextlib.ExitStack() as st:
            condt = sb(st, "condt", [128, 8, 2], F32)
            scb = sb(st, "scb", [128, 8, 16], BF16)
            stg0 = Ring([sb(st, "cvs%d" % i, [128, 6144], BF16) for i in range(3)], "cvs")
            convert_weights(0, stg0)
            load(prm[:], prm_d, [], ["prm"])
            load(cstt[:], cst, [], ["cst"])
            load(condt[:], cond, [], ["condt"])
            cp("dve", identb[:], cstt[:, 384:512], ["cst"], ["identb"])
            S.op("pool", lambda e: e.memset(scb[:], 0.0), writes=["scb"])
            act(scb[:, :, 0:2], condt[:], AF.Silu, ["condt", "scb"], ["scb"])
            _ncg = int(os.environ.get("KDBG_NCG", "12"))
            for l in range(nl):
                for cg in range(_ncg):
                    wv, wk = wload(w_ada[l, :, cg * 512:(cg + 1) * 512], 8, 512)
                    for oc in range(4):
                        ps, pk = pring.next()
                        for kc in range(8):
                            mm(ps[:, 0:16], wv[:, kc, oc * 128:(oc + 1) * 128], scb[:, kc, :], kc == 0, kc == 7, [wk, "scb"], [pk])
                        ci = cg * 4 + oc
                        act(MOD[:, l, ci, :], ps[:, 0:2], AF.Identity, [pk, "prm"], ["MOD"], bias=bada[:, l, ci:ci + 1])
            for l in range(nl):
                for (Ax, k0, nw) in ((A1, 8, n1w), (A2, 32, n2w)):
                    ts("dve", Ax[:, l, :, :], MOD[:, l, k0:k0 + 8, :], 1.0, None, ALU.add, None, ["MOD"], ["A"])
                    tt("dve", Ax[:, l, :, :], Ax[:, l, :, :], nw[:, l, :].unsqueeze(2).to_broadcast([128, 8, 2]), ALU.mult, ["A", "prm"], ["A"])
                tt("dve", dsum[:, l, :], dexp[:, l, 0, :], dexp[:, l, 1, :], ALU.add, ["prm"], ["dsum"])
            S.flush()
        if stop == "p0":
            dbgo = dout("dbg_mod", [128, DEPTH * 48 * 2])
            load(dbgo, MOD[:].rearrange("p l c r -> p (l c r)"), ["MOD"], ["dbgo"])
            S.flush(final=True)
            return nc

        def modcol(l, kind, kc, r):
            return MOD[:, l, kind * 8 + kc, r:r + 1]

        def norm_mod(st_tiles, xt, ht, Acol, Bcol, xkey, hkey):
            sqt, rst, lnt, tmpf = st_tiles
            ps, pk = pring.next()
            for kc in range(8):
                act(sqt[:, kc % 2, :], xt[:, kc, :], AF.Square, [xkey], [("sq", kc % 2)])
                mm(ps[:], ones, sqt[:, kc % 2, :], kc == 0, kc == 7, ["cst", ("sq", kc % 2)], [pk])
            rstd_from_ssq(ps[:], 1024.0, rst[:], lnt[:], [pk], ["rst"], "lnt")
            for kc in range(8):
                if Bcol is None:
                    stt("dve", ht[:, kc, :], xt[:, kc, :], Acol(kc), rst[:], ALU.mult, ALU.mult, [xkey, "rst", "A", "prm"], [hkey])
                else:
                    stt("dve", tmpf[:, kc % 2, :], xt[:, kc, :], Acol(kc), rst[:], ALU.mult, ALU.mult, [xkey, "rst", "A", "prm"], [("tmpf", kc % 2)])
                    act(ht[:, kc, :], tmpf[:, kc % 2, :], AF.Identity, [("tmpf", kc % 2), "MOD"], [hkey], bias=Bcol(kc))

        for l in range(nl):
            xsrc = xT0 if l == 0 else XT
            with contextlib.ExitStack() as st:
                xt = sb(st, "xt", [128, 8, 512], F32)
                htr = [sb(st, "ht%d" % i, [128, 8, 512], BF16) for i in range(2)]
                sqt = sb(st, "sqt", [128, 2, 512], F32)
                rst = sb(st, "rst", [128, 512], F32)
                lnt = sb(st, "lnt", [128, 512], F32)
                tmpf = sb(st, "tmpf", [128, 2, 512], F32)
                stg = Ring([sb(st, "stg%d" % i, [128, 4, 512], BF16) for i in range(3)], "stg")
                axt = sb(st, "axt", [128, 4, 512], BF16)
                cqf = sb(st, "cqf", [128, 4, 512], F32)
                nrf = sb(st, "nrf", [128, 2, 512], F32)
                cqn = sb(st, "cqn", [128, 2, 512], BF16)
                ckvn = sb(st, "ckvn", [128, 2, 512], BF16)
                qst = sb(st, "qst", [128, 8, 512], BF16)
                rC = sb(st, "rC", [128, 512], F32)
                rS = sb(st, "rS", [128, 512], F32)
                t1 = sb(st, "t1", [128, 512], F32)
                t2 = sb(st, "t2", [128, 512], F32)
                krt = sb(st, "krt", [128, 512], BF16)
                krf = sb(st, "krf", [128, 512], F32)
                dts = sb(st, "dts", [128, 4, 16], F32)
                S.dma("pool", lambda e, l=l: e.dma_start(out=uqw[:], in_=w_uq2[l].rearrange("(k p) v h r -> p k (v h r)", p=128)), writes=["uqw"])
                S.dma("pool", lambda e, l=l: e.dma_start(out=krw[:], in_=w_kr2[l].rearrange("(k p) v r -> p k (v r)", p=128)), writes=["krw"])
                S.dma("pool", lambda e, l=l: e.dma_start(out=dtw[:], in_=w_in[l, :, C_DT:C_DT + 16].rearrange("(k p) n -> p k n", p=128)), writes=["dtw"])
                S.dma("pool", lambda e, l=l: e.dma_start(out=ukvw[:], in_=w_ukv[l].rearrange("(k p) n -> p k n", p=128)), writes=["ukvw"])
                uq5 = uqw[:].rearrange("p k (v h r) -> p k v h r", v=2, h=8)
                kr4 = krw[:].rearrange("p k (v r) -> p k v r", v=2)
                _steps = os.environ.get("KDBG_P1", "groups,mla,q,kr,dt").split(",")
                _blks = [int(v) for v in os.environ.get("KDBG_BLKS", ",".join(str(i) for i in range(NB))).split(",")]
                def p1_norm(bb):
                    rr = 0 if bb < 8 else 1
                    tt0 = bb * 512
                    hto = htr[bb % 2]
                    load(xt[:], xsrc[:, tt0:tt0 + 512].rearrange("(k p) t -> p k t", p=128), [], ["xt"])
                    norm_mod((sqt, rst, lnt, tmpf), xt, hto,
                             lambda kc: A1[:, l, kc, rr:rr + 1], lambda kc: modcol(l, 0, kc, rr), "xt", ("ht", bb % 2))
                    store(HT[:, tt0:tt0 + 512].rearrange("(k p) t -> p k t", p=128), hto[:], [("ht", bb % 2)], [])

                p1_norm(_blks[0])
                for bi, b in enumerate(_blks):
                    r = 0 if b < 8 else 1
                    smp = b < 8
                    t0 = b * 512
                    ht = htr[b % 2]
                    htk = ("ht", b % 2)
                    if smp:
                        load(rC[64:96, :], ropeC[:, t0:t0 + 512], [], ["rC"])
                        load(rS[64:96, :], ropeS[:, t0:t0 + 512], [], ["rS"])
                    groups = [("ax", C_AX), ("ab", C_AB), ("ac", C_AC), ("z", C_Z), ("z", C_Z + 512),
                              ("xbc", C_XBC), ("xbc", C_XBC + 512), ("xbc", C_XBC + 1024), ("cqkv", C_CQ)]
                    if "groups" not in _steps:
                        groups = []
                    if not groups and bi + 1 < len(_blks):
                        p1_norm(_blks[bi + 1])
                    for gi, (kind, c0) in enumerate(groups):
                        if gi == 3 and bi + 1 < len(_blks):
                            p1_norm(_blks[bi + 1])
                        wv, wk = wloadb(WB[l % 2]["win"][:, c0:c0 + 512], 8, 512)
                        if kind in ("ab", "ac", "z", "xbc"):
                            sg, sk = stg.next()
                        for oc in range(4):
                            ps, pk = pring.next()
                            for kc in range(8):
                                mm(ps[:], wv[:, kc, oc * 128:(oc + 1) * 128], ht[:, kc, :], kc == 0, kc == 7, [wk, htk], [pk])
                            if kind == "ax":
                                cp("act", axt[:, oc, :], ps[:], [pk], ["axt"])
                            elif kind == "ab":
                                cp("act", sg[:, oc, :], ps[:], [pk], [sk])
                            elif kind == "ac":
                                tt("dve", sg[:, oc, :], ps[:], axt[:, oc, :], ALU.mult, [pk, "axt"], [sk])
                            elif kind == "z":
                                act(sg[:, oc, :], ps[:], AF.Silu, [pk], [sk])
                            elif kind == "xbc":
                                cp("act" if oc % 2 == 0 else "dve", sg[:, oc, :], ps[:], [pk], [sk])
                            else:
                                cp("act" if oc % 2 == 0 else "dve", cqf[:, oc, :], ps[:], [pk], [("cqf", oc // 2)])
                        if kind in ("ab", "ac", "z", "xbc"):
                            dst = {"ab": ABT, "ac": UT, "z": ZT, "xbc": XBCT}[kind]
                            r0 = c0 - {"ab": C_AB, "ac": C_AC, "z": C_Z, "xbc": C_XBC}[kind]
                            store(dst[r0:r0 + 512, t0:t0 + 512].rearrange("(k p) t -> p k t", p=128), sg[:], [sk], [(kind + "T", b, r0)])
                    for half, nw, dstb in ((0, qnw, cqn), (1, kvnw, ckvn)) if "mla" in _steps else ():
                        ps, pk = pring.next()
                        for j in range(2):
                            act(sqt[:, j, :], cqf[:, half * 2 + j, :], AF.Square, [("cqf", half)], [("sq", j)])
                            mm(ps[:], ones, sqt[:, j, :], j == 0, j == 1, ["cst", ("sq", j)], [pk])
                        rstd_from_ssq(ps[:], 256.0, rst[:], lnt[:], [pk], ["rst"], "lnt")
                        for j in range(2):
                            stt("dve", nrf[:, j, :], cqf[:, half * 2 + j, :], nw[:, l, j:j + 1], rst[:], ALU.mult, ALU.mult,
                                [("cqf", half), "rst", "prm"], [("nrf", j)])
                            cp("act", dstb[:, j, :], nrf[:, j, :], [("nrf", j)], [("lat", half)])
                        if half == 1:
                            store(CKVT[:, t0:t0 + 512].rearrange("(k p) t -> p k t", p=128), ckvn[:], [("lat", 1)], [("CKVT", b)])
                            if not smp:
                                store(nckvT[l].rearrange("(k p) t -> p k t", p=128), nrf[:], [("nrf", 0), ("nrf", 1)], [("nckv", l)])
                    for h in range(8) if "q" in _steps else ():
                        psn, pkn = pring.next()
                        for kc in range(2):
                            mm(psn[0:96, :], uq5[:, kc, 0, h, :], cqn[:, kc, :], kc == 0, kc == 1, ["uqw", ("lat", 0)], [pkn])
                        if smp:
                            pss, pks = pring.next()
                            for kc in range(2):
                                mm(pss[0:96, :], uq5[:, kc, 1, h, :], cqn[:, kc, :], kc == 0, kc == 1, ["uqw", ("lat", 0)], [pks])
                            cp("act", qst[0:64, h, :], psn[0:64, :], [pkn], [("qst", h)])
                            tt("dve", t1[64:96, :], psn[64:96, :], rC[64:96, :], ALU.mult, [pkn, "rC"], ["t1"])
                            tt("dve", t2[64:96, :], pss[64:96, :], rS[64:96, :], ALU.mult, [pks, "rS"], ["t2"])
                            tt("dve", qst[64:96, h, :], t1[64:96, :], t2[64:96, :], ALU.add, ["t1", "t2"], [("qst", h)])
                        else:
                            cp("act", qst[0:96, h, :], psn[0:96, :], [pkn], [("qst", h)])
                    if "q" in _steps:
                        store(QT[:, :, t0:t0 + 512].rearrange("h r t -> r h t"), qst[0:96, :, :], [("qst", h) for h in range(8)], [("QT", b)])
                    if "kr" not in _steps:
                        continue
                    psn, pkn = pring.next()
                    for kc in range(8):
                        mm(psn[0:96, :], kr4[:, kc, 0, :], ht[:, kc, :], kc == 0, kc == 7, ["krw", htk], [pkn])
                    if smp:
                        pss, pks = pring.next()
                        for kc in range(8):
                            mm(pss[0:96, :], kr4[:, kc, 1, :], ht[:, kc, :], kc == 0, kc == 7, ["krw", htk], [pks])
                        tt("dve", t1[64:96, :], psn[64:96, :], rC[64:96, :], ALU.mult, [pkn, "rC"], ["t1"])
                        tt("dve", t2[64:96, :], pss[64:96, :], rS[64:96, :], ALU.mult, [pks, "rS"], ["t2"])
                        tt("dve", krt[64:96, :], t1[64:96, :], t2[64:96, :], ALU.add, ["t1", "t2"], ["krt"])
                    else:
                        cp("dve", krf[64:96, :], psn[64:96, :], [pkn], ["krf"])
                        cp("act", krt[64:96, :], krf[64:96, :], ["krf"], ["krt"])
                        store(nkrT[l], krf[64:96, :], ["krf"], [("nkr", l)])
                    store(KRT[:, t0:t0 + 512], krt[64:96, :], ["krt"], [("KRT", b)])
                    if "dt" not in _steps:
                        continue
                    ps, pk = pring.next()
                    for tl in range(4):
                        for kc in range(8):
                            mm(ps[:, tl * 16:(tl + 1) * 16], ht[:, kc, tl * 128:(tl + 1) * 128], dtw[:, kc, :], kc == 0, kc == 7, ["dtw", htk], [pk])
                    cp("dve", dts[:].rearrange("p a h -> p (a h)"), ps[:, 0:64], [pk], ["dts"])
                    store(DTT[t0:t0 + 512, :].rearrange("(a p) h -> p a h", p=128), dts[:], ["dts"], [("DTT", b)])
                S.flush()
            if stop == "p1":
                break
            with contextlib.ExitStack() as st:
                ubr = Ring([sb(st, "ub%d" % i, [128, 4, 514], BF16) for i in range(2)], "ub")
                abtr = Ring([sb(st, "abt%d" % i, [128, 4, 512], BF16) for i in range(2)], "abt")
                acc = sb(st, "acc", [128, 4, 512], F32)
                vat = sb(st, "vat", [128, 4, 512], BF16)
                xbr = Ring([sb(st, "xb%d" % i, [128, 12, 514], BF16) for i in range(2)], "xb")
                xscr = Ring([sb(st, "xsc%d" % i, [128, 12, 512], BF16) for i in range(2)], "xsc")
                xtkr = Ring([sb(st, "xtk%d" % i, [128, 1024], BF16) for i in range(2)], "xtk")
                btkr = Ring([sb(st, "btk%d" % i, [128, 256], BF16) for i in range(2)], "btk")
                segs = [(b * 512, 512, 0, TS) for b in range(8)] + [(TS, 256, TS, TS + 256), (TS + 256, 256, TS + 256, T)]
                for (t0, n, s0, s1) in segs:
                    lo, hi = max(t0 - 1, s0), min(t0 + n + 1, s1)
                    off = lo - (t0 - 1)
                    ub, ubk = ubr.next()
                    abt, abk = abtr.next()
                    xb, xbk = xbr.next()
                    xsc, xsk = xscr.next()
                    if lo == t0:
                        S.op("pool", lambda e, ub=ub: e.memset(ub[:, :, 0:1], 0.0), writes=[ubk])
                        S.op("pool", lambda e, xb=xb: e.memset(xb[:, :, 0:1], 0.0), writes=[xbk])
                    if hi == t0 + n:
                        S.op("pool", lambda e, n=n, ub=ub: e.memset(ub[:, :, n + 1:n + 2], 0.0), writes=[ubk])
                        S.op("pool", lambda e, n=n, xb=xb: e.memset(xb[:, :, n + 1:n + 2], 0.0), writes=[xbk])
                    load(ub[:, :, off:off + hi - lo], UT[:, lo:hi].rearrange("(k p) t -> p k t", p=128), [], [ubk])
                    load(abt[:, :, 0:n], ABT[:, t0:t0 + n].rearrange("(k p) t -> p k t", p=128), [], [abk])
                    load(xb[:, :, off:off + hi - lo], XBCT[:, lo:hi].rearrange("(k p) t -> p k t", p=128), [], [xbk])
                    for c in range(16):
                        src, cw, ci, skey = (ub, aconv, c, ubk) if c < 4 else (xb, sconvw, c - 4, xbk)
                        a = acc[:, c % 4, 0:n]
                        ak = ("acc", c % 4)
                        if c < 4:
                            act(a, src[:, ci, 1:n + 1], AF.Copy, [skey, "prm"], [ak], scale=cw[:, l, 1, ci:ci + 1])
                        else:
                            act(a, src[:, ci, 1:n + 1], AF.Identity, [skey, "prm"], [ak], scale=cw[:, l, 1, ci:ci + 1], bias=sconvb[:, l, ci:ci + 1])
                        stt("dve", a, src[:, ci, 0:n], cw[:, l, 0, ci:ci + 1], a, ALU.mult, ALU.add, [skey, "prm", ak], [ak])
                        stt("dve", a, src[:, ci, 2:n + 2], cw[:, l, 2, ci:ci + 1], a, ALU.mult, ALU.add, [skey, "prm", ak], [ak])
                        if c < 4:
                            tt("dve", vat[:, ci, 0:n], a, abt[:, ci, 0:n], ALU.mult, [ak, abk], ["vat"])
                        else:
                            act(xsc[:, ci, 0:n], a, AF.Silu, [ak], [xsk])
                    store(VAT[:, t0:t0 + n].rearrange("(k p) t -> p k t", p=128), vat[:, :, 0:n], ["vat"], [])
                    store(XSC[:, t0:t0 + n].rearrange("(k p) t -> p k t", p=128), xsc[:, :, 0:n], [xsk], [])
                    for tl in range(n // 128):
                        xtk, xtkk = xtkr.next()
                        btk, btkk = btkr.next()
                        for c in range(8):
                            S.op("pe", lambda e, c=c, tl=tl, xsc=xsc: e.transpose(psbT[:, c * 128:(c + 1) * 128], xsc[:, c, tl * 128:(tl + 1) * 128], identb[:]),
                                 reads=[xsk, "identb"], writes=["psbT"])
                        cp("act", xtk[:], psbT[:, 0:1024], ["psbT"], [xtkk])
                        store(XTOK[t0 + tl * 128:t0 + (tl + 1) * 128, :], xtk[:], [xtkk], [])
                        for c in range(2):
                            S.op("pe", lambda e, c=c, tl=tl, xsc=xsc: e.transpose(psbT[:, c * 128:(c + 1) * 128], xsc[:, 8 + c, tl * 128:(tl + 1) * 128], identb[:]),
                                 reads=[xsk, "identb"], writes=["psbT"])
                        cp("dve", btk[:], psbT[:, 0:256], ["psbT"], [btkk])
                        store(BTOK[t0 + tl * 128:t0 + (tl + 1) * 128, :], btk[:], [btkk], [])
                S.flush()
            if stop == "conv":
                break
            with contextlib.ExitStack() as st:
                dtr = sb(st, "dtr", [128, NCH, 16], F32)
                ea = sb(st, "ea", [128, 2, 16], F32)
                dtd = [sb(st, "dtd%d" % d, [128, NCH, 16], F32) for d in range(2)]
                dta = [sb(st, "dta%d" % d, [128, NCH, 16], F32) for d in range(2)]
                ctk = [sb(st, "ctk%d" % d, [128, NCH, 16], F32) for d in range(2)]
                tot = [sb(st, "tot%d" % d, [128, NCH, 16], F32) for d in range(2)]
                ted = [sb(st, "ted%d" % d, [128, NCH, 16], F32) for d in range(2)]
                cdc = [sb(st, "cdc%d" % d, [128, NCH, 16], F32) for d in range(2)]
                lndt = [sb(st, "lndt%d" % d, [128, NCH, 16], F32) for d in range(2)]
                tri = [triF, triB]
                load(dtr[:], DTT.rearrange("(c p) h -> p c h", p=128), [], ["dtr"])
                act(ea[:], alog[:, l, :, :], AF.Exp, ["prm"], ["ea"])
                for d in range(2):
                    tt("dve", dtd[d][:], dtr[:], dtb[:, l, d, :].unsqueeze(1).to_broadcast([128, NCH, 16]), ALU.add, ["dtr", "prm"], [("dtd", d)])
                    act(dtd[d][:], dtd[d][:], AF.Exp, [("dtd", d)], [("dtd", d)])
                    act(dtd[d][:], dtd[d][:], AF.Ln, [("dtd", d)], [("dtd", d)], bias=1.0)
                    act(lndt[d][:], dtd[d][:], AF.Ln, [("dtd", d)], [("lndt", d)])
                    stt("dve", dta[d][:], dtd[d][:], -1.0, ea[:, d, :].unsqueeze(1).to_broadcast([128, NCH, 16]), ALU.mult, ALU.mult,
                        [("dtd", d), "ea"], [("dta", d)])
                    flat = dta[d][:].rearrange("p c h -> p (c h)")
                    for (lhs, dstt, dk) in ((tri[d], ctk[d], "ctk"), (ones, tot[d], "tot")):
                        dflat = dstt[:].rearrange("p c h -> p (c h)")
                        for (a0, a1) in ((0, 512), (512, NCH * 16)):
                            ps, pk = pring.next()
                            mm(ps[:, 0:a1 - a0], lhs, flat[:, a0:a1], True, True, ["cst", ("dta", d)], [pk])
                            cp("dve", dflat[:, a0:a1], ps[:, 0:a1 - a0], [pk], [(dk, d)])
                    tt("dve", ted[d][:], tot[d][:], ctk[d][:], ALU.subtract, [("tot", d), ("ctk", d)], [("ted", d)])
                    act(ted[d][:], ted[d][:], AF.Exp, [("ted", d)], [("ted", d)])
                    tt("dve", ted[d][:], ted[d][:], dtd[d][:], ALU.mult, [("ted", d), ("dtd", d)], [("ted", d)])
                    act(cdc[d][:], tot[d][:], AF.Exp, [("tot", d)], [("cdc", d)])
                with contextlib.ExitStack() as st2:
                    St = sb(st2, "St", [128, 2, 512], F32)
                    hbr = Ring([sb(st2, "hb%d" % i, [128, 1024], BF16) for i in range(2)], "hb")
                    xkr = Ring([sb(st2, "xk%d" % i, [128, 1024], BF16) for i in range(2)], "xk")
                    bkr = Ring([sb(st2, "bk%d" % i, [128, 256], BF16) for i in range(2)], "bk")
                    xwr = Ring([sb(st2, "xw%d" % i, [128, 1024], BF16) for i in range(2)], "xw")
                    for d in range(2):
                        for si, (s0, slen, smp) in enumerate(SEQS):
                            nchk, c0 = slen // 128, s0 // 128
                            if smp:
                                load(St[:], h0[l, d].rearrange("n (g f) -> n g f", g=2), [], ["St"])
                            else:
                                S.op("pool", lambda e: e.memset(St[:], 0.0), writes=["St"])
                            order = range(nchk) if d == 0 else range(nchk - 1, -1, -1)
                            for ci in order:
                                c = c0 + ci
                                hb, hk = hbr.next()
                                cp("act", hb[:], St[:].rearrange("p g f -> p (g f)"), ["St"], [hk])
                                store(HENT[d, c], hb[:], [hk], [])
                                xk, xkk = xkr.next()
                                bk, bkk = bkr.next()
                                xw, xwk = xwr.next()
                                load(xk[:], XTOK[c * 128:(c + 1) * 128, :], [], [xkk])
                                load(bk[:], BTOK[c * 128:(c + 1) * 128, :], [], [bkk])
                                tt("dve", xw[:].rearrange("p (h q) -> p h q", h=16), xk[:].rearrange("p (h q) -> p h q", h=16),
                                   ted[d][:, c, :].unsqueeze(2).to_broadcast([128, 16, 64]), ALU.mult, [xkk, ("ted", d)], [xwk])
                                for g in range(2):
                                    ps, pk = pring.next()
                                    mm(ps[:], bk[:, g * 128:(g + 1) * 128], xw[:, g * 512:(g + 1) * 512], True, True, [bkk, xwk], [pk])
                                    sg3 = St[:, g, :].rearrange("p (h q) -> p h q", h=8)
                                    tt("dve", sg3, sg3, cdc[d][:, c, g * 8:(g + 1) * 8].unsqueeze(2).to_broadcast([128, 8, 64]), ALU.mult,
                                       ["St", ("cdc", d)], ["St"])
                                    tt("dve", St[:, g, :], St[:, g, :], ps[:], ALU.add, ["St", pk], ["St"])
                            if not smp:
                                store(nssm[l, d, si - 1], St[:].rearrange("p g f -> p (g f)"), ["St"], [])
                    S.flush()
                if stop == "ssd1":
                    break
                with contextlib.ExitStack() as st2:
                    xs3r = Ring([sb(st2, "xs3%d" % i, [128, 12, 128], BF16) for i in range(2)], "xs3")
                    xk2r = Ring([sb(st2, "xk2%d" % i, [128, 1024], BF16) for i in range(2)], "xk2")
                    her = [Ring([sb(st2, "he%d%d" % (d, i), [128, 1024], BF16) for i in range(2)], "he%d" % d) for d in range(2)]
                    zsr = Ring([sb(st2, "zs%d" % i, [128, 8, 128], BF16) for i in range(2)], "zs")
                    cbm = [sb(st2, "cbm%d" % d, [128, 2, 128], F32) for d in range(2)]
                    crs = Ring([sb(st2, "crs%d" % i, [128, 512], F32) for i in range(4)], "crs")
                    arg = Ring([sb(st2, "arg%d" % i, [128, 512], F32) for i in range(4)], "arg")
                    Wt = [sb(st2, "Wt%d" % d, [128, 16, 128], BF16) for d in range(2)]
                    Csc = [sb(st2, "Csc%d" % d, [128, 16, 128], BF16) for d in range(2)]
                    yg = sb(st2, "yg", [128, 8, 128], F32)
                    sqy = sb(st2, "sqy", [128, 2, 128], F32)
                    rsy = sb(st2, "rsy", [128, 128], F32)
                    lny = sb(st2, "lny", [128, 128], F32)
                    ynr = Ring([sb(st2, "yn%d" % i, [128, 8, 128], BF16) for i in range(2)], "yn")
                    cumTr = Ring([sb(st2, "cumT%d" % i, [128, 128], F32) for i in range(2)], "cumT")
                    Wtr = [Ring([Wt[d], sb(st2, "Wtb%d" % d, [128, 16, 128], BF16)], "Wt%d" % d) for d in range(2)]
                    Cscr = [Ring([Csc[d], sb(st2, "Cscb%d" % d, [128, 16, 128], BF16)], "Csc%d" % d) for d in range(2)]

                    def stageA(c):
                        tk0 = c * 128
                        xs3, xs3k = xs3r.next()
                        xk2, xk2k = xk2r.next()
                        zs, zsk = zsr.next()
                        load(xs3[:], XSC[:, tk0:tk0 + 128].rearrange("(k p) t -> p k t", p=128), [], [xs3k])
                        load(xk2[:], XTOK[tk0:tk0 + 128, :], [], [xk2k])
                        load(zs[:], ZT[:, tk0:tk0 + 128].rearrange("(k p) t -> p k t", p=128), [], [zsk])
                        he = []
                        for d in range(2):
                            t_, k_ = her[d].next()
                            load(t_[:], HENT[d, c], [], [k_])
                            he.append((t_, k_))
                        psA, pkA = pring.next()
                        for g in range(2):
                            mm(psA[:, g * 128:(g + 1) * 128], xs3[:, 8 + g, :], xs3[:, 10 + g, :], True, True, [xs3k], [pkA])
                        for d in range(2):
                            tt("dve", cbm[d][:], psA[:, 0:256].rearrange("p (g i) -> p g i", g=2), tri[d].unsqueeze(1).to_broadcast([128, 2, 128]),
                               ALU.mult, [pkA, "cst"], [("cbm", d)])
                        WC = []
                        for d in range(2):
                            wt_, wtk = Wtr[d].next()
                            cs_, csk = Cscr[d].next()
                            WC.append((wt_, wtk, cs_, csk))
                            psT, pkT = pring.next()
                            mm(psT[0:16, 0:128], ctk[d][:, c, :], identF, True, True, [("ctk", d), "cst"], [pkT])
                            cT, cTk = cumTr.next()
                            cp("act", cT[0:16, :], psT[0:16, 0:128], [pkT], [cTk])
                            qs = []
                            for q in range(4):
                                psc, pkc = pring.next()
                                for hh in range(4):
                                    mm(psc[:, hh * 128:(hh + 1) * 128], sel[:, 4 * q + hh, :], cT[0:16, :], True, True, ["cst", cTk], [pkc])
                                crn, arn = crs.next(), arg.next()
                                qs.append((psc, pkc, crn, arn, arn, crn))
                            for q in range(4):
                                psc, pkc, (cr, crk), (ar, ark), _, _ = qs[q]
                                cp("dve", cr[:], psc[:], [pkc], [crk])
                                for hh in range(4):
                                    ts("dve", ar[:, hh * 128:(hh + 1) * 128], psc[:, hh * 128:(hh + 1) * 128], ctk[d][:, c, 4 * q + hh:4 * q + hh + 1], 0.0,
                                       ALU.subtract, ALU.min, [pkc, ("ctk", d)], [ark])
                            for q in range(4):
                                _, _, (cr, crk), (ar, ark), (et, etk), (ec, eck) = qs[q]
                                for hh in range(4):
                                    act(et[:, hh * 128:(hh + 1) * 128], ar[:, hh * 128:(hh + 1) * 128], AF.Exp, [ark, ("lndt", d)], [etk],
                                        bias=lndt[d][:, c, 4 * q + hh:4 * q + hh + 1])
                                act(ec[:], cr[:], AF.Exp, [crk], [eck])
                            for q in range(4):
                                g = q // 2
                                _, _, _, _, (et, etk), (ec, eck) = qs[q]
                                tt("dve", wt_[:, 4 * q:4 * q + 4, :], et[:].rearrange("p (h i) -> p h i", h=4),
                                   cbm[d][:, g, :].unsqueeze(1).to_broadcast([128, 4, 128]), ALU.mult, [etk, ("cbm", d)], [wtk])
                                tt("dve", cs_[:, 4 * q:4 * q + 4, :], ec[:].rearrange("p (h i) -> p h i", h=4),
                                   xs3[:, 10 + g, :].unsqueeze(1).to_broadcast([128, 4, 128]), ALU.mult, [eck, xs3k], [csk])
                        return dict(c=c, xs3=(xs3, xs3k), xk2=(xk2, xk2k), zs=(zs, zsk), he=he, WC=WC)

                    def stageB(cx):
                        c = cx["c"]
                        tk0 = c * 128
                        xs3, xs3k = cx["xs3"]
                        xk2, xk2k = cx["xk2"]
                        zs, zsk = cx["zs"]
                        he, WC = cx["he"], cx["WC"]
                        psY = [plong.next(), plong.next()]
                        for h in range(16):
                            kc, half = h // 2, h % 2
                            pY, pYk = psY[kc // 4]
                            out = pY[half * 64:(half + 1) * 64, (kc % 4) * 128:(kc % 4 + 1) * 128]
                            for d in range(2):
                                wt_, wtk, cs_, csk = WC[d]
                                mm(out, xk2[:, h * 64:(h + 1) * 64], wt_[:, h, :], d == 0, False, [xk2k, wtk], [pYk])
                                mm(out, he[d][0][:, h * 64:(h + 1) * 64], cs_[:, h, :], False, d == 1, [he[d][1], csk], [pYk])
                        for kc in range(8):
                            pY, pYk = psY[kc // 4]
                            stt("dve", yg[:, kc, :], xs3[:, kc, :], dsum[:, l, kc:kc + 1], pY[:, (kc % 4) * 128:(kc % 4 + 1) * 128], ALU.mult, ALU.add,
                                [xs3k, "dsum", pYk], ["yg"])
                        tt("dve", yg[:], yg[:], zs[:], ALU.mult, ["yg", zsk], ["yg"])
                        ps, pk = pring.next()
                        for kc in range(8):
                            act(sqy[:, kc % 2, :], yg[:, kc, :], AF.Square, ["yg"], [("sqy", kc % 2)])
                            mm(ps[:, 0:128], ones, sqy[:, kc % 2, :], kc == 0, kc == 7, ["cst", ("sqy", kc % 2)], [pk])
                        rstd_from_ssq(ps[:, 0:128], 1024.0, rsy[:], lny[:], [pk], ["rsy"], "lny")
                        yn, ynk = ynr.next()
                        for kc in range(8):
                            stt("dve", yn[:, kc, :], yg[:, kc, :], snw[:, l, kc:kc + 1], rsy[:], ALU.mult, ALU.mult, ["yg", "rsy", "prm"], [ynk])
                        store(YBT[:, tk0:tk0 + 128].rearrange("(k p) t -> p k t", p=128), yn[:], [ynk], [])

                    ctxs = {0: stageA(0)}
                    for c in range(NCH):
                        if c + 1 < NCH:
                            ctxs[c + 1] = stageA(c + 1)
                        stageB(ctxs.pop(c))
                    S.flush()
            if stop == "ssd":
                break
            with contextlib.ExitStack() as st:
                ckvall = sb(st, "ckvall", [128, 2, 4608], BF16)
                KTr = [sb(st, "KT%d" % i, [128, 4608], BF16) for i in range(2)]
                VA = [sb(st, "VA%d" % i, [128, 36, 128], BF16) for i in range(2)]
                qhr = Ring([sb(st, "qh%d" % i, [128, 4096], BF16) for i in range(2)], "qh")
                PT = Ring([sb(st, "PT%d" % i, [128, 512], BF16) for i in range(3)], "PT")
                Lt = sb(st, "Lt", [128, 512], F32)
                Rt = sb(st, "Rt", [128, 512], F32)
                ATr = Ring([sb(st, "AT%d" % i, [128, 512], BF16) for i in range(2)], "AT")
                S.op("pool", lambda e: e.memset(VA[0][:, :, 64:128], 1.0), writes=[("VA", 0)])
                S.op("pool", lambda e: e.memset(VA[1][:, :, 0:64], 1.0), writes=[("VA", 1)])
                S.dma("pool", lambda e: e.dma_start(out=ckvall[:, :, 0:512], in_=cckvT[l].rearrange("(k p) t -> p k t", p=128)), writes=["ckvall"])
                for i in range(2):
                    S.dma("pool", lambda e, i=i: e.dma_start(out=KTr[i][64:96, 0:512], in_=ckrT[l]), writes=[("KT", i)])
                if l + 1 < nl:
                    stgA = Ring([sb(st, "cva%d" % i, [128, 6144], BF16) for i in range(3)], "cva")
                    convert_weights(l + 1, stgA)
                for (nctx, k0, nlat, q0, nq, qblk) in ((512, 0, TS, 0, TS, 512), (0, TS, 256, TS, 256, 256), (0, TS + 256, 256, TS + 256, 256, 256)):
                    nk = nctx + nlat
                    ntile = nk // 128
                    load(ckvall[:, :, nctx:nk], CKVT[:, k0:k0 + nlat].rearrange("(k p) t -> p k t", p=128), [], ["ckvall"])
                    for i in range(2):
                        load(KTr[i][64:96, nctx:nk], KRT[:, k0:k0 + nlat], [], [("KT", i)])
                    for h in range(8):
                        par = h % 2
                        va, vak = VA[par], ("VA", par)
                        KT, ktk = KTr[par], ("KT", par)
                        voff = par * 64
                        for kb in range((nk + 511) // 512):
                            w = min(512, nk - kb * 512)
                            ps, pk = pring.next()
                            for kc in range(2):
                                mm(ps[0:64, 0:w], ukvw[:, kc, h * 128:h * 128 + 64], ckvall[:, kc, kb * 512:kb * 512 + w], kc == 0, kc == 1, ["ukvw", "ckvall"], [pk])
                            cp("act" if kb % 2 == 0 else "dve", KT[0:64, kb * 512:kb * 512 + w], ps[0:64, 0:w], [pk], [ktk])
                        for tg in range(0, ntile, 8):
                            nt = min(8, ntile - tg)
                            ps, pk = pring.next()
                            for j in range(nt):
                                for kc in range(2):
                                    mm(ps[:, j * 64:(j + 1) * 64], ckvall[:, kc, (tg + j) * 128:(tg + j + 1) * 128], ukvw[:, kc, h * 128 + 64:h * 128 + 128],
                                       kc == 0, kc == 1, ["ukvw", "ckvall"], [pk])
                            cp("dve", va[:, tg:tg + nt, voff:voff + 64], ps[:, 0:nt * 64].rearrange("p (t d) -> p t d", d=64), [pk], [vak])
                        qh, qhk = qhr.next()
                        load(qh[0:96, 0:nq], QT[h, :, q0:q0 + nq], [], [qhk])
                        items = [(qb, t) for qb in range(nq // qblk) for t in range(ntile)]
                        SKEW = 2
                        pend = {}
                        cur = {}
                        for i in range(len(items) + SKEW):
                            if i < len(items):
                                qb, t = items[i]
                                psS, pkS = pring.next()
                                mm(psS[:, 0:qblk], KT[0:96, t * 128:(t + 1) * 128], qh[0:96, qb * qblk:(qb + 1) * qblk], True, True, [ktk, qhk], [pkS])
                                pend[i] = (psS, pkS)
                            if i >= SKEW:
                                qb, t = items[i - SKEW]
                                psS, pkS = pend.pop(i - SKEW)
                                if t == 0:
                                    cur[qb] = plong.next()
                                psO, pkO = cur[qb]
                                pt, ptk = PT.next()
                                act(pt[:, 0:qblk], psS[:, 0:qblk], AF.Exp, [pkS], [ptk], scale=SCALE)
                                mm(psO[:, 0:qblk], va[:, t, :], pt[:, 0:qblk], t == 0, t == ntile - 1, [vak, ptk], [pkO])
                                if t == ntile - 1:
                                    orow, drow = voff, 64 - voff
                                    act(Lt[orow:orow + 64, 0:qblk], psO[drow:drow + 64, 0:qblk], AF.Ln, [pkO], ["Lt"])
                                    act(Rt[orow:orow + 64, 0:qblk], Lt[orow:orow + 64, 0:qblk], AF.Exp, ["Lt"], ["Rt"], scale=-1.0)
                                    at, atk = ATr.next()
                                    tt("dve", at[orow:orow + 64, 0:qblk], psO[orow:orow + 64, 0:qblk], Rt[orow:orow + 64, 0:qblk], ALU.mult, [pkO, "Rt"], [atk])
                                    r0 = (h // 2) * 128 + orow
                                    store(ATT[r0:r0 + 64, q0 + qb * qblk:q0 + (qb + 1) * qblk], at[orow:orow + 64, 0:qblk], [atk], [])
                S.flush()
            if stop == "attn":
                break
            with contextlib.ExitStack() as st:
                xt = sb(st, "xt3", [128, 8, 512], F32)
                ht = sb(st, "ht3", [128, 8, 512], BF16)
                va_ = sb(st, "va3", [128, 4, 512], BF16)
                yb_ = sb(st, "yb3", [128, 8, 512], BF16)
                at_ = sb(st, "at3", [128, 4, 512], BF16)
                macc = sb(st, "macc", [128, 4, 512], F32)
                sigr = Ring([sb(st, "sig%d" % i, [128, 512], F32) for i in range(2)], "sig")
                tmr = Ring([sb(st, "tm%d" % i, [128, 512], F32) for i in range(2)], "tm")
                merged = sb(st, "merged", [128, 8, 512], BF16)
                h2 = sb(st, "h2", [128, 8, 512], BF16)
                gt = sb(st, "gt", [128, 22, 512], BF16)
                sar = Ring([sb(st, "sa%d" % i, [128, 512], F32) for i in range(2)], "sa")
                sqt = sb(st, "sqt3", [128, 2, 512], F32)
                rst = sb(st, "rst3", [128, 512], F32)
                lnt = sb(st, "lnt3", [128, 512], F32)
                tmpf = sb(st, "tmpf3", [128, 2, 512], F32)
                last = (l == nl - 1)
                yo = sb(st, "yo", [128, 8, 512], F32) if last else None
                for b in range(NB):
                    r = 0 if b < 8 else 1
                    t0 = b * 512
                    fm = lambda dr: dr[:, t0:t0 + 512].rearrange("(k p) t -> p k t", p=128)
                    def p3_loads(bb):
                        f2 = lambda dr: dr[:, bb * 512:(bb + 1) * 512].rearrange("(k p) t -> p k t", p=128)
                        load(ht[:], f2(HT), [], ["ht"])
                        load(va_[:], f2(VAT), [], ["va_"])
                        load(yb_[:], f2(YBT), [], ["yb_"])
                        load(at_[:], f2(ATT), [], ["at_"])

                    if b == 0:
                        p3_loads(0)
                    load(xt[:], fm(xsrc), [], ["xt"])
                    for cg in range(2):
                        for br, (wsrc, nkc, rt_, rk) in enumerate((("wa", 4, va_, "va_"), ("wb", 8, yb_, "yb_"), ("wc", 4, at_, "at_"))):
                            wy, wyk = wloadb(WB[l % 2][wsrc][:, cg * 512:(cg + 1) * 512], nkc, 512)
                            c0 = C_G + br * 1024 + cg * 512
                            wg, wgk = wloadb(WB[l % 2]["win"][:, c0:c0 + 512], 8, 512)
                            for oc in range(4):
                                psYy, pky = pring.next()
                                for kc in range(nkc):
                                    mm(psYy[:], wy[:, kc, oc * 128:(oc + 1) * 128], rt_[:, kc, :], kc == 0, kc == nkc - 1, [wyk, rk], [pky])
                                psG, pkg = pring.next()
                                for kc in range(8):
                                    mm(psG[:], wg[:, kc, oc * 128:(oc + 1) * 128], ht[:, kc, :], kc == 0, kc == 7, [wgk, "ht"], [pkg])
                                sg, sgk = sigr.next()
                                act(sg[:], psG[:], AF.Sigmoid, [pkg], [sgk])
                                if br == 0:
                                    tt("dve", macc[:, oc, :], psYy[:], sg[:], ALU.mult, [pky, sgk], [("macc", oc)])
                                else:
                                    tm, tmk = tmr.next()
                                    tt("dve", tm[:], psYy[:], sg[:], ALU.mult, [pky, sgk], [tmk])
                                    if br == 1:
                                        tt("dve", macc[:, oc, :], macc[:, oc, :], tm[:], ALU.add, [("macc", oc), tmk], [("macc", oc)])
                                    else:
                                        tt("dve", merged[:, cg * 4 + oc, :], macc[:, oc, :], tm[:], ALU.add, [("macc", oc), tmk], ["merged"])
                    if b + 1 < NB:
                        p3_loads(b + 1)
                    for cg in range(2):
                        wo, wok = wloadb(WB[l % 2]["wo"][:, cg * 512:(cg + 1) * 512], 8, 512)
                        for oc in range(4):
                            ps, pk = pring.next()
                            for kc in range(8):
                                mm(ps[:], wo[:, kc, oc * 128:(oc + 1) * 128], merged[:, kc, :], kc == 0, kc == 7, [wok, "merged"], [pk])
                            o8 = cg * 4 + oc
                            stt("dve", xt[:, o8, :], ps[:], modcol(l, 2, o8, r), xt[:, o8, :], ALU.mult, ALU.add, [pk, "MOD", "xt"], ["xt"])
                    norm_mod((sqt, rst, lnt, tmpf), xt, h2, lambda kc: A2[:, l, kc, r:r + 1], lambda kc: modcol(l, 3, kc, r), "xt", "h2")
                    for fg in range(6):
                        ncol = 512 if fg < 5 else 256
                        w1, w1k = wloadb(WB[l % 2]["wf1"][:, fg * 512:fg * 512 + ncol], 8, ncol)
                        w3, w3k = wloadb(WB[l % 2]["wf3"][:, fg * 512:fg * 512 + ncol], 8, ncol)
                        for oc in range(ncol // 128):
                            j = fg * 4 + oc
                            psA, pka = pring.next()
                            for kc in range(8):
                                mm(psA[:], w1[:, kc, oc * 128:(oc + 1) * 128], h2[:, kc, :], kc == 0, kc == 7, [w1k, "h2"], [pka])
                            psB, pkb = pring.next()
                            for kc in range(8):
                                mm(psB[:], w3[:, kc, oc * 128:(oc + 1) * 128], h2[:, kc, :], kc == 0, kc == 7, [w3k, "h2"], [pkb])
                            sa, sak = sar.next()
                            act(sa[:], psA[:], AF.Silu, [pka], [sak])
                            tt("dve", gt[:, j, :], sa[:], psB[:], ALU.mult, [sak, pkb], [("gt", j)])
                    for cg in range(4):
                        w2, w2k = wloadb(WB[l % 2]["wf2"][:, cg * 256:(cg + 1) * 256], 22, 256)
                        for oc in range(2):
                            ps, pk = pring.next()
                            for j in range(22):
                                mm(ps[:], w2[:, j, oc * 128:(oc + 1) * 128], gt[:, j, :], j == 0, j == 21, [w2k, ("gt", j)], [pk])
                            o8 = cg * 2 + oc
                            stt("dve", xt[:, o8, :], ps[:], modcol(l, 5, o8, r), xt[:, o8, :], ALU.mult, ALU.add, [pk, "MOD", "xt"], ["xt"])
                    if not last:
                        store(fm(XT), xt[:], ["xt"], [])
                    else:
                        norm_mod((sqt, rst, lnt, tmpf), xt, yo, lambda kc: fnw[:, kc:kc + 1], None, "xt", "yo")
                        store(fm(yT), yo[:], ["yo"], [])
                    if stop == "p3" and "XT" in dump:
                        store(fm(XT), xt[:], ["xt"], [])
                S.flush()
        S.flush(final=True)
    return nc


def rope_tables():
    n_rows = TS // 64
    row = np.repeat(np.arange(n_rows, dtype=np.float32), 64)
    col = np.tile(np.arange(64, dtype=np.float32), n_rows)
    inv = (np.float32(10000.0) ** (-np.arange(8, dtype=np.float32) / np.float32(8))).astype(np.float32)
    ang = np.concatenate([row[:, None] * inv, col[:, None] * inv], axis=-1).astype(np.float32)
    cos, sin = np.cos(ang).astype(np.float32), np.sin(ang).astype(np.float32)
    C = np.concatenate([cos, cos], axis=1).T
    Sg = np.concatenate([-sin, sin], axis=1).T
    return np.ascontiguousarray(C), np.ascontiguousarray(Sg)


def make_in_maps(inp, nl=DEPTH):
    f = lambda k: np.asarray(inp[k], np.float32)
    ropeC, ropeS = rope_tables()
    k = np.arange(128)
    triF = (k[:, None] <= k[None, :]).astype(np.float32)
    triB = (k[:, None] >= k[None, :]).astype(np.float32)
    selc = np.zeros((128, 16, 128), np.float32)
    for hh in range(16):
        selc[hh, hh, :] = 1.0
    cst = np.concatenate([triF, triB, np.ones((128, 128), np.float32), np.eye(128, dtype=np.float32), selc.reshape(128, 2048)], axis=1)
    prm = pack_params(inp)
    w_in = f("w_in")
    w_kr2 = np.zeros((DEPTH, D, 2, 96), np.float32)
    krc = w_in[:, :, C_KR:C_KR + 32]
    w_kr2[:, :, 0, 64:96] = krc
    w_kr2[:, :, 1, 64:80] = krc[:, :, 16:32]
    w_kr2[:, :, 1, 80:96] = krc[:, :, 0:16]
    wuq = f("w_uq").reshape(DEPTH, 256, 8, 96)
    w_uq2 = np.zeros((DEPTH, 256, 2, 8, 96), np.float32)
    w_uq2[:, :, 0] = wuq
    w_uq2[:, :, 1, :, 0:64] = wuq[..., 0:64]
    w_uq2[:, :, 1, :, 64:80] = wuq[..., 80:96]
    w_uq2[:, :, 1, :, 80:96] = wuq[..., 64:80]
    shared = {
        "ropeC": ropeC, "ropeS": ropeS, "cst": cst, "prm": prm, "w_in": w_in, "w_kr2": w_kr2, "w_uq2": w_uq2,
        "w_ukv": f("w_ukv"), "w_a_out": f("w_a_out"), "w_b_out": f("w_b_out"), "w_c_out": f("w_c_out"),
        "w_o": f("w_o"), "w_ada": f("w_ada"), "w_ff1": f("w_ff1"), "w_ff3": f("w_ff3"), "w_ff2": f("w_ff2"),
    }
    for k in ("w_in", "w_kr2", "w_uq2", "w_ukv", "w_a_out", "w_b_out", "w_c_out", "w_o", "w_ada", "w_ff1", "w_ff3", "w_ff2"):
        shared[k] = np.ascontiguousarray(shared[k][:nl])
    xs, xp, c, cctx = f("x_sample"), f("x_prompt"), f("c"), f("c_ctx")
    cckv, ckr = f("cache_ckv"), f("cache_krope")
    sf, sbw = f("state_ssm_fwd"), f("state_ssm_bwd")
    maps = []
    for r in range(8):
        bs = r % 4
        xT0 = np.concatenate([xs[bs].T, xp[2 * r].T, xp[2 * r + 1].T], axis=1)
        cd = np.stack([c[bs], cctx], axis=1)
        cd = cd.reshape(8, 128, 2).transpose(1, 0, 2)
        h0 = np.stack([sf[bs], sbw[bs]], axis=1)
        h0 = h0.transpose(0, 1, 4, 2, 3).reshape(DEPTH, 2, 128, 1024)
        m = dict(shared)
        m.update({
            "xT0": np.ascontiguousarray(xT0), "cond": np.ascontiguousarray(cd),
            "cckvT": np.ascontiguousarray(cckv[bs].transpose(0, 2, 1)),
            "ckrT": np.ascontiguousarray(ckr[bs].transpose(0, 2, 1)),
            "h0": np.ascontiguousarray(h0),
        })
        maps.append(m)
    return maps


_NC_CACHE = {}


def kernel(**inputs):
    maps = make_in_maps(inputs)
    if "nc" not in _NC_CACHE:
        _NC_CACHE["nc"] = build_program()
    nc = _NC_CACHE["nc"]
    res = run_bass_kernel_spmd(nc, maps, core_ids=list(range(8)))
    R = res.results
    y_prompt = np.zeros((16, 256, D), np.float32)
    y_sample = np.zeros((4, TS, D), np.float32)
    new_ckv = np.zeros((16, DEPTH, 256, 256), np.float32)
    new_kr = np.zeros((16, DEPTH, 256, 32), np.float32)
    new_f = np.zeros((16, DEPTH, 16, 64, 128), np.float32)
    new_b = np.zeros((16, DEPTH, 16, 64, 128), np.float32)
    for r in range(8):
        yT = R[r]["yT"]
        if r < 4:
            y_sample[r] = yT[:, :TS].T
        for s in range(2):
            q = 2 * r + s
            y_prompt[q] = yT[:, TS + s * 256:TS + (s + 1) * 256].T
            new_ckv[q] = R[r]["nckvT"][:, :, s * 256:(s + 1) * 256].transpose(0, 2, 1)
            new_kr[q] = R[r]["nkrT"][:, :, s * 256:(s + 1) * 256].transpose(0, 2, 1)
            st = R[r]["nssm"][:, :, s].reshape(DEPTH, 2, 128, 16, 64).transpose(0, 1, 3, 4, 2)
            new_f[q] = st[:, 0]
            new_b[q] = st[:, 1]
    return (y_prompt, y_sample, new_ckv, new_kr, new_f, new_b)
```

```python
import contextlib
import math
import os

import numpy as np
import concourse.bass as bass
import concourse.mybir as mybir
from concourse.bass_utils import run_bass_kernel_spmd

F32 = mybir.dt.float32
BF16 = mybir.dt.bfloat16
AF = mybir.ActivationFunctionType
ALU = mybir.AluOpType

D = 1024
DEPTH = 4
TS = 4096
TPR = 512
T = TS + TPR
NB = T // 512
NCH = T // 128
EPS = 1e-6
IN_COLS = 7728
FF = 2816
C_AX, C_AB, C_AC, C_Z, C_XBC, C_DT, C_CQ, C_CKV, C_KR, C_G = 0, 512, 1024, 1536, 2560, 4096, 4112, 4368, 4624, 4656
SCALE = 1.0 / math.sqrt(96.0)
SEQS = [(0, 4096, True), (4096, 256, False), (4352, 256, False)]

ENGS = ("pe", "act", "dve", "pool", "sp")


class Sched:
    def __init__(self, nc, st, n_dma_sems=48):
        self.nc = nc
        self.n_dma_sems = n_dma_sems
        self.esem = {e: st.enter_context(nc.semaphore("s_" + e)) for e in ENGS if e != "sp"}
        self.dsem = [st.enter_context(nc.semaphore("d%d" % i)) for i in range(n_dma_sems)]
        self.dummy = st.enter_context(nc.sbuf_tensor("bar_dummy", [128, 2], F32))
        self.ecount = {e: 0 for e in ENGS}
        self.dma_val = [0] * n_dma_sems
        self.dma_last = [None] * n_dma_sems
        self.dma_rr = {"sw": 0, "hw": 0}
        self.n_sw = 16
        self.waited = {e: {} for e in ENGS}
        self.barrier_tok = None
        self.need_barrier = {e: False for e in ENGS}
        self._reset()

    def _reset(self):
        self.ops = {e: [] for e in ENGS}
        self.last_write = {}
        self.readers = {}
        self.dma_toks = []

    def _record(self, eng, fn, reads, writes, dma, extra_deps=()):
        deps = set(extra_deps)
        for r in reads:
            w = self.last_write.get(r)
            if w is not None:
                deps.add(w)
        for r in writes:
            w = self.last_write.get(r)
            if w is not None:
                deps.add(w)
            for rd in self.readers.get(r, ()):
                deps.add(rd)
        idx = len(self.ops[eng])
        if dma:
            if eng == "pool":
                si = self.dma_rr["sw"]
                self.dma_rr["sw"] = (si + 1) % self.n_sw
            else:
                si = self.n_sw + self.dma_rr["hw"]
                self.dma_rr["hw"] = (self.dma_rr["hw"] + 1) % (self.n_dma_sems - self.n_sw)
            prev = self.dma_last[si]
            if prev is not None:
                deps.add(prev)
            self.dma_val[si] += 16
            tok = ("dma", si, self.dma_val[si])
            self.dma_last[si] = tok
            self.dma_toks.append(tok)
        else:
            tok = ("eng", eng, idx)
        deps = {d for d in deps if not (d[0] == "eng" and d[1] == "pe" and eng == "pe")}
        deps.discard(tok)
        if self.need_barrier[eng] and self.barrier_tok is not None:
            deps.add(self.barrier_tok)
            self.need_barrier[eng] = False
        self.ops[eng].append(dict(fn=fn, deps=deps, flag=False, tok=tok))
        for r in reads:
            lst = self.readers.setdefault(r, [])
            if tok[0] == "eng":
                lst[:] = [t for t in lst if not (t[0] == "eng" and t[1] == eng)]
            lst.append(tok)
        for r in writes:
            self.last_write[r] = tok
            self.readers[r] = []
        return tok

    def op(self, eng, fn, reads=(), writes=(), extra_deps=()):
        return self._record(eng, fn, reads, writes, False, extra_deps)

    def dma(self, eng, fn, reads=(), writes=(), extra_deps=()):
        return self._record(eng, fn, reads, writes, True, extra_deps)

    def flush(self, final=False):
        nc = self.nc
        deps = set()
        for e in ENGS:
            if e == "sp":
                continue
            for i in range(len(self.ops[e]) - 1, -1, -1):
                if self.ops[e][i]["tok"][0] == "eng":
                    deps.add(self.ops[e][i]["tok"])
                    break
        latest = {}
        for t in self.dma_toks:
            if t[1] not in latest or latest[t[1]][2] < t[2]:
                latest[t[1]] = t
        deps.update(latest.values())
        dummy = self.dummy
        coll = self._record("dve", lambda e: e.memset(dummy[:, 0:1], 0.0), (), (), False, deps)
        for e in ENGS:
            for o in self.ops[e]:
                for d in o["deps"]:
                    if d[0] == "eng":
                        self.ops[d[1]][d[2]]["flag"] = True
        self.ops["dve"][coll[2]]["flag"] = True
        cnt = {}
        for e in ENGS:
            c = self.ecount[e]
            arr = []
            for o in self.ops[e]:
                if o["flag"]:
                    c += 1
                arr.append(c)
            cnt[e] = arr
        esem, dsem = self.esem, self.dsem

        def resolve(d):
            if d[0] == "eng":
                return esem[d[1]], cnt[d[1]][d[2]]
            if d[0] == "abs":
                return d[1], d[2]
            return dsem[d[1]], d[2]

        coll_abs = ("abs", esem["dve"], cnt["dve"][coll[2]])

        def run(ename, eng):
            waited = self.waited[ename]
            for o in self.ops[ename]:
                need = {}
                for d in o["deps"]:
                    s, v = resolve(d)
                    k = id(s)
                    if waited.get(k, 0) >= v:
                        continue
                    if k not in need or need[k][1] < v:
                        need[k] = (s, v)
                for k, (s, v) in need.items():
                    eng.wait_ge(s, v)
                    waited[k] = v
                ins = o["fn"](eng)
                if o["tok"][0] == "dma":
                    ins.then_inc(dsem[o["tok"][1]], 16)
                elif o["flag"]:
                    ins.then_inc(esem[ename], 1)
            if final and ename == "sp":
                s, v = resolve(coll_abs)
                eng.wait_ge(s, v)

        if os.environ.get("KDBG_SIM"):
            self._simulate(resolve, coll_abs, final)

        with nc.Block() as block:
            @block.sync
            def _(sync):
                run("sp", sync)

            @block.tensor
            def _(tensor):
                run("pe", tensor)

            @block.scalar
            def _(scalar):
                run("act", scalar)

            @block.vector
            def _(vector):
                run("dve", vector)

            @block.gpsimd
            def _(gpsimd):
                run("pool", gpsimd)

        for e in ENGS:
            if cnt[e]:
                self.ecount[e] = cnt[e][-1]
        self.barrier_tok = coll_abs
        self.need_barrier = {e: True for e in ENGS}
        self._reset()


def _sched_simulate(self, resolve, coll_abs, final):
    if not hasattr(self, "sim_sem"):
        self.sim_sem = {}
    sem = self.sim_sem
    pos = {e: 0 for e in ENGS}
    progress = True
    while progress:
        progress = False
        for e in ENGS:
            while pos[e] < len(self.ops[e]):
                o = self.ops[e][pos[e]]
                ok = True
                for d in o["deps"]:
                    s_, v = resolve(d)
                    if sem.get(id(s_), 0) < v:
                        ok = False
                        break
                if not ok:
                    break
                if o["tok"][0] == "dma":
                    k = id(self.dsem[o["tok"][1]])
                    sem[k] = sem.get(k, 0) + 16
                elif o["flag"]:
                    k = id(self.esem[e])
                    sem[k] = sem.get(k, 0) + 1
                pos[e] += 1
                progress = True
    stuck = {e: (pos[e], len(self.ops[e])) for e in ENGS if pos[e] < len(self.ops[e])}
    if stuck:
        print("SCHED DEADLOCK:", stuck)
        for e in stuck:
            o = self.ops[e][pos[e]]
            print("  ", e, "op", pos[e], "tok", o["tok"], "deps", [(d, resolve(d)[1], sem.get(id(resolve(d)[0]), 0)) for d in o["deps"]])
        raise RuntimeError("sched deadlock")
    else:
        print("sched sim ok:", {e: len(self.ops[e]) for e in ENGS})


Sched._simulate = _sched_simulate


class Ring:
    def __init__(self, tiles, name):
        self.tiles = tiles
        self.name = name
        self.i = 0

    def next(self):
        i = self.i
        self.i = (self.i + 1) % len(self.tiles)
        return self.tiles[i], (self.name, i)


PRM_LAYOUT = [("n1w", 4 * 8), ("n2w", 4 * 8), ("fnw", 8), ("snw", 4 * 8), ("qnw", 4 * 2), ("kvnw", 4 * 2),
              ("aconv", 4 * 3 * 4), ("sconvw", 4 * 3 * 12), ("sconvb", 4 * 12), ("dexp", 4 * 2 * 8),
              ("alog", 4 * 2 * 16), ("dtb", 4 * 2 * 16), ("bada", 4 * 48)]
PRM_OFF = {}
_o = 0
for _n, _s in PRM_LAYOUT:
    PRM_OFF[_n] = (_o, _s)
    _o += _s
NPRM = _o


def _pc(v):
    v = np.asarray(v, np.float32)
    lead = v.shape[:-1]
    c = v.shape[-1] // 128
    v = v.reshape(lead + (c, 128))
    v = np.moveaxis(v, -1, 0)
    return np.ascontiguousarray(v).reshape(128, -1)


def pack_params(inp):
    parts = {
        "n1w": _pc(inp["norm1_w"]), "n2w": _pc(inp["norm2_w"]), "fnw": _pc(inp["final_norm_w"]),
        "snw": _pc(inp["ssm_norm_w"]), "qnw": _pc(inp["q_norm_w"]), "kvnw": _pc(inp["kv_norm_w"]),
        "aconv": _pc(inp["a_conv_w"]), "sconvw": _pc(inp["ssm_conv_w"]), "sconvb": _pc(inp["ssm_conv_b"]),
        "dexp": _pc(np.repeat(np.asarray(inp["ssm_d"], np.float32), 64, axis=-1)),
        "alog": np.broadcast_to(np.asarray(inp["ssm_a_log"], np.float32).reshape(1, -1), (128, 128)),
        "dtb": np.broadcast_to(np.asarray(inp["ssm_dt_bias"], np.float32).reshape(1, -1), (128, 128)),
        "bada": _pc(inp["b_ada"]),
    }
    out = np.zeros((128, NPRM), np.float32)
    for n, (o, s) in PRM_OFF.items():
        assert parts[n].shape == (128, s), (n, parts[n].shape, s)
        out[:, o:o + s] = parts[n]
    return out


def build_program(stop=None, dump=(), nl=DEPTH):
    nc = bass.Bass("TRN2", target_bir_lowering=False)

    def din(name, shape, dt=F32):
        return nc.dram_tensor(name, list(shape), dt, kind="ExternalInput").ap()

    def dout(name, shape, dt=F32):
        return nc.dram_tensor(name, list(shape), dt, kind="ExternalOutput").ap()

    def dscr(name, shape, dt):
        kind = "ExternalOutput" if name in dump else "Internal"
        return nc.dram_tensor(name, list(shape), dt, kind=kind).ap()

    xT0 = din("xT0", [D, T])
    cond = din("cond", [128, 8, 2])
    cckvT = din("cckvT", [DEPTH, 256, 512])
    ckrT = din("ckrT", [DEPTH, 32, 512])
    h0 = din("h0", [DEPTH, 2, 128, 1024])
    ropeC = din("ropeC", [32, TS])
    ropeS = din("ropeS", [32, TS])
    cst = din("cst", [128, 2560])
    prm_d = din("prm", [128, NPRM])
    w_in = din("w_in", [nl, D, IN_COLS])
    w_kr2 = din("w_kr2", [nl, D, 2, 96])
    w_uq2 = din("w_uq2", [nl, 256, 2, 8, 96])
    w_ukv = din("w_ukv", [nl, 256, 1024])
    w_a_out = din("w_a_out", [nl, 512, D])
    w_b_out = din("w_b_out", [nl, D, D])
    w_c_out = din("w_c_out", [nl, 512, D])
    w_o = din("w_o", [nl, D, D])
    w_ada = din("w_ada", [nl, D, 6 * D])
    w_ff1 = din("w_ff1", [nl, D, FF])
    w_ff3 = din("w_ff3", [nl, D, FF])
    w_ff2 = din("w_ff2", [nl, FF, D])

    yT = dout("yT", [D, T])
    nckvT = dout("nckvT", [DEPTH, 256, 512])
    nkrT = dout("nkrT", [DEPTH, 32, 512])
    nssm = dout("nssm", [DEPTH, 2, 2, 128, 1024])

    XT = dscr("XT", [D, T], F32)
    HT = dscr("HT", [D, T], BF16)
    UT = dscr("UT", [512, T], BF16)
    ABT = dscr("ABT", [512, T], BF16)
    ZT = dscr("ZT", [D, T], BF16)
    XBCT = dscr("XBCT", [1536, T], BF16)
    DTT = dscr("DTT", [T, 16], F32)
    QT = dscr("QT", [8, 96, T], BF16)
    CKVT = dscr("CKVT", [256, T], BF16)
    KRT = dscr("KRT", [32, T], BF16)
    VAT = dscr("VAT", [512, T], BF16)
    XSC = dscr("XSC", [1536, T], BF16)
    XTOK = dscr("XTOK", [T, 1024], BF16)
    BTOK = dscr("BTOK", [T, 256], BF16)
    HENT = dscr("HENT", [2, NCH, 128, 1024], BF16)
    YBT = dscr("YBT", [D, T], BF16)
    ATT = dscr("ATT", [512, T], BF16)

    WB = []
    for si in range(2):
        WB.append(dict(
            win=dscr("WBwin%d" % si, [D, IN_COLS], BF16), wa=dscr("WBwa%d" % si, [512, D], BF16),
            wb=dscr("WBwb%d" % si, [D, D], BF16), wc=dscr("WBwc%d" % si, [512, D], BF16),
            wo=dscr("WBwo%d" % si, [D, D], BF16), wf1=dscr("WBwf1%d" % si, [D, FF], BF16),
            wf3=dscr("WBwf3%d" % si, [D, FF], BF16), wf2=dscr("WBwf2%d" % si, [FF, D], BF16)))

    with contextlib.ExitStack() as gst:
        S = Sched(nc, gst)

        _uid = [0]

        def sb(st, name, shape, dt):
            _uid[0] += 1
            return st.enter_context(nc.sbuf_tensor("sb%d_%s" % (_uid[0], name), list(shape), dt))

        prm = sb(gst, "prm", [128, NPRM], F32)
        cstt = sb(gst, "cstt", [128, 2560], F32)
        identb = sb(gst, "identb", [128, 128], BF16)
        MOD = sb(gst, "MOD", [128, DEPTH, 48, 2], F32)
        A1 = sb(gst, "A1", [128, DEPTH, 8, 2], F32)
        A2 = sb(gst, "A2", [128, DEPTH, 8, 2], F32)
        dsum = sb(gst, "dsum", [128, DEPTH, 8], F32)
        wring = Ring([sb(gst, "wr%d" % i, [128, 6144], BF16) for i in range(4)], "wr")
        uqw = sb(gst, "uqw", [128, 2, 2 * 8 * 96], BF16)
        krw = sb(gst, "krw", [128, 8, 2 * 96], BF16)
        dtw = sb(gst, "dtw", [128, 8, 16], BF16)
        ukvw = sb(gst, "ukvw", [128, 2, 1024], BF16)
        psum = [gst.enter_context(nc.psum_tensor("ps%d" % i, [128, 512], F32)) for i in range(7)]
        psbT = gst.enter_context(nc.psum_tensor("psbT", [128, 1024], BF16))
        pring = Ring(psum[0:5], "ps")
        plong = Ring(psum[5:7], "pl")
        triF = cstt[:, 0:128]
        triB = cstt[:, 128:256]
        ones = cstt[:, 256:384]
        identF = cstt[:, 384:512]
        sel = cstt[0:16, 512:2560].rearrange("p (h j) -> p h j", h=16)

        def P(name, l=None):
            o, s = PRM_OFF[name]
            v = prm[:, o:o + s]
            return v

        def pv(name, pattern, **kw):
            o, s = PRM_OFF[name]
            return prm[:, o:o + s].rearrange(pattern, **kw)

        n1w = pv("n1w", "p (l c) -> p l c", l=4)
        n2w = pv("n2w", "p (l c) -> p l c", l=4)
        fnw = P("fnw")
        snw = pv("snw", "p (l c) -> p l c", l=4)
        qnw = pv("qnw", "p (l c) -> p l c", l=4)
        kvnw = pv("kvnw", "p (l c) -> p l c", l=4)
        aconv = pv("aconv", "p (l k c) -> p l k c", l=4, k=3)
        sconvw = pv("sconvw", "p (l k c) -> p l k c", l=4, k=3)
        sconvb = pv("sconvb", "p (l c) -> p l c", l=4)
        dexp = pv("dexp", "p (l d c) -> p l d c", l=4, d=2)
        alog = pv("alog", "p (l d h) -> p l d h", l=4, d=2)
        dtb = pv("dtb", "p (l d h) -> p l d h", l=4, d=2)
        bada = pv("bada", "p (l c) -> p l c", l=4)

        def mm(out, lhsT, rhs, start, stop, rd, wr):
            S.op("pe", lambda e: e.matmul(out, lhsT=lhsT, rhs=rhs, start=start, stop=stop), reads=rd, writes=wr)

        def act(out, in_, func, rd, wr, **kw):
            S.op("act", lambda e: e.activation(out=out, in_=in_, func=func, **kw), reads=rd, writes=wr)

        def tt(eng, out, in0, in1, op, rd, wr):
            S.op(eng, lambda e: e.tensor_tensor(out=out, in0=in0, in1=in1, op=op), reads=rd, writes=wr)

        def ts(eng, out, in0, s1, s2, op0, op1, rd, wr):
            if op1 is None:
                S.op(eng, lambda e: e.tensor_scalar(out=out, in0=in0, scalar1=s1, scalar2=None, op0=op0), reads=rd, writes=wr)
            else:
                S.op(eng, lambda e: e.tensor_scalar(out=out, in0=in0, scalar1=s1, scalar2=s2, op0=op0, op1=op1), reads=rd, writes=wr)

        def stt(eng, out, in0, scalar, in1, op0, op1, rd, wr):
            S.op(eng, lambda e: e.scalar_tensor_tensor(out=out, in0=in0, scalar=scalar, in1=in1, op0=op0, op1=op1), reads=rd, writes=wr)

        def cp(eng, out, in_, rd, wr):
            if eng == "act":
                act(out, in_, AF.Copy, rd, wr)
            else:
                S.op(eng, lambda e: e.tensor_copy(out=out, in_=in_), reads=rd, writes=wr)

        def load(out, in_, rd, wr, eng="sp"):
            return S.dma(eng, lambda e: e.dma_start(out=out, in_=in_), reads=rd, writes=wr)

        def store(out, in_, rd, wr, eng="act"):
            return S.dma(eng, lambda e: e.dma_start(out=out, in_=in_), reads=rd, writes=wr)

        def wload(src2d, n_kc, ncols):
            slot, key = wring.next()
            view = slot[:, 0:n_kc * ncols].rearrange("p (k n) -> p k n", k=n_kc)
            S.dma("pool", lambda e: e.dma_start(out=view, in_=src2d.rearrange("(k p) n -> p k n", p=128)), reads=[], writes=[key])
            return view, key

        def wloadb(src2d, n_kc, ncols):
            slot, key = wring.next()
            view = slot[:, 0:n_kc * ncols].rearrange("p (k n) -> p k n", k=n_kc)
            S.dma("pool", lambda e: e.dma_start(out=view, in_=src2d.rearrange("(k p) n -> p k n", p=128)), reads=[], writes=[key])
            return view, key

        def convert_weights(l, stage_ring):
            wb = WB[l % 2]
            jobs = []
            for c0 in list(range(0, 4096, 512)) + [C_CQ] + list(range(C_G, IN_COLS, 512)):
                jobs.append((w_in[l, :, c0:c0 + 512], wb["win"][:, c0:c0 + 512], 8, 512))
            for c0 in (0, 512):
                jobs.append((w_a_out[l, :, c0:c0 + 512], wb["wa"][:, c0:c0 + 512], 4, 512))
                jobs.append((w_b_out[l, :, c0:c0 + 512], wb["wb"][:, c0:c0 + 512], 8, 512))
                jobs.append((w_c_out[l, :, c0:c0 + 512], wb["wc"][:, c0:c0 + 512], 4, 512))
                jobs.append((w_o[l, :, c0:c0 + 512], wb["wo"][:, c0:c0 + 512], 8, 512))
            for fg in range(6):
                ncol = 512 if fg < 5 else 256
                jobs.append((w_ff1[l, :, fg * 512:fg * 512 + ncol], wb["wf1"][:, fg * 512:fg * 512 + ncol], 8, ncol))
                jobs.append((w_ff3[l, :, fg * 512:fg * 512 + ncol], wb["wf3"][:, fg * 512:fg * 512 + ncol], 8, ncol))
            for cg in range(4):
                jobs.append((w_ff2[l, :, cg * 256:(cg + 1) * 256], wb["wf2"][:, cg * 256:(cg + 1) * 256], 22, 256))
            pend = []
            nst = len(stage_ring.tiles)

            def issue_store(item):
                view, key, dst, n_kc = item
                S.dma("pool", lambda e: e.dma_start(out=dst.rearrange("(k p) n -> p k n", p=128), in_=view), reads=[key], writes=[])

            for (src, dst, n_kc, ncols) in jobs:
                if len(pend) == nst:
                    issue_store(pend.pop(0))
                slot, key = stage_ring.next()
                view = slot[:, 0:n_kc * ncols].rearrange("p (k n) -> p k n", k=n_kc)
                S.dma("pool", lambda e, view=view, src=src: e.dma_start(out=view, in_=src.rearrange("(k p) n -> p k n", p=128)), reads=[], writes=[key])
                pend.append((view, key, dst, n_kc))
            while pend:
                issue_store(pend.pop(0))

        def rstd_from_ssq(ps_ap, n_feat, out_ap, tmp_ap, rd, wr, tmpkey):
            act(tmp_ap, ps_ap, AF.Ln, rd, [tmpkey], scale=1.0 / n_feat, bias=EPS)
            act(out_ap, tmp_ap, AF.Exp, [tmpkey], wr, scale=-0.5)

        with contextlib.ExitStack() as st:
            condt = sb(st, "condt", [128, 8, 2], F32)
            scb = sb(st, "scb", [128, 8, 16], BF16)
            stg0 = Ring([sb(st, "cvs%d" % i, [128, 6144], BF16) for i in range(3)], "cvs")
            convert_weights(0, stg0)
            load(prm[:], prm_d, [], ["prm"])
            load(cstt[:], cst, [], ["cst"])
            load(condt[:], cond, [], ["condt"])
            cp("dve", identb[:], cstt[:, 384:512], ["cst"], ["identb"])
            S.op("pool", lambda e: e.memset(scb[:], 0.0), writes=["scb"])
            act(scb[:, :, 0:2], condt[:], AF.Silu, ["condt", "scb"], ["scb"])
            _ncg = int(os.environ.get("KDBG_NCG", "12"))
            for l in range(nl):
                for cg in range(_ncg):
                    wv, wk = wload(w_ada[l, :, cg * 512:(cg + 1) * 512], 8, 512)
                    for oc in range(4):
                        ps, pk = pring.next()
                        for kc in range(8):
                            mm(ps[:, 0:16], wv[:, kc, oc * 128:(oc + 1) * 128], scb[:, kc, :], kc == 0, kc == 7, [wk, "scb"], [pk])
                        ci = cg * 4 + oc
                        act(MOD[:, l, ci, :], ps[:, 0:2], AF.Identity, [pk, "prm"], ["MOD"], bias=bada[:, l, ci:ci + 1])
            for l in range(nl):
                for (Ax, k0, nw) in ((A1, 8, n1w), (A2, 32, n2w)):
                    ts("dve", Ax[:, l, :, :], MOD[:, l, k0:k0 + 8, :], 1.0, None, ALU.add, None, ["MOD"], ["A"])
                    tt("dve", Ax[:, l, :, :], Ax[:, l, :, :], nw[:, l, :].unsqueeze(2).to_broadcast([128, 8, 2]), ALU.mult, ["A", "prm"], ["A"])
                tt("dve", dsum[:, l, :], dexp[:, l, 0, :], dexp[:, l, 1, :], ALU.add, ["prm"], ["dsum"])
            S.flush()
        if stop == "p0":
            dbgo = dout("dbg_mod", [128, DEPTH * 48 * 2])
            load(dbgo, MOD[:].rearrange("p l c r -> p (l c r)"), ["MOD"], ["dbgo"])
            S.flush(final=True)
            return nc

        def modcol(l, kind, kc, r):
            return MOD[:, l, kind * 8 + kc, r:r + 1]

        def norm_mod(st_tiles, xt, ht, Acol, Bcol, xkey, hkey):
            sqt, rst, lnt, tmpf = st_tiles
            ps, pk = pring.next()
            for kc in range(8):
                act(sqt[:, kc % 2, :], xt[:, kc, :], AF.Square, [xkey], [("sq", kc % 2)])
                mm(ps[:], ones, sqt[:, kc % 2, :], kc == 0, kc == 7, ["cst", ("sq", kc % 2)], [pk])
            rstd_from_ssq(ps[:], 1024.0, rst[:], lnt[:], [pk], ["rst"], "lnt")
            for kc in range(8):
                if Bcol is None:
                    stt("dve", ht[:, kc, :], xt[:, kc, :], Acol(kc), rst[:], ALU.mult, ALU.mult, [xkey, "rst", "A", "prm"], [hkey])
                else:
                    stt("dve", tmpf[:, kc % 2, :], xt[:, kc, :], Acol(kc), rst[:], ALU.mult, ALU.mult, [xkey, "rst", "A", "prm"], [("tmpf", kc % 2)])
                    act(ht[:, kc, :], tmpf[:, kc % 2, :], AF.Identity, [("tmpf", kc % 2), "MOD"], [hkey], bias=Bcol(kc))

        for l in range(nl):
            xsrc = xT0 if l == 0 else XT
            with contextlib.ExitStack() as st:
                xt = sb(st, "xt", [128, 8, 512], F32)
                htr = [sb(st, "ht%d" % i, [128, 8, 512], BF16) for i in range(2)]
                sqt = sb(st, "sqt", [128, 2, 512], F32)
                rst = sb(st, "rst", [128, 512], F32)
                lnt = sb(st, "lnt", [128, 512], F32)
                tmpf = sb(st, "tmpf", [128, 2, 512], F32)
                stg = Ring([sb(st, "stg%d" % i, [128, 4, 512], BF16) for i in range(3)], "stg")
                axt = sb(st, "axt", [128, 4, 512], BF16)
                cqf = sb(st, "cqf", [128, 4, 512], F32)
                nrf = sb(st, "nrf", [128, 2, 512], F32)
                cqn = sb(st, "cqn", [128, 2, 512], BF16)
                ckvn = sb(st, "ckvn", [128, 2, 512], BF16)
                qst = sb(st, "qst", [128, 8, 512], BF16)
                rC = sb(st, "rC", [128, 512], F32)
                rS = sb(st, "rS", [128, 512], F32)
                t1 = sb(st, "t1", [128, 512], F32)
                t2 = sb(st, "t2", [128, 512], F32)
                krt = sb(st, "krt", [128, 512], BF16)
                krf = sb(st, "krf", [128, 512], F32)
                dts = sb(st, "dts", [128, 4, 16], F32)
                S.dma("pool", lambda e, l=l: e.dma_start(out=uqw[:], in_=w_uq2[l].rearrange("(k p) v h r -> p k (v h r)", p=128)), writes=["uqw"])
                S.dma("pool", lambda e, l=l: e.dma_start(out=krw[:], in_=w_kr2[l].rearrange("(k p) v r -> p k (v r)", p=128)), writes=["krw"])
                S.dma("pool", lambda e, l=l: e.dma_start(out=dtw[:], in_=w_in[l, :, C_DT:C_DT + 16].rearrange("(k p) n -> p k n", p=128)), writes=["dtw"])
                S.dma("pool", lambda e, l=l: e.dma_start(out=ukvw[:], in_=w_ukv[l].rearrange("(k p) n -> p k n", p=128)), writes=["ukvw"])
                uq5 = uqw[:].rearrange("p k (v h r) -> p k v h r", v=2, h=8)
                kr4 = krw[:].rearrange("p k (v r) -> p k v r", v=2)
                _steps = os.environ.get("KDBG_P1", "groups,mla,q,kr,dt").split(",")
                _blks = [int(v) for v in os.environ.get("KDBG_BLKS", ",".join(str(i) for i in range(NB))).split(",")]
                def p1_norm(bb):
                    rr = 0 if bb < 8 else 1
                    tt0 = bb * 512
                    hto = htr[bb % 2]
                    load(xt[:], xsrc[:, tt0:tt0 + 512].rearrange("(k p) t -> p k t", p=128), [], ["xt"])
                    norm_mod((sqt, rst, lnt, tmpf), xt, hto,
                             lambda kc: A1[:, l, kc, rr:rr + 1], lambda kc: modcol(l, 0, kc, rr), "xt", ("ht", bb % 2))
                    store(HT[:, tt0:tt0 + 512].rearrange("(k p) t -> p k t", p=128), hto[:], [("ht", bb % 2)], [])

                p1_norm(_blks[0])
                for bi, b in enumerate(_blks):
                    r = 0 if b < 8 else 1
                    smp = b < 8
                    t0 = b * 512
                    ht = htr[b % 2]
                    htk = ("ht", b % 2)
                    if smp:
                        load(rC[64:96, :], ropeC[:, t0:t0 + 512], [], ["rC"])
                        load(rS[64:96, :], ropeS[:, t0:t0 + 512], [], ["rS"])
                    groups = [("ax", C_AX), ("ab", C_AB), ("ac", C_AC), ("z", C_Z), ("z", C_Z + 512),
                              ("xbc", C_XBC), ("xbc", C_XBC + 512), ("xbc", C_XBC + 1024), ("cqkv", C_CQ)]
                    if "groups" not in _steps:
                        groups = []
                    if not groups and bi + 1 < len(_blks):
                        p1_norm(_blks[bi + 1])
                    for gi, (kind, c0) in enumerate(groups):
                        if gi == 3 and bi + 1 < len(_blks):
                            p1_norm(_blks[bi + 1])
                        wv, wk = wloadb(WB[l % 2]["win"][:, c0:c0 + 512], 8, 512)
                        if kind in ("ab", "ac", "z", "xbc"):
                            sg, sk = stg.next()
                        for oc in range(4):
                            ps, pk = pring.next()
                            for kc in range(8):
                                mm(ps[:], wv[:, kc, oc * 128:(oc + 1) * 128], ht[:, kc, :], kc == 0, kc == 7, [wk, htk], [pk])
                            if kind == "ax":
                                cp("act", axt[:, oc, :], ps[:], [pk], ["axt"])
                            elif kind == "ab":
                                cp("act", sg[:, oc, :], ps[:], [pk], [sk])
                            elif kind == "ac":
                                tt("dve", sg[:, oc, :], ps[:], axt[:, oc, :], ALU.mult, [pk, "axt"], [sk])
                            elif kind == "z":
                                act(sg[:, oc, :], ps[:], AF.Silu, [pk], [sk])
                            elif kind == "xbc":
                                cp("act" if oc % 2 == 0 else "dve", sg[:, oc, :], ps[:], [pk], [sk])
                            else:
                                cp("act" if oc % 2 == 0 else "dve", cqf[:, oc, :], ps[:], [pk], [("cqf", oc // 2)])
                        if kind in ("ab", "ac", "z", "xbc"):
                            dst = {"ab": ABT, "ac": UT, "z": ZT, "xbc": XBCT}[kind]
                            r0 = c0 - {"ab": C_AB, "ac": C_AC, "z": C_Z, "xbc": C_XBC}[kind]
                            store(dst[r0:r0 + 512, t0:t0 + 512].rearrange("(k p) t -> p k t", p=128), sg[:], [sk], [(kind + "T", b, r0)])
                    for half, nw, dstb in ((0, qnw, cqn), (1, kvnw, ckvn)) if "mla" in _steps else ():
                        ps, pk = pring.next()
                        for j in range(2):
                            act(sqt[:, j, :], cqf[:, half * 2 + j, :], AF.Square, [("cqf", half)], [("sq", j)])
                            mm(ps[:], ones, sqt[:, j, :], j == 0, j == 1, ["cst", ("sq", j)], [pk])
                        rstd_from_ssq(ps[:], 256.0, rst[:], lnt[:], [pk], ["rst"], "lnt")
                        for j in range(2):
                            stt("dve", nrf[:, j, :], cqf[:, half * 2 + j, :], nw[:, l, j:j + 1], rst[:], ALU.mult, ALU.mult,
                                [("cqf", half), "rst", "prm"], [("nrf", j)])
                            cp("act", dstb[:, j, :], nrf[:, j, :], [("nrf", j)], [("lat", half)])
                        if half == 1:
                            store(CKVT[:, t0:t0 + 512].rearrange("(k p) t -> p k t", p=128), ckvn[:], [("lat", 1)], [("CKVT", b)])
                            if not smp:
                                store(nckvT[l].rearrange("(k p) t -> p k t", p=128), nrf[:], [("nrf", 0), ("nrf", 1)], [("nckv", l)])
                    for h in range(8) if "q" in _steps else ():
                        psn, pkn = pring.next()
                        for kc in range(2):
                            mm(psn[0:96, :], uq5[:, kc, 0, h, :], cqn[:, kc, :], kc == 0, kc == 1, ["uqw", ("lat", 0)], [pkn])
                        if smp:
                            pss, pks = pring.next()
                            for kc in range(2):
                                mm(pss[0:96, :], uq5[:, kc, 1, h, :], cqn[:, kc, :], kc == 0, kc == 1, ["uqw", ("lat", 0)], [pks])
                            cp("act", qst[0:64, h, :], psn[0:64, :], [pkn], [("qst", h)])
                            tt("dve", t1[64:96, :], psn[64:96, :], rC[64:96, :], ALU.mult, [pkn, "rC"], ["t1"])
                            tt("dve", t2[64:96, :], pss[64:96, :], rS[64:96, :], ALU.mult, [pks, "rS"], ["t2"])
                            tt("dve", qst[64:96, h, :], t1[64:96, :], t2[64:96, :], ALU.add, ["t1", "t2"], [("qst", h)])
                        else:
                            cp("act", qst[0:96, h, :], psn[0:96, :], [pkn], [("qst", h)])
                    if "q" in _steps:
                        store(QT[:, :, t0:t0 + 512].rearrange("h r t -> r h t"), qst[0:96, :, :], [("qst", h) for h in range(8)], [("QT", b)])
                    if "kr" not in _steps:
                        continue
                    psn, pkn = pring.next()
                    for kc in range(8):
                        mm(psn[0:96, :], kr4[:, kc, 0, :], ht[:, kc, :], kc == 0, kc == 7, ["krw", htk], [pkn])
                    if smp:
                        pss, pks = pring.next()
                        for kc in range(8):
                            mm(pss[0:96, :], kr4[:, kc, 1, :], ht[:, kc, :], kc == 0, kc == 7, ["krw", htk], [pks])
                        tt("dve", t1[64:96, :], psn[64:96, :], rC[64:96, :], ALU.mult, [pkn, "rC"], ["t1"])
                        tt("dve", t2[64:96, :], pss[64:96, :], rS[64:96, :], ALU.mult, [pks, "rS"], ["t2"])
                        tt("dve", krt[64:96, :], t1[64:96, :], t2[64:96, :], ALU.add, ["t1", "t2"], ["krt"])
                    else:
                        cp("dve", krf[64:96, :], psn[64:96, :], [pkn], ["krf"])
                        cp("act", krt[64:96, :], krf[64:96, :], ["krf"], ["krt"])
                        store(nkrT[l], krf[64:96, :], ["krf"], [("nkr", l)])
                    store(KRT[:, t0:t0 + 512], krt[64:96, :], ["krt"], [("KRT", b)])
                    if "dt" not in _steps:
                        continue
                    ps, pk = pring.next()
                    for tl in range(4):
                        for kc in range(8):
                            mm(ps[:, tl * 16:(tl + 1) * 16], ht[:, kc, tl * 128:(tl + 1) * 128], dtw[:, kc, :], kc == 0, kc == 7, ["dtw", htk], [pk])
                    cp("dve", dts[:].rearrange("p a h -> p (a h)"), ps[:, 0:64], [pk], ["dts"])
                    store(DTT[t0:t0 + 512, :].rearrange("(a p) h -> p a h", p=128), dts[:], ["dts"], [("DTT", b)])
                S.flush()
            if stop == "p1":
                break
            with contextlib.ExitStack() as st:
                ubr = Ring([sb(st, "ub%d" % i, [128, 4, 514], BF16) for i in range(2)], "ub")
                abtr = Ring([sb(st, "abt%d" % i, [128, 4, 512], BF16) for i in range(2)], "abt")
                acc = sb(st, "acc", [128, 4, 512], F32)
                vat = sb(st, "vat", [128, 4, 512], BF16)
                xbr = Ring([sb(st, "xb%d" % i, [128, 12, 514], BF16) for i in range(2)], "xb")
                xscr = Ring([sb(st, "xsc%d" % i, [128, 12, 512], BF16) for i in range(2)], "xsc")
                xtkr = Ring([sb(st, "xtk%d" % i, [128, 1024], BF16) for i in range(2)], "xtk")
                btkr = Ring([sb(st, "btk%d" % i, [128, 256], BF16) for i in range(2)], "btk")
                segs = [(b * 512, 512, 0, TS) for b in range(8)] + [(TS, 256, TS, TS + 256), (TS + 256, 256, TS + 256, T)]
                def conv_loads(seg):
                    t0, n, s0, s1 = seg
                    lo, hi = max(t0 - 1, s0), min(t0 + n + 1, s1)
                    off = lo - (t0 - 1)
                    ub, ubk = ubr.next()
                    abt, abk = abtr.next()
                    xb, xbk = xbr.next()
                    if lo == t0:
                        S.op("pool", lambda e, ub=ub: e.memset(ub[:, :, 0:1], 0.0), writes=[ubk])
                        S.op("pool", lambda e, xb=xb: e.memset(xb[:, :, 0:1], 0.0), writes=[xbk])
                    if hi == t0 + n:
                        S.op("pool", lambda e, n=n, ub=ub: e.memset(ub[:, :, n + 1:n + 2], 0.0), writes=[ubk])
                        S.op("pool", lambda e, n=n, xb=xb: e.memset(xb[:, :, n + 1:n + 2], 0.0), writes=[xbk])
                    load(ub[:, :, off:off + hi - lo], UT[:, lo:hi].rearrange("(k p) t -> p k t", p=128), [], [ubk])
                    load(abt[:, :, 0:n], ABT[:, t0:t0 + n].rearrange("(k p) t -> p k t", p=128), [], [abk])
                    load(xb[:, :, off:off + hi - lo], XBCT[:, lo:hi].rearrange("(k p) t -> p k t", p=128), [], [xbk])
                    return (ub, ubk, abt, abk, xb, xbk)

                nxt = conv_loads(segs[0])
                for sgi, (t0, n, s0, s1) in enumerate(segs):
                    ub, ubk, abt, abk, xb, xbk = nxt
                    if sgi + 1 < len(segs):
                        nxt = conv_loads(segs[sgi + 1])
                    xsc, xsk = xscr.next()
                    for c in range(16):
                        src, cw, ci, skey = (ub, aconv, c, ubk) if c < 4 else (xb, sconvw, c - 4, xbk)
                        a = acc[:, c % 4, 0:n]
                        ak = ("acc", c % 4)
                        if c < 4:
                            act(a, src[:, ci, 1:n + 1], AF.Copy, [skey, "prm"], [ak], scale=cw[:, l, 1, ci:ci + 1])
                        else:
                            act(a, src[:, ci, 1:n + 1], AF.Identity, [skey, "prm"], [ak], scale=cw[:, l, 1, ci:ci + 1], bias=sconvb[:, l, ci:ci + 1])
                        stt("dve", a, src[:, ci, 0:n], cw[:, l, 0, ci:ci + 1], a, ALU.mult, ALU.add, [skey, "prm", ak], [ak])
                        stt("dve", a, src[:, ci, 2:n + 2], cw[:, l, 2, ci:ci + 1], a, ALU.mult, ALU.add, [skey, "prm", ak], [ak])
                        if c < 4:
                            tt("dve", vat[:, ci, 0:n], a, abt[:, ci, 0:n], ALU.mult, [ak, abk], ["vat"])
                        else:
                            act(xsc[:, ci, 0:n], a, AF.Silu, [ak], [xsk])
                    store(VAT[:, t0:t0 + n].rearrange("(k p) t -> p k t", p=128), vat[:, :, 0:n], ["vat"], [], eng="sp")
                    store(XSC[:, t0:t0 + n].rearrange("(k p) t -> p k t", p=128), xsc[:, :, 0:n], [xsk], [], eng="sp")
                    for tl in range(n // 128):
                        xtk, xtkk = xtkr.next()
                        btk, btkk = btkr.next()
                        for c in range(8):
                            S.op("pe", lambda e, c=c, tl=tl, xsc=xsc: e.transpose(psbT[:, c * 128:(c + 1) * 128], xsc[:, c, tl * 128:(tl + 1) * 128], identb[:]),
                                 reads=[xsk, "identb"], writes=["psbT"])
                        cp("act", xtk[:], psbT[:, 0:1024], ["psbT"], [xtkk])
                        store(XTOK[t0 + tl * 128:t0 + (tl + 1) * 128, :], xtk[:], [xtkk], [], eng="sp")
                        for c in range(2):
                            S.op("pe", lambda e, c=c, tl=tl, xsc=xsc: e.transpose(psbT[:, c * 128:(c + 1) * 128], xsc[:, 8 + c, tl * 128:(tl + 1) * 128], identb[:]),
                                 reads=[xsk, "identb"], writes=["psbT"])
                        cp("dve", btk[:], psbT[:, 0:256], ["psbT"], [btkk])
                        store(BTOK[t0 + tl * 128:t0 + (tl + 1) * 128, :], btk[:], [btkk], [], eng="sp")
                S.flush()
            if stop == "conv":
                break
            with contextlib.ExitStack() as st:
                dtr = sb(st, "dtr", [128, NCH, 16], F32)
                ea = sb(st, "ea", [128, 2, 16], F32)
                dtd = [sb(st, "dtd%d" % d, [128, NCH, 16], F32) for d in range(2)]
                dta = [sb(st, "dta%d" % d, [128, NCH, 16], F32) for d in range(2)]
                ctk = [sb(st, "ctk%d" % d, [128, NCH, 16], F32) for d in range(2)]
                tot = [sb(st, "tot%d" % d, [128, NCH, 16], F32) for d in range(2)]
                ted = [sb(st, "ted%d" % d, [128, NCH, 16], F32) for d in range(2)]
                cdc = [sb(st, "cdc%d" % d, [128, NCH, 16], F32) for d in range(2)]
                lndt = [sb(st, "lndt%d" % d, [128, NCH, 16], F32) for d in range(2)]
                tri = [triF, triB]
                load(dtr[:], DTT.rearrange("(c p) h -> p c h", p=128), [], ["dtr"])
                act(ea[:], alog[:, l, :, :], AF.Exp, ["prm"], ["ea"])
                for d in range(2):
                    tt("dve", dtd[d][:], dtr[:], dtb[:, l, d, :].unsqueeze(1).to_broadcast([128, NCH, 16]), ALU.add, ["dtr", "prm"], [("dtd", d)])
                    act(dtd[d][:], dtd[d][:], AF.Exp, [("dtd", d)], [("dtd", d)])
                    act(dtd[d][:], dtd[d][:], AF.Ln, [("dtd", d)], [("dtd", d)], bias=1.0)
                    act(lndt[d][:], dtd[d][:], AF.Ln, [("dtd", d)], [("lndt", d)])
                    stt("dve", dta[d][:], dtd[d][:], -1.0, ea[:, d, :].unsqueeze(1).to_broadcast([128, NCH, 16]), ALU.mult, ALU.mult,
                        [("dtd", d), "ea"], [("dta", d)])
                    flat = dta[d][:].rearrange("p c h -> p (c h)")
                    for (lhs, dstt, dk) in ((tri[d], ctk[d], "ctk"), (ones, tot[d], "tot")):
                        dflat = dstt[:].rearrange("p c h -> p (c h)")
                        for (a0, a1) in ((0, 512), (512, NCH * 16)):
                            ps, pk = pring.next()
                            mm(ps[:, 0:a1 - a0], lhs, flat[:, a0:a1], True, True, ["cst", ("dta", d)], [pk])
                            cp("dve", dflat[:, a0:a1], ps[:, 0:a1 - a0], [pk], [(dk, d)])
                    tt("dve", ted[d][:], tot[d][:], ctk[d][:], ALU.subtract, [("tot", d), ("ctk", d)], [("ted", d)])
                    act(ted[d][:], ted[d][:], AF.Exp, [("ted", d)], [("ted", d)])
                    tt("dve", ted[d][:], ted[d][:], dtd[d][:], ALU.mult, [("ted", d), ("dtd", d)], [("ted", d)])
                    act(cdc[d][:], tot[d][:], AF.Exp, [("tot", d)], [("cdc", d)])
                with contextlib.ExitStack() as st2:
                    St = sb(st2, "St", [128, 2, 512], F32)
                    hbr = Ring([sb(st2, "hb%d" % i, [128, 1024], BF16) for i in range(2)], "hb")
                    xkr = Ring([sb(st2, "xk%d" % i, [128, 1024], BF16) for i in range(2)], "xk")
                    bkr = Ring([sb(st2, "bk%d" % i, [128, 256], BF16) for i in range(2)], "bk")
                    xwr = Ring([sb(st2, "xw%d" % i, [128, 1024], BF16) for i in range(2)], "xw")
                    for d in range(2):
                        for si, (s0, slen, smp) in enumerate(SEQS):
                            nchk, c0 = slen // 128, s0 // 128
                            if smp:
                                load(St[:], h0[l, d].rearrange("n (g f) -> n g f", g=2), [], ["St"])
                            else:
                                S.op("pool", lambda e: e.memset(St[:], 0.0), writes=["St"])
                            order = range(nchk) if d == 0 else range(nchk - 1, -1, -1)
                            for ci in order:
                                c = c0 + ci
                                hb, hk = hbr.next()
                                cp("act", hb[:], St[:].rearrange("p g f -> p (g f)"), ["St"], [hk])
                                store(HENT[d, c], hb[:], [hk], [])
                                xk, xkk = xkr.next()
                                bk, bkk = bkr.next()
                                xw, xwk = xwr.next()
                                load(xk[:], XTOK[c * 128:(c + 1) * 128, :], [], [xkk])
                                load(bk[:], BTOK[c * 128:(c + 1) * 128, :], [], [bkk])
                                tt("dve", xw[:].rearrange("p (h q) -> p h q", h=16), xk[:].rearrange("p (h q) -> p h q", h=16),
                                   ted[d][:, c, :].unsqueeze(2).to_broadcast([128, 16, 64]), ALU.mult, [xkk, ("ted", d)], [xwk])
                                for g in range(2):
                                    ps, pk = pring.next()
                                    mm(ps[:], bk[:, g * 128:(g + 1) * 128], xw[:, g * 512:(g + 1) * 512], True, True, [bkk, xwk], [pk])
                                    sg3 = St[:, g, :].rearrange("p (h q) -> p h q", h=8)
                                    tt("dve", sg3, sg3, cdc[d][:, c, g * 8:(g + 1) * 8].unsqueeze(2).to_broadcast([128, 8, 64]), ALU.mult,
                                       ["St", ("cdc", d)], ["St"])
                                    tt("dve", St[:, g, :], St[:, g, :], ps[:], ALU.add, ["St", pk], ["St"])
                            if not smp:
                                store(nssm[l, d, si - 1], St[:].rearrange("p g f -> p (g f)"), ["St"], [])
                    S.flush()
                if stop == "ssd1":
                    break
                with contextlib.ExitStack() as st2:
                    xs3r = Ring([sb(st2, "xs3%d" % i, [128, 12, 128], BF16) for i in range(2)], "xs3")
                    xk2r = Ring([sb(st2, "xk2%d" % i, [128, 1024], BF16) for i in range(2)], "xk2")
                    her = [Ring([sb(st2, "he%d%d" % (d, i), [128, 1024], BF16) for i in range(2)], "he%d" % d) for d in range(2)]
                    zsr = Ring([sb(st2, "zs%d" % i, [128, 8, 128], BF16) for i in range(1)], "zs")
                    cbm = [sb(st2, "cbm%d" % d, [128, 2, 128], F32) for d in range(2)]
                    crs = Ring([sb(st2, "crs%d" % i, [128, 512], F32) for i in range(8)], "crs")
                    arg = Ring([sb(st2, "arg%d" % i, [128, 512], F32) for i in range(8)], "arg")
                    Wt = [sb(st2, "Wt%d" % d, [128, 16, 128], BF16) for d in range(2)]
                    Csc = [sb(st2, "Csc%d" % d, [128, 16, 128], BF16) for d in range(2)]
                    yg = sb(st2, "yg", [128, 8, 128], F32)
                    sqy = sb(st2, "sqy", [128, 2, 128], F32)
                    rsy = sb(st2, "rsy", [128, 128], F32)
                    lny = sb(st2, "lny", [128, 128], F32)
                    ynr = Ring([sb(st2, "yn%d" % i, [128, 8, 128], BF16) for i in range(1)], "yn")
                    cumTr = Ring([sb(st2, "cumT%d" % i, [128, 128], F32) for i in range(2)], "cumT")
                    Wtr = [Ring([Wt[d], sb(st2, "Wtb%d" % d, [128, 16, 128], BF16)], "Wt%d" % d) for d in range(2)]
                    Cscr = [Ring([Csc[d], sb(st2, "Cscb%d" % d, [128, 16, 128], BF16)], "Csc%d" % d) for d in range(2)]

                    def stageA(c):
                        tk0 = c * 128
                        xs3, xs3k = xs3r.next()
                        xk2, xk2k = xk2r.next()
                        load(xs3[:], XSC[:, tk0:tk0 + 128].rearrange("(k p) t -> p k t", p=128), [], [xs3k])
                        load(xk2[:], XTOK[tk0:tk0 + 128, :], [], [xk2k])
                        he = []
                        for d in range(2):
                            t_, k_ = her[d].next()
                            load(t_[:], HENT[d, c], [], [k_])
                            he.append((t_, k_))
                        psA, pkA = pring.next()
                        for g in range(2):
                            mm(psA[:, g * 128:(g + 1) * 128], xs3[:, 8 + g, :], xs3[:, 10 + g, :], True, True, [xs3k], [pkA])
                        for d in range(2):
                            tt("dve", cbm[d][:], psA[:, 0:256].rearrange("p (g i) -> p g i", g=2), tri[d].unsqueeze(1).to_broadcast([128, 2, 128]),
                               ALU.mult, [pkA, "cst"], [("cbm", d)])
                        WC = []
                        QS = []
                        for d in range(2):
                            wt_, wtk = Wtr[d].next()
                            cs_, csk = Cscr[d].next()
                            WC.append((wt_, wtk, cs_, csk))
                            psT, pkT = pring.next()
                            mm(psT[0:16, 0:128], ctk[d][:, c, :], identF, True, True, [("ctk", d), "cst"], [pkT])
                            cT, cTk = cumTr.next()
                            cp("act", cT[0:16, :], psT[0:16, 0:128], [pkT], [cTk])
                            qs = []
                            for q in range(4):
                                psc, pkc = pring.next()
                                for hh in range(4):
                                    mm(psc[:, hh * 128:(hh + 1) * 128], sel[:, 4 * q + hh, :], cT[0:16, :], True, True, ["cst", cTk], [pkc])
                                crn, arn = crs.next(), arg.next()
                                qs.append((psc, pkc, crn, arn, arn, crn))
                            for q in range(4):
                                psc, pkc, (cr, crk), (ar, ark), _, _ = qs[q]
                                cp("dve", cr[:], psc[:], [pkc], [crk])
                                for hh in range(4):
                                    ts("dve", ar[:, hh * 128:(hh + 1) * 128], psc[:, hh * 128:(hh + 1) * 128], ctk[d][:, c, 4 * q + hh:4 * q + hh + 1], 0.0,
                                       ALU.subtract, ALU.min, [pkc, ("ctk", d)], [ark])
                            QS.append(qs)
                        for d in range(2):
                            for q in range(4):
                                _, _, (cr, crk), (ar, ark), (et, etk), (ec, eck) = QS[d][q]
                                for hh in range(4):
                                    act(et[:, hh * 128:(hh + 1) * 128], ar[:, hh * 128:(hh + 1) * 128], AF.Exp, [ark, ("lndt", d)], [etk],
                                        bias=lndt[d][:, c, 4 * q + hh:4 * q + hh + 1])
                                act(ec[:], cr[:], AF.Exp, [crk], [eck])
                        for d in range(2):
                            wt_, wtk, cs_, csk = WC[d]
                            for q in range(4):
                                g = q // 2
                                _, _, _, _, (et, etk), (ec, eck) = QS[d][q]
                                tt("dve", wt_[:, 4 * q:4 * q + 4, :], et[:].rearrange("p (h i) -> p h i", h=4),
                                   cbm[d][:, g, :].unsqueeze(1).to_broadcast([128, 4, 128]), ALU.mult, [etk, ("cbm", d)], [wtk])
                                tt("dve", cs_[:, 4 * q:4 * q + 4, :], ec[:].rearrange("p (h i) -> p h i", h=4),
                                   xs3[:, 10 + g, :].unsqueeze(1).to_broadcast([128, 4, 128]), ALU.mult, [eck, xs3k], [csk])
                        return dict(c=c, xs3=(xs3, xs3k), xk2=(xk2, xk2k), he=he, WC=WC)

                    def stageB(cx):
                        c = cx["c"]
                        tk0 = c * 128
                        xs3, xs3k = cx["xs3"]
                        xk2, xk2k = cx["xk2"]
                        zs, zsk = zsr.next()
                        load(zs[:], ZT[:, tk0:tk0 + 128].rearrange("(k p) t -> p k t", p=128), [], [zsk])
                        he, WC = cx["he"], cx["WC"]
                        psY = [plong.next(), plong.next()]
                        for h in range(16):
                            kc, half = h // 2, h % 2
                            pY, pYk = psY[kc // 4]
                            out = pY[half * 64:(half + 1) * 64, (kc % 4) * 128:(kc % 4 + 1) * 128]
                            for d in range(2):
                                wt_, wtk, cs_, csk = WC[d]
                                mm(out, xk2[:, h * 64:(h + 1) * 64], wt_[:, h, :], d == 0, False, [xk2k, wtk], [pYk])
                                mm(out, he[d][0][:, h * 64:(h + 1) * 64], cs_[:, h, :], False, d == 1, [he[d][1], csk], [pYk])
                        for kc in range(8):
                            pY, pYk = psY[kc // 4]
                            stt("dve", yg[:, kc, :], xs3[:, kc, :], dsum[:, l, kc:kc + 1], pY[:, (kc % 4) * 128:(kc % 4 + 1) * 128], ALU.mult, ALU.add,
                                [xs3k, "dsum", pYk], ["yg"])
                        tt("dve", yg[:], yg[:], zs[:], ALU.mult, ["yg", zsk], ["yg"])
                        ps, pk = pring.next()
                        for kc in range(8):
                            act(sqy[:, kc % 2, :], yg[:, kc, :], AF.Square, ["yg"], [("sqy", kc % 2)])
                            mm(ps[:, 0:128], ones, sqy[:, kc % 2, :], kc == 0, kc == 7, ["cst", ("sqy", kc % 2)], [pk])
                        rstd_from_ssq(ps[:, 0:128], 1024.0, rsy[:], lny[:], [pk], ["rsy"], "lny")
                        yn, ynk = ynr.next()
                        for kc in range(8):
                            stt("dve", yn[:, kc, :], yg[:, kc, :], snw[:, l, kc:kc + 1], rsy[:], ALU.mult, ALU.mult, ["yg", "rsy", "prm"], [ynk])
                        store(YBT[:, tk0:tk0 + 128].rearrange("(k p) t -> p k t", p=128), yn[:], [ynk], [])

                    ctxs = {0: stageA(0)}
                    for c in range(NCH):
                        if c + 1 < NCH:
                            ctxs[c + 1] = stageA(c + 1)
                        stageB(ctxs.pop(c))
                    S.flush()
            if stop == "ssd":
                break
            with contextlib.ExitStack() as st:
                ckvall = sb(st, "ckvall", [128, 2, 4608], BF16)
                KTr = [sb(st, "KT%d" % i, [128, 4608], BF16) for i in range(2)]
                VA = [sb(st, "VA%d" % i, [128, 36, 128], BF16) for i in range(2)]
                qhr = Ring([sb(st, "qh%d" % i, [128, 4096], BF16) for i in range(2)], "qh")
                PT = Ring([sb(st, "PT%d" % i, [128, 512], BF16) for i in range(3)], "PT")
                Lt = sb(st, "Lt", [128, 512], F32)
                Rt = sb(st, "Rt", [128, 512], F32)
                ATr = Ring([sb(st, "AT%d" % i, [128, 512], BF16) for i in range(2)], "AT")
                S.op("pool", lambda e: e.memset(VA[0][:, :, 64:128], 1.0), writes=[("VA", 0)])
                S.op("pool", lambda e: e.memset(VA[1][:, :, 0:64], 1.0), writes=[("VA", 1)])
                S.dma("pool", lambda e: e.dma_start(out=ckvall[:, :, 0:512], in_=cckvT[l].rearrange("(k p) t -> p k t", p=128)), writes=["ckvall"])
                for i in range(2):
                    S.dma("pool", lambda e, i=i: e.dma_start(out=KTr[i][64:96, 0:512], in_=ckrT[l]), writes=[("KT", i)])
                if l + 1 < nl:
                    stgA = Ring([sb(st, "cva%d" % i, [128, 6144], BF16) for i in range(3)], "cva")
                    convert_weights(l + 1, stgA)
                for (nctx, k0, nlat, q0, nq, qblk) in ((512, 0, TS, 0, TS, 512), (0, TS, 256, TS, 256, 256), (0, TS + 256, 256, TS + 256, 256, 256)):
                    nk = nctx + nlat
                    ntile = nk // 128
                    load(ckvall[:, :, nctx:nk], CKVT[:, k0:k0 + nlat].rearrange("(k p) t -> p k t", p=128), [], ["ckvall"])
                    for i in range(2):
                        load(KTr[i][64:96, nctx:nk], KRT[:, k0:k0 + nlat], [], [("KT", i)])
                    for h in range(8):
                        par = h % 2
                        va, vak = VA[par], ("VA", par)
                        KT, ktk = KTr[par], ("KT", par)
                        voff = par * 64
                        for kb in range((nk + 511) // 512):
                            w = min(512, nk - kb * 512)
                            ps, pk = pring.next()
                            for kc in range(2):
                                mm(ps[0:64, 0:w], ukvw[:, kc, h * 128:h * 128 + 64], ckvall[:, kc, kb * 512:kb * 512 + w], kc == 0, kc == 1, ["ukvw", "ckvall"], [pk])
                            cp("act" if kb % 2 == 0 else "dve", KT[0:64, kb * 512:kb * 512 + w], ps[0:64, 0:w], [pk], [ktk])
                        for tg in range(0, ntile, 8):
                            nt = min(8, ntile - tg)
                            ps, pk = pring.next()
                            for j in range(nt):
                                for kc in range(2):
                                    mm(ps[:, j * 64:(j + 1) * 64], ckvall[:, kc, (tg + j) * 128:(tg + j + 1) * 128], ukvw[:, kc, h * 128 + 64:h * 128 + 128],
                                       kc == 0, kc == 1, ["ukvw", "ckvall"], [pk])
                            cp("dve", va[:, tg:tg + nt, voff:voff + 64], ps[:, 0:nt * 64].rearrange("p (t d) -> p t d", d=64), [pk], [vak])
                        qh, qhk = qhr.next()
                        load(qh[0:96, 0:nq], QT[h, :, q0:q0 + nq], [], [qhk])
                        items = [(qb, t) for qb in range(nq // qblk) for t in range(ntile)]
                        SKEW = 2
                        pend = {}
                        cur = {}
                        for i in range(len(items) + SKEW):
                            if i < len(items):
                                qb, t = items[i]
                                psS, pkS = pring.next()
                                mm(psS[:, 0:qblk], KT[0:96, t * 128:(t + 1) * 128], qh[0:96, qb * qblk:(qb + 1) * qblk], True, True, [ktk, qhk], [pkS])
                                pend[i] = (psS, pkS)
                            if i >= SKEW:
                                qb, t = items[i - SKEW]
                                psS, pkS = pend.pop(i - SKEW)
                                if t == 0:
                                    cur[qb] = plong.next()
                                psO, pkO = cur[qb]
                                pt, ptk = PT.next()
                                act(pt[:, 0:qblk], psS[:, 0:qblk], AF.Exp, [pkS], [ptk], scale=SCALE)
                                mm(psO[:, 0:qblk], va[:, t, :], pt[:, 0:qblk], t == 0, t == ntile - 1, [vak, ptk], [pkO])
                                if t == ntile - 1:
                                    orow, drow = voff, 64 - voff
                                    act(Lt[orow:orow + 64, 0:qblk], psO[drow:drow + 64, 0:qblk], AF.Ln, [pkO], ["Lt"])
                                    act(Rt[orow:orow + 64, 0:qblk], Lt[orow:orow + 64, 0:qblk], AF.Exp, ["Lt"], ["Rt"], scale=-1.0)
                                    at, atk = ATr.next()
                                    tt("dve", at[orow:orow + 64, 0:qblk], psO[orow:orow + 64, 0:qblk], Rt[orow:orow + 64, 0:qblk], ALU.mult, [pkO, "Rt"], [atk])
                                    r0 = (h // 2) * 128 + orow
                                    store(ATT[r0:r0 + 64, q0 + qb * qblk:q0 + (qb + 1) * qblk], at[orow:orow + 64, 0:qblk], [atk], [])
                S.flush()
            if stop == "attn":
                break
            with contextlib.ExitStack() as st:
                xt = sb(st, "xt3", [128, 8, 512], F32)
                ht = sb(st, "ht3", [128, 8, 512], BF16)
                va_ = sb(st, "va3", [128, 4, 512], BF16)
                yb_ = sb(st, "yb3", [128, 8, 512], BF16)
                at_ = sb(st, "at3", [128, 4, 512], BF16)
                macc = sb(st, "macc", [128, 4, 512], F32)
                sigr = Ring([sb(st, "sig%d" % i, [128, 512], F32) for i in range(2)], "sig")
                tmr = Ring([sb(st, "tm%d" % i, [128, 512], F32) for i in range(2)], "tm")
                merged = sb(st, "merged", [128, 8, 512], BF16)
                h2 = sb(st, "h2", [128, 8, 512], BF16)
                gt = sb(st, "gt", [128, 22, 512], BF16)
                sar = Ring([sb(st, "sa%d" % i, [128, 512], F32) for i in range(2)], "sa")
                sqt = sb(st, "sqt3", [128, 2, 512], F32)
                rst = sb(st, "rst3", [128, 512], F32)
                lnt = sb(st, "lnt3", [128, 512], F32)
                tmpf = sb(st, "tmpf3", [128, 2, 512], F32)
                last = (l == nl - 1)
                yo = sb(st, "yo", [128, 8, 512], F32) if last else None
                for b in range(NB):
                    r = 0 if b < 8 else 1
                    t0 = b * 512
                    fm = lambda dr: dr[:, t0:t0 + 512].rearrange("(k p) t -> p k t", p=128)
                    def p3_loads(bb):
                        f2 = lambda dr: dr[:, bb * 512:(bb + 1) * 512].rearrange("(k p) t -> p k t", p=128)
                        load(ht[:], f2(HT), [], ["ht"])
                        load(va_[:], f2(VAT), [], ["va_"])
                        load(yb_[:], f2(YBT), [], ["yb_"])
                        load(at_[:], f2(ATT), [], ["at_"])

                    if b == 0:
                        p3_loads(0)
                    load(xt[:], fm(xsrc), [], ["xt"])
                    for cg in range(2):
                        for br, (wsrc, nkc, rt_, rk) in enumerate((("wa", 4, va_, "va_"), ("wb", 8, yb_, "yb_"), ("wc", 4, at_, "at_"))):
                            wy, wyk = wloadb(WB[l % 2][wsrc][:, cg * 512:(cg + 1) * 512], nkc, 512)
                            c0 = C_G + br * 1024 + cg * 512
                            wg, wgk = wloadb(WB[l % 2]["win"][:, c0:c0 + 512], 8, 512)
                            for oc in range(4):
                                psYy, pky = pring.next()
                                for kc in range(nkc):
                                    mm(psYy[:], wy[:, kc, oc * 128:(oc + 1) * 128], rt_[:, kc, :], kc == 0, kc == nkc - 1, [wyk, rk], [pky])
                                psG, pkg = pring.next()
                                for kc in range(8):
                                    mm(psG[:], wg[:, kc, oc * 128:(oc + 1) * 128], ht[:, kc, :], kc == 0, kc == 7, [wgk, "ht"], [pkg])
                                sg, sgk = sigr.next()
                                act(sg[:], psG[:], AF.Sigmoid, [pkg], [sgk])
                                if br == 0:
                                    tt("dve", macc[:, oc, :], psYy[:], sg[:], ALU.mult, [pky, sgk], [("macc", oc)])
                                else:
                                    tm, tmk = tmr.next()
                                    tt("dve", tm[:], psYy[:], sg[:], ALU.mult, [pky, sgk], [tmk])
                                    if br == 1:
                                        tt("dve", macc[:, oc, :], macc[:, oc, :], tm[:], ALU.add, [("macc", oc), tmk], [("macc", oc)])
                                    else:
                                        tt("dve", merged[:, cg * 4 + oc, :], macc[:, oc, :], tm[:], ALU.add, [("macc", oc), tmk], ["merged"])
                    if b + 1 < NB:
                        p3_loads(b + 1)
                    for cg in range(2):
                        wo, wok = wloadb(WB[l % 2]["wo"][:, cg * 512:(cg + 1) * 512], 8, 512)
                        for oc in range(4):
                            ps, pk = pring.next()
                            for kc in range(8):
                                mm(ps[:], wo[:, kc, oc * 128:(oc + 1) * 128], merged[:, kc, :], kc == 0, kc == 7, [wok, "merged"], [pk])
                            o8 = cg * 4 + oc
                            stt("dve", xt[:, o8, :], ps[:], modcol(l, 2, o8, r), xt[:, o8, :], ALU.mult, ALU.add, [pk, "MOD", "xt"], ["xt"])
                    norm_mod((sqt, rst, lnt, tmpf), xt, h2, lambda kc: A2[:, l, kc, r:r + 1], lambda kc: modcol(l, 3, kc, r), "xt", "h2")
                    for fg in range(6):
                        ncol = 512 if fg < 5 else 256
                        w1, w1k = wloadb(WB[l % 2]["wf1"][:, fg * 512:fg * 512 + ncol], 8, ncol)
                        w3, w3k = wloadb(WB[l % 2]["wf3"][:, fg * 512:fg * 512 + ncol], 8, ncol)
                        for oc in range(ncol // 128):
                            j = fg * 4 + oc
                            psA, pka = pring.next()
                            for kc in range(8):
                                mm(psA[:], w1[:, kc, oc * 128:(oc + 1) * 128], h2[:, kc, :], kc == 0, kc == 7, [w1k, "h2"], [pka])
                            psB, pkb = pring.next()
                            for kc in range(8):
                                mm(psB[:], w3[:, kc, oc * 128:(oc + 1) * 128], h2[:, kc, :], kc == 0, kc == 7, [w3k, "h2"], [pkb])
                            sa, sak = sar.next()
                            act(sa[:], psA[:], AF.Silu, [pka], [sak])
                            tt("dve", gt[:, j, :], sa[:], psB[:], ALU.mult, [sak, pkb], [("gt", j)])
                    for cg in range(4):
                        w2, w2k = wloadb(WB[l % 2]["wf2"][:, cg * 256:(cg + 1) * 256], 22, 256)
                        for oc in range(2):
                            ps, pk = pring.next()
                            for j in range(22):
                                mm(ps[:], w2[:, j, oc * 128:(oc + 1) * 128], gt[:, j, :], j == 0, j == 21, [w2k, ("gt", j)], [pk])
                            o8 = cg * 2 + oc
                            stt("dve", xt[:, o8, :], ps[:], modcol(l, 5, o8, r), xt[:, o8, :], ALU.mult, ALU.add, [pk, "MOD", "xt"], ["xt"])
                    if not last:
                        store(fm(XT), xt[:], ["xt"], [])
                    else:
                        norm_mod((sqt, rst, lnt, tmpf), xt, yo, lambda kc: fnw[:, kc:kc + 1], None, "xt", "yo")
                        store(fm(yT), yo[:], ["yo"], [])
                    if stop == "p3" and "XT" in dump:
                        store(fm(XT), xt[:], ["xt"], [])
                S.flush()
        S.flush(final=True)
    return nc


def rope_tables():
    n_rows = TS // 64
    row = np.repeat(np.arange(n_rows, dtype=np.float32), 64)
    col = np.tile(np.arange(64, dtype=np.float32), n_rows)
    inv = (np.float32(10000.0) ** (-np.arange(8, dtype=np.float32) / np.float32(8))).astype(np.float32)
    ang = np.concatenate([row[:, None] * inv, col[:, None] * inv], axis=-1).astype(np.float32)
    cos, sin = np.cos(ang).astype(np.float32), np.sin(ang).astype(np.float32)
    C = np.concatenate([cos, cos], axis=1).T
    Sg = np.concatenate([-sin, sin], axis=1).T
    return np.ascontiguousarray(C), np.ascontiguousarray(Sg)


def make_in_maps(inp, nl=DEPTH):
    f = lambda k: np.asarray(inp[k], np.float32)
    ropeC, ropeS = rope_tables()
    k = np.arange(128)
    triF = (k[:, None] <= k[None, :]).astype(np.float32)
    triB = (k[:, None] >= k[None, :]).astype(np.float32)
    selc = np.zeros((128, 16, 128), np.float32)
    for hh in range(16):
        selc[hh, hh, :] = 1.0
    cst = np.concatenate([triF, triB, np.ones((128, 128), np.float32), np.eye(128, dtype=np.float32), selc.reshape(128, 2048)], axis=1)
    prm = pack_params(inp)
    w_in = f("w_in")
    w_kr2 = np.zeros((DEPTH, D, 2, 96), np.float32)
    krc = w_in[:, :, C_KR:C_KR + 32]
    w_kr2[:, :, 0, 64:96] = krc
    w_kr2[:, :, 1, 64:80] = krc[:, :, 16:32]
    w_kr2[:, :, 1, 80:96] = krc[:, :, 0:16]
    wuq = f("w_uq").reshape(DEPTH, 256, 8, 96)
    w_uq2 = np.zeros((DEPTH, 256, 2, 8, 96), np.float32)
    w_uq2[:, :, 0] = wuq
    w_uq2[:, :, 1, :, 0:64] = wuq[..., 0:64]
    w_uq2[:, :, 1, :, 64:80] = wuq[..., 80:96]
    w_uq2[:, :, 1, :, 80:96] = wuq[..., 64:80]
    shared = {
        "ropeC": ropeC, "ropeS": ropeS, "cst": cst, "prm": prm, "w_in": w_in, "w_kr2": w_kr2, "w_uq2": w_uq2,
        "w_ukv": f("w_ukv"), "w_a_out": f("w_a_out"), "w_b_out": f("w_b_out"), "w_c_out": f("w_c_out"),
        "w_o": f("w_o"), "w_ada": f("w_ada"), "w_ff1": f("w_ff1"), "w_ff3": f("w_ff3"), "w_ff2": f("w_ff2"),
    }
    for k in ("w_in", "w_kr2", "w_uq2", "w_ukv", "w_a_out", "w_b_out", "w_c_out", "w_o", "w_ada", "w_ff1", "w_ff3", "w_ff2"):
        shared[k] = np.ascontiguousarray(shared[k][:nl])
    xs, xp, c, cctx = f("x_sample"), f("x_prompt"), f("c"), f("c_ctx")
    cckv, ckr = f("cache_ckv"), f("cache_krope")
    sf, sbw = f("state_ssm_fwd"), f("state_ssm_bwd")
    maps = []
    for r in range(8):
        bs = r % 4
        xT0 = np.concatenate([xs[bs].T, xp[2 * r].T, xp[2 * r + 1].T], axis=1)
        cd = np.stack([c[bs], cctx], axis=1)
        cd = cd.reshape(8, 128, 2).transpose(1, 0, 2)
        h0 = np.stack([sf[bs], sbw[bs]], axis=1)
        h0 = h0.transpose(0, 1, 4, 2, 3).reshape(DEPTH, 2, 128, 1024)
        m = dict(shared)
        m.update({
            "xT0": np.ascontiguousarray(xT0), "cond": np.ascontiguousarray(cd),
            "cckvT": np.ascontiguousarray(cckv[bs].transpose(0, 2, 1)),
            "ckrT": np.ascontiguousarray(ckr[bs].transpose(0, 2, 1)),
            "h0": np.ascontiguousarray(h0),
        })
        maps.append(m)
    return maps


_NC_CACHE = {}


def kernel(**inputs):
    maps = make_in_maps(inputs)
    if "nc" not in _NC_CACHE:
        _NC_CACHE["nc"] = build_program()
    nc = _NC_CACHE["nc"]
    res = run_bass_kernel_spmd(nc, maps, core_ids=list(range(8)))
    R = res.results
    y_prompt = np.zeros((16, 256, D), np.float32)
    y_sample = np.zeros((4, TS, D), np.float32)
    new_ckv = np.zeros((16, DEPTH, 256, 256), np.float32)
    new_kr = np.zeros((16, DEPTH, 256, 32), np.float32)
    new_f = np.zeros((16, DEPTH, 16, 64, 128), np.float32)
    new_b = np.zeros((16, DEPTH, 16, 64, 128), np.float32)
    for r in range(8):
        yT = R[r]["yT"]
        if r < 4:
            y_sample[r] = yT[:, :TS].T
        for s in range(2):
            q = 2 * r + s
            y_prompt[q] = yT[:, TS + s * 256:TS + (s + 1) * 256].T
            new_ckv[q] = R[r]["nckvT"][:, :, s * 256:(s + 1) * 256].transpose(0, 2, 1)
            new_kr[q] = R[r]["nkrT"][:, :, s * 256:(s + 1) * 256].transpose(0, 2, 1)
            st = R[r]["nssm"][:, :, s].reshape(DEPTH, 2, 128, 16, 64).transpose(0, 1, 3, 4, 2)
            new_f[q] = st[:, 0]
            new_b[q] = st[:, 1]
    return (y_prompt, y_sample, new_ckv, new_kr, new_f, new_b)
```

```python
import contextlib
import math
import os

import numpy as np
import concourse.bass as bass
import concourse.mybir as mybir
from concourse.bass_utils import run_bass_kernel_spmd

F32 = mybir.dt.float32
BF16 = mybir.dt.bfloat16
AF = mybir.ActivationFunctionType
ALU = mybir.AluOpType

D = 1024
DEPTH = 4
TS = 4096
TPR = 512
T = TS + TPR
NB = T // 512
NCH = T // 128
EPS = 1e-6
IN_COLS = 7728
FF = 2816
C_AX, C_AB, C_AC, C_Z, C_XBC, C_DT, C_CQ, C_CKV, C_KR, C_G = 0, 512, 1024, 1536, 2560, 4096, 4112, 4368, 4624, 4656
SCALE = 1.0 / math.sqrt(96.0)
SEQS = [(0, 4096, True), (4096, 256, False), (4352, 256, False)]

ENGS = ("pe", "act", "dve", "pool", "sp")


class Sched:
    def __init__(self, nc, st, n_dma_sems=48):
        self.nc = nc
        self.n_dma_sems = n_dma_sems
        self.esem = {e: st.enter_context(nc.semaphore("s_" + e)) for e in ENGS if e != "sp"}
        self.dsem = [st.enter_context(nc.semaphore("d%d" % i)) for i in range(n_dma_sems)]
        self.dummy = st.enter_context(nc.sbuf_tensor("bar_dummy", [128, 2], F32))
        self.ecount = {e: 0 for e in ENGS}
        self.dma_val = [0] * n_dma_sems
        self.dma_last = [None] * n_dma_sems
        self.dma_rr = {"sw": 0, "hw": 0}
        self.n_sw = 16
        self.waited = {e: {} for e in ENGS}
        self.barrier_tok = None
        self.need_barrier = {e: False for e in ENGS}
        self._reset()

    def _reset(self):
        self.ops = {e: [] for e in ENGS}
        self.last_write = {}
        self.readers = {}
        self.dma_toks = []

    def _record(self, eng, fn, reads, writes, dma, extra_deps=()):
        deps = set(extra_deps)
        for r in reads:
            w = self.last_write.get(r)
            if w is not None:
                deps.add(w)
        for r in writes:
            w = self.last_write.get(r)
            if w is not None:
                deps.add(w)
            for rd in self.readers.get(r, ()):
                deps.add(rd)
        idx = len(self.ops[eng])
        if dma:
            if eng == "pool":
                si = self.dma_rr["sw"]
                self.dma_rr["sw"] = (si + 1) % self.n_sw
            else:
                si = self.n_sw + self.dma_rr["hw"]
                self.dma_rr["hw"] = (self.dma_rr["hw"] + 1) % (self.n_dma_sems - self.n_sw)
            prev = self.dma_last[si]
            if prev is not None:
                deps.add(prev)
            self.dma_val[si] += 16
            tok = ("dma", si, self.dma_val[si])
            self.dma_last[si] = tok
            self.dma_toks.append(tok)
        else:
            tok = ("eng", eng, idx)
        deps = {d for d in deps if not (d[0] == "eng" and d[1] == "pe" and eng == "pe")}
        deps.discard(tok)
        if self.need_barrier[eng] and self.barrier_tok is not None:
            deps.add(self.barrier_tok)
            self.need_barrier[eng] = False
        self.ops[eng].append(dict(fn=fn, deps=deps, flag=False, tok=tok))
        for r in reads:
            lst = self.readers.setdefault(r, [])
            if tok[0] == "eng":
                lst[:] = [t for t in lst if not (t[0] == "eng" and t[1] == eng)]
            lst.append(tok)
        for r in writes:
            self.last_write[r] = tok
            self.readers[r] = []
        return tok

    def op(self, eng, fn, reads=(), writes=(), extra_deps=()):
        return self._record(eng, fn, reads, writes, False, extra_deps)

    def dma(self, eng, fn, reads=(), writes=(), extra_deps=()):
        return self._record(eng, fn, reads, writes, True, extra_deps)

    def flush(self, final=False):
        nc = self.nc
        deps = set()
        for e in ENGS:
            if e == "sp":
                continue
            for i in range(len(self.ops[e]) - 1, -1, -1):
                if self.ops[e][i]["tok"][0] == "eng":
                    deps.add(self.ops[e][i]["tok"])
                    break
        latest = {}
        for t in self.dma_toks:
            if t[1] not in latest or latest[t[1]][2] < t[2]:
                latest[t[1]] = t
        deps.update(latest.values())
        dummy = self.dummy
        coll = self._record("dve", lambda e: e.memset(dummy[:, 0:1], 0.0), (), (), False, deps)
        for e in ENGS:
            for o in self.ops[e]:
                for d in o["deps"]:
                    if d[0] == "eng":
                        self.ops[d[1]][d[2]]["flag"] = True
        self.ops["dve"][coll[2]]["flag"] = True
        cnt = {}
        for e in ENGS:
            c = self.ecount[e]
            arr = []
            for o in self.ops[e]:
                if o["flag"]:
                    c += 1
                arr.append(c)
            cnt[e] = arr
        esem, dsem = self.esem, self.dsem

        def resolve(d):
            if d[0] == "eng":
                return esem[d[1]], cnt[d[1]][d[2]]
            if d[0] == "abs":
                return d[1], d[2]
            return dsem[d[1]], d[2]

        coll_abs = ("abs", esem["dve"], cnt["dve"][coll[2]])

        def run(ename, eng):
            waited = self.waited[ename]
            for o in self.ops[ename]:
                need = {}
                for d in o["deps"]:
                    s, v = resolve(d)
                    k = id(s)
                    if waited.get(k, 0) >= v:
                        continue
                    if k not in need or need[k][1] < v:
                        need[k] = (s, v)
                for k, (s, v) in need.items():
                    eng.wait_ge(s, v)
                    waited[k] = v
                ins = o["fn"](eng)
                if o["tok"][0] == "dma":
                    ins.then_inc(dsem[o["tok"][1]], 16)
                elif o["flag"]:
                    ins.then_inc(esem[ename], 1)
            if final and ename == "sp":
                s, v = resolve(coll_abs)
                eng.wait_ge(s, v)

        if os.environ.get("KDBG_SIM"):
            self._simulate(resolve, coll_abs, final)

        with nc.Block() as block:
            @block.sync
            def _(sync):
                run("sp", sync)

            @block.tensor
            def _(tensor):
                run("pe", tensor)

            @block.scalar
            def _(scalar):
                run("act", scalar)

            @block.vector
            def _(vector):
                run("dve", vector)

            @block.gpsimd
            def _(gpsimd):
                run("pool", gpsimd)

        for e in ENGS:
            if cnt[e]:
                self.ecount[e] = cnt[e][-1]
        self.barrier_tok = coll_abs
        self.need_barrier = {e: True for e in ENGS}
        self._reset()


def _sched_simulate(self, resolve, coll_abs, final):
    if not hasattr(self, "sim_sem"):
        self.sim_sem = {}
    sem = self.sim_sem
    pos = {e: 0 for e in ENGS}
    progress = True
    while progress:
        progress = False
        for e in ENGS:
            while pos[e] < len(self.ops[e]):
                o = self.ops[e][pos[e]]
                ok = True
                for d in o["deps"]:
                    s_, v = resolve(d)
                    if sem.get(id(s_), 0) < v:
                        ok = False
                        break
                if not ok:
                    break
                if o["tok"][0] == "dma":
                    k = id(self.dsem[o["tok"][1]])
                    sem[k] = sem.get(k, 0) + 16
                elif o["flag"]:
                    k = id(self.esem[e])
                    sem[k] = sem.get(k, 0) + 1
                pos[e] += 1
                progress = True
    stuck = {e: (pos[e], len(self.ops[e])) for e in ENGS if pos[e] < len(self.ops[e])}
    if stuck:
        print("SCHED DEADLOCK:", stuck)
        for e in stuck:
            o = self.ops[e][pos[e]]
            print("  ", e, "op", pos[e], "tok", o["tok"], "deps", [(d, resolve(d)[1], sem.get(id(resolve(d)[0]), 0)) for d in o["deps"]])
        raise RuntimeError("sched deadlock")
    else:
        print("sched sim ok:", {e: len(self.ops[e]) for e in ENGS})


Sched._simulate = _sched_simulate


class Ring:
    def __init__(self, tiles, name):
        self.tiles = tiles
        self.name = name
        self.i = 0

    def next(self):
        i = self.i
        self.i = (self.i + 1) % len(self.tiles)
        return self.tiles[i], (self.name, i)


PRM_LAYOUT = [("n1w", 4 * 8), ("n2w", 4 * 8), ("fnw", 8), ("snw", 4 * 8), ("qnw", 4 * 2), ("kvnw", 4 * 2),
              ("aconv", 4 * 3 * 4), ("sconvw", 4 * 3 * 12), ("sconvb", 4 * 12), ("dexp", 4 * 2 * 8),
              ("alog", 4 * 2 * 16), ("dtb", 4 * 2 * 16), ("bada", 4 * 48)]
PRM_OFF = {}
_o = 0
for _n, _s in PRM_LAYOUT:
    PRM_OFF[_n] = (_o, _s)
    _o += _s
NPRM = _o


def _pc(v):
    v = np.asarray(v, np.float32)
    lead = v.shape[:-1]
    c = v.shape[-1] // 128
    v = v.reshape(lead + (c, 128))
    v = np.moveaxis(v, -1, 0)
    return np.ascontiguousarray(v).reshape(128, -1)


def pack_params(inp):
    parts = {
        "n1w": _pc(inp["norm1_w"]), "n2w": _pc(inp["norm2_w"]), "fnw": _pc(inp["final_norm_w"]),
        "snw": _pc(inp["ssm_norm_w"]), "qnw": _pc(inp["q_norm_w"]), "kvnw": _pc(inp["kv_norm_w"]),
        "aconv": _pc(inp["a_conv_w"]), "sconvw": _pc(inp["ssm_conv_w"]), "sconvb": _pc(inp["ssm_conv_b"]),
        "dexp": _pc(np.repeat(np.asarray(inp["ssm_d"], np.float32), 64, axis=-1)),
        "alog": np.broadcast_to(np.asarray(inp["ssm_a_log"], np.float32).reshape(1, -1), (128, 128)),
        "dtb": np.broadcast_to(np.asarray(inp["ssm_dt_bias"], np.float32).reshape(1, -1), (128, 128)),
        "bada": _pc(inp["b_ada"]),
    }
    out = np.zeros((128, NPRM), np.float32)
    for n, (o, s) in PRM_OFF.items():
        assert parts[n].shape == (128, s), (n, parts[n].shape, s)
        out[:, o:o + s] = parts[n]
    return out


def build_program(stop=None, dump=(), nl=DEPTH):
    nc = bass.Bass("TRN2", target_bir_lowering=False)

    def din(name, shape, dt=F32):
        return nc.dram_tensor(name, list(shape), dt, kind="ExternalInput").ap()

    def dout(name, shape, dt=F32):
        return nc.dram_tensor(name, list(shape), dt, kind="ExternalOutput").ap()

    def dscr(name, shape, dt):
        kind = "ExternalOutput" if name in dump else "Internal"
        return nc.dram_tensor(name, list(shape), dt, kind=kind).ap()

    xT0 = din("xT0", [D, T])
    cond = din("cond", [128, 8, 2])
    cckvT = din("cckvT", [DEPTH, 256, 512])
    ckrT = din("ckrT", [DEPTH, 32, 512])
    h0 = din("h0", [DEPTH, 2, 128, 1024])
    ropeC = din("ropeC", [32, TS])
    ropeS = din("ropeS", [32, TS])
    cst = din("cst", [128, 2560])
    prm_d = din("prm", [128, NPRM])
    w_in = din("w_in", [nl, D, IN_COLS])
    w_kr2 = din("w_kr2", [nl, D, 2, 96])
    w_uq2 = din("w_uq2", [nl, 256, 2, 8, 96])
    w_ukv = din("w_ukv", [nl, 256, 1024])
    w_a_out = din("w_a_out", [nl, 512, D])
    w_b_out = din("w_b_out", [nl, D, D])
    w_c_out = din("w_c_out", [nl, 512, D])
    w_o = din("w_o", [nl, D, D])
    w_ada = din("w_ada", [nl, D, 6 * D])
    w_ff1 = din("w_ff1", [nl, D, FF])
    w_ff3 = din("w_ff3", [nl, D, FF])
    w_ff2 = din("w_ff2", [nl, FF, D])

    yT = dout("yT", [D, T])
    nckvT = dout("nckvT", [DEPTH, 256, 512])
    nkrT = dout("nkrT", [DEPTH, 32, 512])
    nssm = dout("nssm", [DEPTH, 2, 2, 128, 1024])

    XT = dscr("XT", [D, T], F32)
    HT = dscr("HT", [D, T], BF16)
    UT = dscr("UT", [512, T], BF16)
    ABT = dscr("ABT", [512, T], BF16)
    ZT = dscr("ZT", [D, T], BF16)
    XBCT = dscr("XBCT", [1536, T], BF16)
    DTT = dscr("DTT", [T, 16], F32)
    QT = dscr("QT", [8, 96, T], BF16)
    CKVT = dscr("CKVT", [256, T], BF16)
    KRT = dscr("KRT", [32, T], BF16)
    VAT = dscr("VAT", [512, T], BF16)
    XSC = dscr("XSC", [1536, T], BF16)
    XTOK = dscr("XTOK", [T, 1024], BF16)
    BTOK = dscr("BTOK", [T, 256], BF16)
    HENT = dscr("HENT", [2, NCH, 128, 1024], BF16)
    YBT = dscr("YBT", [D, T], BF16)
    ATT = dscr("ATT", [512, T], BF16)

    WB = []
    for si in range(2):
        WB.append(dict(
            win=dscr("WBwin%d" % si, [D, IN_COLS], BF16), wa=dscr("WBwa%d" % si, [512, D], BF16),
            wb=dscr("WBwb%d" % si, [D, D], BF16), wc=dscr("WBwc%d" % si, [512, D], BF16),
            wo=dscr("WBwo%d" % si, [D, D], BF16), wf1=dscr("WBwf1%d" % si, [D, FF], BF16),
            wf3=dscr("WBwf3%d" % si, [D, FF], BF16), wf2=dscr("WBwf2%d" % si, [FF, D], BF16)))

    with contextlib.ExitStack() as gst:
        S = Sched(nc, gst)

        _uid = [0]

        def sb(st, name, shape, dt):
            _uid[0] += 1
            return st.enter_context(nc.sbuf_tensor("sb%d_%s" % (_uid[0], name), list(shape), dt))

        prm = sb(gst, "prm", [128, NPRM], F32)
        cstt = sb(gst, "cstt", [128, 2560], F32)
        identb = sb(gst, "identb", [128, 128], BF16)
        MOD = sb(gst, "MOD", [128, DEPTH, 48, 2], F32)
        A1 = sb(gst, "A1", [128, DEPTH, 8, 2], F32)
        A2 = sb(gst, "A2", [128, DEPTH, 8, 2], F32)
        dsum = sb(gst, "dsum", [128, DEPTH, 8], F32)
        wring = Ring([sb(gst, "wr%d" % i, [128, 6144], BF16) for i in range(4)], "wr")
        uqw = sb(gst, "uqw", [128, 2, 2 * 8 * 96], BF16)
        krw = sb(gst, "krw", [128, 8, 2 * 96], BF16)
        dtw = sb(gst, "dtw", [128, 8, 16], BF16)
        ukvw = sb(gst, "ukvw", [128, 2, 1024], BF16)
        psum = [gst.enter_context(nc.psum_tensor("ps%d" % i, [128, 512], F32)) for i in range(7)]
        psbT = gst.enter_context(nc.psum_tensor("psbT", [128, 1024], BF16))
        pring = Ring(psum[0:5], "ps")
        plong = Ring(psum[5:7], "pl")
        triF = cstt[:, 0:128]
        triB = cstt[:, 128:256]
        ones = cstt[:, 256:384]
        identF = cstt[:, 384:512]
        sel = cstt[0:16, 512:2560].rearrange("p (h j) -> p h j", h=16)

        def P(name, l=None):
            o, s = PRM_OFF[name]
            v = prm[:, o:o + s]
            return v

        def pv(name, pattern, **kw):
            o, s = PRM_OFF[name]
            return prm[:, o:o + s].rearrange(pattern, **kw)

        n1w = pv("n1w", "p (l c) -> p l c", l=4)
        n2w = pv("n2w", "p (l c) -> p l c", l=4)
        fnw = P("fnw")
        snw = pv("snw", "p (l c) -> p l c", l=4)
        qnw = pv("qnw", "p (l c) -> p l c", l=4)
        kvnw = pv("kvnw", "p (l c) -> p l c", l=4)
        aconv = pv("aconv", "p (l k c) -> p l k c", l=4, k=3)
        sconvw = pv("sconvw", "p (l k c) -> p l k c", l=4, k=3)
        sconvb = pv("sconvb", "p (l c) -> p l c", l=4)
        dexp = pv("dexp", "p (l d c) -> p l d c", l=4, d=2)
        alog = pv("alog", "p (l d h) -> p l d h", l=4, d=2)
        dtb = pv("dtb", "p (l d h) -> p l d h", l=4, d=2)
        bada = pv("bada", "p (l c) -> p l c", l=4)

        def mm(out, lhsT, rhs, start, stop, rd, wr):
            S.op("pe", lambda e: e.matmul(out, lhsT=lhsT, rhs=rhs, start=start, stop=stop), reads=rd, writes=wr)

        def act(out, in_, func, rd, wr, **kw):
            S.op("act", lambda e: e.activation(out=out, in_=in_, func=func, **kw), reads=rd, writes=wr)

        def tt(eng, out, in0, in1, op, rd, wr):
            S.op(eng, lambda e: e.tensor_tensor(out=out, in0=in0, in1=in1, op=op), reads=rd, writes=wr)

        def ts(eng, out, in0, s1, s2, op0, op1, rd, wr):
            if op1 is None:
                S.op(eng, lambda e: e.tensor_scalar(out=out, in0=in0, scalar1=s1, scalar2=None, op0=op0), reads=rd, writes=wr)
            else:
                S.op(eng, lambda e: e.tensor_scalar(out=out, in0=in0, scalar1=s1, scalar2=s2, op0=op0, op1=op1), reads=rd, writes=wr)

        def stt(eng, out, in0, scalar, in1, op0, op1, rd, wr):
            S.op(eng, lambda e: e.scalar_tensor_tensor(out=out, in0=in0, scalar=scalar, in1=in1, op0=op0, op1=op1), reads=rd, writes=wr)

        def cp(eng, out, in_, rd, wr):
            if eng == "act":
                act(out, in_, AF.Copy, rd, wr)
            else:
                S.op(eng, lambda e: e.tensor_copy(out=out, in_=in_), reads=rd, writes=wr)

        def load(out, in_, rd, wr, eng="sp"):
            return S.dma(eng, lambda e: e.dma_start(out=out, in_=in_), reads=rd, writes=wr)

        def store(out, in_, rd, wr, eng="act"):
            return S.dma(eng, lambda e: e.dma_start(out=out, in_=in_), reads=rd, writes=wr)

        def wload(src2d, n_kc, ncols):
            slot, key = wring.next()
            view = slot[:, 0:n_kc * ncols].rearrange("p (k n) -> p k n", k=n_kc)
            S.dma("pool", lambda e: e.dma_start(out=view, in_=src2d.rearrange("(k p) n -> p k n", p=128)), reads=[], writes=[key])
            return view, key

        def wloadb(src2d, n_kc, ncols):
            slot, key = wring.next()
            view = slot[:, 0:n_kc * ncols].rearrange("p (k n) -> p k n", k=n_kc)
            S.dma("pool", lambda e: e.dma_start(out=view, in_=src2d.rearrange("(k p) n -> p k n", p=128)), reads=[], writes=[key])
            return view, key

        def convert_weights(l, stage_ring):
            wb = WB[l % 2]
            jobs = []
            for c0 in list(range(0, 4096, 512)) + [C_CQ] + list(range(C_G, IN_COLS, 512)):
                jobs.append((w_in[l, :, c0:c0 + 512], wb["win"][:, c0:c0 + 512], 8, 512))
            for c0 in (0, 512):
                jobs.append((w_a_out[l, :, c0:c0 + 512], wb["wa"][:, c0:c0 + 512], 4, 512))
                jobs.append((w_b_out[l, :, c0:c0 + 512], wb["wb"][:, c0:c0 + 512], 8, 512))
                jobs.append((w_c_out[l, :, c0:c0 + 512], wb["wc"][:, c0:c0 + 512], 4, 512))
                jobs.append((w_o[l, :, c0:c0 + 512], wb["wo"][:, c0:c0 + 512], 8, 512))
            for fg in range(6):
                ncol = 512 if fg < 5 else 256
                jobs.append((w_ff1[l, :, fg * 512:fg * 512 + ncol], wb["wf1"][:, fg * 512:fg * 512 + ncol], 8, ncol))
                jobs.append((w_ff3[l, :, fg * 512:fg * 512 + ncol], wb["wf3"][:, fg * 512:fg * 512 + ncol], 8, ncol))
            for cg in range(4):
                jobs.append((w_ff2[l, :, cg * 256:(cg + 1) * 256], wb["wf2"][:, cg * 256:(cg + 1) * 256], 22, 256))
            pend = []
            nst = len(stage_ring.tiles)

            def issue_store(item):
                view, key, dst, n_kc = item
                S.dma("pool", lambda e: e.dma_start(out=dst.rearrange("(k p) n -> p k n", p=128), in_=view), reads=[key], writes=[])

            for (src, dst, n_kc, ncols) in jobs:
                if len(pend) == nst:
                    issue_store(pend.pop(0))
                slot, key = stage_ring.next()
                view = slot[:, 0:n_kc * ncols].rearrange("p (k n) -> p k n", k=n_kc)
                S.dma("pool", lambda e, view=view, src=src: e.dma_start(out=view, in_=src.rearrange("(k p) n -> p k n", p=128)), reads=[], writes=[key])
                pend.append((view, key, dst, n_kc))
            while pend:
                issue_store(pend.pop(0))

        def rstd_from_ssq(ps_ap, n_feat, out_ap, tmp_ap, rd, wr, tmpkey):
            act(tmp_ap, ps_ap, AF.Ln, rd, [tmpkey], scale=1.0 / n_feat, bias=EPS)
            act(out_ap, tmp_ap, AF.Exp, [tmpkey], wr, scale=-0.5)

        with contextlib.ExitStack() as st:
            condt = sb(st, "condt", [128, 8, 2], F32)
            scb = sb(st, "scb", [128, 8, 16], BF16)
            stg0 = Ring([sb(st, "cvs%d" % i, [128, 6144], BF16) for i in range(3)], "cvs")
            convert_weights(0, stg0)
            load(prm[:], prm_d, [], ["prm"])
            load(cstt[:], cst, [], ["cst"])
            load(condt[:], cond, [], ["condt"])
            cp("dve", identb[:], cstt[:, 384:512], ["cst"], ["identb"])
            S.op("pool", lambda e: e.memset(scb[:], 0.0), writes=["scb"])
            act(scb[:, :, 0:2], condt[:], AF.Silu, ["condt", "scb"], ["scb"])
            _ncg = int(os.environ.get("KDBG_NCG", "12"))
            for l in range(nl):
                for cg in range(_ncg):
                    wv, wk = wload(w_ada[l, :, cg * 512:(cg + 1) * 512], 8, 512)
                    for oc in range(4):
                        ps, pk = pring.next()
                        for kc in range(8):
                            mm(ps[:, 0:16], wv[:, kc, oc * 128:(oc + 1) * 128], scb[:, kc, :], kc == 0, kc == 7, [wk, "scb"], [pk])
                        ci = cg * 4 + oc
                        act(MOD[:, l, ci, :], ps[:, 0:2], AF.Identity, [pk, "prm"], ["MOD"], bias=bada[:, l, ci:ci + 1])
            for l in range(nl):
                for (Ax, k0, nw) in ((A1, 8, n1w), (A2, 32, n2w)):
                    ts("dve", Ax[:, l, :, :], MOD[:, l, k0:k0 + 8, :], 1.0, None, ALU.add, None, ["MOD"], ["A"])
                    tt("dve", Ax[:, l, :, :], Ax[:, l, :, :], nw[:, l, :].unsqueeze(2).to_broadcast([128, 8, 2]), ALU.mult, ["A", "prm"], ["A"])
                tt("dve", dsum[:, l, :], dexp[:, l, 0, :], dexp[:, l, 1, :], ALU.add, ["prm"], ["dsum"])
            S.flush()
        if stop == "p0":
            dbgo = dout("dbg_mod", [128, DEPTH * 48 * 2])
            load(dbgo, MOD[:].rearrange("p l c r -> p (l c r)"), ["MOD"], ["dbgo"])
            S.flush(final=True)
            return nc

        def modcol(l, kind, kc, r):
            return MOD[:, l, kind * 8 + kc, r:r + 1]

        def norm_mod(st_tiles, xt, ht, Acol, Bcol, xkey, hkey):
            sqt, rst, lnt, tmpf = st_tiles
            ps, pk = pring.next()
            for kc in range(8):
                act(sqt[:, kc % 2, :], xt[:, kc, :], AF.Square, [xkey], [("sq", kc % 2)])
                mm(ps[:], ones, sqt[:, kc % 2, :], kc == 0, kc == 7, ["cst", ("sq", kc % 2)], [pk])
            rstd_from_ssq(ps[:], 1024.0, rst[:], lnt[:], [pk], ["rst"], "lnt")
            for kc in range(8):
                if Bcol is None:
                    stt("dve", ht[:, kc, :], xt[:, kc, :], Acol(kc), rst[:], ALU.mult, ALU.mult, [xkey, "rst", "A", "prm"], [hkey])
                else:
                    stt("dve", tmpf[:, kc % 2, :], xt[:, kc, :], Acol(kc), rst[:], ALU.mult, ALU.mult, [xkey, "rst", "A", "prm"], [("tmpf", kc % 2)])
                    act(ht[:, kc, :], tmpf[:, kc % 2, :], AF.Identity, [("tmpf", kc % 2), "MOD"], [hkey], bias=Bcol(kc))

        for l in range(nl):
            xsrc = xT0 if l == 0 else XT
            with contextlib.ExitStack() as st:
                xt = sb(st, "xt", [128, 8, 512], F32)
                htr = [sb(st, "ht%d" % i, [128, 8, 512], BF16) for i in range(2)]
                sqt = sb(st, "sqt", [128, 2, 512], F32)
                rst = sb(st, "rst", [128, 512], F32)
                lnt = sb(st, "lnt", [128, 512], F32)
                tmpf = sb(st, "tmpf", [128, 2, 512], F32)
                stg = Ring([sb(st, "stg%d" % i, [128, 4, 512], BF16) for i in range(3)], "stg")
                axt = sb(st, "axt", [128, 4, 512], BF16)
                cqf = sb(st, "cqf", [128, 4, 512], F32)
                nrf = sb(st, "nrf", [128, 2, 512], F32)
                cqn = sb(st, "cqn", [128, 2, 512], BF16)
                ckvn = sb(st, "ckvn", [128, 2, 512], BF16)
                qst = sb(st, "qst", [128, 8, 512], BF16)
                rC = sb(st, "rC", [128, 512], F32)
                rS = sb(st, "rS", [128, 512], F32)
                t1 = sb(st, "t1", [128, 512], F32)
                t2 = sb(st, "t2", [128, 512], F32)
                krt = sb(st, "krt", [128, 512], BF16)
                krf = sb(st, "krf", [128, 512], F32)
                dts = sb(st, "dts", [128, 4, 16], F32)
                S.dma("pool", lambda e, l=l: e.dma_start(out=uqw[:], in_=w_uq2[l].rearrange("(k p) v h r -> p k (v h r)", p=128)), writes=["uqw"])
                S.dma("pool", lambda e, l=l: e.dma_start(out=krw[:], in_=w_kr2[l].rearrange("(k p) v r -> p k (v r)", p=128)), writes=["krw"])
                S.dma("pool", lambda e, l=l: e.dma_start(out=dtw[:], in_=w_in[l, :, C_DT:C_DT + 16].rearrange("(k p) n -> p k n", p=128)), writes=["dtw"])
                S.dma("pool", lambda e, l=l: e.dma_start(out=ukvw[:], in_=w_ukv[l].rearrange("(k p) n -> p k n", p=128)), writes=["ukvw"])
                uq5 = uqw[:].rearrange("p k (v h r) -> p k v h r", v=2, h=8)
                kr4 = krw[:].rearrange("p k (v r) -> p k v r", v=2)
                _steps = os.environ.get("KDBG_P1", "groups,mla,q,kr,dt").split(",")
                _blks = [int(v) for v in os.environ.get("KDBG_BLKS", ",".join(str(i) for i in range(NB))).split(",")]
                def p1_norm(bb):
                    rr = 0 if bb < 8 else 1
                    tt0 = bb * 512
                    hto = htr[bb % 2]
                    load(xt[:], xsrc[:, tt0:tt0 + 512].rearrange("(k p) t -> p k t", p=128), [], ["xt"])
                    norm_mod((sqt, rst, lnt, tmpf), xt, hto,
                             lambda kc: A1[:, l, kc, rr:rr + 1], lambda kc: modcol(l, 0, kc, rr), "xt", ("ht", bb % 2))
                    store(HT[:, tt0:tt0 + 512].rearrange("(k p) t -> p k t", p=128), hto[:], [("ht", bb % 2)], [])

                p1_norm(_blks[0])
                for bi, b in enumerate(_blks):
                    r = 0 if b < 8 else 1
                    smp = b < 8
                    t0 = b * 512
                    ht = htr[b % 2]
                    htk = ("ht", b % 2)
                    if smp:
                        load(rC[64:96, :], ropeC[:, t0:t0 + 512], [], ["rC"])
                        load(rS[64:96, :], ropeS[:, t0:t0 + 512], [], ["rS"])
                    groups = [("ax", C_AX), ("ab", C_AB), ("ac", C_AC), ("z", C_Z), ("z", C_Z + 512),
                              ("xbc", C_XBC), ("xbc", C_XBC + 512), ("xbc", C_XBC + 1024), ("cqkv", C_CQ)]
                    if "groups" not in _steps:
                        groups = []
                    if not groups and bi + 1 < len(_blks):
                        p1_norm(_blks[bi + 1])
                    for gi, (kind, c0) in enumerate(groups):
                        if gi == 3 and bi + 1 < len(_blks):
                            p1_norm(_blks[bi + 1])
                        wv, wk = wloadb(WB[l % 2]["win"][:, c0:c0 + 512], 8, 512)
                        if kind in ("ab", "ac", "z", "xbc"):
                            sg, sk = stg.next()
                        for oc in range(4):
                            ps, pk = pring.next()
                            for kc in range(8):
                                mm(ps[:], wv[:, kc, oc * 128:(oc + 1) * 128], ht[:, kc, :], kc == 0, kc == 7, [wk, htk], [pk])
                            if kind == "ax":
                                cp("act", axt[:, oc, :], ps[:], [pk], ["axt"])
                            elif kind == "ab":
                                cp("act", sg[:, oc, :], ps[:], [pk], [sk])
                            elif kind == "ac":
                                tt("dve", sg[:, oc, :], ps[:], axt[:, oc, :], ALU.mult, [pk, "axt"], [sk])
                            elif kind == "z":
                                act(sg[:, oc, :], ps[:], AF.Silu, [pk], [sk])
                            elif kind == "xbc":
                                cp("act" if oc % 2 == 0 else "dve", sg[:, oc, :], ps[:], [pk], [sk])
                            else:
                                cp("act" if oc % 2 == 0 else "dve", cqf[:, oc, :], ps[:], [pk], [("cqf", oc // 2)])
                        if kind in ("ab", "ac", "z", "xbc"):
                            dst = {"ab": ABT, "ac": UT, "z": ZT, "xbc": XBCT}[kind]
                            r0 = c0 - {"ab": C_AB, "ac": C_AC, "z": C_Z, "xbc": C_XBC}[kind]
                            store(dst[r0:r0 + 512, t0:t0 + 512].rearrange("(k p) t -> p k t", p=128), sg[:], [sk], [(kind + "T", b, r0)])
                    for half, nw, dstb in ((0, qnw, cqn), (1, kvnw, ckvn)) if "mla" in _steps else ():
                        ps, pk = pring.next()
                        for j in range(2):
                            act(sqt[:, j, :], cqf[:, half * 2 + j, :], AF.Square, [("cqf", half)], [("sq", j)])
                            mm(ps[:], ones, sqt[:, j, :], j == 0, j == 1, ["cst", ("sq", j)], [pk])
                        rstd_from_ssq(ps[:], 256.0, rst[:], lnt[:], [pk], ["rst"], "lnt")
                        for j in range(2):
                            stt("dve", nrf[:, j, :], cqf[:, half * 2 + j, :], nw[:, l, j:j + 1], rst[:], ALU.mult, ALU.mult,
                                [("cqf", half), "rst", "prm"], [("nrf", j)])
                            cp("act", dstb[:, j, :], nrf[:, j, :], [("nrf", j)], [("lat", half)])
                        if half == 1:
                            store(CKVT[:, t0:t0 + 512].rearrange("(k p) t -> p k t", p=128), ckvn[:], [("lat", 1)], [("CKVT", b)])
                            if not smp:
                                store(nckvT[l].rearrange("(k p) t -> p k t", p=128), nrf[:], [("nrf", 0), ("nrf", 1)], [("nckv", l)])
                    for h in range(8) if "q" in _steps else ():
                        psn, pkn = pring.next()
                        for kc in range(2):
                            mm(psn[0:96, :], uq5[:, kc, 0, h, :], cqn[:, kc, :], kc == 0, kc == 1, ["uqw", ("lat", 0)], [pkn])
                        if smp:
                            pss, pks = pring.next()
                            for kc in range(2):
                                mm(pss[0:96, :], uq5[:, kc, 1, h, :], cqn[:, kc, :], kc == 0, kc == 1, ["uqw", ("lat", 0)], [pks])
                            cp("act", qst[0:64, h, :], psn[0:64, :], [pkn], [("qst", h)])
                            tt("dve", t1[64:96, :], psn[64:96, :], rC[64:96, :], ALU.mult, [pkn, "rC"], ["t1"])
                            tt("dve", t2[64:96, :], pss[64:96, :], rS[64:96, :], ALU.mult, [pks, "rS"], ["t2"])
                            tt("dve", qst[64:96, h, :], t1[64:96, :], t2[64:96, :], ALU.add, ["t1", "t2"], [("qst", h)])
                        else:
                            cp("act", qst[0:96, h, :], psn[0:96, :], [pkn], [("qst", h)])
                    if "q" in _steps:
                        store(QT[:, :, t0:t0 + 512].rearrange("h r t -> r h t"), qst[0:96, :, :], [("qst", h) for h in range(8)], [("QT", b)])
                    if "kr" not in _steps:
                        continue
                    psn, pkn = pring.next()
                    for kc in range(8):
                        mm(psn[0:96, :], kr4[:, kc, 0, :], ht[:, kc, :], kc == 0, kc == 7, ["krw", htk], [pkn])
                    if smp:
                        pss, pks = pring.next()
                        for kc in range(8):
                            mm(pss[0:96, :], kr4[:, kc, 1, :], ht[:, kc, :], kc == 0, kc == 7, ["krw", htk], [pks])
                        tt("dve", t1[64:96, :], psn[64:96, :], rC[64:96, :], ALU.mult, [pkn, "rC"], ["t1"])
                        tt("dve", t2[64:96, :], pss[64:96, :], rS[64:96, :], ALU.mult, [pks, "rS"], ["t2"])
                        tt("dve", krt[64:96, :], t1[64:96, :], t2[64:96, :], ALU.add, ["t1", "t2"], ["krt"])
                    else:
                        cp("dve", krf[64:96, :], psn[64:96, :], [pkn], ["krf"])
                        cp("act", krt[64:96, :], krf[64:96, :], ["krf"], ["krt"])
                        store(nkrT[l], krf[64:96, :], ["krf"], [("nkr", l)])
                    store(KRT[:, t0:t0 + 512], krt[64:96, :], ["krt"], [("KRT", b)])
                    if "dt" not in _steps:
                        continue
                    ps, pk = pring.next()
                    for tl in range(4):
                        for kc in range(8):
                            mm(ps[:, tl * 16:(tl + 1) * 16], ht[:, kc, tl * 128:(tl + 1) * 128], dtw[:, kc, :], kc == 0, kc == 7, ["dtw", htk], [pk])
                    cp("dve", dts[:].rearrange("p a h -> p (a h)"), ps[:, 0:64], [pk], ["dts"])
                    store(DTT[t0:t0 + 512, :].rearrange("(a p) h -> p a h", p=128), dts[:], ["dts"], [("DTT", b)])
                S.flush()
            if stop == "p1":
                break
            with contextlib.ExitStack() as st:
                ubr = Ring([sb(st, "ub%d" % i, [128, 4, 514], BF16) for i in range(2)], "ub")
                abtr = Ring([sb(st, "abt%d" % i, [128, 4, 512], BF16) for i in range(2)], "abt")
                acc = sb(st, "acc", [128, 4, 512], F32)
                vat = sb(st, "vat", [128, 4, 512], BF16)
                xbr = Ring([sb(st, "xb%d" % i, [128, 12, 514], BF16) for i in range(2)], "xb")
                xscr = Ring([sb(st, "xsc%d" % i, [128, 12, 512], BF16) for i in range(2)], "xsc")
                xtkr = Ring([sb(st, "xtk%d" % i, [128, 1024], BF16) for i in range(2)], "xtk")
                btkr = Ring([sb(st, "btk%d" % i, [128, 256], BF16) for i in range(2)], "btk")
                segs = [(b * 512, 512, 0, TS) for b in range(8)] + [(TS, 256, TS, TS + 256), (TS + 256, 256, TS + 256, T)]
                for (t0, n, s0, s1) in segs:
                    lo, hi = max(t0 - 1, s0), min(t0 + n + 1, s1)
                    off = lo - (t0 - 1)
                    ub, ubk = ubr.next()
                    abt, abk = abtr.next()
                    xb, xbk = xbr.next()
                    xsc, xsk = xscr.next()
                    if lo == t0:
                        S.op("pool", lambda e, ub=ub: e.memset(ub[:, :, 0:1], 0.0), writes=[ubk])
                        S.op("pool", lambda e, xb=xb: e.memset(xb[:, :, 0:1], 0.0), writes=[xbk])
                    if hi == t0 + n:
                        S.op("pool", lambda e, n=n, ub=ub: e.memset(ub[:, :, n + 1:n + 2], 0.0), writes=[ubk])
                        S.op("pool", lambda e, n=n, xb=xb: e.memset(xb[:, :, n + 1:n + 2], 0.0), writes=[xbk])
                    load(ub[:, :, off:off + hi - lo], UT[:, lo:hi].rearrange("(k p) t -> p k t", p=128), [], [ubk])
                    load(abt[:, :, 0:n], ABT[:, t0:t0 + n].rearrange("(k p) t -> p k t", p=128), [], [abk])
                    load(xb[:, :, off:off + hi - lo], XBCT[:, lo:hi].rearrange("(k p) t -> p k t", p=128), [], [xbk])
                    for c in range(16):
                        src, cw, ci, skey = (ub, aconv, c, ubk) if c < 4 else (xb, sconvw, c - 4, xbk)
                        a = acc[:, c % 4, 0:n]
                        ak = ("acc", c % 4)
                        if c < 4:
                            act(a, src[:, ci, 1:n + 1], AF.Copy, [skey, "prm"], [ak], scale=cw[:, l, 1, ci:ci + 1])
                        else:
                            act(a, src[:, ci, 1:n + 1], AF.Identity, [skey, "prm"], [ak], scale=cw[:, l, 1, ci:ci + 1], bias=sconvb[:, l, ci:ci + 1])
                        stt("dve", a, src[:, ci, 0:n], cw[:, l, 0, ci:ci + 1], a, ALU.mult, ALU.add, [skey, "prm", ak], [ak])
                        stt("dve", a, src[:, ci, 2:n + 2], cw[:, l, 2, ci:ci + 1], a, ALU.mult, ALU.add, [skey, "prm", ak], [ak])
                        if c < 4:
                            tt("dve", vat[:, ci, 0:n], a, abt[:, ci, 0:n], ALU.mult, [ak, abk], ["vat"])
                        else:
                            act(xsc[:, ci, 0:n], a, AF.Silu, [ak], [xsk])
                    store(VAT[:, t0:t0 + n].rearrange("(k p) t -> p k t", p=128), vat[:, :, 0:n], ["vat"], [])
                    store(XSC[:, t0:t0 + n].rearrange("(k p) t -> p k t", p=128), xsc[:, :, 0:n], [xsk], [])
                    for tl in range(n // 128):
                        xtk, xtkk = xtkr.next()
                        btk, btkk = btkr.next()
                        for c in range(8):
                            S.op("pe", lambda e, c=c, tl=tl, xsc=xsc: e.transpose(psbT[:, c * 128:(c + 1) * 128], xsc[:, c, tl * 128:(tl + 1) * 128], identb[:]),
                                 reads=[xsk, "identb"], writes=["psbT"])
                        cp("act", xtk[:], psbT[:, 0:1024], ["psbT"], [xtkk])
                        store(XTOK[t0 + tl * 128:t0 + (tl + 1) * 128, :], xtk[:], [xtkk], [])
                        for c in range(2):
                            S.op("pe", lambda e, c=c, tl=tl, xsc=xsc: e.transpose(psbT[:, c * 128:(c + 1) * 128], xsc[:, 8 + c, tl * 128:(tl + 1) * 128], identb[:]),
                                 reads=[xsk, "identb"], writes=["psbT"])
                        cp("dve", btk[:], psbT[:, 0:256], ["psbT"], [btkk])
                        store(BTOK[t0 + tl * 128:t0 + (tl + 1) * 128, :], btk[:], [btkk], [])
                S.flush()
            if stop == "conv":
                break
            with contextlib.ExitStack() as st:
                dtr = sb(st, "dtr", [128, NCH, 16], F32)
                ea = sb(st, "ea", [128, 2, 16], F32)
                dtd = [sb(st, "dtd%d" % d, [128, NCH, 16], F32) for d in range(2)]
                dta = [sb(st, "dta%d" % d, [128, NCH, 16], F32) for d in range(2)]
                ctk = [sb(st, "ctk%d" % d, [128, NCH, 16], F32) for d in range(2)]
                tot = [sb(st, "tot%d" % d, [128, NCH, 16], F32) for d in range(2)]
                ted = [sb(st, "ted%d" % d, [128, NCH, 16], F32) for d in range(2)]
                cdc = [sb(st, "cdc%d" % d, [128, NCH, 16], F32) for d in range(2)]
                lndt = [sb(st, "lndt%d" % d, [128, NCH, 16], F32) for d in range(2)]
                tri = [triF, triB]
                load(dtr[:], DTT.rearrange("(c p) h -> p c h", p=128), [], ["dtr"])
                act(ea[:], alog[:, l, :, :], AF.Exp, ["prm"], ["ea"])
                for d in range(2):
                    tt("dve", dtd[d][:], dtr[:], dtb[:, l, d, :].unsqueeze(1).to_broadcast([128, NCH, 16]), ALU.add, ["dtr", "prm"], [("dtd", d)])
                    act(dtd[d][:], dtd[d][:], AF.Exp, [("dtd", d)], [("dtd", d)])
                    act(dtd[d][:], dtd[d][:], AF.Ln, [("dtd", d)], [("dtd", d)], bias=1.0)
                    act(lndt[d][:], dtd[d][:], AF.Ln, [("dtd", d)], [("lndt", d)])
                    stt("dve", dta[d][:], dtd[d][:], -1.0, ea[:, d, :].unsqueeze(1).to_broadcast([128, NCH, 16]), ALU.mult, ALU.mult,
                        [("dtd", d), "ea"], [("dta", d)])
                    flat = dta[d][:].rearrange("p c h -> p (c h)")
                    for (lhs, dstt, dk) in ((tri[d], ctk[d], "ctk"), (ones, tot[d], "tot")):
                        dflat = dstt[:].rearrange("p c h -> p (c h)")
                        for (a0, a1) in ((0, 512), (512, NCH * 16)):
                            ps, pk = pring.next()
                            mm(ps[:, 0:a1 - a0], lhs, flat[:, a0:a1], True, True, ["cst", ("dta", d)], [pk])
                            cp("dve", dflat[:, a0:a1], ps[:, 0:a1 - a0], [pk], [(dk, d)])
                    tt("dve", ted[d][:], tot[d][:], ctk[d][:], ALU.subtract, [("tot", d), ("ctk", d)], [("ted", d)])
                    act(ted[d][:], ted[d][:], AF.Exp, [("ted", d)], [("ted", d)])
                    tt("dve", ted[d][:], ted[d][:], dtd[d][:], ALU.mult, [("ted", d), ("dtd", d)], [("ted", d)])
                    act(cdc[d][:], tot[d][:], AF.Exp, [("tot", d)], [("cdc", d)])
                with contextlib.ExitStack() as st2:
                    St = sb(st2, "St", [128, 2, 512], F32)
                    hbr = Ring([sb(st2, "hb%d" % i, [128, 1024], BF16) for i in range(2)], "hb")
                    xkr = Ring([sb(st2, "xk%d" % i, [128, 1024], BF16) for i in range(2)], "xk")
                    bkr = Ring([sb(st2, "bk%d" % i, [128, 256], BF16) for i in range(2)], "bk")
                    xwr = Ring([sb(st2, "xw%d" % i, [128, 1024], BF16) for i in range(2)], "xw")
                    for d in range(2):
                        for si, (s0, slen, smp) in enumerate(SEQS):
                            nchk, c0 = slen // 128, s0 // 128
                            if smp:
                                load(St[:], h0[l, d].rearrange("n (g f) -> n g f", g=2), [], ["St"])
                            else:
                                S.op("pool", lambda e: e.memset(St[:], 0.0), writes=["St"])
                            order = range(nchk) if d == 0 else range(nchk - 1, -1, -1)
                            for ci in order:
                                c = c0 + ci
                                hb, hk = hbr.next()
                                cp("act", hb[:], St[:].rearrange("p g f -> p (g f)"), ["St"], [hk])
                                store(HENT[d, c], hb[:], [hk], [])
                                xk, xkk = xkr.next()
                                bk, bkk = bkr.next()
                                xw, xwk = xwr.next()
                                load(xk[:], XTOK[c * 128:(c + 1) * 128, :], [], [xkk])
                                load(bk[:], BTOK[c * 128:(c + 1) * 128, :], [], [bkk])
                                tt("dve", xw[:].rearrange("p (h q) -> p h q", h=16), xk[:].rearrange("p (h q) -> p h q", h=16),
                                   ted[d][:, c, :].unsqueeze(2).to_broadcast([128, 16, 64]), ALU.mult, [xkk, ("ted", d)], [xwk])
                                for g in range(2):
                                    ps, pk = pring.next()
                                    mm(ps[:], bk[:, g * 128:(g + 1) * 128], xw[:, g * 512:(g + 1) * 512], True, True, [bkk, xwk], [pk])
                                    sg3 = St[:, g, :].rearrange("p (h q) -> p h q", h=8)
                                    tt("dve", sg3, sg3, cdc[d][:, c, g * 8:(g + 1) * 8].unsqueeze(2).to_broadcast([128, 8, 64]), ALU.mult,
                                       ["St", ("cdc", d)], ["St"])
                                    tt("dve", St[:, g, :], St[:, g, :], ps[:], ALU.add, ["St", pk], ["St"])
                            if not smp:
                                store(nssm[l, d, si - 1], St[:].rearrange("p g f -> p (g f)"), ["St"], [])
                    S.flush()
                if stop == "ssd1":
                    break
                with contextlib.ExitStack() as st2:
                    xs3r = Ring([sb(st2, "xs3%d" % i, [128, 12, 128], BF16) for i in range(2)], "xs3")
                    xk2r = Ring([sb(st2, "xk2%d" % i, [128, 1024], BF16) for i in range(2)], "xk2")
                    her = [Ring([sb(st2, "he%d%d" % (d, i), [128, 1024], BF16) for i in range(2)], "he%d" % d) for d in range(2)]
                    zsr = Ring([sb(st2, "zs%d" % i, [128, 8, 128], BF16) for i in range(2)], "zs")
                    cbm = [sb(st2, "cbm%d" % d, [128, 2, 128], F32) for d in range(2)]
                    crs = Ring([sb(st2, "crs%d" % i, [128, 512], F32) for i in range(4)], "crs")
                    arg = Ring([sb(st2, "arg%d" % i, [128, 512], F32) for i in range(4)], "arg")
                    Wt = [sb(st2, "Wt%d" % d, [128, 16, 128], BF16) for d in range(2)]
                    Csc = [sb(st2, "Csc%d" % d, [128, 16, 128], BF16) for d in range(2)]
                    yg = sb(st2, "yg", [128, 8, 128], F32)
                    sqy = sb(st2, "sqy", [128, 2, 128], F32)
                    rsy = sb(st2, "rsy", [128, 128], F32)
                    lny = sb(st2, "lny", [128, 128], F32)
                    ynr = Ring([sb(st2, "yn%d" % i, [128, 8, 128], BF16) for i in range(2)], "yn")
                    cumTr = Ring([sb(st2, "cumT%d" % i, [128, 128], F32) for i in range(2)], "cumT")
                    Wtr = [Ring([Wt[d], sb(st2, "Wtb%d" % d, [128, 16, 128], BF16)], "Wt%d" % d) for d in range(2)]
                    Cscr = [Ring([Csc[d], sb(st2, "Cscb%d" % d, [128, 16, 128], BF16)], "Csc%d" % d) for d in range(2)]

                    def stageA(c):
                        tk0 = c * 128
                        xs3, xs3k = xs3r.next()
                        xk2, xk2k = xk2r.next()
                        zs, zsk = zsr.next()
                        load(xs3[:], XSC[:, tk0:tk0 + 128].rearrange("(k p) t -> p k t", p=128), [], [xs3k])
                        load(xk2[:], XTOK[tk0:tk0 + 128, :], [], [xk2k])
                        load(zs[:], ZT[:, tk0:tk0 + 128].rearrange("(k p) t -> p k t", p=128), [], [zsk])
                        he = []
                        for d in range(2):
                            t_, k_ = her[d].next()
                            load(t_[:], HENT[d, c], [], [k_])
                            he.append((t_, k_))
                        psA, pkA = pring.next()
                        for g in range(2):
                            mm(psA[:, g * 128:(g + 1) * 128], xs3[:, 8 + g, :], xs3[:, 10 + g, :], True, True, [xs3k], [pkA])
                        for d in range(2):
                            tt("dve", cbm[d][:], psA[:, 0:256].rearrange("p (g i) -> p g i", g=2), tri[d].unsqueeze(1).to_broadcast([128, 2, 128]),
                               ALU.mult, [pkA, "cst"], [("cbm", d)])
                        WC = []
                        for d in range(2):
                            wt_, wtk = Wtr[d].next()
                            cs_, csk = Cscr[d].next()
                            WC.append((wt_, wtk, cs_, csk))
                            psT, pkT = pring.next()
                            mm(psT[0:16, 0:128], ctk[d][:, c, :], identF, True, True, [("ctk", d), "cst"], [pkT])
                            cT, cTk = cumTr.next()
                            cp("act", cT[0:16, :], psT[0:16, 0:128], [pkT], [cTk])
                            qs = []
                            for q in range(4):
                                psc, pkc = pring.next()
                                for hh in range(4):
                                    mm(psc[:, hh * 128:(hh + 1) * 128], sel[:, 4 * q + hh, :], cT[0:16, :], True, True, ["cst", cTk], [pkc])
                                crn, arn = crs.next(), arg.next()
                                qs.append((psc, pkc, crn, arn, arn, crn))
                            for q in range(4):
                                psc, pkc, (cr, crk), (ar, ark), _, _ = qs[q]
                                cp("dve", cr[:], psc[:], [pkc], [crk])
                                for hh in range(4):
                                    ts("dve", ar[:, hh * 128:(hh + 1) * 128], psc[:, hh * 128:(hh + 1) * 128], ctk[d][:, c, 4 * q + hh:4 * q + hh + 1], 0.0,
                                       ALU.subtract, ALU.min, [pkc, ("ctk", d)], [ark])
                            for q in range(4):
                                _, _, (cr, crk), (ar, ark), (et, etk), (ec, eck) = qs[q]
                                for hh in range(4):
                                    act(et[:, hh * 128:(hh + 1) * 128], ar[:, hh * 128:(hh + 1) * 128], AF.Exp, [ark, ("lndt", d)], [etk],
                                        bias=lndt[d][:, c, 4 * q + hh:4 * q + hh + 1])
                                act(ec[:], cr[:], AF.Exp, [crk], [eck])
                            for q in range(4):
                                g = q // 2
                                _, _, _, _, (et, etk), (ec, eck) = qs[q]
                                tt("dve", wt_[:, 4 * q:4 * q + 4, :], et[:].rearrange("p (h i) -> p h i", h=4),
                                   cbm[d][:, g, :].unsqueeze(1).to_broadcast([128, 4, 128]), ALU.mult, [etk, ("cbm", d)], [wtk])
                                tt("dve", cs_[:, 4 * q:4 * q + 4, :], ec[:].rearrange("p (h i) -> p h i", h=4),
                                   xs3[:, 10 + g, :].unsqueeze(1).to_broadcast([128, 4, 128]), ALU.mult, [eck, xs3k], [csk])
                        return dict(c=c, xs3=(xs3, xs3k), xk2=(xk2, xk2k), zs=(zs, zsk), he=he, WC=WC)

                    def stageB(cx):
                        c = cx["c"]
                        tk0 = c * 128
                        xs3, xs3k = cx["xs3"]
                        xk2, xk2k = cx["xk2"]
                        zs, zsk = cx["zs"]
                        he, WC = cx["he"], cx["WC"]
                        psY = [plong.next(), plong.next()]
                        for h in range(16):
                            kc, half = h // 2, h % 2
                            pY, pYk = psY[kc // 4]
                            out = pY[half * 64:(half + 1) * 64, (kc % 4) * 128:(kc % 4 + 1) * 128]
                            for d in range(2):
                                wt_, wtk, cs_, csk = WC[d]
                                mm(out, xk2[:, h * 64:(h + 1) * 64], wt_[:, h, :], d == 0, False, [xk2k, wtk], [pYk])
                                mm(out, he[d][0][:, h * 64:(h + 1) * 64], cs_[:, h, :], False, d == 1, [he[d][1], csk], [pYk])
                        for kc in range(8):
                            pY, pYk = psY[kc // 4]
                            stt("dve", yg[:, kc, :], xs3[:, kc, :], dsum[:, l, kc:kc + 1], pY[:, (kc % 4) * 128:(kc % 4 + 1) * 128], ALU.mult, ALU.add,
                                [xs3k, "dsum", pYk], ["yg"])
                        tt("dve", yg[:], yg[:], zs[:], ALU.mult, ["yg", zsk], ["yg"])
                        ps, pk = pring.next()
                        for kc in range(8):
                            act(sqy[:, kc % 2, :], yg[:, kc, :], AF.Square, ["yg"], [("sqy", kc % 2)])
                            mm(ps[:, 0:128], ones, sqy[:, kc % 2, :], kc == 0, kc == 7, ["cst", ("sqy", kc % 2)], [pk])
                        rstd_from_ssq(ps[:, 0:128], 1024.0, rsy[:], lny[:], [pk], ["rsy"], "lny")
                        yn, ynk = ynr.next()
                        for kc in range(8):
                            stt("dve", yn[:, kc, :], yg[:, kc, :], snw[:, l, kc:kc + 1], rsy[:], ALU.mult, ALU.mult, ["yg", "rsy", "prm"], [ynk])
                        store(YBT[:, tk0:tk0 + 128].rearrange("(k p) t -> p k t", p=128), yn[:], [ynk], [])

                    ctxs = {0: stageA(0)}
                    for c in range(NCH):
                        if c + 1 < NCH:
                            ctxs[c + 1] = stageA(c + 1)
                        stageB(ctxs.pop(c))
                    S.flush()
            if stop == "ssd":
                break
            with contextlib.ExitStack() as st:
                ckvall = sb(st, "ckvall", [128, 2, 4608], BF16)
                KTr = [sb(st, "KT%d" % i, [128, 4608], BF16) for i in range(2)]
                VA = [sb(st, "VA%d" % i, [128, 36, 128], BF16) for i in range(2)]
                qhr = Ring([sb(st, "qh%d" % i, [128, 4096], BF16) for i in range(2)], "qh")
                PT = Ring([sb(st, "PT%d" % i, [128, 512], BF16) for i in range(4)], "PT")
                Lt = sb(st, "Lt", [128, 512], F32)
                Rt = sb(st, "Rt", [128, 512], F32)
                ATr = Ring([sb(st, "AT%d" % i, [128, 512], BF16) for i in range(2)], "AT")
                S.op("pool", lambda e: e.memset(VA[0][:, :, 64:128], 1.0), writes=[("VA", 0)])
                S.op("pool", lambda e: e.memset(VA[1][:, :, 0:64], 1.0), writes=[("VA", 1)])
                S.dma("pool", lambda e: e.dma_start(out=ckvall[:, :, 0:512], in_=cckvT[l].rearrange("(k p) t -> p k t", p=128)), writes=["ckvall"])
                for i in range(2):
                    S.dma("pool", lambda e, i=i: e.dma_start(out=KTr[i][64:96, 0:512], in_=ckrT[l]), writes=[("KT", i)])
                if l + 1 < nl:
                    stgA = Ring([sb(st, "cva%d" % i, [128, 6144], BF16) for i in range(3)], "cva")
                    convert_weights(l + 1, stgA)
                for (nctx, k0, nlat, q0, nq, qblk) in ((512, 0, TS, 0, TS, 512), (0, TS, 256, TS, 256, 256), (0, TS + 256, 256, TS + 256, 256, 256)):
                    nk = nctx + nlat
                    ntile = nk // 128
                    load(ckvall[:, :, nctx:nk], CKVT[:, k0:k0 + nlat].rearrange("(k p) t -> p k t", p=128), [], ["ckvall"])
                    for i in range(2):
                        load(KTr[i][64:96, nctx:nk], KRT[:, k0:k0 + nlat], [], [("KT", i)])
                    for h in range(8):
                        par = h % 2
                        va, vak = VA[par], ("VA", par)
                        KT, ktk = KTr[par], ("KT", par)
                        voff = par * 64
                        for kb in range((nk + 511) // 512):
                            w = min(512, nk - kb * 512)
                            ps, pk = pring.next()
                            for kc in range(2):
                                mm(ps[0:64, 0:w], ukvw[:, kc, h * 128:h * 128 + 64], ckvall[:, kc, kb * 512:kb * 512 + w], kc == 0, kc == 1, ["ukvw", "ckvall"], [pk])
                            cp("dve", KT[0:64, kb * 512:kb * 512 + w], ps[0:64, 0:w], [pk], [ktk])
                        for tg in range(0, ntile, 8):
                            nt = min(8, ntile - tg)
                            ps, pk = pring.next()
                            for j in range(nt):
                                for kc in range(2):
                                    mm(ps[:, j * 64:(j + 1) * 64], ckvall[:, kc, (tg + j) * 128:(tg + j + 1) * 128], ukvw[:, kc, h * 128 + 64:h * 128 + 128],
                                       kc == 0, kc == 1, ["ukvw", "ckvall"], [pk])
                            cp("dve", va[:, tg:tg + nt, voff:voff + 64], ps[:, 0:nt * 64].rearrange("p (t d) -> p t d", d=64), [pk], [vak])
                        qh, qhk = qhr.next()
                        load(qh[0:96, 0:nq], QT[h, :, q0:q0 + nq], [], [qhk])
                        items = [(qb, t) for qb in range(nq // qblk) for t in range(ntile)]
                        SKEW = 3
                        pend = {}
                        cur = {}
                        for i in range(len(items) + SKEW):
                            if i < len(items):
                                qb, t = items[i]
                                psS, pkS = pring.next()
                                mm(psS[:, 0:qblk], KT[0:96, t * 128:(t + 1) * 128], qh[0:96, qb * qblk:(qb + 1) * qblk], True, True, [ktk, qhk], [pkS])
                                pend[i] = (psS, pkS)
                            if i >= SKEW:
                                qb, t = items[i - SKEW]
                                psS, pkS = pend.pop(i - SKEW)
                                if t == 0:
                                    cur[qb] = plong.next()
                                psO, pkO = cur[qb]
                                pt, ptk = PT.next()
                                act(pt[:, 0:qblk], psS[:, 0:qblk], AF.Exp, [pkS], [ptk], scale=SCALE)
                                mm(psO[:, 0:qblk], va[:, t, :], pt[:, 0:qblk], t == 0, t == ntile - 1, [vak, ptk], [pkO])
                                if t == ntile - 1:
                                    orow, drow = voff, 64 - voff
                                    act(Lt[orow:orow + 64, 0:qblk], psO[drow:drow + 64, 0:qblk], AF.Ln, [pkO], ["Lt"])
                                    act(Rt[orow:orow + 64, 0:qblk], Lt[orow:orow + 64, 0:qblk], AF.Exp, ["Lt"], ["Rt"], scale=-1.0)
                                    at, atk = ATr.next()
                                    tt("dve", at[orow:orow + 64, 0:qblk], psO[orow:orow + 64, 0:qblk], Rt[orow:orow + 64, 0:qblk], ALU.mult, [pkO, "Rt"], [atk])
                                    r0 = (h // 2) * 128 + orow
                                    store(ATT[r0:r0 + 64, q0 + qb * qblk:q0 + (qb + 1) * qblk], at[orow:orow + 64, 0:qblk], [atk], [])
                S.flush()
            if stop == "attn":
                break
            with contextlib.ExitStack() as st:
                xt = sb(st, "xt3", [128, 8, 512], F32)
                ht = sb(st, "ht3", [128, 8, 512], BF16)
                va_ = sb(st, "va3", [128, 4, 512], BF16)
                yb_ = sb(st, "yb3", [128, 8, 512], BF16)
                at_ = sb(st, "at3", [128, 4, 512], BF16)
                macc = sb(st, "macc", [128, 4, 512], F32)
                sigr = Ring([sb(st, "sig%d" % i, [128, 512], F32) for i in range(2)], "sig")
                tmr = Ring([sb(st, "tm%d" % i, [128, 512], F32) for i in range(2)], "tm")
                merged = sb(st, "merged", [128, 8, 512], BF16)
                h2 = sb(st, "h2", [128, 8, 512], BF16)
                gt = sb(st, "gt", [128, 22, 512], BF16)
                sar = Ring([sb(st, "sa%d" % i, [128, 512], F32) for i in range(2)], "sa")
                sqt = sb(st, "sqt3", [128, 2, 512], F32)
                rst = sb(st, "rst3", [128, 512], F32)
                lnt = sb(st, "lnt3", [128, 512], F32)
                tmpf = sb(st, "tmpf3", [128, 2, 512], F32)
                last = (l == nl - 1)
                yo = sb(st, "yo", [128, 8, 512], F32) if last else None
                for b in range(NB):
                    r = 0 if b < 8 else 1
                    t0 = b * 512
                    fm = lambda dr: dr[:, t0:t0 + 512].rearrange("(k p) t -> p k t", p=128)
                    def p3_loads(bb):
                        f2 = lambda dr: dr[:, bb * 512:(bb + 1) * 512].rearrange("(k p) t -> p k t", p=128)
                        load(ht[:], f2(HT), [], ["ht"])
                        load(va_[:], f2(VAT), [], ["va_"])
                        load(yb_[:], f2(YBT), [], ["yb_"])
                        load(at_[:], f2(ATT), [], ["at_"])

                    if b == 0:
                        p3_loads(0)
                    load(xt[:], fm(xsrc), [], ["xt"])
                    for cg in range(2):
                        for br, (wsrc, nkc, rt_, rk) in enumerate((("wa", 4, va_, "va_"), ("wb", 8, yb_, "yb_"), ("wc", 4, at_, "at_"))):
                            wy, wyk = wloadb(WB[l % 2][wsrc][:, cg * 512:(cg + 1) * 512], nkc, 512)
                            c0 = C_G + br * 1024 + cg * 512
                            wg, wgk = wloadb(WB[l % 2]["win"][:, c0:c0 + 512], 8, 512)
                            for oc in range(4):
                                psYy, pky = pring.next()
                                for kc in range(nkc):
                                    mm(psYy[:], wy[:, kc, oc * 128:(oc + 1) * 128], rt_[:, kc, :], kc == 0, kc == nkc - 1, [wyk, rk], [pky])
                                psG, pkg = pring.next()
                                for kc in range(8):
                                    mm(psG[:], wg[:, kc, oc * 128:(oc + 1) * 128], ht[:, kc, :], kc == 0, kc == 7, [wgk, "ht"], [pkg])
                                sg, sgk = sigr.next()
                                act(sg[:], psG[:], AF.Sigmoid, [pkg], [sgk])
                                if br == 0:
                                    tt("dve", macc[:, oc, :], psYy[:], sg[:], ALU.mult, [pky, sgk], [("macc", oc)])
                                else:
                                    tm, tmk = tmr.next()
                                    tt("dve", tm[:], psYy[:], sg[:], ALU.mult, [pky, sgk], [tmk])
                                    if br == 1:
                                        tt("dve", macc[:, oc, :], macc[:, oc, :], tm[:], ALU.add, [("macc", oc), tmk], [("macc", oc)])
                                    else:
                                        tt("dve", merged[:, cg * 4 + oc, :], macc[:, oc, :], tm[:], ALU.add, [("macc", oc), tmk], ["merged"])
                    if b + 1 < NB:
                        p3_loads(b + 1)
                    for cg in range(2):
                        wo, wok = wloadb(WB[l % 2]["wo"][:, cg * 512:(cg + 1) * 512], 8, 512)
                        for oc in range(4):
                            ps, pk = pring.next()
                            for kc in range(8):
                                mm(ps[:], wo[:, kc, oc * 128:(oc + 1) * 128], merged[:, kc, :], kc == 0, kc == 7, [wok, "merged"], [pk])
                            o8 = cg * 4 + oc
                            stt("dve", xt[:, o8, :], ps[:], modcol(l, 2, o8, r), xt[:, o8, :], ALU.mult, ALU.add, [pk, "MOD", "xt"], ["xt"])
                    norm_mod((sqt, rst, lnt, tmpf), xt, h2, lambda kc: A2[:, l, kc, r:r + 1], lambda kc: modcol(l, 3, kc, r), "xt", "h2")
                    for fg in range(6):
                        ncol = 512 if fg < 5 else 256
                        w1, w1k = wloadb(WB[l % 2]["wf1"][:, fg * 512:fg * 512 + ncol], 8, ncol)
                        w3, w3k = wloadb(WB[l % 2]["wf3"][:, fg * 512:fg * 512 + ncol], 8, ncol)
                        for oc in range(ncol // 128):
                            j = fg * 4 + oc
                            psA, pka = pring.next()
                            for kc in range(8):
                                mm(psA[:], w1[:, kc, oc * 128:(oc + 1) * 128], h2[:, kc, :], kc == 0, kc == 7, [w1k, "h2"], [pka])
                            psB, pkb = pring.next()
                            for kc in range(8):
                                mm(psB[:], w3[:, kc, oc * 128:(oc + 1) * 128], h2[:, kc, :], kc == 0, kc == 7, [w3k, "h2"], [pkb])
                            sa, sak = sar.next()
                            act(sa[:], psA[:], AF.Silu, [pka], [sak])
                            tt("dve", gt[:, j, :], sa[:], psB[:], ALU.mult, [sak, pkb], [("gt", j)])
                    for cg in range(4):
                        w2, w2k = wloadb(WB[l % 2]["wf2"][:, cg * 256:(cg + 1) * 256], 22, 256)
                        for oc in range(2):
                            ps, pk = pring.next()
                            for j in range(22):
                                mm(ps[:], w2[:, j, oc * 128:(oc + 1) * 128], gt[:, j, :], j == 0, j == 21, [w2k, ("gt", j)], [pk])
                            o8 = cg * 2 + oc
                            stt("dve", xt[:, o8, :], ps[:], modcol(l, 5, o8, r), xt[:, o8, :], ALU.mult, ALU.add, [pk, "MOD", "xt"], ["xt"])
                    if not last:
                        store(fm(XT), xt[:], ["xt"], [])
                    else:
                        norm_mod((sqt, rst, lnt, tmpf), xt, yo, lambda kc: fnw[:, kc:kc + 1], None, "xt", "yo")
                        store(fm(yT), yo[:], ["yo"], [])
                    if stop == "p3" and "XT" in dump:
                        store(fm(XT), xt[:], ["xt"], [])
                S.flush()
        S.flush(final=True)
    return nc


def rope_tables():
    n_rows = TS // 64
    row = np.repeat(np.arange(n_rows, dtype=np.float32), 64)
    col = np.tile(np.arange(64, dtype=np.float32), n_rows)
    inv = (np.float32(10000.0) ** (-np.arange(8, dtype=np.float32) / np.float32(8))).astype(np.float32)
    ang = np.concatenate([row[:, None] * inv, col[:, None] * inv], axis=-1).astype(np.float32)
    cos, sin = np.cos(ang).astype(np.float32), np.sin(ang).astype(np.float32)
    C = np.concatenate([cos, cos], axis=1).T
    Sg = np.concatenate([-sin, sin], axis=1).T
    return np.ascontiguousarray(C), np.ascontiguousarray(Sg)


def make_in_maps(inp, nl=DEPTH):
    f = lambda k: np.asarray(inp[k], np.float32)
    ropeC, ropeS = rope_tables()
    k = np.arange(128)
    triF = (k[:, None] <= k[None, :]).astype(np.float32)
    triB = (k[:, None] >= k[None, :]).astype(np.float32)
    selc = np.zeros((128, 16, 128), np.float32)
    for hh in range(16):
        selc[hh, hh, :] = 1.0
    cst = np.concatenate([triF, triB, np.ones((128, 128), np.float32), np.eye(128, dtype=np.float32), selc.reshape(128, 2048)], axis=1)
    prm = pack_params(inp)
    w_in = f("w_in")
    w_kr2 = np.zeros((DEPTH, D, 2, 96), np.float32)
    krc = w_in[:, :, C_KR:C_KR + 32]
    w_kr2[:, :, 0, 64:96] = krc
    w_kr2[:, :, 1, 64:80] = krc[:, :, 16:32]
    w_kr2[:, :, 1, 80:96] = krc[:, :, 0:16]
    wuq = f("w_uq").reshape(DEPTH, 256, 8, 96)
    w_uq2 = np.zeros((DEPTH, 256, 2, 8, 96), np.float32)
    w_uq2[:, :, 0] = wuq
    w_uq2[:, :, 1, :, 0:64] = wuq[..., 0:64]
    w_uq2[:, :, 1, :, 64:80] = wuq[..., 80:96]
    w_uq2[:, :, 1, :, 80:96] = wuq[..., 64:80]
    shared = {
        "ropeC": ropeC, "ropeS": ropeS, "cst": cst, "prm": prm, "w_in": w_in, "w_kr2": w_kr2, "w_uq2": w_uq2,
        "w_ukv": f("w_ukv"), "w_a_out": f("w_a_out"), "w_b_out": f("w_b_out"), "w_c_out": f("w_c_out"),
        "w_o": f("w_o"), "w_ada": f("w_ada"), "w_ff1": f("w_ff1"), "w_ff3": f("w_ff3"), "w_ff2": f("w_ff2"),
    }
    for k in ("w_in", "w_kr2", "w_uq2", "w_ukv", "w_a_out", "w_b_out", "w_c_out", "w_o", "w_ada", "w_ff1", "w_ff3", "w_ff2"):
        shared[k] = np.ascontiguousarray(shared[k][:nl])
    xs, xp, c, cctx = f("x_sample"), f("x_prompt"), f("c"), f("c_ctx")
    cckv, ckr = f("cache_ckv"), f("cache_krope")
    sf, sbw = f("state_ssm_fwd"), f("state_ssm_bwd")
    maps = []
    for r in range(8):
        bs = r % 4
        xT0 = np.concatenate([xs[bs].T, xp[2 * r].T, xp[2 * r + 1].T], axis=1)
        cd = np.stack([c[bs], cctx], axis=1)
        cd = cd.reshape(8, 128, 2).transpose(1, 0, 2)
        h0 = np.stack([sf[bs], sbw[bs]], axis=1)
        h0 = h0.transpose(0, 1, 4, 2, 3).reshape(DEPTH, 2, 128, 1024)
        m = dict(shared)
        m.update({
            "xT0": np.ascontiguousarray(xT0), "cond": np.ascontiguousarray(cd),
            "cckvT": np.ascontiguousarray(cckv[bs].transpose(0, 2, 1)),
            "ckrT": np.ascontiguousarray(ckr[bs].transpose(0, 2, 1)),
            "h0": np.ascontiguousarray(h0),
        })
        maps.append(m)
    return maps


_NC_CACHE = {}


def kernel(**inputs):
    maps = make_in_maps(inputs)
    if "nc" not in _NC_CACHE:
        _NC_CACHE["nc"] = build_program()
    nc = _NC_CACHE["nc"]
    res = run_bass_kernel_spmd(nc, maps, core_ids=list(range(8)))
    R = res.results
    y_prompt = np.zeros((16, 256, D), np.float32)
    y_sample = np.zeros((4, TS, D), np.float32)
    new_ckv = np.zeros((16, DEPTH, 256, 256), np.float32)
    new_kr = np.zeros((16, DEPTH, 256, 32), np.float32)
    new_f = np.zeros((16, DEPTH, 16, 64, 128), np.float32)
    new_b = np.zeros((16, DEPTH, 16, 64, 128), np.float32)
    for r in range(8):
        yT = R[r]["yT"]
        if r < 4:
            y_sample[r] = yT[:, :TS].T
        for s in range(2):
            q = 2 * r + s
            y_prompt[q] = yT[:, TS + s * 256:TS + (s + 1) * 256].T
            new_ckv[q] = R[r]["nckvT"][:, :, s * 256:(s + 1) * 256].transpose(0, 2, 1)
            new_kr[q] = R[r]["nkrT"][:, :, s * 256:(s + 1) * 256].transpose(0, 2, 1)
            st = R[r]["nssm"][:, :, s].reshape(DEPTH, 2, 128, 16, 64).transpose(0, 1, 3, 4, 2)
            new_f[q] = st[:, 0]
            new_b[q] = st[:, 1]
    return (y_prompt, y_sample, new_ckv, new_kr, new_f, new_b)
```
